# Optimizing a Trainium2 kernel written in Bass

```python
import math
import jax
import jax.numpy as jnp
from jax import lax
import numpy as np

D_MODEL = 1024
BATCH = 8
SEQ = 2048
DEPTH = 4
DEC_BATCH = 128
DEC_SEQ = 4
PAST_LEN = 8192
PAGE_SIZE = 128

N_MIXERS = 3
LAYER_KIND = tuple(i % N_MIXERS for i in range(DEPTH))
LAYER_SLOT = tuple(LAYER_KIND[:i].count(LAYER_KIND[i]) for i in range(DEPTH))
N_ATTN = LAYER_KIND.count(0)
N_HGRN = LAYER_KIND.count(1)
N_SSD = LAYER_KIND.count(2)
RMS_EPS = 1e-6

ATT_HEAD_DIM = 64
ATT_HEADS = D_MODEL // ATT_HEAD_DIM
ATT_KV_HEADS = 4
ATT_GROUP = ATT_HEADS // ATT_KV_HEADS
ATT_Q_DIM = ATT_HEADS * ATT_HEAD_DIM
ATT_KV_DIM = ATT_KV_HEADS * ATT_HEAD_DIM
WINDOW = 128

HG_HEADS = 8
HG_EXPAND = 128
HG_HEAD_V = D_MODEL // HG_HEADS
HG_F_DIM = HG_HEADS * HG_EXPAND
HG_V_DIM = HG_HEADS * HG_HEAD_V
HG_CHUNK = 16

SSD_D_INNER = 2 * D_MODEL
SSD_HEAD_DIM = 64
SSD_HEADS = SSD_D_INNER // SSD_HEAD_DIM
SSD_STATE = 128
SSD_GROUPS = 4
SSD_HPG = SSD_HEADS // SSD_GROUPS
SSD_CONV = 4
SSD_CONV_DIM = SSD_D_INNER + 2 * SSD_GROUPS * SSD_STATE
SSD_IN_DIM = SSD_D_INNER + SSD_CONV_DIM + SSD_HEADS
SSD_CHUNK = 64

D_FF = 2816
FFN_CONV = 3

kernel_name = 'hybrid_swa_hgrn2_ssd_convffn_step'


def rmsnorm(x, w):
    xf = x.astype(jnp.float32)
    y = xf * lax.rsqrt(jnp.mean(xf * xf, axis=-1, keepdims=True) + RMS_EPS)
    return (y * w.astype(jnp.float32)).astype(x.dtype)


def group_rmsnorm(x, w):
    xf = x.astype(jnp.float32)
    y = xf * lax.rsqrt(jnp.mean(xf * xf, axis=-1, keepdims=True) + RMS_EPS)
    y = y.reshape(x.shape[:-2] + (-1,))
    return (y * w.astype(jnp.float32)).astype(x.dtype)


def pad_time(a, pad):
    return jnp.pad(a, [(0, 0), (0, pad)] + [(0, 0)] * (a.ndim - 2))


def causal_dwconv(u, w, b, buf):
    width = w.shape[0]
    bsz, t, ch = u.shape
    if buf is None:
        buf = jnp.zeros((bsz, width - 1, ch), u.dtype)
    full = jnp.concatenate([buf.astype(u.dtype), u], axis=1)
    y = b.astype(u.dtype)
    for k in range(width):
        y = y + full[:, k:k + t] * w[k]
    return y, full[:, t:]


def sink_attend(q, k, v, mask, sinks):
    s = jnp.einsum('bnqkgd,bnskd->bnkgqs', q, k).astype(jnp.float32) * (ATT_HEAD_DIM ** -0.5)
    s = jnp.where(mask[None, :, None, None], s, -jnp.inf)
    sink = sinks.astype(jnp.float32).reshape(ATT_KV_HEADS, ATT_GROUP)[None, None, :, :, None, None]
    m = jnp.maximum(jnp.max(s, axis=-1, keepdims=True), sink)
    p = jnp.exp(s - m)
    p = p / (jnp.sum(p, axis=-1, keepdims=True) + jnp.exp(sink - m))
    return jnp.einsum('bnkgqs,bnskd->bnqkgd', p.astype(v.dtype), v)


def swa_mixer(h, w_qkv, b_qkv, sinks, w_o, b_o, cache_k, cache_v):
    bsz, t, _ = h.shape
    qkv = h @ w_qkv + b_qkv
    q, k, v = jnp.split(qkv, [ATT_Q_DIM, ATT_Q_DIM + ATT_KV_DIM], axis=-1)
    q = q.reshape(bsz, t, ATT_KV_HEADS, ATT_GROUP, ATT_HEAD_DIM)
    k = k.reshape(bsz, t, ATT_KV_HEADS, ATT_HEAD_DIM)
    v = v.reshape(bsz, t, ATT_KV_HEADS, ATT_HEAD_DIM)
    if cache_k is None:
        nb = t // WINDOW
        qb = q.reshape(bsz, nb, WINDOW, ATT_KV_HEADS, ATT_GROUP, ATT_HEAD_DIM)
        kb = k.reshape(bsz, nb, WINDOW, ATT_KV_HEADS, ATT_HEAD_DIM)
        vb = v.reshape(bsz, nb, WINDOW, ATT_KV_HEADS, ATT_HEAD_DIM)
        kk = jnp.concatenate([jnp.concatenate([jnp.zeros_like(kb[:, :1]), kb[:, :-1]], axis=1), kb], axis=2)
        vv = jnp.concatenate([jnp.concatenate([jnp.zeros_like(vb[:, :1]), vb[:, :-1]], axis=1), vb], axis=2)
        rel_q = jnp.arange(WINDOW) + WINDOW
        rel_k = jnp.arange(2 * WINDOW)
        diff = rel_q[:, None] - rel_k[None, :]
        band = (diff >= 0) & (diff < WINDOW)
        has_prev = jnp.arange(nb) > 0
        mask = band[None] & (has_prev[:, None, None] | (rel_k >= WINDOW)[None, None, :])
        o = sink_attend(qb, kk, vv, mask, sinks)
        new_k, new_v = k[:, t - WINDOW:], v[:, t - WINDOW:]
    else:
        kk = jnp.concatenate([cache_k.astype(k.dtype), k], axis=1)
        vv = jnp.concatenate([cache_v.astype(v.dtype), v], axis=1)
        qpos = jnp.arange(t)
        kpos = jnp.arange(WINDOW + t) - WINDOW
        diff = qpos[:, None] - kpos[None, :]
        mask = (diff >= 0) & (diff < WINDOW) & (kpos + PAST_LEN >= 0)[None, :]
        o = sink_attend(q[:, None], kk[:, None], vv[:, None], mask[None], sinks)
        new_k, new_v = kk[:, t:], vv[:, t:]
    o = o.reshape(bsz, t, ATT_Q_DIM)
    return o @ w_o + b_o, new_k, new_v


def gla_chunked(q, k, v, g, s0, chunk):
    bsz, t, nh, dk = q.shape
    dv = v.shape[-1]
    c = min(chunk, t)
    pad = (-t) % c
    q, k, v, g = (pad_time(a.astype(jnp.float32), pad) for a in (q, k, v, g))
    n = (t + pad) // c
    q = q.reshape(bsz, n, c, nh, dk)
    k = k.reshape(bsz, n, c, nh, dk)
    v = v.reshape(bsz, n, c, nh, dv)
    b = jnp.cumsum(g.reshape(bsz, n, c, nh, dk), axis=2)
    b_last = b[:, :, -1]
    q_in = q * jnp.exp(b)
    k_in = k * jnp.exp(-b)
    tril = jnp.tril(jnp.ones((c, c), bool))
    a = jnp.where(tril, jnp.einsum('bnthk,bnshk->bnhts', q_in, k_in), 0.0)
    o_intra = jnp.einsum('bnhts,bnshv->bnthv', a, v)
    k_dec = k * jnp.exp(b_last[:, :, None] - b)
    decay = jnp.exp(b_last)
    s = jnp.zeros((bsz, nh, dk, dv), jnp.float32) if s0 is None else s0.astype(jnp.float32)

    def step(state, inp):
        q_c, k_c, v_c, d_c = inp
        o_c = jnp.einsum('bthk,bhkv->bthv', q_c, state)
        state = state * d_c[..., None] + jnp.einsum('bthk,bthv->bhkv', k_c, v_c)
        return state, o_c

    xs = tuple(jnp.moveaxis(a_, 1, 0) for a_ in (q_in, k_dec, v, decay))
    s, o_inter = lax.scan(step, s, xs)
    o = (o_intra + jnp.moveaxis(o_inter, 0, 1)).reshape(bsz, n * c, nh, dv)[:, :t]
    return o, s


def hgrn2_mixer(h, w_in, lb, norm_w, w_o, s0):
    bsz, t, _ = h.shape
    q, fr, i, g = jnp.split(h @ w_in, [HG_F_DIM, 2 * HG_F_DIM, 2 * HG_F_DIM + HG_V_DIM], axis=-1)
    q = jax.nn.silu(q).reshape(bsz, t, HG_HEADS, HG_EXPAND)
    f = lb + (1.0 - lb) * jax.nn.sigmoid(fr.astype(jnp.float32))
    log_f = jnp.log(f).reshape(bsz, t, HG_HEADS, HG_EXPAND)
    k = (1.0 - f).reshape(bsz, t, HG_HEADS, HG_EXPAND)
    v = i.reshape(bsz, t, HG_HEADS, HG_HEAD_V)
    o, s = gla_chunked(q, k, v, log_f, s0, HG_CHUNK)
    o = group_rmsnorm(o, norm_w) * jax.nn.silu(g.astype(jnp.float32))
    return o.astype(h.dtype) @ w_o, s.astype(h.dtype)


def ssd_chunked(x, dt, a_neg, bm, cm, s0, chunk):
    bsz, t, nh, hp = x.shape
    c = min(chunk, t)
    pad = (-t) % c
    x, dt, bm, cm = (pad_time(a.astype(jnp.float32), pad) for a in (x, dt, bm, cm))
    nc = (t + pad) // c
    xf = x.reshape(bsz, nc, c, SSD_GROUPS, SSD_HPG, hp)
    dtf = dt.reshape(bsz, nc, c, SSD_GROUPS, SSD_HPG)
    bf = bm.reshape(bsz, nc, c, SSD_GROUPS, SSD_STATE)
    cf = cm.reshape(bsz, nc, c, SSD_GROUPS, SSD_STATE)
    acum = jnp.cumsum(dtf * a_neg.reshape(SSD_GROUPS, SSD_HPG), axis=2)
    xdt = xf * dtf[..., None]
    seg = acum[:, :, :, None] - acum[:, :, None]
    tril = jnp.tril(jnp.ones((c, c), bool))[:, :, None, None]
    lmat = jnp.exp(jnp.where(tril, seg, -jnp.inf))
    cb = jnp.einsum('bctgn,bcsgn->bctsg', cf, bf)
    y_intra = jnp.einsum('bctsg,bctsgh,bcsghp->bctghp', cb, lmat, xdt)
    a_last = acum[:, :, -1]
    in_dec = jnp.exp(a_last[:, :, None] - acum)
    out_dec = jnp.exp(acum)
    if s0 is None:
        s = jnp.zeros((bsz, SSD_GROUPS, SSD_HPG, hp, SSD_STATE), jnp.float32)
    else:
        s = s0.astype(jnp.float32).reshape(bsz, SSD_GROUPS, SSD_HPG, hp, SSD_STATE)

    def step(state, inp):
        c_c, b_c, xdt_c, in_c, out_c, al_c = inp
        y_c = jnp.einsum('btgn,bghpn->btghp', c_c, state) * out_c[..., None]
        state = state * jnp.exp(al_c)[..., None, None] + jnp.einsum('btgn,btghp->bghpn', b_c, xdt_c * in_c[..., None])
        return state, y_c

    xs = tuple(jnp.moveaxis(a_, 1, 0) for a_ in (cf, bf, xdt, in_dec, out_dec, a_last))
    s, y_inter = lax.scan(step, s, xs)
    y = (y_intra + jnp.moveaxis(y_inter, 0, 1)).reshape(bsz, nc * c, nh, hp)[:, :t]
    return y, s.reshape(bsz, nh, hp, SSD_STATE)


def ssd_mixer(h, w_in, conv_w, conv_b, dt_bias, a_log, d_skip, norm_w, w_o, s0, conv_buf):
    bsz, t, _ = h.shape
    z, xbc, dt_raw = jnp.split(h @ w_in, [SSD_D_INNER, SSD_D_INNER + SSD_CONV_DIM], axis=-1)
    xbc, new_buf = causal_dwconv(xbc, conv_w, conv_b, conv_buf)
    xbc = jax.nn.silu(xbc)
    xs, bm, cm = jnp.split(xbc, [SSD_D_INNER, SSD_D_INNER + SSD_GROUPS * SSD_STATE], axis=-1)
    xs = xs.reshape(bsz, t, SSD_HEADS, SSD_HEAD_DIM)
    bm = bm.reshape(bsz, t, SSD_GROUPS, SSD_STATE)
    cm = cm.reshape(bsz, t, SSD_GROUPS, SSD_STATE)
    dt = jax.nn.softplus(dt_raw.astype(jnp.float32) + dt_bias.astype(jnp.float32))
    a_neg = -jnp.exp(a_log.astype(jnp.float32))
    y, s = ssd_chunked(xs, dt, a_neg, bm, cm, s0, SSD_CHUNK)
    y = y + d_skip.astype(jnp.float32)[:, None] * xs.astype(jnp.float32)
    y = y.reshape(bsz, t, SSD_D_INNER) * jax.nn.silu(z.astype(jnp.float32))
    y = group_rmsnorm(y.reshape(bsz, t, SSD_GROUPS, SSD_D_INNER // SSD_GROUPS), norm_w)
    return y.astype(h.dtype) @ w_o, s.astype(h.dtype), new_buf


def conv_ffn(h, w_up, conv_w, conv_b, w_down, buf):
    a, b = jnp.split(h @ w_up, 2, axis=-1)
    a, new_buf = causal_dwconv(a, conv_w, conv_b, buf)
    return (jax.nn.silu(a) * b) @ w_down, new_buf


def setup_inputs(seed: int = 0) -> dict:
    key = jax.random.key(seed)
    keys = iter(jax.random.split(key, 48))

    def nrm(shape, scale):
        return scale * jax.random.normal(next(keys), shape, jnp.float32)

    def gain(shape):
        return 1.0 + nrm(shape, 0.02)

    x_prompt = nrm((BATCH, SEQ, D_MODEL), 1.0)
    x_sample = nrm((DEC_BATCH, DEC_SEQ, D_MODEL), 1.0)
    cache_attn_k = nrm((N_ATTN, DEC_BATCH, WINDOW, ATT_KV_HEADS, ATT_HEAD_DIM), 1.0)
    cache_attn_v = nrm((N_ATTN, DEC_BATCH, WINDOW, ATT_KV_HEADS, ATT_HEAD_DIM), 1.0)
    state_hgrn = nrm((N_HGRN, DEC_BATCH, HG_HEADS, HG_EXPAND, HG_HEAD_V), 0.5)
    state_ssm = nrm((N_SSD, DEC_BATCH, SSD_HEADS, SSD_HEAD_DIM, SSD_STATE), 0.2)
    state_ssm_conv = nrm((N_SSD, DEC_BATCH, SSD_CONV - 1, SSD_CONV_DIM), 1.0)
    state_ffn_conv = nrm((DEPTH, DEC_BATCH, FFN_CONV - 1, D_FF), 1.0)
    norm_mix_w = gain((DEPTH, D_MODEL))
    norm_ffn_w = gain((DEPTH, D_MODEL))
    norm_final_w = gain((D_MODEL,))
    attn_w_qkv = nrm((N_ATTN, D_MODEL, ATT_Q_DIM + 2 * ATT_KV_DIM), D_MODEL ** -0.5)
    attn_b_qkv = nrm((N_ATTN, ATT_Q_DIM + 2 * ATT_KV_DIM), 0.02)
    attn_sinks = nrm((N_ATTN, ATT_HEADS), 0.5)
    attn_w_o = nrm((N_ATTN, ATT_Q_DIM, D_MODEL), ATT_Q_DIM ** -0.5)
    attn_b_o = nrm((N_ATTN, D_MODEL), 0.02)
    hgrn_w_in = nrm((N_HGRN, D_MODEL, 2 * HG_F_DIM + 2 * HG_V_DIM), D_MODEL ** -0.5)
    hgrn_lb_logits = nrm((DEPTH, HG_F_DIM), 0.1)
    hgrn_norm_w = gain((N_HGRN, HG_V_DIM))
    hgrn_w_o = nrm((N_HGRN, HG_V_DIM, D_MODEL), HG_V_DIM ** -0.5)
    ssd_w_in = nrm((N_SSD, D_MODEL, SSD_IN_DIM), D_MODEL ** -0.5)
    ssd_conv_w = nrm((N_SSD, SSD_CONV, SSD_CONV_DIM), SSD_CONV ** -0.5)
    ssd_conv_b = nrm((N_SSD, SSD_CONV_DIM), 0.02)
    dt0 = jnp.exp(jax.random.uniform(next(keys), (N_SSD, SSD_HEADS), jnp.float32, math.log(1e-3), math.log(1e-1)))
    ssd_dt_bias = dt0 + jnp.log(-jnp.expm1(-dt0))
    ssd_a_log = jnp.log(jax.random.uniform(next(keys), (N_SSD, SSD_HEADS), jnp.float32, 1.0, 16.0))
    ssd_d = gain((N_SSD, SSD_HEADS))
    ssd_norm_w = gain((N_SSD, SSD_D_INNER))
    ssd_w_o = nrm((N_SSD, SSD_D_INNER, D_MODEL), SSD_D_INNER ** -0.5)
    ffn_w_up = nrm((DEPTH, D_MODEL, 2 * D_FF), D_MODEL ** -0.5)
    ffn_conv_w = nrm((DEPTH, FFN_CONV, D_FF), FFN_CONV ** -0.5)
    ffn_conv_b = nrm((DEPTH, D_FF), 0.02)
    ffn_w_down = nrm((DEPTH, D_FF, D_MODEL), D_FF ** -0.5)
    return {
        'x_prompt': x_prompt, 'x_sample': x_sample,
        'cache_attn_k': cache_attn_k, 'cache_attn_v': cache_attn_v,
        'state_hgrn': state_hgrn, 'state_ssm': state_ssm,
        'state_ssm_conv': state_ssm_conv, 'state_ffn_conv': state_ffn_conv,
        'norm_mix_w': norm_mix_w, 'norm_ffn_w': norm_ffn_w, 'norm_final_w': norm_final_w,
        'attn_w_qkv': attn_w_qkv, 'attn_b_qkv': attn_b_qkv, 'attn_sinks': attn_sinks,
        'attn_w_o': attn_w_o, 'attn_b_o': attn_b_o,
        'hgrn_w_in': hgrn_w_in, 'hgrn_lb_logits': hgrn_lb_logits,
        'hgrn_norm_w': hgrn_norm_w, 'hgrn_w_o': hgrn_w_o,
        'ssd_w_in': ssd_w_in, 'ssd_conv_w': ssd_conv_w, 'ssd_conv_b': ssd_conv_b,
        'ssd_dt_bias': ssd_dt_bias, 'ssd_a_log': ssd_a_log, 'ssd_d': ssd_d,
        'ssd_norm_w': ssd_norm_w, 'ssd_w_o': ssd_w_o,
        'ffn_w_up': ffn_w_up, 'ffn_conv_w': ffn_conv_w, 'ffn_conv_b': ffn_conv_b,
        'ffn_w_down': ffn_w_down,
    }


def reference(x_prompt, x_sample, cache_attn_k, cache_attn_v, state_hgrn, state_ssm,
              state_ssm_conv, state_ffn_conv, norm_mix_w, norm_ffn_w, norm_final_w,
              attn_w_qkv, attn_b_qkv, attn_sinks, attn_w_o, attn_b_o,
              hgrn_w_in, hgrn_lb_logits, hgrn_norm_w, hgrn_w_o,
              ssd_w_in, ssd_conv_w, ssd_conv_b, ssd_dt_bias, ssd_a_log, ssd_d, ssd_norm_w, ssd_w_o,
              ffn_w_up, ffn_conv_w, ffn_conv_b, ffn_w_down):
    lb_cum = jnp.cumsum(jax.nn.softmax(hgrn_lb_logits.astype(jnp.float32), axis=0), axis=0)
    hgrn_lb = lb_cum - lb_cum[0]

    def pick(a, j):
        return None if a is None else a[j]

    def trunk(x, c_k, c_v, s_hg, s_ssm, s_sconv, s_fconv):
        nk, nv, nhg, nssm, nsconv, nfconv = [], [], [], [], [], []
        for i in range(DEPTH):
            kind, j = LAYER_KIND[i], LAYER_SLOT[i]
            h = rmsnorm(x, norm_mix_w[i])
            if kind == 0:
                y, k_new, v_new = swa_mixer(h, attn_w_qkv[j], attn_b_qkv[j], attn_sinks[j], attn_w_o[j],
                                            attn_b_o[j], pick(c_k, j), pick(c_v, j))
                nk.append(k_new)
                nv.append(v_new)
            elif kind == 1:
                y, s_new = hgrn2_mixer(h, hgrn_w_in[j], hgrn_lb[i], hgrn_norm_w[j], hgrn_w_o[j], pick(s_hg, j))
                nhg.append(s_new)
            else:
                y, s_new, conv_new = ssd_mixer(h, ssd_w_in[j], ssd_conv_w[j], ssd_conv_b[j], ssd_dt_bias[j],
                                               ssd_a_log[j], ssd_d[j], ssd_norm_w[j], ssd_w_o[j],
                                               pick(s_ssm, j), pick(s_sconv, j))
                nssm.append(s_new)
                nsconv.append(conv_new)
            x = x + y
            h = rmsnorm(x, norm_ffn_w[i])
            y, f_new = conv_ffn(h, ffn_w_up[i], ffn_conv_w[i], ffn_conv_b[i], ffn_w_down[i], pick(s_fconv, i))
            nfconv.append(f_new)
            x = x + y
        return (rmsnorm(x, norm_final_w), jnp.stack(nk), jnp.stack(nv), jnp.stack(nhg),
                jnp.stack(nssm), jnp.stack(nsconv), jnp.stack(nfconv))

    y_prompt, k_p, v_p, hg_p, ssm_p, sconv_p, fconv_p = trunk(x_prompt, None, None, None, None, None, None)
    y_sample, k_s, v_s, hg_s, ssm_s, sconv_s, fconv_s = trunk(x_sample, cache_attn_k, cache_attn_v, state_hgrn,
                                                               state_ssm, state_ssm_conv, state_ffn_conv)
    return (y_prompt, y_sample, k_p, v_p, hg_p, ssm_p, sconv_p, fconv_p,
            k_s, v_s, hg_s, ssm_s, sconv_s, fconv_s)
```

```python
import contextlib
import os
import numpy as np
import concourse.bass as bass
import concourse.mybir as mybir
from concourse.bass_utils import run_bass_kernel_spmd

F32 = mybir.dt.float32
BF16 = mybir.dt.bfloat16
AF = mybir.ActivationFunctionType
ALU = mybir.AluOpType
AX = mybir.AxisListType

EPOCH = 12000
NCORES = 8
D = 1024
KC = 8
TP = 2048
TS = 64
T = TP + TS
NB = 16
DFF = 2816
NF = 22
DEPTH = 4
EPS = 1e-6
TILES = [(0, 512), (512, 512), (1024, 512), (1536, 512), (2048, 64)]
GROUPS = [[0, 1], [2, 3, 4]]

CFG = {"layers": DEPTH, "mixers": (True, True, True, True), "ap": True, "as": True, "dd": True, "so": True}


class Buf:
    __slots__ = ("name", "last_w", "readers", "excl")

    def __init__(self, name="", excl=False):
        self.name = name
        self.excl = excl
        self.last_w = None
        self.readers = []


class Op:
    __slots__ = ("eng", "pos", "fn", "waits", "is_dma", "signal", "signum", "dma_sem", "dma_val", "dma_prev")

    def __init__(self, eng, pos, fn, is_dma):
        self.eng = eng
        self.pos = pos
        self.fn = fn
        self.waits = []
        self.is_dma = is_dma
        self.signal = False
        self.signum = None
        self.dma_sem = None
        self.dma_val = None
        self.dma_prev = None


class Sched:
    ENGS = ("pe", "act", "dve", "pool", "sp")

    def __init__(self, nc):
        self.nc = nc
        self.ops = {e: [] for e in self.ENGS}
        self.waited = {e: {p: -1 for p in self.ENGS} for e in self.ENGS}
        self.dma_waited = {e: set() for e in self.ENGS}
        self.dma_pool = {"sp": 16, "act": 20, "pool": 16}
        self.dma_rr = {e: 0 for e in self.ENGS}
        self.dma_last = {}
        self.out_dmas = []
        self.out_seen = 0
        self.anchor_fn = None

    def _add_wait(self, op, dep):
        if dep is None or dep is op:
            return
        e = op.eng
        if dep.is_dma:
            if id(dep) in self.dma_waited[e]:
                return
            self.dma_waited[e].add(id(dep))
            op.waits.append(dep)
            return
        if self.waited[e][dep.eng] >= dep.pos:
            return
        self.waited[e][dep.eng] = dep.pos
        dep.signal = True
        op.waits.append(dep)

    def _issue(self, eng, fn, reads, writes, is_dma, deps=()):
        op = Op(eng, len(self.ops[eng]), fn, is_dma)
        for b in reads:
            if b.last_w is not None:
                self._add_wait(op, b.last_w)
            if b.excl:
                lastr = {}
                for r in b.readers:
                    if r.eng != eng and (r.eng not in lastr or lastr[r.eng].pos < r.pos):
                        lastr[r.eng] = r
                for r in lastr.values():
                    self._add_wait(op, r)
        for b in writes:
            w = b.last_w
            if w is not None and (is_dma or w.is_dma or w.eng != eng):
                self._add_wait(op, w)
            lastr = {}
            for r in b.readers:
                if r.is_dma:
                    self._add_wait(op, r)
                elif is_dma or r.eng != eng:
                    if r.eng not in lastr or lastr[r.eng].pos < r.pos:
                        lastr[r.eng] = r
            for r in lastr.values():
                self._add_wait(op, r)
        for d in deps:
            self._add_wait(op, d)
        for b in reads:
            b.readers.append(op)
        for b in writes:
            b.last_w = op
            b.readers = []
        self.ops[eng].append(op)
        return op

    def op(self, eng, fn, reads=(), writes=(), deps=()):
        return self._issue(eng, fn, reads, writes, False, deps)

    def dma(self, queue, out, in_, reads=(), writes=(), is_output=False, deps=(), **kw):
        def fn(e, out=out, in_=in_, kw=kw):
            return e.dma_start(out=out, in_=(in_() if callable(in_) else in_), **kw)
        op = self._issue(queue, fn, reads, writes, True, deps)
        n = self.dma_pool[queue]
        idx = self.dma_rr[queue] % n
        self.dma_rr[queue] += 1
        key = (queue, idx)
        prev = self.dma_last.get(key)
        op.dma_sem = key
        op.dma_prev = prev
        op.dma_val = (prev.dma_val if prev is not None else 0) + 16
        self.dma_last[key] = op
        if is_output:
            self.out_dmas.append(op)
        return op

    def barrier(self, engs=("pe", "act", "dve", "pool", "sp")):
        lasts = []
        for e in engs:
            for o in reversed(self.ops[e]):
                if not o.is_dma and o.fn is not None:
                    lasts.append(o)
                    break
        anchor = Op("dve", len(self.ops["dve"]), self.anchor_fn, False)
        for l in lasts:
            if l.eng != "dve":
                self._add_wait(anchor, l)
        for o in self.out_dmas[self.out_seen:]:
            self._add_wait(anchor, o)
        self.out_seen = len(self.out_dmas)
        self.ops["dve"].append(anchor)
        for e in engs:
            if e == "dve":
                continue
            op = Op(e, len(self.ops[e]), None, False)
            self._add_wait(op, anchor)
            self.ops[e].append(op)

    def emit(self):
        nc = self.nc
        with contextlib.ExitStack() as st:
            eng_sems = {}
            for e in self.ENGS:
                n = 0
                for o in self.ops[e]:
                    if o.signal and not o.is_dma:
                        n += 1
                        o.signum = n
                nep = max((n + EPOCH - 1) // EPOCH, 1)
                eng_sems[e] = [st.enter_context(nc.semaphore(f"s_{e}_{i}")) for i in range(nep)]
            dma_sems = {}
            for q, n in self.dma_pool.items():
                for i in range(n):
                    if (q, i) in self.dma_last:
                        dma_sems[(q, i)] = st.enter_context(nc.semaphore(f"d_{q}_{i}"))

            def sem_of(o):
                ep = (o.signum - 1) // EPOCH
                return eng_sems[o.eng][ep], o.signum - ep * EPOCH

            block = st.enter_context(nc.Block())

            def make(e):
                def body(eng):
                    for o in self.ops[e]:
                        for d in o.waits:
                            if d.is_dma:
                                eng.wait_ge(dma_sems[d.dma_sem], d.dma_val)
                            else:
                                s, v = sem_of(d)
                                eng.wait_ge(s, v)
                        if o.is_dma:
                            if o.dma_prev is not None:
                                eng.wait_ge(dma_sems[o.dma_sem], o.dma_prev.dma_val)
                            o.fn(eng).then_inc(dma_sems[o.dma_sem], 16)
                        elif o.fn is not None:
                            ins = o.fn(eng)
                            if o.signal:
                                ins.then_inc(sem_of(o)[0], 1)
                    for o in self.out_dmas:
                        if o.eng == e:
                            eng.wait_ge(dma_sems[o.dma_sem], self.dma_last[o.dma_sem].dma_val)
                return body

            block.tensor(make("pe"))
            block.scalar(make("act"))
            block.vector(make("dve"))
            block.gpsimd(make("pool"))
            block.sync(make("sp"))


class Bump:
    def __init__(self, start, size):
        self.start, self.size, self.cur = start, size, start

    def take(self, nbytes):
        o = self.cur
        self.cur += (nbytes + 63) // 64 * 64
        assert self.cur - self.start <= self.size, ("arena overflow", self.cur - self.start, self.size)
        return o


class SWP:
    def __init__(self):
        self.q = []

    def step(self, stages):
        self.q.append(stages)
        n = len(self.q)
        for k in range(8):
            it = n - 1 - k
            if it >= 0 and k < len(self.q[it]):
                self.q[it][k]()

    def flush(self):
        n = len(self.q)
        more = True
        d = 1
        while more:
            more = False
            for k in range(d, 8):
                it = n - 1 - (k - d)
                if it >= 0 and k < len(self.q[it]):
                    self.q[it][k]()
                    more = True
            d += 1
        self.q = []


class Prog:
    def __init__(self):
        self.nc = bass.Bass("TRN2", target_bir_lowering=False)
        self.S = Sched(self.nc)
        self.wspecs = []
        self.wcache = {}
        self.woff = 0
        self.inputs = {}
        self.outputs = {}
        self._n = 0
        nc = self.nc
        self.base = (nc.sbuf_base + 63) // 64 * 64
        self.top = nc.sbuf_top

    def uid(self, p):
        self._n += 1
        return f"{p}{self._n}"

    def sb(self, off, shape, dt):
        return self.nc.alloc_sbuf_tensor_at(self.uid("t"), list(shape), dt, offset=off)

    def din(self, name, shape, builder):
        ap = self.nc.dram_tensor(name, list(shape), F32, kind="ExternalInput").ap()
        self.inputs[name] = (tuple(shape), builder)
        return ap

    def dout(self, name, shape):
        ap = self.nc.dram_tensor(name, list(shape), F32, kind="ExternalOutput").ap()
        self.outputs[name] = tuple(shape)
        return ap


def build():
    P = Prog()
    nc, S = P.nc, P.S
    NL = CFG["layers"]

    off = P.base
    X = P.sb(off, [128, KC, T], F32); off += KC * T * 4
    off = (off + 63) // 64 * 64
    CONST = off; off += 3072
    WB_SLOT, NWB = 4096, 6
    WB0 = off; off += WB_SLOT * NWB
    ARENA = off
    ARENA_SZ = P.top - ARENA
    assert ARENA_SZ > 96000, ARENA_SZ

    xbuf = [Buf(f"x{i}") for i in range(len(TILES))]

    coff = CONST
    ones_bf = P.sb(coff, [128, 128], BF16); coff += 256
    ident_bf = P.sb(coff, [128, 128], BF16); coff += 256
    ident_f = P.sb(coff, [128, 128], F32); coff += 512
    NORMW = P.sb(coff, [128, 9, KC], F32); coff += 9 * KC * 4
    FCW = P.sb(coff, [128, DEPTH, 4, NF], F32); coff += DEPTH * 4 * NF * 4
    epsb = P.sb(coff, [128, 1], F32); coff += 32
    oneb = P.sb(coff, [128, 1], F32); coff += 32
    dummy = P.sb(coff, [128, 1], F32); coff += 32
    S.anchor_fn = lambda e: e.memset(dummy[:], 0.0)
    cbuf = Buf("const")

    xT_d = P.din("xT", [D, T], lambda I, c: np.concatenate(
        [I["x_prompt"][c].T, I["x_sample"][c * NB:(c + 1) * NB].reshape(TS, D).T], axis=1))
    normw_d = P.din("normw", [128, 9, KC], lambda I, c: np.concatenate(
        [I["norm_mix_w"], I["norm_ffn_w"], I["norm_final_w"][None]], axis=0).reshape(9, KC, 128).transpose(2, 0, 1))
    fcw_d = P.din("fcw", [128, DEPTH, 4, NF], lambda I, c: np.concatenate(
        [I["ffn_conv_w"], I["ffn_conv_b"][:, None]], axis=1).reshape(DEPTH, 4, NF, 128).transpose(3, 0, 1, 2))
    ident_d = P.din("ident", [128, 128], lambda I, c: np.eye(128, dtype=np.float32))
    fbuf_d = P.din("fbuf", [DEPTH, 128, NF, NB, 2], lambda I, c: I["state_ffn_conv"][:, c * NB:(c + 1) * NB].reshape(
        DEPTH, NB, 2, NF, 128).transpose(0, 4, 3, 1, 2))
    yT_d = P.dout("yT", [D, T])
    fconv_d = P.dout("fconvT", [DEPTH, 128, NF, 2 + 2 * NB])

    wlist = []

    def wtile(key, n, builder):
        if key in P.wcache:
            return P.wcache[key]
        idx = len(wlist)
        wlist.append((P.woff, n, builder))
        P.woff += n
        P.wcache[key] = idx
        return idx

    wbbuf = [Buf(f"wb{i}") for i in range(NWB)]
    wctr = [0]
    WSTREAM = [None]

    def wget(idx, shape):
        o, n, _ = wlist[idx]
        i = wctr[0]; wctr[0] += 1
        t = i % NWB
        bf_t = P.sb(WB0 + t * WB_SLOT, [128, n], BF16)
        S.dma("pool", bf_t[:], (lambda o=o, n=n: WSTREAM[0][:, o:o + n]), writes=[wbbuf[t]])
        view = P.sb(WB0 + t * WB_SLOT, list(shape), BF16)
        return view, wbbuf[t]

    PSA = nc.alloc_psum_tensor("psa", [128, 6 * 512], F32)
    PST = nc.alloc_psum_tensor("pst", [128, 2 * 1024], BF16)
    psbuf = [Buf(f"ps{i}", excl=True) for i in range(8)]
    psctr = [0]
    pstctr = [0]

    PSMODE = [0]

    def ps():
        if PSMODE[0] == 2:
            i = psctr[0] % 6
        elif PSMODE[0]:
            i = 2 + psctr[0] % 2
        else:
            i = psctr[0] % 4
        psctr[0] += 1
        return PSA[:, i * 512:(i + 1) * 512], psbuf[i]

    def psb(i):
        return PSA[:, i * 512:(i + 1) * 512], psbuf[i]

    def ps2():
        if psctr[0] % 2:
            psctr[0] += 1
        i = psctr[0] % 4
        psctr[0] += 2
        return PSA[:, i * 512:(i + 2) * 512], [psbuf[i], psbuf[i + 1]]

    def ps_long():
        return PSA[:, 4 * 512:6 * 512], [psbuf[4], psbuf[5]]

    def pst():
        i = pstctr[0] % 2
        pstctr[0] += 1
        return PST[:, i * 1024:(i + 1) * 1024], psbuf[6 + i]

    S.dma("act", ident_f[:], ident_d, writes=[cbuf])
    S.dma("act", NORMW[:], normw_d, writes=[cbuf])
    S.dma("act", FCW[:], fcw_d, writes=[cbuf])
    S.op("dve", lambda e: e.memset(ones_bf[:], 1.0), writes=[cbuf])
    S.op("dve", lambda e: e.memset(epsb[:], EPS), writes=[cbuf])
    S.op("dve", lambda e: e.memset(oneb[:], 1.0), writes=[cbuf])
    S.op("dve", lambda e: e.tensor_copy(out=ident_bf[:], in_=ident_f[:]), reads=[cbuf], writes=[cbuf])
    xv = xT_d.rearrange("(c p) t -> p c t", p=128)
    for ti, (t0, tn) in enumerate(TILES):
        S.dma("act", X[:, :, t0:t0 + tn], xv[:, :, t0:t0 + tn], writes=[xbuf[ti]])

    def rmsnorm(norm_idx, tiles, H, hbufs, hoff, sq_off):
        SQ = [P.sb(sq_off, [128, KC, 512], BF16) for i in range(2)]
        RS = [P.sb(sq_off + KC * 512 * 2, [128, 512], F32)] * 2
        sqb = [Buf()] * 2
        rsb = [Buf()] * 2
        for j, ti in enumerate(tiles):
            t0, tn = TILES[ti]
            q = j % 2
            S.op("act", lambda e, q=q, t0=t0, tn=tn: e.activation(out=SQ[q][:, :, 0:tn], in_=X[:, :, t0:t0 + tn], func=AF.Square),
                 reads=[xbuf[ti]], writes=[sqb[q]])
            pt, pb = ps()
            for c in range(KC):
                S.op("pe", lambda e, q=q, c=c, tn=tn, pt=pt: e.matmul(pt[:, 0:tn], lhsT=ones_bf[:], rhs=SQ[q][:, c, 0:tn],
                                                                     start=(c == 0), stop=(c == KC - 1)),
                     reads=[sqb[q], cbuf], writes=[pb])
            S.op("act", lambda e, q=q, tn=tn, pt=pt: e.activation(out=RS[q][:, 0:tn], in_=pt[:, 0:tn], func=AF.Sqrt,
                                                                 bias=epsb[:], scale=1.0 / D),
                 reads=[pb, cbuf], writes=[rsb[q]])
            S.op("dve", lambda e, q=q, tn=tn: e.reciprocal(out=RS[q][:, 0:tn], in_=RS[q][:, 0:tn]), reads=[rsb[q]], writes=[rsb[q]])
            for c in range(KC):
                S.op("dve", lambda e, q=q, c=c, t0=t0, tn=tn: e.scalar_tensor_tensor(
                    out=H[:, c, t0 - hoff:t0 - hoff + tn], in0=X[:, c, t0:t0 + tn], scalar=NORMW[:, norm_idx, c:c + 1],
                    in1=RS[q][:, 0:tn], op0=ALU.mult, op1=ALU.mult),
                    reads=[xbuf[ti], rsb[q], cbuf], writes=[hbufs[ti]])

    def ffn(layer):
        up_idx = [wtile(("up", layer, f), 2 * KC * 128, (lambda I, layer=layer, f=f: np.stack(
            [I["ffn_w_up"][layer][:, f * 128:(f + 1) * 128].reshape(KC, 128, 128),
             I["ffn_w_up"][layer][:, DFF + f * 128:DFF + (f + 1) * 128].reshape(KC, 128, 128)], axis=1)
            .transpose(2, 0, 1, 3).reshape(128, -1))) for f in range(NF)]
        dn_idx = [[wtile(("dn", layer, c, hh), 11 * 128, (lambda I, layer=layer, c=c, hh=hh:
                   I["ffn_w_down"][layer][hh * 1408:(hh + 1) * 1408, c * 128:(c + 1) * 128].reshape(11, 128, 128)
                   .transpose(1, 0, 2).reshape(128, -1))) for hh in range(2)] for c in range(KC)]
        ar = Bump(ARENA, ARENA_SZ)
        TG = 1088
        HGs = [P.sb(ar.take(KC * TG * 2), [128, KC, TG], BF16) for _ in range(2)]
        GG = P.sb(ar.take(NF * TG * 2), [128, NF, TG], BF16)
        AFB = [P.sb(ar.take(1026 * 4), [128, 1026], F32) for i in range(2)]
        AFS = [P.sb(ar.take(NB * 6 * 4), [128, NB, 6], F32) for i in range(2)]
        U = [P.sb(ar.take(2048), [128, 512], F32) for i in range(3)]
        CAR = P.sb(ar.take(NF * 2 * 4), [128, NF, 2], F32)
        FC = P.sb(ar.take(NF * (2 + 2 * NB) * 4), [128, NF, 2 + 2 * NB], F32)
        FB = P.sb(ar.take(NF * NB * 2 * 4), [128, NF, NB, 2], F32)
        sq_off = ar.take(KC * 512 * 2 + 2048)
        hbs = [[Buf() for _ in TILES] for _ in range(2)]
        gb = [[Buf() for _ in TILES] for _ in range(NF)]
        afb = [Buf(), Buf()]
        afsb = [Buf(), Buf()]
        ub = [Buf() for _ in range(3)]
        carb, fcb, fbb = Buf(), Buf(), Buf()
        S.dma("act", FB[:], fbuf_d[layer], writes=[fbb])
        S.op("dve", lambda e: e.memset(CAR[:], 0.0), writes=[carb])
        uctr = 0
        PSMODE[0] = 2
        swp = SWP()
        rmsnorm(DEPTH + layer, GROUPS[0], HGs[0], hbs[0], TILES[GROUPS[0][0]][0], sq_off)
        for gi, grp in enumerate(GROUPS):
            g0 = TILES[grp[0]][0]
            HG, hb = HGs[gi], hbs[gi]
            for f in range(NF):
                wv, wb_ = wget(up_idx[f], [128, KC, 2, 128])
                q = f % 2
                A_, ab_ = AFB[q], afb[q]
                As_, asb_ = AFS[q], afsb[q]
                S.op("act", lambda e, A_=A_, f=f: e.copy(out=A_[:, 0:2], in_=CAR[:, f, :]), reads=[carb], writes=[ab_])
                if gi == 1:
                    S.op("act", lambda e, As_=As_, f=f: e.copy(out=As_[:, :, 0:2], in_=FB[:, f, :, :]), reads=[fbb], writes=[asb_])
                for ti in grp:
                    t0, tn = TILES[ti]
                    l0 = t0 - g0
                    samp = (ti == 4)
                    pa, pab = ps()
                    for c in range(KC):
                        S.op("pe", lambda e, c=c, pa=pa, tn=tn, l0=l0, wv=wv, HG=HG: e.matmul(
                            pa[:, 0:tn], lhsT=wv[:, c, 0, :], rhs=HG[:, c, l0:l0 + tn], start=(c == 0), stop=(c == KC - 1)),
                            reads=[wb_, hb[ti]], writes=[pab])
                    pbt, pbb = ps()
                    for c in range(KC):
                        S.op("pe", lambda e, c=c, pbt=pbt, tn=tn, l0=l0, wv=wv, HG=HG: e.matmul(
                            pbt[:, 0:tn], lhsT=wv[:, c, 1, :], rhs=HG[:, c, l0:l0 + tn], start=(c == 0), stop=(c == KC - 1)),
                            reads=[wb_, hb[ti]], writes=[pbb])
                    u, ubf = U[uctr % 3], ub[uctr % 3]; uctr += 1
                    if not samp:
                        pav = pa[:, 0:tn]
                        adst = A_[:, 2 + l0:2 + l0 + tn]
                        srcs = [A_[:, l0 + k:l0 + k + tn] for k in range(3)]
                        uo = u[:, 0:tn]
                        pbv = pbt[:, 0:tn]
                        go = GG[:, f, l0:l0 + tn]
                        cb_ = ab_
                    else:
                        pav = pa[:, 0:TS].rearrange("p (b t) -> p b t", t=4)
                        adst = As_[:, :, 2:6]
                        srcs = [As_[:, :, k:k + 4] for k in range(3)]
                        uo = u[:, 0:TS].rearrange("p (b t) -> p b t", t=4)
                        pbv = pbt[:, 0:TS].rearrange("p (b t) -> p b t", t=4)
                        go = GG[:, f, l0:l0 + TS].rearrange("p (b t) -> p b t", t=4)
                        cb_ = asb_
                    S.op("act", lambda e, adst=adst, pav=pav: e.activation(out=adst, in_=pav, func=AF.Copy), reads=[pab], writes=[cb_])
                    S.op("act", lambda e, uo=uo, pav=pav, f=f: e.activation(out=uo, in_=pav, func=AF.Identity, scale=FCW[:, layer, 2, f:f + 1], bias=FCW[:, layer, 3, f:f + 1]),
                         reads=[pab, cbuf], writes=[ubf])
                    for k in (1, 0):
                        S.op("dve", lambda e, uo=uo, s=srcs[k], f=f, k=k: e.scalar_tensor_tensor(
                            out=uo, in0=s, scalar=FCW[:, layer, k, f:f + 1], in1=uo, op0=ALU.mult, op1=ALU.add),
                            reads=[cb_, cbuf, ubf], writes=[ubf])
                    def tail(uo=uo, pbv=pbv, go=go, ubf=ubf, pbb=pbb, gbuf=gb[f][ti]):
                        S.op("act", lambda e: e.activation(out=uo, in_=uo, func=AF.Silu), reads=[ubf], writes=[ubf])
                        S.op("dve", lambda e: e.tensor_tensor(out=go, in0=uo, in1=pbv, op=ALU.mult), reads=[ubf, pbb], writes=[gbuf])
                    swp.step([lambda: None, tail])
                if gi == 0:
                    S.op("act", lambda e, A_=A_, f=f: e.copy(out=CAR[:, f, :], in_=A_[:, 1024:1026]), reads=[ab_], writes=[carb])
                else:
                    S.op("act", lambda e, A_=A_, f=f: e.copy(out=FC[:, f, 0:2], in_=A_[:, 1024:1026]), reads=[ab_], writes=[fcb])
                    S.op("act", lambda e, As_=As_, f=f: e.copy(out=FC[:, f, 2:2 + 2 * NB].rearrange("p (b j) -> p b j", j=2),
                                                                       in_=As_[:, :, 4:6]), reads=[asb_], writes=[fcb])
            swp.flush()
            if gi == 0:
                rmsnorm(DEPTH + layer, GROUPS[1], HGs[1], hbs[1], TILES[GROUPS[1][0]][0], sq_off)
            for c in range(KC):
                w0v, w0b = wget(dn_idx[c][0], [128, 11, 128])
                w1v, w1b = wget(dn_idx[c][1], [128, 11, 128])
                for ti in grp:
                    t0, tn = TILES[ti]
                    l0 = t0 - g0
                    pt, pb = ps()
                    for f in range(NF):
                        wv_, wbb_ = (w0v, w0b) if f < 11 else (w1v, w1b)
                        S.op("pe", lambda e, f=f, pt=pt, wv_=wv_, l0=l0, tn=tn: e.matmul(
                            pt[:, 0:tn], lhsT=wv_[:, f % 11, :], rhs=GG[:, f, l0:l0 + tn], start=(f == 0), stop=(f == NF - 1)),
                            reads=[wbb_, gb[f][ti]], writes=[pb])
                    S.op("dve", lambda e, c=c, pt=pt, t0=t0, tn=tn: e.tensor_tensor(
                        out=X[:, c, t0:t0 + tn], in0=X[:, c, t0:t0 + tn], in1=pt[:, 0:tn], op=ALU.add),
                        reads=[pb, xbuf[ti]], writes=[xbuf[ti]])
        PSMODE[0] = 0
        S.dma("act", fconv_d[layer], FC[:], reads=[fcb], is_output=True)

    NEG = -30000.0
    QPERM = [8 * jj + i + 4 * hf for jj in range(2) for i in range(4) for hf in range(2)]
    qcols = np.concatenate([np.arange(h * 64, (h + 1) * 64) for h in QPERM])

    def _maskp():
        q = np.arange(128)[:, None]; s_ = np.arange(256)[None, :]
        return np.where((s_ <= q + 128) & (s_ > q), 0.0, NEG).astype(np.float32)

    def _maskc():
        t_ = (np.arange(16) % 4)[:, None]; r = np.arange(128)[None, :]
        return np.where(r > t_, 0.0, NEG).astype(np.float32)

    def _maskn():
        t_ = (np.arange(16) % 4)[:, None, None]; b_ = np.arange(NB)[None, :, None]
        col = np.arange(64)[None, None, :]
        return np.where((col // 4 == b_) & (col % 4 <= t_), 0.0, NEG).astype(np.float32)

    if any(CFG["mixers"][l] and l % 3 == 0 for l in range(NL)):
        maskp_d = P.din("maskp", [128, 256], lambda I, c: _maskp())
        maskc_d = P.din("maskc", [16, 128], lambda I, c: _maskc())
        maskn_d = P.din("maskn", [16, NB, 64], lambda I, c: _maskn())
        abias_d = P.din("abias", [2, 128, 8 + 2 + 8], lambda I, c: np.stack([np.concatenate([
            I["attn_b_qkv"][s_][qcols].reshape(8, 128).T, I["attn_b_qkv"][s_][1024:1280].reshape(2, 128).T,
            I["attn_b_o"][s_].reshape(8, 128).T], axis=1) for s_ in range(2)]))
        vbias_d = P.din("vbias", [2, 128, 256], lambda I, c: np.broadcast_to(I["attn_b_qkv"][:, None, 1280:1536], (2, 128, 256)))
        sinkp_d = P.din("sinkp", [2, 128, 16], lambda I, c: np.broadcast_to(I["attn_sinks"][:, None, :], (2, 128, 16)))
        sinks_d = P.din("sinks", [2, 16, 4], lambda I, c: np.stack([np.repeat(I["attn_sinks"][s_].reshape(4, 4).T, 4, axis=0) for s_ in range(2)]))
        cacheKT_d = P.din("cacheKT", [2, NB, 2, 128, 128], lambda I, c: I["cache_attn_k"][:, c * NB:(c + 1) * NB].reshape(2, NB, 128, 2, 128).transpose(0, 1, 3, 4, 2))
        cacheV_d = P.din("cacheV", [2, NB, 128, 256], lambda I, c: I["cache_attn_v"][:, c * NB:(c + 1) * NB].reshape(2, NB, 128, 256))
        kTp_d = P.dout("kTp", [2, 2, 128, 128])
        vp_d = P.dout("vp", [2, 128, 256])
        kTs_d = P.dout("ks", [2, NB, 128, 256])
        cacheK_d = P.din("cacheK", [2, NB, 128, 256], lambda I, c: I["cache_attn_k"][:, c * NB:(c + 1) * NB].reshape(2, NB, 128, 256))
        vs_d = P.dout("vs", [2, NB, 128, 256])

    def attention(layer):
        sl = layer // 3
        Wqkv = lambda I: I["attn_w_qkv"][sl]
        q_idx = [wtile(("aq", sl, g), 2048, (lambda I, g=g: Wqkv(I)[:, qcols[g * 256:(g + 1) * 256]].reshape(KC, 128, 256).transpose(1, 0, 2).reshape(128, -1))) for g in range(4)]
        k_idx = wtile(("ak", sl), 2048, (lambda I: Wqkv(I)[:, 1024:1280].reshape(KC, 128, 256).transpose(1, 0, 2).reshape(128, -1)))
        v_idx = wtile(("av", sl), 2048, (lambda I: Wqkv(I)[:, 1280:1536].reshape(KC, 128, 256).transpose(1, 0, 2).reshape(128, -1)))
        qrows = qcols
        o_idx = [wtile(("ao", sl, g), 2048, (lambda I, g=g: I["attn_w_o"][sl][qrows][:, g * 256:(g + 1) * 256].reshape(KC, 128, 256).transpose(1, 0, 2).reshape(128, -1))) for g in range(4)]
        ar = Bump(ARENA, ARENA_SZ)
        HTs = [P.sb(ar.take(KC * 512 * 2), [128, KC, 512], BF16) for _ in range(2)]
        QT = P.sb(ar.take(KC * 512 * 2), [128, KC, 512], BF16)
        OT = [P.sb(ar.take(KC * 512 * 2), [128, KC, 512], BF16) for _ in range(2)]
        KT = P.sb(ar.take(2 * T * 2), [128, 2, T], BF16)
        VT = P.sb(ar.take(16 * 256 * 2), [128, 16, 256], BF16)
        VS = P.sb(ar.take(512), [64, 256], BF16)
        QS = P.sb(ar.take(1024), [128, NB, 2, 16], BF16)
        qsb = Buf()
        VO = [P.sb(ar.take(1024), [128, 256], F32)] * 2
        KO = P.sb(ar.take(2 * 128 * 4), [128, 2, 128], F32)
        KSO = P.sb(ar.take(2 * 64 * 4), [128, 2, 64], F32)
        KST = P.sb(ar.take(1024), [64, 256], F32)
        kstb = Buf()
        sq_off = ar.take(KC * 512 * 2 + 2048)
        SC = [P.sb(ar.take(4096), [128, 4, 256], F32) for _ in range(3)]
        PN = [P.sb(ar.take(2048), [128, 4, 256], BF16) for _ in range(2)]
        PTs = [P.sb(ar.take(2048), [128, 8, 128], BF16) for _ in range(2)]
        ST = [P.sb(ar.take(128), [128, 6, 4], F32) for _ in range(3)]
        MASKP = P.sb(ar.take(1024), [128, 256], F32)
        MASKC = P.sb(ar.take(512), [16, 128], F32)
        MASKN = P.sb(ar.take(NB * 64 * 4), [16, NB, 64], F32)
        AB = P.sb(ar.take(18 * 4), [128, 18], F32)
        VB = P.sb(ar.take(1024), [128, 256], F32)
        SKP = P.sb(ar.take(64), [128, 16], F32)
        SKP8 = P.sb(ar.take(64), [128, 16], F32)
        SKMX8 = P.sb(ar.take(16), [128, 4], F32)
        SKS = P.sb(ar.take(16), [16, 4], F32)
        SKS8 = P.sb(ar.take(16), [16, 4], F32)
        KCf = [P.sb(ar.take(1024), [128, 2, 128], F32) for _ in range(2)]
        VCf = [P.sb(ar.take(1024), [128, 256], F32) for _ in range(2)]
        KCb = [P.sb(ar.take(512), [128, 2, 128], BF16) for _ in range(2)]
        VCb = [P.sb(ar.take(512), [128, 256], BF16) for _ in range(2)]
        SS = [P.sb(ar.take(4 * 192 * 4), [16, 4, 192], F32) for _ in range(2)]
        PNS = [P.sb(ar.take(4 * 192 * 2), [16, 4, 192], BF16) for _ in range(2)]
        PTC = [P.sb(ar.take(128), [128, 4, 16], BF16) for _ in range(2)]
        PTN = [P.sb(ar.take(128), [64, 4, 16], BF16) for _ in range(2)]
        cb2 = Buf()
        S.dma("act", MASKP[:], maskp_d, writes=[cb2])
        S.dma("act", MASKC[:], maskc_d, writes=[cb2])
        S.dma("act", MASKN[:], maskn_d, writes=[cb2])
        S.dma("act", AB[:], abias_d[sl], writes=[cb2])
        S.dma("act", VB[:], vbias_d[sl], writes=[cb2])
        S.dma("act", SKP[:], sinkp_d[sl], writes=[cb2])
        S.dma("act", SKS[:], sinks_d[sl], writes=[cb2])
        S.op("dve", lambda e: e.tensor_scalar(out=SKP8[:], in0=SKP[:], scalar1=8.0, scalar2=None, op0=ALU.mult), reads=[cb2], writes=[cb2])
        S.op("dve", lambda e: e.tensor_scalar(out=SKS8[:], in0=SKS[:], scalar1=8.0, scalar2=None, op0=ALU.mult), reads=[cb2], writes=[cb2])
        S.op("dve", lambda e: e.tensor_reduce(out=SKMX8[:], in_=SKP8[:, :].rearrange("p (a b) -> p a b", a=4), op=ALU.max, axis=AX.X), reads=[cb2], writes=[cb2])
        if CFG["dd"]:
            S.dma("act", kTs_d[sl][:, 0:124, :], cacheK_d[sl][:, 4:128, :], is_output=True)
            S.dma("act", vs_d[sl][:, 0:124, :], cacheV_d[sl][:, 4:128, :], is_output=True)
        hbs, qb = [Buf(), Buf()], Buf()
        ob = [Buf(), Buf()]
        ktb = [Buf() for _ in TILES]
        vtb = [Buf() for _ in range(17)]
        vob = [Buf()] * 2
        kob, ksob = Buf(), Buf()
        scb = [Buf(), Buf(), Buf()]; pnb = [Buf(), Buf()]; ptb = [Buf(), Buf()]; stb = [Buf(), Buf(), Buf()]
        kcfb = [Buf(), Buf()]; vcfb = [Buf(), Buf()]; kcbb = [Buf(), Buf()]; vcbb = [Buf(), Buf()]
        ssb = [Buf(), Buf()]; pnsb = [Buf(), Buf()]; ptcb = [Buf(), Buf()]; ptnb = [Buf(), Buf()]
        gctr = 0
        rmsnorm(layer, [0], HTs[0], {0: hbs[0]}, 0, sq_off)
        for ti, (t0, tn) in enumerate(TILES):
            HT, hb = HTs[ti % 2], hbs[ti % 2]
            samp = (ti == 4)
            for g in range(4):
                wv, wb_ = wget(q_idx[g], [128, KC, 256])
                for cc in range(2):
                    c = 2 * g + cc
                    pt, pb = ps()
                    for kc in range(KC):
                        S.op("pe", lambda e, pt=pt, wv=wv, cc=cc, kc=kc, tn=tn, HT=HT: e.matmul(pt[:, 0:tn], lhsT=wv[:, kc, cc * 128:(cc + 1) * 128], rhs=HT[:, kc, 0:tn],
                                                                                       start=(kc == 0), stop=(kc == KC - 1)), reads=[wb_, hb], writes=[pb])
                    S.op("act", lambda e, pt=pt, c=c, tn=tn: e.activation(out=QT[:, c, 0:tn], in_=pt[:, 0:tn], func=AF.Identity, bias=AB[:, c:c + 1], scale=1.0),
                         reads=[pb, cb2], writes=[qb])
            if samp:
                for jj in range(2):
                    S.op("act", lambda e, jj=jj: e.copy(out=QS[:, :, jj, :].rearrange("p b (i t) -> p b i t", t=4),
                                                                in_=QT[:, jj * 4:(jj + 1) * 4, 0:64].rearrange("p i (b t) -> p b i t", t=4)), reads=[qb], writes=[qsb])
            wv, wb_ = wget(k_idx, [128, KC, 256])
            for jj in range(2):
                pt, pb = ps()
                for kc in range(KC):
                    S.op("pe", lambda e, pt=pt, wv=wv, jj=jj, kc=kc, tn=tn, HT=HT: e.matmul(pt[:, 0:tn], lhsT=wv[:, kc, jj * 128:(jj + 1) * 128], rhs=HT[:, kc, 0:tn],
                                                                                   start=(kc == 0), stop=(kc == KC - 1)), reads=[wb_, hb], writes=[pb])
                S.op("act", lambda e, pt=pt, jj=jj, t0=t0, tn=tn: e.activation(out=KT[:, jj, t0:t0 + tn], in_=pt[:, 0:tn], func=AF.Identity, bias=AB[:, 8 + jj:9 + jj], scale=1.0),
                     reads=[pb, cb2], writes=[ktb[ti]])
                if ti == 3:
                    S.op("dve", lambda e, pt=pt, jj=jj: e.tensor_scalar(out=KO[:, jj, :], in0=pt[:, 384:512], scalar1=AB[:, 8 + jj:9 + jj], scalar2=None, op0=ALU.add),
                         reads=[pb, cb2], writes=[kob])
                if samp:
                    S.op("dve", lambda e, pt=pt, jj=jj: e.tensor_scalar(out=KSO[:, jj, :], in0=pt[:, 0:64], scalar1=AB[:, 8 + jj:9 + jj], scalar2=None, op0=ALU.add),
                         reads=[pb, cb2], writes=[ksob])
            if ti == 3:
                S.dma("act", kTp_d[sl].rearrange("j p k -> p j k"), KO[:], reads=[kob], is_output=True)
            if samp:
                pt, pb = ps()
                for jj in range(2):
                    S.op("pe", lambda e, pt=pt, jj=jj: e.transpose(out=pt[0:64, jj * 128:(jj + 1) * 128], in_=KSO[:, jj, :], identity=ident_f[:]), reads=[ksob, cbuf], writes=[pb])
                S.op("act", lambda e, pt=pt: e.activation(out=KST[:, :], in_=pt[0:64, 0:256], func=AF.Copy), reads=[pb], writes=[kstb])
                for t_ in range(4 if CFG["so"] else 0):
                    S.dma("act", kTs_d[sl][:, 124 + t_, :], KST[t_:64:4, :], reads=[kstb], is_output=True)
            wv, wb_ = wget(v_idx, [128, KC, 256])
            nblk = 1 if samp else 4
            for bi in range(nblk):
                rows = 64 if samp else 128
                pt, pb = ps()
                for kc in range(KC):
                    S.op("pe", lambda e, pt=pt, wv=wv, kc=kc, bi=bi, rows=rows, HT=HT: e.matmul(pt[0:rows, 0:256], lhsT=HT[:, kc, bi * 128:bi * 128 + rows], rhs=wv[:, kc, :],
                                                                                       start=(kc == 0), stop=(kc == KC - 1)), reads=[wb_, hb], writes=[pb])
                gblk = ti * 4 + bi
                if samp:
                    S.op("dve", lambda e, pt=pt: e.tensor_tensor(out=VS[:, :], in0=pt[0:64, 0:256], in1=VB[0:64, :], op=ALU.add), reads=[pb, cb2], writes=[vtb[16]])
                    S.op("dve", lambda e, pt=pt: e.tensor_tensor(out=VO[0][0:64, :], in0=pt[0:64, 0:256], in1=VB[0:64, :], op=ALU.add), reads=[pb, cb2], writes=[vob[0]])
                    for t_ in range(4 if CFG["so"] else 0):
                        S.dma("act", vs_d[sl][:, 124 + t_, :], VO[0][t_:64:4, :], reads=[vob[0]], is_output=True)
                else:
                    S.op("dve", lambda e, pt=pt, gblk=gblk: e.tensor_tensor(out=VT[:, gblk, :], in0=pt[:, 0:256], in1=VB[:, :], op=ALU.add), reads=[pb, cb2], writes=[vtb[gblk]])
                    if gblk == 15:
                        S.op("dve", lambda e, pt=pt: e.tensor_tensor(out=VO[1][:, :], in0=pt[:, 0:256], in1=VB[:, :], op=ALU.add), reads=[pb, cb2], writes=[vob[1]])
                        S.dma("act", vp_d[sl], VO[1][:], reads=[vob[1]], is_output=True)
            if ti + 1 < len(TILES):
                rmsnorm(layer, [ti + 1], HTs[(ti + 1) % 2], {ti + 1: hbs[(ti + 1) % 2]}, TILES[ti + 1][0], sq_off)
            oq = ti % 2
            if (samp and not CFG['as']) or (not samp and not CFG['ap']):
                S.op('dve', lambda e, oq=oq: e.memset(OT[oq][:], 0.0), writes=[ob[oq]])
            elif not samp:
                swp = SWP()
                po_state = {}
                for bi in range(4):
                    gblk = ti * 4 + bi
                    first = (gblk == 0)
                    ncol = 128 if first else 256
                    k0 = gblk * 128 if first else (gblk - 1) * 128
                    for j in range(4):
                        jj, base = j // 2, (j % 2) * 64
                        g3 = gctr % 3; g2 = gctr % 2; gctr += 1
                        sc, stt, scb_, stb_ = SC[g3], ST[g3], scb[g3], stb[g3]
                        pn, ptt, pnb_, ptb_ = PN[g2], PTs[g2], pnb[g2], ptb[g2]
                        nkb = ncol // 128
                        mk = MASKP[:, 128:256] if first else MASKP[:, :]
                        hold = {}

                        def st0a(jj=jj, base=base, bi=bi, k0=k0, ncol=ncol, hold=hold):
                            psc, pscb = ps2()
                            hold["psc"] = (psc, pscb)
                            for i in range(4):
                                cq = jj * 4 + i
                                S.op("pe", lambda e, i=i, cq=cq: e.matmul(
                                    psc[:, i * 256:i * 256 + ncol], lhsT=QT[base:base + 64, cq, bi * 128:(bi + 1) * 128], rhs=KT[base:base + 64, jj, k0:k0 + ncol],
                                    start=True, stop=True), reads=[qb] + [ktb[x] for x in {k0 // 512, (k0 + ncol - 1) // 512}], writes=[pscb[i // 2]])

                        def st0(sc=sc, stt=stt, scb_=scb_, stb_=stb_, ncol=ncol, mk=mk, j=j, hold=hold):
                            psc, pscb = hold["psc"]
                            S.op("dve", lambda e: e.tensor_tensor(
                                out=sc[:, :, 0:ncol], in0=psc.rearrange("p (a b) -> p a b", a=4)[:, :, 0:ncol], in1=mk.unsqueeze(1).broadcast_to([128, 4, ncol]), op=ALU.add),
                                reads=pscb + [cb2], writes=[scb_])
                            S.op("dve", lambda e: e.tensor_reduce(out=stt[:, 0, :], in_=sc[:, :, 0:ncol], op=ALU.max, axis=AX.X), reads=[scb_], writes=[stb_])
                            S.op("dve", lambda e: e.tensor_tensor(out=stt[:, 0, :], in0=stt[:, 0, :], in1=SKP8[:, 4 * j:4 * j + 4], op=ALU.max), reads=[stb_, cb2], writes=[stb_])
                            S.op("dve", lambda e: e.tensor_scalar(out=stt[:, 1, :], in0=stt[:, 0, :], scalar1=-0.125, scalar2=None, op0=ALU.mult), reads=[stb_], writes=[stb_])
                            S.op("dve", lambda e: e.tensor_tensor(out=stt[:, 3, :], in0=stt[:, 1, :], in1=SKP[:, 4 * j:4 * j + 4], op=ALU.add), reads=[stb_, cb2], writes=[stb_])

                        def st1(sc=sc, stt=stt, scb_=scb_, stb_=stb_, ncol=ncol):
                            for i in range(4):
                                S.op("act", lambda e, i=i: e.activation(out=sc[:, i, 0:ncol], in_=sc[:, i, 0:ncol], func=AF.Exp, bias=stt[:, 1, i:i + 1], scale=0.125,
                                                                        accum_out=stt[:, 2, i:i + 1]), reads=[scb_, stb_], writes=[scb_, stb_])
                            S.op("act", lambda e: e.activation(out=stt[:, 3, :], in_=stt[:, 3, :], func=AF.Exp), reads=[stb_], writes=[stb_])

                        def st2(sc=sc, stt=stt, scb_=scb_, stb_=stb_, pn=pn, pnb_=pnb_, ncol=ncol, nkb=nkb, hold=hold):
                            S.op("dve", lambda e: e.tensor_tensor(out=stt[:, 4, :], in0=stt[:, 2, :], in1=stt[:, 3, :], op=ALU.add), reads=[stb_], writes=[stb_])
                            S.op("dve", lambda e: e.reciprocal(out=stt[:, 5, :], in_=stt[:, 4, :]), reads=[stb_], writes=[stb_])
                            S.op("dve", lambda e: e.tensor_tensor(out=pn[:, :, 0:ncol], in0=sc[:, :, 0:ncol],
                                                                  in1=stt[:, 5, :].unsqueeze(2).broadcast_to([128, 4, ncol]), op=ALU.mult),
                                 reads=[scb_, stb_], writes=[pnb_])
                            ptp, ptpb = pst()
                            hold["ptp"] = (ptp, ptpb)
                            for i in range(4):
                                for kb in range(nkb):
                                    S.op("pe", lambda e, i=i, kb=kb: e.transpose(out=ptp[:, (i * 2 + kb) * 128:(i * 2 + kb + 1) * 128], in_=pn[:, i, kb * 128:(kb + 1) * 128],
                                                                                 identity=ident_bf[:]), reads=[pnb_, cbuf], writes=[ptpb])

                        def st3(ptt=ptt, ptb_=ptb_, hold=hold, jj=jj, base=base, j=j, nkb=nkb, gblk=gblk, first=first, bi=bi, oq=oq):
                            ptp, ptpb = hold["ptp"]
                            po, pob = ps_long()
                            S.op("act", lambda e: e.activation(out=ptt[:].rearrange("p a b -> p (a b)"), in_=ptp[:, :], func=AF.Copy), reads=[ptpb], writes=[ptb_])
                            for i in range(4):
                                cq = jj * 4 + i
                                for kb in range(nkb):
                                    vblk = gblk if (first or kb == 1) else gblk - 1
                                    S.op("pe", lambda e, i=i, kb=kb, cq=cq, vblk=vblk: e.matmul(
                                        po[base:base + 64, cq * 128:(cq + 1) * 128], lhsT=VT[:, vblk, j * 64:(j + 1) * 64], rhs=ptt[:, i * 2 + kb, :],
                                        start=(kb == 0), stop=(kb == nkb - 1), tile_position=(0, base)), reads=[ptb_, vtb[vblk]], writes=[pob[cq // 4]])
                            if j == 3:
                                S.op("act", lambda e: e.activation(out=OT[oq][:, :, bi * 128:(bi + 1) * 128], in_=po.rearrange("p (a b) -> p a b", a=8), func=AF.Copy),
                                     reads=pob, writes=[ob[oq]])

                        swp.step([st0a, st0, st1, st2, st3])
                swp.flush()
            else:
                def samp_gen(b, q2):
                    S.dma("sp", KCf[q2][:], cacheKT_d[sl][b].rearrange("j p k -> p j k"), writes=[kcfb[q2]])
                    S.dma("sp", VCf[q2][:], cacheV_d[sl][b], writes=[vcfb[q2]])
                    S.op("act", lambda e: e.copy(out=KCb[q2][:], in_=KCf[q2][:]), reads=[kcfb[q2]], writes=[kcbb[q2]])
                    S.op("act", lambda e: e.copy(out=VCb[q2][:], in_=VCf[q2][:]), reads=[vcfb[q2]], writes=[vcbb[q2]])
                    yield
                    pcs = [psb(2 * q2), psb(2 * q2 + 1)]
                    for j in range(4):
                        jj, base = j // 2, (j % 2) * 64
                        pc, pcb = pcs[j % 2]
                        S.op("pe", lambda e, pc=pc, jj=jj, base=base: e.matmul(pc[0:16, jj * 192:jj * 192 + 128], lhsT=QS[base:base + 64, b, jj, :],
                                                                            rhs=KCb[q2][base:base + 64, jj, :], start=True, stop=True), reads=[qsb, kcbb[q2]], writes=[pcb])
                        S.op("pe", lambda e, pc=pc, jj=jj, base=base: e.matmul(pc[0:16, jj * 192 + 128:jj * 192 + 192], lhsT=QS[base:base + 64, b, jj, :],
                                                                            rhs=KT[base:base + 64, jj, TP:T], start=True, stop=True), reads=[qsb, ktb[4]], writes=[pcb])
                    yield
                    ss, pns, stt = SS[q2], PNS[q2], ST[q2]
                    for par in range(2):
                        pc, pcb = pcs[par]
                        pv_ = pc[0:16, 0:384].rearrange("p (a b) -> p a b", a=2)
                        S.op("dve", lambda e, pv_=pv_, par=par: e.tensor_tensor(out=ss[:, par:4:2, 0:128], in0=pv_[:, :, 0:128],
                                                                               in1=MASKC[:, :].unsqueeze(1).broadcast_to([16, 2, 128]), op=ALU.add), reads=[pcb, cb2], writes=[ssb[q2]])
                        S.op("dve", lambda e, pv_=pv_, par=par: e.tensor_tensor(out=ss[:, par:4:2, 128:192], in0=pv_[:, :, 128:192],
                                                                               in1=MASKN[:, b, :].unsqueeze(1).broadcast_to([16, 2, 64]), op=ALU.add), reads=[pcb, cb2], writes=[ssb[q2]])
                    S.op("dve", lambda e: e.tensor_reduce(out=stt[0:16, 0, :], in_=ss[:, :, :], op=ALU.max, axis=AX.X), reads=[ssb[q2]], writes=[stb[q2]])
                    S.op("dve", lambda e: e.tensor_tensor(out=stt[0:16, 0, :], in0=stt[0:16, 0, :], in1=SKS8[:, :], op=ALU.max), reads=[stb[q2], cb2], writes=[stb[q2]])
                    S.op("dve", lambda e: e.tensor_scalar(out=stt[0:16, 1, :], in0=stt[0:16, 0, :], scalar1=-0.125, scalar2=None, op0=ALU.mult), reads=[stb[q2]], writes=[stb[q2]])
                    S.op("dve", lambda e: e.tensor_tensor(out=stt[0:16, 3, :], in0=stt[0:16, 1, :], in1=SKS[:, :], op=ALU.add), reads=[stb[q2], cb2], writes=[stb[q2]])
                    yield
                    for j in range(4):
                        S.op("act", lambda e, j=j: e.activation(out=ss[:, j, :], in_=ss[:, j, :], func=AF.Exp, bias=stt[0:16, 1, j:j + 1], scale=0.125,
                                                                accum_out=stt[0:16, 2, j:j + 1]), reads=[ssb[q2], stb[q2]], writes=[ssb[q2], stb[q2]])
                    S.op("act", lambda e: e.activation(out=stt[0:16, 3, :], in_=stt[0:16, 3, :], func=AF.Exp), reads=[stb[q2]], writes=[stb[q2]])
                    yield
                    S.op("dve", lambda e: e.tensor_tensor(out=stt[0:16, 4, :], in0=stt[0:16, 2, :], in1=stt[0:16, 3, :], op=ALU.add), reads=[stb[q2]], writes=[stb[q2]])
                    S.op("dve", lambda e: e.reciprocal(out=stt[0:16, 5, :], in_=stt[0:16, 4, :]), reads=[stb[q2]], writes=[stb[q2]])
                    S.op("dve", lambda e: e.tensor_tensor(out=pns[:, :, :], in0=ss[:, :, :], in1=stt[0:16, 5, :].unsqueeze(2).broadcast_to([16, 4, 192]), op=ALU.mult),
                         reads=[ssb[q2], stb[q2]], writes=[pnsb[q2]])
                    yield
                    ptp, ptpb = PST[:, q2 * 1024:(q2 + 1) * 1024], psbuf[6 + q2]
                    for j in range(4):
                        S.op("pe", lambda e, j=j: e.transpose(out=ptp[:, j * 16:(j + 1) * 16], in_=pns[:, j, 0:128], identity=ident_bf[0:16, 0:16]),
                             reads=[pnsb[q2], cbuf], writes=[ptpb])
                        S.op("pe", lambda e, j=j: e.transpose(out=ptp[0:64, 64 + j * 16:64 + (j + 1) * 16], in_=pns[:, j, 128:192], identity=ident_bf[0:16, 0:16]),
                             reads=[pnsb[q2], cbuf], writes=[ptpb])
                    yield
                    S.op("act", lambda e: e.activation(out=PTC[q2][:].rearrange("p a b -> p (a b)"), in_=ptp[:, 0:64], func=AF.Copy), reads=[ptpb], writes=[ptcb[q2]])
                    S.op("act", lambda e: e.activation(out=PTN[q2][:].rearrange("p a b -> p (a b)"), in_=ptp[0:64, 64:128], func=AF.Copy), reads=[ptpb], writes=[ptnb[q2]])
                    yield
                    po, pob = psb(4 + q2)
                    for j in range(4):
                        jj, base = j // 2, (j % 2) * 64
                        S.op("pe", lambda e, j=j, jj=jj, base=base: e.matmul(po[base:base + 64, jj * 16:(jj + 1) * 16], lhsT=VCb[q2][:, j * 64:(j + 1) * 64], rhs=PTC[q2][:, j, :],
                                                                           start=True, stop=False, tile_position=(0, base)), reads=[vcbb[q2], ptcb[q2]], writes=[pob])
                        S.op("pe", lambda e, j=j, jj=jj, base=base: e.matmul(po[base:base + 64, jj * 16:(jj + 1) * 16], lhsT=VS[0:64, j * 64:(j + 1) * 64], rhs=PTN[q2][0:64, j, :],
                                                                           start=False, stop=True, tile_position=(0, base)), reads=[vtb[16], ptnb[q2]], writes=[pob])
                    yield
                    S.op("act", lambda e: e.activation(out=OT[oq][:, :, 4 * b:4 * b + 4], in_=po[:, 0:32].rearrange("p (a b) -> p a b", b=4), func=AF.Copy),
                         reads=[pob], writes=[ob[oq]])

                pending = list(range(NB))
                active = {}
                for slot in range(2):
                    active[slot] = samp_gen(pending.pop(0), slot)
                while active:
                    for slot in list(active.keys()):
                        try:
                            next(active[slot])
                        except StopIteration:
                            if pending:
                                active[slot] = samp_gen(pending.pop(0), slot)
                            else:
                                del active[slot]
            for g in range(4):
                wv, wb_ = wget(o_idx[g], [128, KC, 256])
                for cc in range(2):
                    c = 2 * g + cc
                    pt, pb = ps()
                    for kc in range(KC):
                        S.op("pe", lambda e, pt=pt, wv=wv, cc=cc, kc=kc, tn=tn, oq=oq: e.matmul(pt[:, 0:tn], lhsT=wv[:, kc, cc * 128:(cc + 1) * 128], rhs=OT[oq][:, kc, 0:tn],
                                                                                              start=(kc == 0), stop=(kc == KC - 1)), reads=[wb_, ob[oq]], writes=[pb])
                    S.op("dve", lambda e, pt=pt, c=c, t0=t0, tn=tn: e.scalar_tensor_tensor(out=X[:, c, t0:t0 + tn], in0=pt[:, 0:tn], scalar=AB[:, 10 + c:11 + c], in1=X[:, c, t0:t0 + tn],
                                                                                          op0=ALU.add, op1=ALU.add), reads=[pb, cb2, xbuf[ti]], writes=[xbuf[ti]])

    HTILES = [(t0, 256) for t0 in range(0, TP, 256)] + [(TP, TS)]

    def _hmask():
        s_ = np.arange(128)[:, None]; t_ = np.arange(128)[None, :]
        same = (s_ // 16) == (t_ // 16)
        tri = (same & (s_ <= t_)).astype(np.float32)
        tri2 = (same & (s_ > t_)).astype(np.float32)
        cm = ((np.arange(128)[:, None] // 16) == np.arange(8)[None, :]).astype(np.float32)
        return np.concatenate([tri, tri2, -tri, cm], axis=1)

    def _hmask_s():
        s_ = np.arange(64)[:, None]; t_ = np.arange(64)[None, :]
        same = (s_ // 4) == (t_ // 4)
        tri = (same & (s_ <= t_)).astype(np.float32)
        tri2 = (same & (s_ > t_)).astype(np.float32)
        cm = ((np.arange(64)[:, None] // 4) == np.arange(16)[None, :]).astype(np.float32)
        return np.concatenate([tri, tri2, -tri, cm], axis=1)

    if NL > 1 and CFG["mixers"][1]:
        hmask_d = P.din("hmask", [128, 392], lambda I, c: _hmask())
        hmasks_d = P.din("hmasks", [64, 208], lambda I, c: _hmask_s())
        lbl_d = P.din("lbl", [128, 4, 1024], lambda I, c: np.broadcast_to(I["hgrn_lb_logits"][None], (128, 4, 1024)))
        hnw_d = P.din("hnw", [128, 8], lambda I, c: I["hgrn_norm_w"][0].reshape(8, 128).T)
        hgS_d = P.din("hgS", [NB, 128, 8, 128], lambda I, c: I["state_hgrn"][0, c * NB:(c + 1) * NB].transpose(0, 2, 1, 3))
        hgP_d = P.dout("hgP", [128, 8, 128])
        hgSo_d = P.dout("hgSo", [NB, 128, 8, 128])

    def hgrn(layer):
        Win = lambda I: I["hgrn_w_in"][0]
        def wt(name, c0):
            return wtile((name, c0), 2048, (lambda I, c0=c0: Win(I)[:, c0:c0 + 256].reshape(KC, 128, 256).transpose(1, 0, 2).reshape(128, -1)))
        q_idx = [wt("hq", 256 * g) for g in range(4)]
        f_idx = [wt("hf", 1024 + 256 * g) for g in range(4)]
        i_idx = [wt("hi", 2048 + 256 * g) for g in range(4)]
        g_idx = [wt("hg", 3072 + 256 * g) for g in range(4)]
        o_idx = [wtile(("ho", g), 2048, (lambda I, g=g: I["hgrn_w_o"][0][:, g * 256:(g + 1) * 256].reshape(KC, 128, 256).transpose(1, 0, 2).reshape(128, -1))) for g in range(4)]
        ar = Bump(ARENA, ARENA_SZ)
        TN = 256
        HTs = [P.sb(ar.take(KC * TN * 2), [128, KC, TN], BF16) for _ in range(2)]
        SQN2 = P.sb(ar.take(KC * TN * 2), [128, KC, TN], BF16)
        RSN2 = P.sb(ar.take(1024), [128, TN], F32)
        sqnb, rsnb = Buf(), Buf()
        hbs = [Buf(), Buf()]
        SQ = P.sb(ar.take(KC * TN * 2), [128, KC, TN], BF16)
        GS = P.sb(ar.take(KC * TN * 2), [128, KC, TN], BF16)
        OT = P.sb(ar.take(KC * 512 * 2), [128, KC, 512], BF16)
        RSn = ar.take(2048)
        lf_off = ar.take(2 * 4096)
        LF = P.sb(lf_off, [128, 2, 1024], F32)
        l1_off = ar.take(2 * 4096)
        L1 = P.sb(l1_off, [128, 2, 1024], F32)
        V = P.sb(ar.take(2 * 2048), [128, 2, 1024], BF16)
        E = P.sb(ar.take(4096), [128, 8, 128], F32)
        DEC = P.sb(ar.take(8 * 16 * 4), [128, 8, 16], F32)
        QIb = P.sb(ar.take(2048), [128, 8, 128], BF16)
        KI = P.sb(ar.take(2048), [128, 8, 128], BF16)
        KD = P.sb(ar.take(2048), [128, 1024], BF16)
        kdm_off = ar.take(16384)
        KDM = P.sb(kdm_off, [128, 8, 1024], BF16)
        AT = P.sb(ar.take(2048), [128, 8, 128], BF16)
        S32 = [P.sb(ar.take(4096), [128, 8, 128], F32) for _ in range(2)]
        SB16 = [P.sb(ar.take(2048), [128, 8, 128], BF16) for _ in range(2)]
        sbb = [Buf(), Buf()]
        SQO = P.sb(ar.take(2048), [128, 1024], BF16)
        RSTD = P.sb(ar.take(4096), [128, 1024], F32)
        FT = [P.sb(ar.take(1024), [128, 256], F32) for _ in range(2)]
        LB = P.sb(ar.take(4096), [128, 1024], F32)
        OML = P.sb(ar.take(4096), [128, 1024], F32)
        HM = P.sb(ar.take(392 * 4), [128, 392], F32)
        HMS = P.sb(ar.take(208 * 4), [64, 208], F32)
        NW = P.sb(ar.take(32), [128, 8], F32)
        TRIb = P.sb(ar.take(256), [128, 128], BF16)
        TRIsb = P.sb(ar.take(128), [64, 64], BF16)
        S0 = [P.sb(lf_off + 4096, [128, 8, 128], F32), P.sb(l1_off + 4096, [128, 8, 128], F32)]
        cb3 = Buf()
        hb, sqb, gsb, ob = Buf(), Buf(), Buf(), Buf()
        lfb, l1b, vb_ = [Buf(), Buf()], [Buf(), Buf()], [Buf(), Buf()]
        eb, decb, qib, kib, kdb, atb = Buf(), Buf(), Buf(), Buf(), Buf(), Buf()
        kdmb = [Buf() for _ in range(16)]
        sb_ = [[Buf() for _ in range(8)] for _ in range(2)]
        sqob, rstdb = Buf(), Buf()
        ftb = [Buf(), Buf()]
        s0b = [lfb[1], l1b[1]]
        S.dma("act", HM[:], hmask_d, writes=[cb3])
        S.dma("act", HMS[:], hmasks_d, writes=[cb3])
        S.dma("act", NW[:], hnw_d, writes=[cb3])
        LGt = P.sb(kdm_off, [128, 4, 1024], F32)
        S.dma("act", LGt[:], lbl_d, writes=[kdmb[0]])
        S.op("act", lambda e: e.activation(out=LGt[:], in_=LGt[:], func=AF.Exp), reads=[kdmb[0]], writes=[kdmb[0]])
        S.op("dve", lambda e: e.tensor_tensor(out=LB[:], in0=LGt[:, 0, :], in1=LGt[:, 1, :], op=ALU.add), reads=[kdmb[0]], writes=[cb3])
        S.op("dve", lambda e: e.tensor_tensor(out=LB[:], in0=LB[:], in1=LGt[:, 2, :], op=ALU.add), reads=[kdmb[0], cb3], writes=[cb3])
        S.op("dve", lambda e: e.tensor_tensor(out=LB[:], in0=LB[:], in1=LGt[:, 3, :], op=ALU.add), reads=[kdmb[0], cb3], writes=[cb3])
        S.op("dve", lambda e: e.reciprocal(out=LB[:], in_=LB[:]), reads=[cb3], writes=[cb3])
        S.op("dve", lambda e: e.tensor_tensor(out=LB[:], in0=LB[:], in1=LGt[:, 1, :], op=ALU.mult), reads=[kdmb[0], cb3], writes=[cb3] + kdmb[0:8])
        S.op("dve", lambda e: e.tensor_scalar(out=OML[:], in0=LB[:], scalar1=-1.0, scalar2=1.0, op0=ALU.mult, op1=ALU.add), reads=[cb3], writes=[cb3])
        S.op("dve", lambda e: e.tensor_copy(out=TRIb[:], in_=HM[:, 0:128]), reads=[cb3], writes=[cb3])
        S.op("dve", lambda e: e.tensor_copy(out=TRIsb[:], in_=HMS[:, 0:64]), reads=[cb3], writes=[cb3])
        for h in range(8):
            S.op("dve", lambda e, h=h: e.memset(S32[0][:, h, :], 0.0), writes=[sb_[0][h]])
        S.op("dve", lambda e: e.memset(SB16[0][:], 0.0), writes=[sbb[0]])
        def do_norm(idx):
            t0, tn = HTILES[idx]
            ti = t0 // 512
            HTd, hbd = HTs[idx % 2], hbs[idx % 2]
            S.op("act", lambda e: e.activation(out=SQN2[:, :, 0:tn], in_=X[:, :, t0:t0 + tn], func=AF.Square), reads=[xbuf[ti]], writes=[sqnb])
            pt, pb = ps()
            for c in range(KC):
                S.op("pe", lambda e, c=c: e.matmul(pt[:, 0:tn], lhsT=ones_bf[:], rhs=SQN2[:, c, 0:tn], start=(c == 0), stop=(c == KC - 1)), reads=[sqnb, cbuf], writes=[pb])
            S.op("act", lambda e: e.activation(out=RSN2[:, 0:tn], in_=pt[:, 0:tn], func=AF.Sqrt, bias=epsb[:], scale=1.0 / D), reads=[pb, cbuf], writes=[rsnb])
            S.op("dve", lambda e: e.reciprocal(out=RSN2[:, 0:tn], in_=RSN2[:, 0:tn]), reads=[rsnb], writes=[rsnb])
            for c in range(KC):
                S.op("dve", lambda e, c=c: e.scalar_tensor_tensor(out=HTd[:, c, 0:tn], in0=X[:, c, t0:t0 + tn], scalar=NORMW[:, layer, c:c + 1], in1=RSN2[:, 0:tn],
                                                                op0=ALU.mult, op1=ALU.mult), reads=[xbuf[ti], rsnb, cbuf], writes=[hbd])

        do_norm(0)
        cur = 0
        for hi_, (t0, tn) in enumerate(HTILES):
            samp = (t0 == TP)
            ti = t0 // 512
            HT, hb = HTs[hi_ % 2], hbs[hi_ % 2]
            for g in range(4):
                wv, wb_ = wget(q_idx[g], [128, KC, 256])
                for cc in range(2):
                    c = 2 * g + cc
                    pt, pb = ps()
                    for kc in range(KC):
                        S.op("pe", lambda e, pt=pt, wv=wv, cc=cc, kc=kc, tn=tn, HT=HT: e.matmul(pt[:, 0:tn], lhsT=wv[:, kc, cc * 128:(cc + 1) * 128], rhs=HT[:, kc, 0:tn],
                                                                                       start=(kc == 0), stop=(kc == KC - 1)), reads=[wb_, hb], writes=[pb])
                    S.op("act", lambda e, pt=pt, c=c, tn=tn: e.activation(out=SQ[:, c, 0:tn], in_=pt[:, 0:tn], func=AF.Silu), reads=[pb], writes=[sqb])
            for g in range(4):
                wv, wb_ = wget(g_idx[g], [128, KC, 256])
                for cc in range(2):
                    c = 2 * g + cc
                    pt, pb = ps()
                    for kc in range(KC):
                        S.op("pe", lambda e, pt=pt, wv=wv, cc=cc, kc=kc, tn=tn, HT=HT: e.matmul(pt[:, 0:tn], lhsT=wv[:, kc, cc * 128:(cc + 1) * 128], rhs=HT[:, kc, 0:tn],
                                                                                       start=(kc == 0), stop=(kc == KC - 1)), reads=[wb_, hb], writes=[pb])
                    S.op("act", lambda e, pt=pt, c=c, tn=tn: e.activation(out=GS[:, c, 0:tn], in_=pt[:, 0:tn], func=AF.Silu), reads=[pb], writes=[gsb])
                    S.op("dve", lambda e, c=c, tn=tn: e.tensor_scalar(out=GS[:, c, 0:tn], in0=GS[:, c, 0:tn], scalar1=NW[:, c:c + 1], scalar2=None, op0=ALU.mult), reads=[gsb, cb3], writes=[gsb])
            nblk = 1 if samp else 2
            rows = 64 if samp else 128
            fctr = 0
            swpf = SWP()
            for g in range(4):
                wv, wb_ = wget(f_idx[g], [128, KC, 256])
                for bi in range(nblk):
                    pt, pb = ps()
                    for kc in range(KC):
                        S.op("pe", lambda e, pt=pt, wv=wv, kc=kc, bi=bi, rows=rows, HT=HT: e.matmul(pt[0:rows, 0:256], lhsT=HT[:, kc, bi * 128:bi * 128 + rows], rhs=wv[:, kc, :],
                                                                                           start=(kc == 0), stop=(kc == KC - 1)), reads=[wb_, hb], writes=[pb])
                    fq = fctr % 2; fctr += 1
                    ft = FT[fq]
                    cs = slice(g * 256, (g + 1) * 256)
                    S.op("act", lambda e, pt=pt, ft=ft, rows=rows: e.activation(out=ft[0:rows, :], in_=pt[0:rows, 0:256], func=AF.Sigmoid), reads=[pb], writes=[ftb[fq]])
                    S.op("dve", lambda e, ft=ft, rows=rows, cs=cs: e.tensor_tensor(out=ft[0:rows, :], in0=ft[0:rows, :], in1=OML[0:rows, cs], op=ALU.mult), reads=[ftb[fq], cb3], writes=[ftb[fq]])
                    S.op("dve", lambda e, ft=ft, rows=rows, cs=cs: e.tensor_tensor(out=ft[0:rows, :], in0=ft[0:rows, :], in1=LB[0:rows, cs], op=ALU.add), reads=[ftb[fq], cb3], writes=[ftb[fq]])
                    def ftail(ft=ft, rows=rows, cs=cs, bi=bi, fq=fq):
                        S.op("act", lambda e: e.activation(out=LF[0:rows, bi, cs], in_=ft[0:rows, :], func=AF.Ln), reads=[ftb[fq]], writes=[lfb[bi]])
                        S.op("act", lambda e: e.activation(out=L1[0:rows, bi, cs], in_=ft[0:rows, :], func=AF.Ln, scale=-1.0, bias=oneb[0:rows, :]), reads=[ftb[fq], cbuf], writes=[l1b[bi]])
                    swpf.step([lambda: None, ftail])
            swpf.flush()
            for g in range(4):
                wv, wb_ = wget(i_idx[g], [128, KC, 256])
                for bi in range(nblk):
                    pt, pb = ps()
                    for kc in range(KC):
                        S.op("pe", lambda e, pt=pt, wv=wv, kc=kc, bi=bi, rows=rows, HT=HT: e.matmul(pt[0:rows, 0:256], lhsT=HT[:, kc, bi * 128:bi * 128 + rows], rhs=wv[:, kc, :],
                                                                                           start=(kc == 0), stop=(kc == KC - 1)), reads=[wb_, hb], writes=[pb])
                    S.op("act", lambda e, pt=pt, rows=rows, bi=bi, g=g: e.activation(out=V[0:rows, bi, g * 256:(g + 1) * 256], in_=pt[0:rows, 0:256], func=AF.Copy), reads=[pb], writes=[vb_[bi]])
            if hi_ + 1 < len(HTILES):
                do_norm(hi_ + 1)
            for bi in range(nblk):
                c0 = bi * 128
                if samp:
                    R, NCH = 64, 16
                    tri, tri2, ntri, cmm, trib = HMS[:, 0:64], HMS[:, 64:128], HMS[:, 128:192], HMS[:, 192:208], TRIsb
                else:
                    R, NCH = 128, 8
                    tri, tri2, ntri, cmm, trib = HM[:, 0:128], HM[:, 128:256], HM[:, 256:384], HM[:, 384:392], TRIb
                CH = R // NCH
                for hg in range(2):
                    pt, pb = ps()
                    for hh in range(4):
                        h = hg * 4 + hh
                        S.op("pe", lambda e, pt=pt, hh=hh, h=h, bi=bi, R=R, tri=tri: e.matmul(pt[:, hh * 128:hh * 128 + R], lhsT=LF[0:R, bi, h * 128:(h + 1) * 128], rhs=tri[0:R, :],
                                                                                          start=True, stop=True), reads=[lfb[bi], cb3], writes=[pb])
                    S.op("act", lambda e, pt=pt, hg=hg, R=R: e.activation(out=E[:, hg * 4:hg * 4 + 4, 0:R], in_=pt[:, :].rearrange("p (a b) -> p a b", a=4)[:, :, 0:R], func=AF.Exp),
                         reads=[pb], writes=[eb])
                S.op("dve", lambda e, R=R, NCH=NCH, CH=CH: e.tensor_copy(out=DEC[:, :, 0:NCH], in_=E[:, :, CH - 1:R:CH]), reads=[eb], writes=[decb])
                for hg in range(2):
                    pt, pb = ps()
                    for hh in range(4):
                        h = hg * 4 + hh
                        S.op("pe", lambda e, pt=pt, hh=hh, h=h, bi=bi, R=R, ntri=ntri: e.matmul(pt[:, hh * 128:hh * 128 + R], lhsT=LF[0:R, bi, h * 128:(h + 1) * 128], rhs=ntri[0:R, :],
                                                                                            start=(hh == 0), stop=False, skip_group_check=True), reads=[lfb[bi], cb3], writes=[pb])
                        S.op("pe", lambda e, pt=pt, hh=hh, h=h, bi=bi, R=R: e.matmul(pt[:, hh * 128:hh * 128 + R], lhsT=L1[0:R, bi, h * 128:(h + 1) * 128], rhs=ident_f[0:R, 0:R],
                                                                                 start=False, stop=True, skip_group_check=True), reads=[l1b[bi], cbuf], writes=[pb])
                    S.op("act", lambda e, pt=pt, hg=hg, R=R: e.activation(out=KI[:, hg * 4:hg * 4 + 4, 0:R], in_=pt[:, :].rearrange("p (a b) -> p a b", a=4)[:, :, 0:R], func=AF.Exp),
                         reads=[pb], writes=[kib])
                for hg in range(2):
                    pt, pb = ps()
                    S.op("pe", lambda e, pt=pt, hg=hg, bi=bi, R=R, tri2=tri2: e.matmul(pt[0:R, :], lhsT=tri2[0:R, :], rhs=LF[0:R, bi, hg * 512:(hg + 1) * 512], start=True, stop=False),
                         reads=[lfb[bi], cb3], writes=[pb])
                    S.op("pe", lambda e, pt=pt, hg=hg, bi=bi, R=R: e.matmul(pt[0:R, :], lhsT=ident_f[0:R, 0:R], rhs=L1[0:R, bi, hg * 512:(hg + 1) * 512], start=False, stop=True),
                         reads=[l1b[bi], cbuf], writes=[pb])
                    S.op("act", lambda e, pt=pt, hg=hg, R=R: e.activation(out=KD[0:R, hg * 512:(hg + 1) * 512], in_=pt[0:R, :], func=AF.Exp), reads=[pb], writes=[kdb])
                S.op("dve", lambda e, R=R, c0=c0: e.tensor_tensor(out=E[:, :, 0:R], in0=E[:, :, 0:R], in1=SQ[:, :, c0:c0 + R], op=ALU.mult), reads=[eb, sqb, decb], writes=[eb])
                S.op("act", lambda e, R=R: e.copy(out=QIb[:, :, 0:R], in_=E[:, :, 0:R]), reads=[eb], writes=[qib])
                for c in range(0 if samp else NCH):
                    S.op("act", lambda e, c=c, R=R, cmm=cmm: e.activation(out=KDM[0:R, c, :], in_=KD[0:R, :], func=AF.Copy, scale=cmm[0:R, c:c + 1]),
                         reads=[kdb, cb3], writes=[kdmb[c]])
                for hg in range(2):
                    pt, pb = ps()
                    for hh in range(4):
                        h = hg * 4 + hh
                        S.op("pe", lambda e, pt=pt, hh=hh, h=h, R=R: e.matmul(pt[0:R, hh * 128:hh * 128 + R], lhsT=KI[:, h, 0:R], rhs=QIb[:, h, 0:R], start=True, stop=True),
                             reads=[kib, qib], writes=[pb])
                    S.op("dve", lambda e, pt=pt, hg=hg, R=R, trib=trib: e.tensor_tensor(out=AT[0:R, hg * 4:hg * 4 + 4, 0:R], in0=pt[0:R, :].rearrange("p (a b) -> p a b", a=4)[:, :, 0:R],
                                                                                       in1=trib[0:R, 0:R].unsqueeze(1).broadcast_to([R, 4, R]), op=ALU.mult), reads=[pb, cb3], writes=[atb])
                po, pob = ps_long()
                for h in range(8):
                    S.op("pe", lambda e, po=po, h=h, bi=bi, R=R: e.matmul(po[:, h * 128:h * 128 + R], lhsT=V[0:R, bi, h * 128:(h + 1) * 128], rhs=AT[0:R, h, 0:R],
                                                                       start=(h % 4 == 0), stop=False, skip_group_check=True), reads=[vb_[bi], atb], writes=[pob[h // 4]])
                if not samp:
                    for c in range(NCH):
                        nxt = 1 - cur
                        dps = []
                        for hg in range(2):
                            pt, pb = ps()
                            dps.append((pt, pb))
                            for hh in range(4):
                                h = hg * 4 + hh
                                S.op("pe", lambda e, pt=pt, hh=hh, h=h, c=c, bi=bi: e.matmul(pt[:, hh * 128:(hh + 1) * 128], lhsT=KDM[:, c, h * 128:(h + 1) * 128], rhs=V[:, bi, h * 128:(h + 1) * 128],
                                                                                         start=True, stop=True), reads=[kdmb[c], vb_[bi]], writes=[pb])
                        for h in range(8):
                            S.op("pe", lambda e, po=po, h=h, c=c, cur=cur: e.matmul(po[:, h * 128 + c * 16:h * 128 + (c + 1) * 16], lhsT=SB16[cur][:, h, :], rhs=QIb[:, h, c * 16:(c + 1) * 16],
                                                                                start=False, stop=(c == NCH - 1), skip_group_check=True), reads=[sbb[cur], qib], writes=[pob[h // 4]])
                        for hg in range(2):
                            pt, pb = dps[hg]
                            for hh in range(4):
                                h = hg * 4 + hh
                                S.op("dve", lambda e, pt=pt, hh=hh, h=h, c=c, cur=cur, nxt=nxt: e.scalar_tensor_tensor(out=S32[nxt][:, h, :], in0=S32[cur][:, h, :], scalar=DEC[:, h, c:c + 1],
                                                                                                                in1=pt[:, hh * 128:(hh + 1) * 128], op0=ALU.mult, op1=ALU.add),
                                     reads=[sb_[cur][h], decb, pb], writes=[sb_[nxt][h]])
                        S.op("act", lambda e, nxt=nxt: e.copy(out=SB16[nxt][:], in_=S32[nxt][:]), reads=sb_[nxt], writes=[sbb[nxt]])
                        cur = nxt
                else:
                    for b in range(NB):
                        q2 = b % 2
                        S.dma("sp", S0[q2][:], hgS_d[b], writes=[s0b[q2]])
                        for h in range(8):
                            S.op("pe", lambda e, po=po, h=h, b=b, q2=q2: e.matmul(po[:, h * 128 + b * 4:h * 128 + (b + 1) * 4], lhsT=S0[q2][:, h, :], rhs=E[:, h, b * 4:(b + 1) * 4],
                                                                              start=False, stop=(b == NB - 1), skip_group_check=True), reads=[s0b[q2], eb], writes=[pob[h // 4]])
                        SNb = S32[q2]
                        S.op("act", lambda e, b=b, cmm=cmm: e.activation(out=KDM[0:64, b % 8, :], in_=KD[0:64, :], func=AF.Copy, scale=cmm[0:64, b:b + 1]),
                             reads=[kdb, cb3], writes=[kdmb[b % 8]])
                        for hg in range(2):
                            pt, pb = ps()
                            for hh in range(4):
                                h = hg * 4 + hh
                                S.op("pe", lambda e, pt=pt, hh=hh, h=h, b=b: e.matmul(pt[:, hh * 128:(hh + 1) * 128], lhsT=KDM[0:64, b % 8, h * 128:(h + 1) * 128], rhs=V[0:64, 0, h * 128:(h + 1) * 128],
                                                                                  start=True, stop=True), reads=[kdmb[b % 8], vb_[0]], writes=[pb])
                            for hh in range(4):
                                h = hg * 4 + hh
                                S.op("dve", lambda e, pt=pt, hh=hh, h=h, b=b, q2=q2, SNb=SNb: e.scalar_tensor_tensor(out=SNb[:, h, :], in0=S0[q2][:, h, :], scalar=DEC[:, h, b:b + 1],
                                                                                                             in1=pt[:, hh * 128:(hh + 1) * 128], op0=ALU.mult, op1=ALU.add),
                                     reads=[s0b[q2], decb, pb], writes=[sb_[q2][h]])
                        S.dma("sp", hgSo_d[b], SNb[:], reads=sb_[q2], is_output=True)
                S.op("act", lambda e, po=po, R=R: e.activation(out=SQO[:, :].rearrange("p (a b) -> p a b", a=8)[:, :, 0:R], in_=po.rearrange("p (a b) -> p a b", a=8)[:, :, 0:R], func=AF.Square),
                     reads=pob, writes=[sqob])
                for hg in range(2):
                    pt, pb = ps()
                    S.op("pe", lambda e, pt=pt, hg=hg: e.matmul(pt[:, :], lhsT=ones_bf[:], rhs=SQO[:, hg * 512:(hg + 1) * 512], start=True, stop=True), reads=[sqob, cbuf], writes=[pb])
                    S.op("act", lambda e, pt=pt, hg=hg: e.activation(out=RSTD[:, hg * 512:(hg + 1) * 512], in_=pt[:, :], func=AF.Sqrt, bias=epsb[:], scale=1.0 / 128), reads=[pb, cbuf], writes=[rstdb])
                S.op("dve", lambda e: e.reciprocal(out=RSTD[:, :], in_=RSTD[:, :]), reads=[rstdb], writes=[rstdb])
                S.op("dve", lambda e, po=po, R=R: e.tensor_tensor(out=RSTD[:, :].rearrange("p (a b) -> p a b", a=8)[:, :, 0:R], in0=po.rearrange("p (a b) -> p a b", a=8)[:, :, 0:R],
                                                                 in1=RSTD[:, :].rearrange("p (a b) -> p a b", a=8)[:, :, 0:R], op=ALU.mult), reads=pob + [rstdb], writes=[rstdb])
                S.op("dve", lambda e, R=R, c0=c0: e.tensor_tensor(out=OT[:, :, c0:c0 + R], in0=RSTD[:, :].rearrange("p (a b) -> p a b", a=8)[:, :, 0:R], in1=GS[:, :, c0:c0 + R], op=ALU.mult),
                     reads=[rstdb, gsb], writes=[ob])
            for g in range(4):
                wv, wb_ = wget(o_idx[g], [128, KC, 256])
                for cc in range(2):
                    c = 2 * g + cc
                    pt, pb = ps()
                    for kc in range(KC):
                        S.op("pe", lambda e, pt=pt, wv=wv, cc=cc, kc=kc, tn=tn: e.matmul(pt[:, 0:tn], lhsT=wv[:, kc, cc * 128:(cc + 1) * 128], rhs=OT[:, kc, 0:tn],
                                                                                       start=(kc == 0), stop=(kc == KC - 1)), reads=[wb_, ob], writes=[pb])
                    S.op("dve", lambda e, pt=pt, c=c, t0=t0, tn=tn: e.tensor_tensor(out=X[:, c, t0:t0 + tn], in0=X[:, c, t0:t0 + tn], in1=pt[:, 0:tn], op=ALU.add),
                         reads=[pb, xbuf[ti]], writes=[xbuf[ti]])
            if t0 + tn == TP:
                S.dma("act", hgP_d, S32[cur][:], reads=sb_[cur], is_output=True)

    def _ssd_masks():
        r = np.arange(128)
        triu = (r[:, None] <= r[None, :]).astype(np.float32)
        tril2 = (r[:, None] > r[None, :]).astype(np.float32)
        negm = np.where(r[:, None] <= r[None, :], 0.0, NEG).astype(np.float32)
        return np.concatenate([triu, tril2, np.tile(negm, (1, 4)), np.ones((128, 128), np.float32)], axis=1)

    def _ssd_masks_s():
        r = np.arange(64)
        same = (r[:, None] // 4) == (r[None, :] // 4)
        triu = (same & (r[:, None] <= r[None, :])).astype(np.float32)
        tril2 = (same & (r[:, None] > r[None, :])).astype(np.float32)
        negm = np.where(same & (r[:, None] <= r[None, :]), 0.0, NEG).astype(np.float32)
        seq = ((r[:, None] // 4) == np.arange(16)[None, :]).astype(np.float32)
        return np.concatenate([triu, tril2, np.tile(negm, (1, 4)), seq], axis=1)

    def _sel8():
        s_ = np.zeros((8, 8, 128), np.float32)
        for h in range(8):
            s_[h, h, :] = 1.0
        return s_

    if NL > 2 and CFG["mixers"][2]:
        smask_d = P.din("smask", [128, 896], lambda I, c: _ssd_masks())
        smasks_d = P.din("smasks", [64, 400], lambda I, c: _ssd_masks_s())
        sel8_d = P.din("sel8", [8, 8, 128], lambda I, c: _sel8())
        scw_d = P.din("scw", [128, 24, 5], lambda I, c: np.concatenate([I["ssd_conv_w"][0], I["ssd_conv_b"]], axis=0).reshape(5, 24, 128).transpose(2, 1, 0))
        sfb_d = P.din("sfb", [128, 24, NB, 3], lambda I, c: I["state_ssm_conv"][0, c * NB:(c + 1) * NB].reshape(NB, 3, 24, 128).transpose(3, 2, 0, 1))
        svec_d = P.din("svec", [128, 96], lambda I, c: np.concatenate([
            np.broadcast_to(I["ssd_dt_bias"][0][None], (128, 32)), np.broadcast_to(I["ssd_a_log"][0][None], (128, 32)),
            np.repeat(I["ssd_d"][0].reshape(16, 2), 64, axis=1).T, I["ssd_norm_w"][0].reshape(16, 128).T], axis=1))
        ssmS_d = P.din("ssmS", [NB, 128, 32, 64], lambda I, c: I["state_ssm"][0, c * NB:(c + 1) * NB].transpose(0, 3, 1, 2))
        ssmP_d = P.dout("ssmP", [128, 32, 64])
        ssmSo_d = P.dout("ssmSo", [NB, 128, 32, 64])
        sconv_d = P.dout("sconvT", [128, 24, 3 + 3 * NB])

    def ssd(layer):
        Win = lambda I: I["ssd_w_in"][0]
        def wt(name, c0):
            return wtile((name, c0), 2048, (lambda I, c0=c0: Win(I)[:, c0:c0 + 256].reshape(KC, 128, 256).transpose(1, 0, 2).reshape(128, -1)))
        z_idx = [wt("sz", 256 * g) for g in range(8)]
        x_idx = [wt("sx", 2048 + 256 * g) for g in range(12)]
        dt_idx = wtile(("sdt",), 256, (lambda I: Win(I)[:, 5120:5152].reshape(KC, 128, 32).transpose(1, 0, 2).reshape(128, -1)))
        o_idx = [wtile(("so", c), 2048, (lambda I, c=c: I["ssd_w_o"][0][:, c * 128:(c + 1) * 128].reshape(16, 128, 128).transpose(1, 0, 2).reshape(128, -1))) for c in range(8)]
        ar = Bump(ARENA, ARENA_SZ)
        TN = 256
        tile_off = ar.take(32768)
        def tile_bufs(tn_alloc):
            b_ = Bump(tile_off, 32768)
            return (P.sb(b_.take(KC * tn_alloc * 2), [128, KC, tn_alloc], BF16), P.sb(b_.take(16 * tn_alloc * 2), [128, 16, tn_alloc], BF16),
                    P.sb(b_.take(16 * tn_alloc * 2), [128, 16, tn_alloc], BF16), P.sb(b_.take(8 * tn_alloc * 2), [128, 8, tn_alloc], BF16),
                    P.sb(b_.take(16 * tn_alloc * 2), [128, 16, tn_alloc], BF16), b_)
        RSn = ar.take(1024)
        AFB = [P.sb(ar.take(259 * 4), [128, 259], F32) for _ in range(2)]
        AFS = [P.sb(ar.take(NB * 7 * 4), [128, NB, 7], F32) for _ in range(2)]
        U = [P.sb(ar.take(1024), [128, 256], F32) for _ in range(2)]
        CAR = P.sb(ar.take(24 * 3 * 4), [128, 24, 3], F32)
        FB = P.sb(ar.take(24 * NB * 3 * 4), [128, 24, NB, 3], F32)
        SCO = P.sb(ar.take(24 * 51 * 4), [128, 24, 51], F32)
        XDT = P.sb(ar.take(4096), [128, 32, 64], BF16)
        XDEC = P.sb(ar.take(4096), [128, 32, 64], BF16)
        BT = P.sb(ar.take(1024), [128, 4, 128], BF16)
        BTM = [P.sb(ar.take(1024), [64, 4, 128], BF16) for _ in range(2)]
        DTT = P.sb(ar.take(6 * 128), [128, 6, 32], F32)
        aseq_off = ar.take(2048)
        ASEQ = P.sb(aseq_off, [64, 16, 32], F32)
        els_off = ar.take(2048)
        ELS = P.sb(els_off, [128, 16, 32], F32)
        ACGH = P.sb(aseq_off, [8, 4, 128], BF16)
        ACGL = P.sb(aseq_off + 1024, [8, 4, 128], BF16)
        SELB = P.sb(els_off, [8, 8, 128], BF16)
        ACG4 = P.sb(ar.take(2048), [8, 4, 128], F32)
        EBs = [P.sb(ar.take(2048), [128, 8, 128], BF16) for _ in range(2)]
        LTs = [P.sb(ar.take(2048), [128, 8, 128], BF16) for _ in range(2)]
        MTs = [P.sb(ar.take(2048), [128, 8, 128], BF16) for _ in range(2)]
        CDECs = [P.sb(ar.take(2048), [128, 8, 128], BF16) for _ in range(2)]
        YGs = [P.sb(ar.take(2048), [128, 4, 128], F32) for _ in range(2)]
        SQYs = [P.sb(ar.take(1024), [128, 4, 128], BF16) for _ in range(2)]
        RSTDs = [P.sb(ar.take(512), [128, 128], F32) for _ in range(2)]
        EB, LT, MT, CDEC, YG, SQY, RSTD = EBs[0], LTs[0], MTs[0], CDECs[0], YGs[0], SQYs[0], RSTDs[0]
        st_off = ar.take(8192)
        ST = P.sb(st_off, [128, 32, 64], F32)
        stb_off = ar.take(4096)
        STb = P.sb(stb_off, [128, 32, 64], BF16)
        CDECF = P.sb(st_off, [128, 32, 64], F32)
        EBF = P.sb(stb_off, [128, 8, 128], F32)
        CBS4 = P.sb(ar.take(2048), [128, 4, 128], F32)
        cbsb = Buf()
        SM = P.sb(ar.take(896 * 4), [128, 896], F32)
        SMS = P.sb(ar.take(400 * 4), [64, 400], F32)
        SEL = P.sb(ar.take(4096), [8, 8, 128], F32)
        CW = P.sb(ar.take(24 * 5 * 4), [128, 24, 5], F32)
        SV = P.sb(ar.take(96 * 4), [128, 96], F32)
        NEGA = P.sb(ar.take(128), [128, 32], F32)
        NEGMB = P.sb(ar.take(1024), [128, 512], BF16)
        cb4 = Buf()
        S.dma("act", SM[:], smask_d, writes=[cb4])
        S.dma("act", SMS[:], smasks_d, writes=[cb4])
        S.dma("act", SEL[:], sel8_d, writes=[cb4])
        S.dma("act", CW[:], scw_d, writes=[cb4])
        S.dma("act", SV[:], svec_d, writes=[cb4])
        S.dma("act", FB[:], sfb_d, writes=[cb4])
        S.op("act", lambda e: e.activation(out=NEGA[:], in_=SV[:, 32:64], func=AF.Exp), reads=[cb4], writes=[cb4])
        S.op("dve", lambda e: e.tensor_scalar(out=NEGA[:], in0=NEGA[:], scalar1=-1.0, scalar2=None, op0=ALU.mult), reads=[cb4], writes=[cb4])
        S.op("dve", lambda e: e.memset(CAR[:], 0.0), writes=[cb4])
        S.op("dve", lambda e: e.tensor_copy(out=NEGMB[:], in_=SM[:, 256:768]), reads=[cb4], writes=[cb4])
        S.op("dve", lambda e: e.tensor_copy(out=SELB[:], in_=SEL[:]), reads=[cb4], writes=[cb4])
        S.op("dve", lambda e: e.memset(ST[:], 0.0), writes=[cb4])
        S.op("dve", lambda e: e.memset(STb[:], 0.0), writes=[cb4])
        DTB, DSK, SNW = SV[:, 0:32], SV[:, 64:80], SV[:, 80:96]
        hb, zsb, xtb, bctb, ob = Buf(), Buf(), Buf(), Buf(), Buf()
        afb, afsb, ub = [Buf(), Buf()], [Buf(), Buf()], [Buf(), Buf()]
        carb, scob = Buf(), Buf()
        xdtb, xdecb, btb, dttb, acgb, ebb, ltb, mtb, cdb, ygb, sqyb, rsb2, stb_, stbb, tmpb = [Buf() for _ in range(15)]
        btmb = [Buf(), Buf()]
        aseqb, elsb = Buf(), Buf()
        def finish_group(g, py, pyb, c0, R, ZS, XT, OT):
            for cl in range(4):
                c = 4 * g + cl
                S.op("dve", lambda e, cl=cl, c=c: e.scalar_tensor_tensor(out=YG[:, cl, 0:R], in0=XT[:, c, c0:c0 + R], scalar=DSK[:, c:c + 1], in1=py[:, cl * 128:cl * 128 + R],
                                                                        op0=ALU.mult, op1=ALU.add), reads=[xtb, cb4, pyb], writes=[ygb])
            S.op("dve", lambda e: e.tensor_tensor(out=YG[:, :, 0:R], in0=YG[:, :, 0:R], in1=ZS[:, 4 * g:4 * g + 4, c0:c0 + R], op=ALU.mult), reads=[ygb, zsb], writes=[ygb])
            S.op("act", lambda e: e.activation(out=SQY[:, :, 0:R], in_=YG[:, :, 0:R], func=AF.Square), reads=[ygb], writes=[sqyb])
            pt, pb = ps()
            for cl in range(4):
                S.op("pe", lambda e, cl=cl, pt=pt: e.matmul(pt[:, 0:R], lhsT=ones_bf[:], rhs=SQY[:, cl, 0:R], start=(cl == 0), stop=(cl == 3)), reads=[sqyb, cbuf], writes=[pb])
            S.op("act", lambda e, pt=pt: e.activation(out=RSTD[:, 0:R], in_=pt[:, 0:R], func=AF.Sqrt, bias=epsb[:], scale=1.0 / 512), reads=[pb, cbuf], writes=[rsb2])
            S.op("dve", lambda e: e.reciprocal(out=RSTD[:, 0:R], in_=RSTD[:, 0:R]), reads=[rsb2], writes=[rsb2])
            for cl in range(4):
                c = 4 * g + cl
                S.op("dve", lambda e, cl=cl, c=c: e.scalar_tensor_tensor(out=OT[:, c, c0:c0 + R], in0=YG[:, cl, 0:R], scalar=SNW[:, c:c + 1], in1=RSTD[:, 0:R],
                                                                        op0=ALU.mult, op1=ALU.mult), reads=[ygb, cb4, rsb2], writes=[ob])

        slot_bufs = [[Buf() for _ in range(7)] for _ in range(2)]

        def group_gen(slot, g, c0, BCT, ZS, XT, OT):
            R = 128
            EBx, LTx, MTx, CDx, YGx, SQx, RSx = EBs[slot], LTs[slot], MTs[slot], CDECs[slot], YGs[slot], SQYs[slot], RSTDs[slot]
            ebx, ltx, mtx, cdx, ygx, sqx, rsx = slot_bufs[slot]
            pbcs = [psb(2 * slot), psb(2 * slot + 1)]
            for hq in range(2):
                pbc, pbcb = pbcs[hq]
                for hh in range(4):
                    S.op("pe", lambda e, pbc=pbc, hh=hh, hq=hq: e.matmul(pbc[:, hh * 128:hh * 128 + R], lhsT=SELB[:, hq * 4 + hh, :], rhs=ACGH[:, g, 0:R], start=(hh == 0), stop=False,
                                                                       skip_group_check=True), reads=[acgb, cb4], writes=[pbcb])
                    S.op("pe", lambda e, pbc=pbc, hh=hh, hq=hq: e.matmul(pbc[:, hh * 128:hh * 128 + R], lhsT=SELB[:, hq * 4 + hh, :], rhs=ACGL[:, g, 0:R], start=False, stop=False,
                                                                       skip_group_check=True), reads=[acgb, cb4], writes=[pbcb])
            yield
            for hq in range(2):
                pbc, pbcb = pbcs[hq]
                S.op("act", lambda e, pbc=pbc, hq=hq: e.activation(out=EBx[:, hq * 4:hq * 4 + 4, 0:R], in_=pbc[:, :].rearrange("p (a b) -> p a b", a=4)[:, :, 0:R], func=AF.Exp),
                     reads=[pbcb], writes=[ebx])
            yield
            for hq in range(2):
                pbc, pbcb = pbcs[hq]
                S.op("pe", lambda e, pbc=pbc: e.matmul(pbc[:, :], lhsT=ident_bf[:, :], rhs=NEGMB[:, :], start=False, stop=True, skip_group_check=True),
                     reads=[cb4, cbuf, ebx], writes=[pbcb])
            yield
            for hq in range(2):
                pbc, pbcb = pbcs[hq]
                for hh in range(4):
                    h = 8 * g + hq * 4 + hh
                    S.op("act", lambda e, pbc=pbc, hh=hh, hq=hq, h=h: e.activation(out=LTx[0:R, hq * 4 + hh, 0:R], in_=pbc[0:R, hh * 128:hh * 128 + R], func=AF.Exp,
                                                                                 bias=DTT[0:R, 2, h:h + 1], scale=1.0), reads=[pbcb, dttb], writes=[ltx])
            yield
            for hq in range(2):
                S.op("dve", lambda e, hq=hq: e.tensor_tensor(out=MTx[0:R, hq * 4:hq * 4 + 4, 0:R], in0=LTx[0:R, hq * 4:hq * 4 + 4, 0:R],
                                                            in1=CBS4[0:R, g, 0:R].unsqueeze(1).broadcast_to([R, 4, R]), op=ALU.mult), reads=[ltx, cbsb], writes=[mtx])
                S.op("dve", lambda e, hq=hq: e.tensor_tensor(out=CDx[:, hq * 4:hq * 4 + 4, :], in0=EBx[:, hq * 4:hq * 4 + 4, :],
                                                            in1=BCT[:, 4 + g, c0:c0 + 128].unsqueeze(1).broadcast_to([128, 4, 128]), op=ALU.mult), reads=[ebx, bctb], writes=[cdx])
            yield
            py, pyb = psb(4 + slot)
            for hl in range(8):
                h = 8 * g + hl
                half, cl = (hl % 2) * 64, hl // 2
                S.op("pe", lambda e, h=h, hl=hl, half=half, cl=cl: e.matmul(py[half:half + 64, cl * 128:cl * 128 + R], lhsT=XDT[0:R, h, :], rhs=MTx[0:R, hl, 0:R],
                                                                          start=(hl < 2), stop=False, tile_position=(0, half), skip_group_check=True), reads=[xdtb, mtx], writes=[pyb])
                S.op("pe", lambda e, h=h, hl=hl, half=half, cl=cl: e.matmul(py[half:half + 64, cl * 128:(cl + 1) * 128], lhsT=STb[:, h, :], rhs=CDx[:, hl, :],
                                                                          start=False, stop=True, tile_position=(0, half), skip_group_check=True), reads=[stbb, cdx], writes=[pyb])
            yield
            for cl in range(4):
                c = 4 * g + cl
                S.op("dve", lambda e, cl=cl, c=c: e.scalar_tensor_tensor(out=YGx[:, cl, 0:R], in0=XT[:, c, c0:c0 + R], scalar=DSK[:, c:c + 1], in1=py[:, cl * 128:cl * 128 + R],
                                                                        op0=ALU.mult, op1=ALU.add), reads=[xtb, cb4, pyb], writes=[ygx])
            S.op("dve", lambda e: e.tensor_tensor(out=YGx[:, :, 0:R], in0=YGx[:, :, 0:R], in1=ZS[:, 4 * g:4 * g + 4, c0:c0 + R], op=ALU.mult), reads=[ygx, zsb], writes=[ygx])
            S.op("act", lambda e: e.activation(out=SQx[:, :, 0:R], in_=YGx[:, :, 0:R], func=AF.Square), reads=[ygx], writes=[sqx])
            yield
            pt, pb = pbcs[0]
            for cl in range(4):
                S.op("pe", lambda e, cl=cl: e.matmul(pt[:, 0:R], lhsT=ones_bf[:], rhs=SQx[:, cl, 0:R], start=(cl == 0), stop=(cl == 3)), reads=[sqx, cbuf], writes=[pb])
            yield
            S.op("act", lambda e: e.activation(out=RSx[:, 0:R], in_=pt[:, 0:R], func=AF.Sqrt, bias=epsb[:], scale=1.0 / 512), reads=[pb, cbuf], writes=[rsx])
            yield
            S.op("dve", lambda e: e.reciprocal(out=RSx[:, 0:R], in_=RSx[:, 0:R]), reads=[rsx], writes=[rsx])
            for cl in range(4):
                c = 4 * g + cl
                S.op("dve", lambda e, cl=cl, c=c: e.scalar_tensor_tensor(out=OT[:, c, c0:c0 + R], in0=YGx[:, cl, 0:R], scalar=SNW[:, c:c + 1], in1=RSx[:, 0:R],
                                                                        op0=ALU.mult, op1=ALU.mult), reads=[ygx, cb4, rsx], writes=[ob])

        def run_groups(c0, BCT, ZS, XT, OT):
            pending = [0, 1, 2, 3]
            active = {}
            for slot in range(2):
                active[slot] = group_gen(slot, pending.pop(0), c0, BCT, ZS, XT, OT)
            while active:
                for slot in list(active.keys()):
                    try:
                        next(active[slot])
                    except StopIteration:
                        if pending:
                            active[slot] = group_gen(slot, pending.pop(0), c0, BCT, ZS, XT, OT)
                        else:
                            del active[slot]

        def out_proj(t0, tn, ti, OT):
            for c in range(KC):
                wv, wb_ = wget(o_idx[c], [128, 16, 128])
                pt, pb = ps()
                for kc in range(16):
                    S.op("pe", lambda e, pt=pt, wv=wv, kc=kc: e.matmul(pt[:, 0:tn], lhsT=wv[:, kc, :], rhs=OT[:, kc, 0:tn], start=(kc == 0), stop=(kc == 15)), reads=[wb_, ob], writes=[pb])
                S.op("dve", lambda e, pt=pt, c=c: e.tensor_tensor(out=X[:, c, t0:t0 + tn], in0=X[:, c, t0:t0 + tn], in1=pt[:, 0:tn], op=ALU.add), reads=[pb, xbuf[ti]], writes=[xbuf[ti]])

        uctr = 0
        gcnt = [0]
        for hi_, (t0, tn) in enumerate(HTILES):
            samp = (t0 == TP)
            ti = t0 // 512
            if samp:
                S.barrier()
                PSMODE[0] = 1
                ylist = []
            HT, ZS, XT, BCT, OT, tb_ = tile_bufs(64 if samp else TN)
            tna = 64 if samp else TN
            SQn = P.sb(tile_off + 48 * tna * 2, [128, KC, tna], BF16)
            RS = P.sb(RSn, [128, 256], F32)
            S.op("act", lambda e, t0=t0, tn=tn, SQn=SQn: e.activation(out=SQn[:, :, 0:tn], in_=X[:, :, t0:t0 + tn], func=AF.Square), reads=[xbuf[ti]], writes=[ob])
            pt, pb = ps()
            for c in range(KC):
                S.op("pe", lambda e, c=c, tn=tn, pt=pt, SQn=SQn: e.matmul(pt[:, 0:tn], lhsT=ones_bf[:], rhs=SQn[:, c, 0:tn], start=(c == 0), stop=(c == KC - 1)), reads=[ob, cbuf], writes=[pb])
            S.op("act", lambda e, tn=tn, pt=pt: e.activation(out=RS[:, 0:tn], in_=pt[:, 0:tn], func=AF.Sqrt, bias=epsb[:], scale=1.0 / D), reads=[pb, cbuf], writes=[rsb2])
            S.op("dve", lambda e, tn=tn: e.reciprocal(out=RS[:, 0:tn], in_=RS[:, 0:tn]), reads=[rsb2], writes=[rsb2])
            for c in range(KC):
                S.op("dve", lambda e, c=c, t0=t0, tn=tn, HT=HT: e.scalar_tensor_tensor(out=HT[:, c, 0:tn], in0=X[:, c, t0:t0 + tn], scalar=NORMW[:, layer, c:c + 1], in1=RS[:, 0:tn],
                                                                                       op0=ALU.mult, op1=ALU.mult), reads=[xbuf[ti], rsb2, cbuf], writes=[hb])
            for g in range(8):
                wv, wb_ = wget(z_idx[g], [128, KC, 256])
                for cc in range(2):
                    c = 2 * g + cc
                    pt, pb = ps()
                    for kc in range(KC):
                        S.op("pe", lambda e, pt=pt, wv=wv, cc=cc, kc=kc, tn=tn, HT=HT: e.matmul(pt[:, 0:tn], lhsT=wv[:, kc, cc * 128:(cc + 1) * 128], rhs=HT[:, kc, 0:tn],
                                                                                              start=(kc == 0), stop=(kc == KC - 1)), reads=[wb_, hb], writes=[pb])
                    S.op("act", lambda e, pt=pt, c=c, tn=tn, ZS=ZS: e.activation(out=ZS[:, c, 0:tn], in_=pt[:, 0:tn], func=AF.Silu), reads=[pb], writes=[zsb])
            swpc = SWP()
            swpg = SWP()
            for g in range(12):
                wv, wb_ = wget(x_idx[g], [128, KC, 256])
                for cc in range(2):
                    ch = 2 * g + cc
                    pt, pb = ps()
                    for kc in range(KC):
                        S.op("pe", lambda e, pt=pt, wv=wv, cc=cc, kc=kc, tn=tn, HT=HT: e.matmul(pt[:, 0:tn], lhsT=wv[:, kc, cc * 128:(cc + 1) * 128], rhs=HT[:, kc, 0:tn],
                                                                                              start=(kc == 0), stop=(kc == KC - 1)), reads=[wb_, hb], writes=[pb])
                    q = ch % 2
                    u, ubf = U[uctr % 2], ub[uctr % 2]; uctr += 1
                    if not samp:
                        A_, ab_ = AFB[q], afb[q]
                        S.op("act", lambda e, A_=A_, ch=ch: e.copy(out=A_[:, 0:3], in_=CAR[:, ch, :]), reads=[carb], writes=[ab_])
                        S.op("act", lambda e, A_=A_, pt=pt, tn=tn: e.activation(out=A_[:, 3:3 + tn], in_=pt[:, 0:tn], func=AF.Copy), reads=[pb], writes=[ab_])
                        srcs = [A_[:, k:k + tn] for k in range(4)]
                        uo = u[:, 0:tn]
                        S.op("act", lambda e, A_=A_, ch=ch, tn=tn: e.copy(out=CAR[:, ch, :], in_=A_[:, tn:tn + 3]), reads=[ab_], writes=[carb])
                        if t0 + tn == TP:
                            S.op("act", lambda e, A_=A_, ch=ch, tn=tn: e.copy(out=SCO[:, ch, 0:3], in_=A_[:, tn:tn + 3]), reads=[ab_], writes=[scob])
                        dst = (XT[:, ch, 0:tn] if ch < 16 else BCT[:, ch - 16, 0:tn])
                    else:
                        A_, ab_ = AFS[q], afsb[q]
                        S.op("act", lambda e, A_=A_, ch=ch: e.copy(out=A_[:, :, 0:3], in_=FB[:, ch, :, :]), reads=[cb4], writes=[ab_])
                        S.op("act", lambda e, A_=A_, pt=pt: e.activation(out=A_[:, :, 3:7], in_=pt[:, 0:TS].rearrange("p (b t) -> p b t", t=4), func=AF.Copy), reads=[pb], writes=[ab_])
                        srcs = [A_[:, :, k:k + 4] for k in range(4)]
                        uo = u[:, 0:TS].rearrange("p (b t) -> p b t", t=4)
                        S.op("act", lambda e, A_=A_, ch=ch: e.copy(out=SCO[:, ch, 3:51].rearrange("p (b j) -> p b j", j=3), in_=A_[:, :, 4:7]), reads=[ab_], writes=[scob])
                        dst = (XT[:, ch, 0:TS] if ch < 16 else BCT[:, ch - 16, 0:TS]).rearrange("p (b t) -> p b t", t=4)
                    S.op("dve", lambda e, uo=uo, s=srcs[3], ch=ch: e.tensor_scalar(out=uo, in0=s, scalar1=CW[:, ch, 3:4], scalar2=CW[:, ch, 4:5], op0=ALU.mult, op1=ALU.add),
                         reads=[ab_, cb4], writes=[ubf])
                    for k in (2, 1, 0):
                        S.op("dve", lambda e, uo=uo, s=srcs[k], ch=ch, k=k: e.scalar_tensor_tensor(out=uo, in0=s, scalar=CW[:, ch, k:k + 1], in1=uo, op0=ALU.mult, op1=ALU.add),
                             reads=[ab_, cb4, ubf], writes=[ubf])
                    def ctail(uo=uo, dst=dst, ubf=ubf, ch=ch):
                        S.op("act", lambda e: e.activation(out=dst, in_=uo, func=AF.Silu), reads=[ubf], writes=[xtb if ch < 16 else bctb])
                    swpc.step([lambda: None, ctail])
            swpc.flush()
            wvd, wbd = wget(dt_idx, [128, KC, 32])
            nblk = 1 if samp else 2
            for bi in range(nblk):
                c0 = bi * 128
                if samp:
                    R = 64
                    triu, tril2, negm4 = SMS[:, 0:64], SMS[:, 64:128], SMS[:, 128:384]
                else:
                    R = 128
                    triu, tril2, negm4 = SM[:, 0:128], SM[:, 128:256], SM[:, 256:768]
                onesf = SM[:, 768:896]
                pt, pb = ps()
                for kc in range(KC):
                    S.op("pe", lambda e, pt=pt, kc=kc, c0=c0, R=R, HT=HT, wvd=wvd: e.matmul(pt[0:R, 0:32], lhsT=HT[:, kc, c0:c0 + R], rhs=wvd[:, kc, :], start=(kc == 0), stop=(kc == KC - 1)),
                         reads=[wbd, hb], writes=[pb])
                S.op("dve", lambda e, pt=pt, R=R: e.tensor_tensor(out=DTT[0:R, 0, :], in0=pt[0:R, 0:32], in1=DTB[0:R, :], op=ALU.add), reads=[pb, cb4], writes=[dttb])
                S.op("act", lambda e, R=R: e.activation(out=DTT[0:R, 0, :], in_=DTT[0:R, 0, :], func=AF.Exp), reads=[dttb], writes=[dttb])
                S.op("act", lambda e, R=R: e.activation(out=DTT[0:R, 0, :], in_=DTT[0:R, 0, :], func=AF.Ln, bias=oneb[0:R, :], scale=1.0), reads=[dttb, cbuf], writes=[dttb])
                S.op("dve", lambda e, R=R: e.tensor_tensor(out=DTT[0:R, 1, :], in0=DTT[0:R, 0, :], in1=NEGA[0:R, :], op=ALU.mult), reads=[dttb, cb4], writes=[dttb])
                pt, pb = ps()
                S.op("pe", lambda e, pt=pt, R=R, triu=triu: e.matmul(pt[0:R, 0:32], lhsT=triu[0:R, :], rhs=DTT[0:R, 1, :], start=True, stop=True), reads=[dttb, cb4], writes=[pb])
                S.op("pe", lambda e, pt=pt, R=R, tril2=tril2: e.matmul(pt[0:R, 32:64], lhsT=tril2[0:R, :], rhs=DTT[0:R, 1, :], start=True, stop=True), reads=[dttb, cb4], writes=[pb])
                if not samp:
                    S.op("pe", lambda e, pt=pt, onesf=onesf: e.matmul(pt[:, 64:96], lhsT=onesf[:, :], rhs=DTT[:, 1, :], start=True, stop=True), reads=[dttb, cb4], writes=[pb])
                S.op("dve", lambda e, pt=pt, R=R: e.tensor_scalar(out=DTT[0:R, 2, :], in0=pt[0:R, 0:32], scalar1=-1.0, scalar2=None, op0=ALU.mult), reads=[pb], writes=[dttb])
                S.op("act", lambda e, pt=pt, R=R: e.activation(out=DTT[0:R, 3, :], in_=pt[0:R, 32:64], func=AF.Exp), reads=[pb], writes=[dttb])
                if not samp:
                    S.op("act", lambda e, pt=pt: e.activation(out=DTT[:, 4, :], in_=pt[:, 64:96], func=AF.Exp), reads=[pb], writes=[dttb])
                else:
                    S.op("dve", lambda e: e.tensor_tensor(out=ASEQ[:, :, :], in0=DTT[0:64, 1, :].unsqueeze(1).broadcast_to([64, 16, 32]),
                                                          in1=SMS[:, 384:400].unsqueeze(2).broadcast_to([64, 16, 32]), op=ALU.mult), reads=[dttb, cb4], writes=[aseqb])
                    pt2, pb2 = ps()
                    S.op("pe", lambda e, pt2=pt2, onesf=onesf: e.matmul(pt2[:, :], lhsT=onesf[0:64, :], rhs=ASEQ[:, :, :].rearrange("p a b -> p (a b)"), start=True, stop=True),
                         reads=[aseqb, cb4], writes=[pb2])
                    S.op("act", lambda e, pt2=pt2: e.activation(out=ELS[:, :, :].rearrange("p a b -> p (a b)"), in_=pt2[:, :], func=AF.Exp), reads=[pb2], writes=[elsb])
                for half in range(2):
                    ptp, ptpb = pst()
                    for cc in range(8):
                        c = half * 8 + cc
                        S.op("pe", lambda e, ptp=ptp, cc=cc, c=c, c0=c0, R=R, XT=XT: e.transpose(out=ptp[0:R, cc * 128:(cc + 1) * 128], in_=XT[:, c, c0:c0 + R], identity=ident_bf[:]),
                             reads=[xtb, cbuf], writes=[ptpb])
                    S.op("dve", lambda e, ptp=ptp, half=half, R=R: e.tensor_tensor(out=XDT[0:R, half * 16:(half + 1) * 16, :], in0=ptp[0:R, :].rearrange("p (a b) -> p a b", b=64),
                                                                                  in1=DTT[0:R, 0, half * 16:(half + 1) * 16].unsqueeze(2).broadcast_to([R, 16, 64]), op=ALU.mult),
                         reads=[ptpb, dttb], writes=[xdtb])
                S.op("dve", lambda e, R=R: e.tensor_tensor(out=XDEC[0:R, :, :], in0=XDT[0:R, :, :], in1=DTT[0:R, 3, :].unsqueeze(2).broadcast_to([R, 32, 64]), op=ALU.mult),
                     reads=[xdtb, dttb], writes=[xdecb])
                ptp, ptpb = pst()
                for g in range(4):
                    S.op("pe", lambda e, ptp=ptp, g=g, c0=c0, R=R, BCT=BCT: e.transpose(out=ptp[0:R, g * 128:(g + 1) * 128], in_=BCT[:, g, c0:c0 + R], identity=ident_bf[:]),
                         reads=[bctb, cbuf], writes=[ptpb])
                S.op("act", lambda e, ptp=ptp, R=R: e.activation(out=BT[0:R, :, :].rearrange("p a b -> p (a b)"), in_=ptp[0:R, 0:512], func=AF.Copy), reads=[ptpb], writes=[btb])
                pt, pb = ps()
                for g in range(4):
                    S.op("pe", lambda e, pt=pt, g=g, R=R, triu=triu: e.matmul(pt[0:8, g * 128:g * 128 + R], lhsT=DTT[0:R, 1, 8 * g:8 * g + 8], rhs=triu[0:R, :], start=True, stop=True),
                         reads=[dttb, cb4], writes=[pb])
                S.op("act", lambda e, pt=pt, R=R: e.activation(out=ACG4[:, :, 0:R], in_=pt[0:8, :].rearrange("p (a b) -> p a b", a=4)[:, :, 0:R], func=AF.Copy), reads=[pb], writes=[acgb])
                if not samp:
                    S.op("dve", lambda e: e.tensor_copy(out=ACGH[:], in_=ACG4[:]), reads=[acgb], writes=[acgb])
                    S.op("dve", lambda e: e.tensor_tensor(out=ACGL[:], in0=ACG4[:], in1=ACGH[:], op=ALU.subtract), reads=[acgb], writes=[acgb])
                pcb, pcbb = ps()
                for g in range(4):
                    S.op("pe", lambda e, pcb=pcb, g=g, c0=c0, R=R, BCT=BCT: e.matmul(pcb[0:R, g * 128:g * 128 + R], lhsT=BCT[:, g, c0:c0 + R], rhs=BCT[:, 4 + g, c0:c0 + R], start=True, stop=True),
                         reads=[bctb], writes=[pcbb])
                S.op("act", lambda e, pcb=pcb, R=R: e.activation(out=CBS4[0:R, :, 0:R], in_=pcb[0:R, :].rearrange("p (a b) -> p a b", a=4)[:, :, 0:R], func=AF.Copy), reads=[pcbb], writes=[cbsb])
                if not samp:
                    run_groups(c0, BCT, ZS, XT, OT)
                for g in (range(4) if samp else []):
                    EBx, ebx = (EBF, stbb) if samp else (EB, ebb)
                    pbcs = [ps(), ps()]
                    for hq in range(2):
                        pbc, pbcb = pbcs[hq]
                        for hh in range(4):
                            S.op("pe", lambda e, pbc=pbc, hh=hh, hq=hq, R=R, g=g: e.matmul(pbc[:, hh * 128:hh * 128 + R], lhsT=SEL[:, hq * 4 + hh, :], rhs=ACG4[:, g, 0:R], start=(hh == 0), stop=False,
                                                                                        skip_group_check=True), reads=[acgb, cb4], writes=[pbcb])
                    for hq in range(2):
                        pbc, pbcb = pbcs[hq]
                        S.op("act", lambda e, pbc=pbc, hq=hq, R=R, EBx=EBx: e.activation(out=EBx[:, hq * 4:hq * 4 + 4, 0:R], in_=pbc[:, :].rearrange("p (a b) -> p a b", a=4)[:, :, 0:R], func=AF.Exp),
                             reads=[pbcb], writes=[ebx])
                    for hq in range(2):
                        pbc, pbcb = pbcs[hq]
                        if samp:
                            for hh in range(4):
                                S.op("pe", lambda e, pbc=pbc, negm4=negm4, hh=hh: e.matmul(pbc[0:64, hh * 128:hh * 128 + 64], lhsT=ident_f[0:64, 0:64], rhs=negm4[0:64, 0:64],
                                                                                        start=False, stop=True, skip_group_check=True), reads=[cb4, cbuf, stbb], writes=[pbcb])
                        else:
                            S.op("pe", lambda e, pbc=pbc: e.matmul(pbc[:, :], lhsT=ident_bf[:, :], rhs=NEGMB[:, :], start=False, stop=True, skip_group_check=True),
                                 reads=[cb4, cbuf, ebb], writes=[pbcb])
                    for hq in range(2):
                        pbc, pbcb = pbcs[hq]
                        for hh in range(4):
                            h = 8 * g + hq * 4 + hh
                            S.op("act", lambda e, pbc=pbc, hh=hh, hq=hq, h=h, R=R: e.activation(out=LT[0:R, hq * 4 + hh, 0:R], in_=pbc[0:R, hh * 128:hh * 128 + R], func=AF.Exp,
                                                                                              bias=DTT[0:R, 2, h:h + 1], scale=1.0), reads=[pbcb, dttb], writes=[ltb])
                    for hq in range(2):
                        S.op("dve", lambda e, hq=hq, R=R, g=g: e.tensor_tensor(out=MT[0:R, hq * 4:hq * 4 + 4, 0:R], in0=LT[0:R, hq * 4:hq * 4 + 4, 0:R],
                                                                              in1=CBS4[0:R, g, 0:R].unsqueeze(1).broadcast_to([R, 4, R]), op=ALU.mult), reads=[ltb, cbsb], writes=[mtb])
                        if samp:
                            S.op("dve", lambda e, hq=hq, g=g, c0=c0, BCT=BCT: e.tensor_tensor(out=CDECF[:, 8 * g + hq * 4:8 * g + hq * 4 + 4, 0:64], in0=EBF[:, hq * 4:hq * 4 + 4, 0:64],
                                                                                            in1=BCT[:, 4 + g, 0:64].unsqueeze(1).broadcast_to([128, 4, 64]), op=ALU.mult), reads=[stbb, bctb], writes=[stb_])
                        else:
                            S.op("dve", lambda e, hq=hq, g=g, c0=c0, BCT=BCT: e.tensor_tensor(out=CDEC[:, hq * 4:hq * 4 + 4, :], in0=EB[:, hq * 4:hq * 4 + 4, :],
                                                                                            in1=BCT[:, 4 + g, c0:c0 + 128].unsqueeze(1).broadcast_to([128, 4, 128]), op=ALU.mult), reads=[ebb, bctb], writes=[cdb])
                    py, pyb = psb([4, 5, 0, 1][g]) if samp else psb(4 + gcnt[0] % 2)
                    gcnt[0] += 1
                    for hl in range(8):
                        h = 8 * g + hl
                        half, cl = (hl % 2) * 64, hl // 2
                        S.op("pe", lambda e, py=py, h=h, hl=hl, half=half, cl=cl, R=R: e.matmul(py[half:half + 64, cl * 128:cl * 128 + R], lhsT=XDT[0:R, h, :], rhs=MT[0:R, hl, 0:R],
                                                                                              start=(hl < 2), stop=False, tile_position=(0, half), skip_group_check=True), reads=[xdtb, mtb], writes=[pyb])
                        if not samp:
                            S.op("pe", lambda e, py=py, h=h, hl=hl, half=half, cl=cl: e.matmul(py[half:half + 64, cl * 128:(cl + 1) * 128], lhsT=STb[:, h, :], rhs=CDEC[:, hl, :],
                                                                                             start=False, stop=True, tile_position=(0, half), skip_group_check=True), reads=[stbb, cdb], writes=[pyb])
                    if samp:
                        ylist.append((g, py, pyb))
                        continue
                    swpg.step([lambda: None, (lambda g=g, py=py, pyb=pyb, c0=c0, R=R, ZS=ZS, XT=XT, OT=OT: finish_group(g, py, pyb, c0, R, ZS, XT, OT))])
                if samp:
                    S0 = [P.sb(tb_.take(8192), [128, 32, 64], F32) for _ in range(2)]
                    SN = P.sb(tb_.take(8192), [128, 32, 64], F32)
                    s0b = [Buf(), Buf()]
                    snb = Buf()
                    for b in range(NB):
                        q2 = b % 2
                        S.dma("sp", S0[q2][:], ssmS_d[b], writes=[s0b[q2]])
                        for (g, py, pyb) in ylist:
                            for hl in range(8):
                                h = 8 * g + hl
                                half, cl = (hl % 2) * 64, hl // 2
                                S.op("pe", lambda e, py=py, h=h, half=half, cl=cl, b=b, q2=q2: e.matmul(py[half:half + 64, cl * 128 + 4 * b:cl * 128 + 4 * b + 4], lhsT=S0[q2][:, h, :], rhs=CDECF[:, h, 4 * b:4 * b + 4],
                                                                                                    start=False, stop=(b == NB - 1), tile_position=(0, half), skip_group_check=True),
                                     reads=[s0b[q2], stb_], writes=[pyb])
                        S.op("dve", lambda e, b=b, q2=q2: e.tensor_tensor(out=SN[:, :, :], in0=S0[q2][:, :, :], in1=ELS[:, b, :].unsqueeze(2).broadcast_to([128, 32, 64]), op=ALU.mult),
                             reads=[s0b[q2], elsb], writes=[snb])
                        S.op("act", lambda e, b=b, q2=q2: e.activation(out=BTM[q2][:, :, :].rearrange("p a b -> p (a b)"), in_=BT[0:64, :, :].rearrange("p a b -> p (a b)"),
                                                                        func=AF.Copy, scale=SMS[:, 384 + b:385 + b]), reads=[btb, cb4], writes=[btmb[q2]])
                        for g in range(4):
                            pt, pb = ps()
                            S.op("pe", lambda e, pt=pt, g=g, q2=q2: e.matmul(pt[:, :], lhsT=BTM[q2][:, g, :], rhs=XDEC[0:64, 8 * g:8 * g + 8, :].rearrange("p a b -> p (a b)"), start=True, stop=True),
                                 reads=[btmb[q2], xdecb], writes=[pb])
                            S.op("dve", lambda e, pt=pt, g=g: e.tensor_tensor(out=SN[:, 8 * g:8 * g + 8, :].rearrange("p a b -> p (a b)"), in0=SN[:, 8 * g:8 * g + 8, :].rearrange("p a b -> p (a b)"),
                                                                             in1=pt[:, :], op=ALU.add), reads=[snb, pb], writes=[snb])
                        S.dma("sp", ssmSo_d[b], SN[:], reads=[snb], is_output=True)
                    for (g, py, pyb) in ylist:
                        finish_group(g, py, pyb, 0, 64, ZS, XT, OT)
                    continue
                S.op("dve", lambda e: e.tensor_tensor(out=ST[:, :, :], in0=ST[:, :, :], in1=DTT[:, 4, :].unsqueeze(2).broadcast_to([128, 32, 64]), op=ALU.mult), reads=[stb_, dttb], writes=[stb_])
                for g in range(4):
                    pt, pb = ps()
                    S.op("pe", lambda e, pt=pt, g=g: e.matmul(pt[:, :], lhsT=BT[:, g, :], rhs=XDEC[:, 8 * g:8 * g + 8, :].rearrange("p a b -> p (a b)"), start=True, stop=True),
                         reads=[btb, xdecb], writes=[pb])
                    S.op("dve", lambda e, pt=pt, g=g: e.tensor_tensor(out=ST[:, 8 * g:8 * g + 8, :].rearrange("p a b -> p (a b)"), in0=ST[:, 8 * g:8 * g + 8, :].rearrange("p a b -> p (a b)"),
                                                                     in1=pt[:, :], op=ALU.add), reads=[stb_, pb], writes=[stb_])
                S.op("act", lambda e: e.copy(out=STb[:, :, :], in_=ST[:, :, :]), reads=[stb_], writes=[stbb])
            swpg.flush()
            out_proj(t0, tn, ti, OT)
            if t0 + tn == TP:
                S.dma("act", ssmP_d, ST[:], reads=[stb_], is_output=True)
            if samp:
                S.dma("act", sconv_d, SCO[:], reads=[scob], is_output=True)
                PSMODE[0] = 0
    for layer in range(NL):
        S.barrier()
        if CFG["mixers"][layer]:
            if layer % 3 == 0:
                attention(layer)
            elif layer % 3 == 1:
                hgrn(layer)
            else:
                ssd(layer)
        S.barrier()
        ffn(layer)
    S.barrier()
    ar = Bump(ARENA, ARENA_SZ)
    YF = [P.sb(ar.take(KC * 512 * 4), [128, KC, 512], F32) for i in range(2)]
    a = ar.take(KC * 512 * 2 + 2048)
    yb = [Buf(), Buf()]
    yv = yT_d.rearrange("(c p) t -> p c t", p=128)
    for ti, (t0, tn) in enumerate(TILES):
        q = ti % 2
        hbufs = {ti: yb[q]}
        rmsnorm(2 * DEPTH, [ti], YF[q], hbufs, t0, a)
        S.dma("act", yv[:, :, t0:t0 + tn], YF[q][:, :, 0:tn], reads=[yb[q]], is_output=True)

    WSTREAM[0] = nc.dram_tensor("wstream", [128, max(P.woff, 1)], F32, kind="ExternalInput").ap()
    P.wlist = wlist
    S.emit()
    return P


_PROG = None


def _prog():
    global _PROG
    if _PROG is None:
        _PROG = build()
    return _PROG


def kernel(**inputs):
    I = {k: np.asarray(v) for k, v in inputs.items()}
    P = _prog()
    wstream = np.zeros((128, max(P.woff, 1)), np.float32)
    for (o, n, builder) in P.wlist:
        wstream[:, o:o + n] = builder(I)
    in_maps = []
    for c in range(NCORES):
        m = {"wstream": wstream}
        for name, (shape, builder) in P.inputs.items():
            m[name] = np.ascontiguousarray(builder(I, c), dtype=np.float32).reshape(shape)
        in_maps.append(m)
    res = run_bass_kernel_spmd(P.nc, in_maps, core_ids=list(range(NCORES)))
    R = res.results
    out = assemble(R)
    order = ["y_prompt", "y_sample", "k_p", "v_p", "hg_p", "ssm_p", "sconv_p", "fconv_p", "k_s", "v_s", "hg_s", "ssm_s", "sconv_s", "fconv_s"]
    if all(k in out for k in order):
        return tuple(np.ascontiguousarray(out[k], dtype=np.float32) for k in order)
    return out


def assemble(R):
    yT = np.stack([R[c]["yT"] for c in range(NCORES)])
    y_prompt = np.ascontiguousarray(yT[:, :, :TP].transpose(0, 2, 1))
    y_sample = np.ascontiguousarray(yT[:, :, TP:].transpose(0, 2, 1)).reshape(NCORES * NB, 4, D)
    fc = np.stack([R[c]["fconvT"] for c in range(NCORES)])
    fc = fc.transpose(1, 0, 4, 3, 2).reshape(DEPTH, NCORES, 2 + 2 * NB, DFF)
    fconv_p = np.ascontiguousarray(fc[:, :, 0:2])
    fconv_s = np.ascontiguousarray(fc[:, :, 2:].reshape(DEPTH, NCORES * NB, 2, DFF))
    out = {"y_prompt": y_prompt, "y_sample": y_sample, "fconv_p": fconv_p, "fconv_s": fconv_s}
    if "kTp" in R[0]:
        kTp = np.stack([R[c]["kTp"] for c in range(NCORES)])
        out["k_p"] = np.ascontiguousarray(kTp.transpose(1, 0, 4, 2, 3)).reshape(2, NCORES, 128, 4, 64)
        vp = np.stack([R[c]["vp"] for c in range(NCORES)])
        out["v_p"] = np.ascontiguousarray(vp.transpose(1, 0, 2, 3)).reshape(2, NCORES, 128, 4, 64)
        ks = np.stack([R[c]["ks"] for c in range(NCORES)])
        out["k_s"] = np.ascontiguousarray(ks.transpose(1, 0, 2, 3, 4)).reshape(2, NCORES * NB, 128, 4, 64)
        vs = np.stack([R[c]["vs"] for c in range(NCORES)])
        out["v_s"] = np.ascontiguousarray(vs.transpose(1, 0, 2, 3, 4)).reshape(2, NCORES * NB, 128, 4, 64)
    if "ssmP" in R[0]:
        sp = np.stack([R[c]["ssmP"] for c in range(NCORES)])
        out["ssm_p"] = np.ascontiguousarray(sp.transpose(0, 2, 3, 1))[None]
        ss_ = np.stack([R[c]["ssmSo"] for c in range(NCORES)])
        out["ssm_s"] = np.ascontiguousarray(ss_.transpose(0, 1, 3, 4, 2)).reshape(1, NCORES * NB, 32, 64, 128)
        sc = np.stack([R[c]["sconvT"] for c in range(NCORES)])
        sc = sc.transpose(0, 3, 2, 1).reshape(NCORES, 51, 3072)
        out["sconv_p"] = np.ascontiguousarray(sc[:, 0:3])[None]
        out["sconv_s"] = np.ascontiguousarray(sc[:, 3:].reshape(NCORES * NB, 3, 3072))[None]
    if "hgP" in R[0]:
        hgP = np.stack([R[c]["hgP"] for c in range(NCORES)])
        out["hg_p"] = np.ascontiguousarray(hgP.transpose(0, 2, 1, 3))[None]
        hgS = np.stack([R[c]["hgSo"] for c in range(NCORES)])
        out["hg_s"] = np.ascontiguousarray(hgS.transpose(0, 1, 3, 2, 4)).reshape(1, NCORES * NB, 8, 128, 128)
    return out
```

```python
import contextlib
import os
import numpy as np
import concourse.bass as bass
import concourse.mybir as mybir
from concourse.bass_utils import run_bass_kernel_spmd

F32 = mybir.dt.float32
BF16 = mybir.dt.bfloat16
AF = mybir.ActivationFunctionType
ALU = mybir.AluOpType
AX = mybir.AxisListType

EPOCH = 12000
NCORES = 8
D = 1024
KC = 8
TP = 2048
TS = 64
T = TP + TS
NB = 16
DFF = 2816
NF = 22
DEPTH = 4
EPS = 1e-6
TILES = [(0, 512), (512, 512), (1024, 512), (1536, 512), (2048, 64)]
GROUPS = [[0, 1], [2, 3, 4]]

CFG = {"layers": DEPTH, "mixers": (True, True, True, True), "ap": True, "as": True, "dd": True, "so": True}


class Buf:
    __slots__ = ("name", "last_w", "readers", "excl")

    def __init__(self, name="", excl=False):
        self.name = name
        self.excl = excl
        self.last_w = None
        self.readers = []


class Op:
    __slots__ = ("eng", "pos", "fn", "waits", "is_dma", "signal", "signum", "dma_sem", "dma_val", "dma_prev")

    def __init__(self, eng, pos, fn, is_dma):
        self.eng = eng
        self.pos = pos
        self.fn = fn
        self.waits = []
        self.is_dma = is_dma
        self.signal = False
        self.signum = None
        self.dma_sem = None
        self.dma_val = None
        self.dma_prev = None


class Sched:
    ENGS = ("pe", "act", "dve", "pool", "sp")

    def __init__(self, nc):
        self.nc = nc
        self.ops = {e: [] for e in self.ENGS}
        self.waited = {e: {p: -1 for p in self.ENGS} for e in self.ENGS}
        self.dma_waited = {e: set() for e in self.ENGS}
        self.dma_pool = {"sp": 16, "act": 20, "pool": 16}
        self.dma_rr = {e: 0 for e in self.ENGS}
        self.dma_last = {}
        self.out_dmas = []
        self.out_seen = 0
        self.anchor_fn = None

    def _add_wait(self, op, dep):
        if dep is None or dep is op:
            return
        e = op.eng
        if dep.is_dma:
            if id(dep) in self.dma_waited[e]:
                return
            self.dma_waited[e].add(id(dep))
            op.waits.append(dep)
            return
        if self.waited[e][dep.eng] >= dep.pos:
            return
        self.waited[e][dep.eng] = dep.pos
        dep.signal = True
        op.waits.append(dep)

    def _issue(self, eng, fn, reads, writes, is_dma, deps=()):
        op = Op(eng, len(self.ops[eng]), fn, is_dma)
        for b in reads:
            if b.last_w is not None:
                self._add_wait(op, b.last_w)
            if b.excl:
                lastr = {}
                for r in b.readers:
                    if r.eng != eng and (r.eng not in lastr or lastr[r.eng].pos < r.pos):
                        lastr[r.eng] = r
                for r in lastr.values():
                    self._add_wait(op, r)
        for b in writes:
            w = b.last_w
            if w is not None and (is_dma or w.is_dma or w.eng != eng):
                self._add_wait(op, w)
            lastr = {}
            for r in b.readers:
                if r.is_dma:
                    self._add_wait(op, r)
                elif is_dma or r.eng != eng:
                    if r.eng not in lastr or lastr[r.eng].pos < r.pos:
                        lastr[r.eng] = r
            for r in lastr.values():
                self._add_wait(op, r)
        for d in deps:
            self._add_wait(op, d)
        for b in reads:
            b.readers.append(op)
        for b in writes:
            b.last_w = op
            b.readers = []
        self.ops[eng].append(op)
        return op

    def op(self, eng, fn, reads=(), writes=(), deps=()):
        return self._issue(eng, fn, reads, writes, False, deps)

    def dma(self, queue, out, in_, reads=(), writes=(), is_output=False, deps=(), **kw):
        def fn(e, out=out, in_=in_, kw=kw):
            return e.dma_start(out=out, in_=(in_() if callable(in_) else in_), **kw)
        op = self._issue(queue, fn, reads, writes, True, deps)
        n = self.dma_pool[queue]
        idx = self.dma_rr[queue] % n
        self.dma_rr[queue] += 1
        key = (queue, idx)
        prev = self.dma_last.get(key)
        op.dma_sem = key
        op.dma_prev = prev
        op.dma_val = (prev.dma_val if prev is not None else 0) + 16
        self.dma_last[key] = op
        if is_output:
            self.out_dmas.append(op)
        return op

    def barrier(self, engs=("pe", "act", "dve", "pool", "sp")):
        lasts = []
        for e in engs:
            for o in reversed(self.ops[e]):
                if not o.is_dma and o.fn is not None:
                    lasts.append(o)
                    break
        anchor = Op("dve", len(self.ops["dve"]), self.anchor_fn, False)
        for l in lasts:
            if l.eng != "dve":
                self._add_wait(anchor, l)
        for o in self.out_dmas[self.out_seen:]:
            self._add_wait(anchor, o)
        self.out_seen = len(self.out_dmas)
        self.ops["dve"].append(anchor)
        for e in engs:
            if e == "dve":
                continue
            op = Op(e, len(self.ops[e]), None, False)
            self._add_wait(op, anchor)
            self.ops[e].append(op)

    def emit(self):
        nc = self.nc
        with contextlib.ExitStack() as st:
            eng_sems = {}
            for e in self.ENGS:
                n = 0
                for o in self.ops[e]:
                    if o.signal and not o.is_dma:
                        n += 1
                        o.signum = n
                nep = max((n + EPOCH - 1) // EPOCH, 1)
                eng_sems[e] = [st.enter_context(nc.semaphore(f"s_{e}_{i}")) for i in range(nep)]
            dma_sems = {}
            for q, n in self.dma_pool.items():
                for i in range(n):
                    if (q, i) in self.dma_last:
                        dma_sems[(q, i)] = st.enter_context(nc.semaphore(f"d_{q}_{i}"))

            def sem_of(o):
                ep = (o.signum - 1) // EPOCH
                return eng_sems[o.eng][ep], o.signum - ep * EPOCH

            block = st.enter_context(nc.Block())

            def make(e):
                def body(eng):
                    for o in self.ops[e]:
                        for d in o.waits:
                            if d.is_dma:
                                eng.wait_ge(dma_sems[d.dma_sem], d.dma_val)
                            else:
                                s, v = sem_of(d)
                                eng.wait_ge(s, v)
                        if o.is_dma:
                            if o.dma_prev is not None:
                                eng.wait_ge(dma_sems[o.dma_sem], o.dma_prev.dma_val)
                            o.fn(eng).then_inc(dma_sems[o.dma_sem], 16)
                        elif o.fn is not None:
                            ins = o.fn(eng)
                            if o.signal:
                                ins.then_inc(sem_of(o)[0], 1)
                    for o in self.out_dmas:
                        if o.eng == e:
                            eng.wait_ge(dma_sems[o.dma_sem], self.dma_last[o.dma_sem].dma_val)
                return body

            block.tensor(make("pe"))
            block.scalar(make("act"))
            block.vector(make("dve"))
            block.gpsimd(make("pool"))
            block.sync(make("sp"))


class Bump:
    def __init__(self, start, size):
        self.start, self.size, self.cur = start, size, start

    def take(self, nbytes):
        o = self.cur
        self.cur += (nbytes + 63) // 64 * 64
        assert self.cur - self.start <= self.size, ("arena overflow", self.cur - self.start, self.size)
        return o


class SWP:
    def __init__(self):
        self.q = []

    def step(self, stages):
        self.q.append(stages)
        n = len(self.q)
        for k in range(8):
            it = n - 1 - k
            if it >= 0 and k < len(self.q[it]):
                self.q[it][k]()

    def flush(self):
        n = len(self.q)
        more = True
        d = 1
        while more:
            more = False
            for k in range(d, 8):
                it = n - 1 - (k - d)
                if it >= 0 and k < len(self.q[it]):
                    self.q[it][k]()
                    more = True
            d += 1
        self.q = []


class Prog:
    def __init__(self):
        self.nc = bass.Bass("TRN2", target_bir_lowering=False)
        self.S = Sched(self.nc)
        self.wspecs = []
        self.wcache = {}
        self.woff = 0
        self.inputs = {}
        self.outputs = {}
        self._n = 0
        nc = self.nc
        self.base = (nc.sbuf_base + 63) // 64 * 64
        self.top = nc.sbuf_top

    def uid(self, p):
        self._n += 1
        return f"{p}{self._n}"

    def sb(self, off, shape, dt):
        return self.nc.alloc_sbuf_tensor_at(self.uid("t"), list(shape), dt, offset=off)

    def din(self, name, shape, builder):
        ap = self.nc.dram_tensor(name, list(shape), F32, kind="ExternalInput").ap()
        self.inputs[name] = (tuple(shape), builder)
        return ap

    def dout(self, name, shape):
        ap = self.nc.dram_tensor(name, list(shape), F32, kind="ExternalOutput").ap()
        self.outputs[name] = tuple(shape)
        return ap


def build():
    P = Prog()
    nc, S = P.nc, P.S
    NL = CFG["layers"]

    off = P.base
    X = P.sb(off, [128, KC, T], F32); off += KC * T * 4
    off = (off + 63) // 64 * 64
    CONST = off; off += 3072
    WB_SLOT, NWB = 4096, 6
    WB0 = off; off += WB_SLOT * NWB
    ARENA = off
    ARENA_SZ = P.top - ARENA
    assert ARENA_SZ > 96000, ARENA_SZ

    xbuf = [Buf(f"x{i}") for i in range(len(TILES))]

    coff = CONST
    ones_bf = P.sb(coff, [128, 128], BF16); coff += 256
    ident_bf = P.sb(coff, [128, 128], BF16); coff += 256
    ident_f = P.sb(coff, [128, 128], F32); coff += 512
    NORMW = P.sb(coff, [128, 9, KC], F32); coff += 9 * KC * 4
    FCW = P.sb(coff, [128, DEPTH, 4, NF], F32); coff += DEPTH * 4 * NF * 4
    epsb = P.sb(coff, [128, 1], F32); coff += 32
    oneb = P.sb(coff, [128, 1], F32); coff += 32
    dummy = P.sb(coff, [128, 1], F32); coff += 32
    S.anchor_fn = lambda e: e.memset(dummy[:], 0.0)
    cbuf = Buf("const")

    xT_d = P.din("xT", [D, T], lambda I, c: np.concatenate(
        [I["x_prompt"][c].T, I["x_sample"][c * NB:(c + 1) * NB].reshape(TS, D).T], axis=1))
    normw_d = P.din("normw", [128, 9, KC], lambda I, c: np.concatenate(
        [I["norm_mix_w"], I["norm_ffn_w"], I["norm_final_w"][None]], axis=0).reshape(9, KC, 128).transpose(2, 0, 1))
    fcw_d = P.din("fcw", [128, DEPTH, 4, NF], lambda I, c: np.concatenate(
        [I["ffn_conv_w"], I["ffn_conv_b"][:, None]], axis=1).reshape(DEPTH, 4, NF, 128).transpose(3, 0, 1, 2))
    ident_d = P.din("ident", [128, 128], lambda I, c: np.eye(128, dtype=np.float32))
    fbuf_d = P.din("fbuf", [DEPTH, 128, NF, NB, 2], lambda I, c: I["state_ffn_conv"][:, c * NB:(c + 1) * NB].reshape(
        DEPTH, NB, 2, NF, 128).transpose(0, 4, 3, 1, 2))
    yT_d = P.dout("yT", [D, T])
    fconv_d = P.dout("fconvT", [DEPTH, 128, NF, 2 + 2 * NB])

    wlist = []

    def wtile(key, n, builder):
        if key in P.wcache:
            return P.wcache[key]
        idx = len(wlist)
        wlist.append((P.woff, n, builder))
        P.woff += n
        P.wcache[key] = idx
        return idx

    wbbuf = [Buf(f"wb{i}") for i in range(NWB)]
    wctr = [0]
    WSTREAM = [None]

    def wget(idx, shape):
        o, n, _ = wlist[idx]
        i = wctr[0]; wctr[0] += 1
        t = i % NWB
        bf_t = P.sb(WB0 + t * WB_SLOT, [128, n], BF16)
        S.dma("pool", bf_t[:], (lambda o=o, n=n: WSTREAM[0][:, o:o + n]), writes=[wbbuf[t]])
        view = P.sb(WB0 + t * WB_SLOT, list(shape), BF16)
        return view, wbbuf[t]

    PSA = nc.alloc_psum_tensor("psa", [128, 6 * 512], F32)
    PST = nc.alloc_psum_tensor("pst", [128, 2 * 1024], BF16)
    psbuf = [Buf(f"ps{i}", excl=True) for i in range(8)]
    psctr = [0]
    pstctr = [0]

    PSMODE = [0]

    def ps():
        if PSMODE[0] == 2:
            i = psctr[0] % 6
        elif PSMODE[0]:
            i = 2 + psctr[0] % 2
        else:
            i = psctr[0] % 4
        psctr[0] += 1
        return PSA[:, i * 512:(i + 1) * 512], psbuf[i]

    def psb(i):
        return PSA[:, i * 512:(i + 1) * 512], psbuf[i]

    def ps2():
        if psctr[0] % 2:
            psctr[0] += 1
        i = psctr[0] % 4
        psctr[0] += 2
        return PSA[:, i * 512:(i + 2) * 512], [psbuf[i], psbuf[i + 1]]

    def ps_long():
        return PSA[:, 4 * 512:6 * 512], [psbuf[4], psbuf[5]]

    def pst():
        i = pstctr[0] % 2
        pstctr[0] += 1
        return PST[:, i * 1024:(i + 1) * 1024], psbuf[6 + i]

    S.dma("act", ident_f[:], ident_d, writes=[cbuf])
    S.dma("act", NORMW[:], normw_d, writes=[cbuf])
    S.dma("act", FCW[:], fcw_d, writes=[cbuf])
    S.op("dve", lambda e: e.memset(ones_bf[:], 1.0), writes=[cbuf])
    S.op("dve", lambda e: e.memset(epsb[:], EPS), writes=[cbuf])
    S.op("dve", lambda e: e.memset(oneb[:], 1.0), writes=[cbuf])
    S.op("dve", lambda e: e.tensor_copy(out=ident_bf[:], in_=ident_f[:]), reads=[cbuf], writes=[cbuf])
    xv = xT_d.rearrange("(c p) t -> p c t", p=128)
    for ti, (t0, tn) in enumerate(TILES):
        S.dma("act", X[:, :, t0:t0 + tn], xv[:, :, t0:t0 + tn], writes=[xbuf[ti]])

    def rmsnorm(norm_idx, tiles, H, hbufs, hoff, sq_off):
        SQ = [P.sb(sq_off, [128, KC, 512], BF16) for i in range(2)]
        RS = [P.sb(sq_off + KC * 512 * 2, [128, 512], F32)] * 2
        sqb = [Buf()] * 2
        rsb = [Buf()] * 2
        for j, ti in enumerate(tiles):
            t0, tn = TILES[ti]
            q = j % 2
            S.op("act", lambda e, q=q, t0=t0, tn=tn: e.activation(out=SQ[q][:, :, 0:tn], in_=X[:, :, t0:t0 + tn], func=AF.Square),
                 reads=[xbuf[ti]], writes=[sqb[q]])
            pt, pb = ps()
            for c in range(KC):
                S.op("pe", lambda e, q=q, c=c, tn=tn, pt=pt: e.matmul(pt[:, 0:tn], lhsT=ones_bf[:], rhs=SQ[q][:, c, 0:tn],
                                                                     start=(c == 0), stop=(c == KC - 1)),
                     reads=[sqb[q], cbuf], writes=[pb])
            S.op("act", lambda e, q=q, tn=tn, pt=pt: e.activation(out=RS[q][:, 0:tn], in_=pt[:, 0:tn], func=AF.Sqrt,
                                                                 bias=epsb[:], scale=1.0 / D),
                 reads=[pb, cbuf], writes=[rsb[q]])
            S.op("dve", lambda e, q=q, tn=tn: e.reciprocal(out=RS[q][:, 0:tn], in_=RS[q][:, 0:tn]), reads=[rsb[q]], writes=[rsb[q]])
            for c in range(KC):
                S.op("dve", lambda e, q=q, c=c, t0=t0, tn=tn: e.scalar_tensor_tensor(
                    out=H[:, c, t0 - hoff:t0 - hoff + tn], in0=X[:, c, t0:t0 + tn], scalar=NORMW[:, norm_idx, c:c + 1],
                    in1=RS[q][:, 0:tn], op0=ALU.mult, op1=ALU.mult),
                    reads=[xbuf[ti], rsb[q], cbuf], writes=[hbufs[ti]])

    def ffn(layer):
        up_idx = [wtile(("up", layer, f), 2 * KC * 128, (lambda I, layer=layer, f=f: np.stack(
            [I["ffn_w_up"][layer][:, f * 128:(f + 1) * 128].reshape(KC, 128, 128),
             I["ffn_w_up"][layer][:, DFF + f * 128:DFF + (f + 1) * 128].reshape(KC, 128, 128)], axis=1)
            .transpose(2, 0, 1, 3).reshape(128, -1))) for f in range(NF)]
        dn_idx = [[wtile(("dn", layer, c, hh), 11 * 128, (lambda I, layer=layer, c=c, hh=hh:
                   I["ffn_w_down"][layer][hh * 1408:(hh + 1) * 1408, c * 128:(c + 1) * 128].reshape(11, 128, 128)
                   .transpose(1, 0, 2).reshape(128, -1))) for hh in range(2)] for c in range(KC)]
        ar = Bump(ARENA, ARENA_SZ)
        TG = 1088
        HGs = [P.sb(ar.take(KC * TG * 2), [128, KC, TG], BF16) for _ in range(2)]
        GG = P.sb(ar.take(NF * TG * 2), [128, NF, TG], BF16)
        AFB = [P.sb(ar.take(1026 * 4), [128, 1026], F32) for i in range(2)]
        AFS = [P.sb(ar.take(NB * 6 * 4), [128, NB, 6], F32) for i in range(2)]
        U = [P.sb(ar.take(2048), [128, 512], F32) for i in range(3)]
        CAR = P.sb(ar.take(NF * 2 * 4), [128, NF, 2], F32)
        FC = P.sb(ar.take(NF * (2 + 2 * NB) * 4), [128, NF, 2 + 2 * NB], F32)
        FB = P.sb(ar.take(NF * NB * 2 * 4), [128, NF, NB, 2], F32)
        sq_off = ar.take(KC * 512 * 2 + 2048)
        hbs = [[Buf() for _ in TILES] for _ in range(2)]
        gb = [[Buf() for _ in TILES] for _ in range(NF)]
        afb = [Buf(), Buf()]
        afsb = [Buf(), Buf()]
        ub = [Buf() for _ in range(3)]
        carb, fcb, fbb = Buf(), Buf(), Buf()
        S.dma("act", FB[:], fbuf_d[layer], writes=[fbb])
        S.op("dve", lambda e: e.memset(CAR[:], 0.0), writes=[carb])
        uctr = 0
        PSMODE[0] = 2
        swp = SWP()
        rmsnorm(DEPTH + layer, GROUPS[0], HGs[0], hbs[0], TILES[GROUPS[0][0]][0], sq_off)
        for gi, grp in enumerate(GROUPS):
            g0 = TILES[grp[0]][0]
            HG, hb = HGs[gi], hbs[gi]
            for f in range(NF):
                wv, wb_ = wget(up_idx[f], [128, KC, 2, 128])
                q = f % 2
                A_, ab_ = AFB[q], afb[q]
                As_, asb_ = AFS[q], afsb[q]
                S.op("act", lambda e, A_=A_, f=f: e.copy(out=A_[:, 0:2], in_=CAR[:, f, :]), reads=[carb], writes=[ab_])
                if gi == 1:
                    S.op("act", lambda e, As_=As_, f=f: e.copy(out=As_[:, :, 0:2], in_=FB[:, f, :, :]), reads=[fbb], writes=[asb_])
                for ti in grp:
                    t0, tn = TILES[ti]
                    l0 = t0 - g0
                    samp = (ti == 4)
                    pa, pab = ps()
                    for c in range(KC):
                        S.op("pe", lambda e, c=c, pa=pa, tn=tn, l0=l0, wv=wv, HG=HG: e.matmul(
                            pa[:, 0:tn], lhsT=wv[:, c, 0, :], rhs=HG[:, c, l0:l0 + tn], start=(c == 0), stop=(c == KC - 1)),
                            reads=[wb_, hb[ti]], writes=[pab])
                    pbt, pbb = ps()
                    for c in range(KC):
                        S.op("pe", lambda e, c=c, pbt=pbt, tn=tn, l0=l0, wv=wv, HG=HG: e.matmul(
                            pbt[:, 0:tn], lhsT=wv[:, c, 1, :], rhs=HG[:, c, l0:l0 + tn], start=(c == 0), stop=(c == KC - 1)),
                            reads=[wb_, hb[ti]], writes=[pbb])
                    u, ubf = U[uctr % 3], ub[uctr % 3]; uctr += 1
                    if not samp:
                        pav = pa[:, 0:tn]
                        adst = A_[:, 2 + l0:2 + l0 + tn]
                        srcs = [A_[:, l0 + k:l0 + k + tn] for k in range(3)]
                        uo = u[:, 0:tn]
                        pbv = pbt[:, 0:tn]
                        go = GG[:, f, l0:l0 + tn]
                        cb_ = ab_
                    else:
                        pav = pa[:, 0:TS].rearrange("p (b t) -> p b t", t=4)
                        adst = As_[:, :, 2:6]
                        srcs = [As_[:, :, k:k + 4] for k in range(3)]
                        uo = u[:, 0:TS].rearrange("p (b t) -> p b t", t=4)
                        pbv = pbt[:, 0:TS].rearrange("p (b t) -> p b t", t=4)
                        go = GG[:, f, l0:l0 + TS].rearrange("p (b t) -> p b t", t=4)
                        cb_ = asb_
                    S.op("act", lambda e, adst=adst, pav=pav: e.activation(out=adst, in_=pav, func=AF.Copy), reads=[pab], writes=[cb_])
                    S.op("act", lambda e, uo=uo, pav=pav, f=f: e.activation(out=uo, in_=pav, func=AF.Identity, scale=FCW[:, layer, 2, f:f + 1], bias=FCW[:, layer, 3, f:f + 1]),
                         reads=[pab, cbuf], writes=[ubf])
                    for k in (1, 0):
                        S.op("dve", lambda e, uo=uo, s=srcs[k], f=f, k=k: e.scalar_tensor_tensor(
                            out=uo, in0=s, scalar=FCW[:, layer, k, f:f + 1], in1=uo, op0=ALU.mult, op1=ALU.add),
                            reads=[cb_, cbuf, ubf], writes=[ubf])
                    def tail(uo=uo, pbv=pbv, go=go, ubf=ubf, pbb=pbb, gbuf=gb[f][ti]):
                        S.op("act", lambda e: e.activation(out=uo, in_=uo, func=AF.Silu), reads=[ubf], writes=[ubf])
                        S.op("dve", lambda e: e.tensor_tensor(out=go, in0=uo, in1=pbv, op=ALU.mult), reads=[ubf, pbb], writes=[gbuf])
                    swp.step([lambda: None, tail])
                if gi == 0:
                    S.op("act", lambda e, A_=A_, f=f: e.copy(out=CAR[:, f, :], in_=A_[:, 1024:1026]), reads=[ab_], writes=[carb])
                else:
                    S.op("act", lambda e, A_=A_, f=f: e.copy(out=FC[:, f, 0:2], in_=A_[:, 1024:1026]), reads=[ab_], writes=[fcb])
                    S.op("act", lambda e, As_=As_, f=f: e.copy(out=FC[:, f, 2:2 + 2 * NB].rearrange("p (b j) -> p b j", j=2),
                                                                       in_=As_[:, :, 4:6]), reads=[asb_], writes=[fcb])
            swp.flush()
            if gi == 0:
                rmsnorm(DEPTH + layer, GROUPS[1], HGs[1], hbs[1], TILES[GROUPS[1][0]][0], sq_off)
            for c in range(KC):
                w0v, w0b = wget(dn_idx[c][0], [128, 11, 128])
                w1v, w1b = wget(dn_idx[c][1], [128, 11, 128])
                for ti in grp:
                    t0, tn = TILES[ti]
                    l0 = t0 - g0
                    pt, pb = ps()
                    for f in range(NF):
                        wv_, wbb_ = (w0v, w0b) if f < 11 else (w1v, w1b)
                        S.op("pe", lambda e, f=f, pt=pt, wv_=wv_, l0=l0, tn=tn: e.matmul(
                            pt[:, 0:tn], lhsT=wv_[:, f % 11, :], rhs=GG[:, f, l0:l0 + tn], start=(f == 0), stop=(f == NF - 1)),
                            reads=[wbb_, gb[f][ti]], writes=[pb])
                    S.op("dve", lambda e, c=c, pt=pt, t0=t0, tn=tn: e.tensor_tensor(
                        out=X[:, c, t0:t0 + tn], in0=X[:, c, t0:t0 + tn], in1=pt[:, 0:tn], op=ALU.add),
                        reads=[pb, xbuf[ti]], writes=[xbuf[ti]])
        PSMODE[0] = 0
        S.dma("act", fconv_d[layer], FC[:], reads=[fcb], is_output=True)

    NEG = -30000.0
    QPERM = [8 * jj + i + 4 * hf for jj in range(2) for i in range(4) for hf in range(2)]
    qcols = np.concatenate([np.arange(h * 64, (h + 1) * 64) for h in QPERM])

    def _maskp():
        q = np.arange(128)[:, None]; s_ = np.arange(256)[None, :]
        return np.where((s_ <= q + 128) & (s_ > q), 0.0, NEG).astype(np.float32)

    def _maskc():
        t_ = (np.arange(16) % 4)[:, None]; r = np.arange(128)[None, :]
        return np.where(r > t_, 0.0, NEG).astype(np.float32)

    def _maskn():
        t_ = (np.arange(16) % 4)[:, None, None]; b_ = np.arange(NB)[None, :, None]
        col = np.arange(64)[None, None, :]
        return np.where((col // 4 == b_) & (col % 4 <= t_), 0.0, NEG).astype(np.float32)

    if any(CFG["mixers"][l] and l % 3 == 0 for l in range(NL)):
        maskp_d = P.din("maskp", [128, 256], lambda I, c: _maskp())
        maskc_d = P.din("maskc", [16, 128], lambda I, c: _maskc())
        maskn_d = P.din("maskn", [16, NB, 64], lambda I, c: _maskn())
        abias_d = P.din("abias", [2, 128, 8 + 2 + 8], lambda I, c: np.stack([np.concatenate([
            I["attn_b_qkv"][s_][qcols].reshape(8, 128).T, I["attn_b_qkv"][s_][1024:1280].reshape(2, 128).T,
            I["attn_b_o"][s_].reshape(8, 128).T], axis=1) for s_ in range(2)]))
        vbias_d = P.din("vbias", [2, 128, 256], lambda I, c: np.broadcast_to(I["attn_b_qkv"][:, None, 1280:1536], (2, 128, 256)))
        sinkp_d = P.din("sinkp", [2, 128, 16], lambda I, c: np.broadcast_to(I["attn_sinks"][:, None, :], (2, 128, 16)))
        sinks_d = P.din("sinks", [2, 16, 4], lambda I, c: np.stack([np.repeat(I["attn_sinks"][s_].reshape(4, 4).T, 4, axis=0) for s_ in range(2)]))
        cacheKT_d = P.din("cacheKT", [2, NB, 2, 128, 128], lambda I, c: I["cache_attn_k"][:, c * NB:(c + 1) * NB].reshape(2, NB, 128, 2, 128).transpose(0, 1, 3, 4, 2))
        cacheV_d = P.din("cacheV", [2, NB, 128, 256], lambda I, c: I["cache_attn_v"][:, c * NB:(c + 1) * NB].reshape(2, NB, 128, 256))
        kTp_d = P.dout("kTp", [2, 2, 128, 128])
        vp_d = P.dout("vp", [2, 128, 256])
        kTs_d = P.dout("ks", [2, NB, 128, 256])
        cacheK_d = P.din("cacheK", [2, NB, 128, 256], lambda I, c: I["cache_attn_k"][:, c * NB:(c + 1) * NB].reshape(2, NB, 128, 256))
        vs_d = P.dout("vs", [2, NB, 128, 256])

    def attention(layer):
        sl = layer // 3
        Wqkv = lambda I: I["attn_w_qkv"][sl]
        q_idx = [wtile(("aq", sl, g), 2048, (lambda I, g=g: Wqkv(I)[:, qcols[g * 256:(g + 1) * 256]].reshape(KC, 128, 256).transpose(1, 0, 2).reshape(128, -1))) for g in range(4)]
        k_idx = wtile(("ak", sl), 2048, (lambda I: Wqkv(I)[:, 1024:1280].reshape(KC, 128, 256).transpose(1, 0, 2).reshape(128, -1)))
        v_idx = wtile(("av", sl), 2048, (lambda I: Wqkv(I)[:, 1280:1536].reshape(KC, 128, 256).transpose(1, 0, 2).reshape(128, -1)))
        qrows = qcols
        o_idx = [wtile(("ao", sl, g), 2048, (lambda I, g=g: I["attn_w_o"][sl][qrows][:, g * 256:(g + 1) * 256].reshape(KC, 128, 256).transpose(1, 0, 2).reshape(128, -1))) for g in range(4)]
        ar = Bump(ARENA, ARENA_SZ)
        HT = P.sb(ar.take(KC * 512 * 2), [128, KC, 512], BF16)
        QT = P.sb(ar.take(KC * 512 * 2), [128, KC, 512], BF16)
        OT = [P.sb(ar.take(KC * 512 * 2), [128, KC, 512], BF16) for _ in range(2)]
        KT = P.sb(ar.take(2 * T * 2), [128, 2, T], BF16)
        VT = P.sb(ar.take(16 * 256 * 2), [128, 16, 256], BF16)
        VS = P.sb(ar.take(512), [64, 256], BF16)
        QS = P.sb(ar.take(1024), [128, NB, 2, 16], BF16)
        qsb = Buf()
        VO = [P.sb(ar.take(1024), [128, 256], F32)] * 2
        KO = P.sb(ar.take(2 * 128 * 4), [128, 2, 128], F32)
        KSO = P.sb(ar.take(2 * 64 * 4), [128, 2, 64], F32)
        KST = P.sb(ar.take(1024), [64, 256], F32)
        kstb = Buf()
        sq_off = ar.take(KC * 512 * 2 + 2048)
        SC = [P.sb(ar.take(4096), [128, 4, 256], F32) for _ in range(3)]
        PN = [P.sb(ar.take(2048), [128, 4, 256], BF16) for _ in range(2)]
        PTs = [P.sb(ar.take(2048), [128, 8, 128], BF16) for _ in range(2)]
        ST = [P.sb(ar.take(128), [128, 6, 4], F32) for _ in range(3)]
        MASKP = P.sb(ar.take(1024), [128, 256], F32)
        MASKC = P.sb(ar.take(512), [16, 128], F32)
        MASKN = P.sb(ar.take(NB * 64 * 4), [16, NB, 64], F32)
        AB = P.sb(ar.take(18 * 4), [128, 18], F32)
        VB = P.sb(ar.take(1024), [128, 256], F32)
        SKP = P.sb(ar.take(64), [128, 16], F32)
        SKP8 = P.sb(ar.take(64), [128, 16], F32)
        SKMX8 = P.sb(ar.take(16), [128, 4], F32)
        SKS = P.sb(ar.take(16), [16, 4], F32)
        SKS8 = P.sb(ar.take(16), [16, 4], F32)
        KCf = [P.sb(ar.take(1024), [128, 2, 128], F32) for _ in range(2)]
        VCf = [P.sb(ar.take(1024), [128, 256], F32) for _ in range(2)]
        KCb = [P.sb(ar.take(512), [128, 2, 128], BF16) for _ in range(2)]
        VCb = [P.sb(ar.take(512), [128, 256], BF16) for _ in range(2)]
        SS = [P.sb(ar.take(4 * 192 * 4), [16, 4, 192], F32) for _ in range(2)]
        PNS = [P.sb(ar.take(4 * 192 * 2), [16, 4, 192], BF16) for _ in range(2)]
        PTC = [P.sb(ar.take(128), [128, 4, 16], BF16) for _ in range(2)]
        PTN = [P.sb(ar.take(128), [64, 4, 16], BF16) for _ in range(2)]
        cb2 = Buf()
        S.dma("act", MASKP[:], maskp_d, writes=[cb2])
        S.dma("act", MASKC[:], maskc_d, writes=[cb2])
        S.dma("act", MASKN[:], maskn_d, writes=[cb2])
        S.dma("act", AB[:], abias_d[sl], writes=[cb2])
        S.dma("act", VB[:], vbias_d[sl], writes=[cb2])
        S.dma("act", SKP[:], sinkp_d[sl], writes=[cb2])
        S.dma("act", SKS[:], sinks_d[sl], writes=[cb2])
        S.op("dve", lambda e: e.tensor_scalar(out=SKP8[:], in0=SKP[:], scalar1=8.0, scalar2=None, op0=ALU.mult), reads=[cb2], writes=[cb2])
        S.op("dve", lambda e: e.tensor_scalar(out=SKS8[:], in0=SKS[:], scalar1=8.0, scalar2=None, op0=ALU.mult), reads=[cb2], writes=[cb2])
        S.op("dve", lambda e: e.tensor_reduce(out=SKMX8[:], in_=SKP8[:, :].rearrange("p (a b) -> p a b", a=4), op=ALU.max, axis=AX.X), reads=[cb2], writes=[cb2])
        if CFG["dd"]:
            S.dma("act", kTs_d[sl][:, 0:124, :], cacheK_d[sl][:, 4:128, :], is_output=True)
            S.dma("act", vs_d[sl][:, 0:124, :], cacheV_d[sl][:, 4:128, :], is_output=True)
        hb, qb = Buf(), Buf()
        ob = [Buf(), Buf()]
        ktb = [Buf() for _ in TILES]
        vtb = [Buf() for _ in range(17)]
        vob = [Buf()] * 2
        kob, ksob = Buf(), Buf()
        scb = [Buf(), Buf(), Buf()]; pnb = [Buf(), Buf()]; ptb = [Buf(), Buf()]; stb = [Buf(), Buf(), Buf()]
        kcfb = [Buf(), Buf()]; vcfb = [Buf(), Buf()]; kcbb = [Buf(), Buf()]; vcbb = [Buf(), Buf()]
        ssb = [Buf(), Buf()]; pnsb = [Buf(), Buf()]; ptcb = [Buf(), Buf()]; ptnb = [Buf(), Buf()]
        gctr = 0
        for ti, (t0, tn) in enumerate(TILES):
            samp = (ti == 4)
            rmsnorm(layer, [ti], HT, {ti: hb}, t0, sq_off)
            for g in range(4):
                wv, wb_ = wget(q_idx[g], [128, KC, 256])
                for cc in range(2):
                    c = 2 * g + cc
                    pt, pb = ps()
                    for kc in range(KC):
                        S.op("pe", lambda e, pt=pt, wv=wv, cc=cc, kc=kc, tn=tn: e.matmul(pt[:, 0:tn], lhsT=wv[:, kc, cc * 128:(cc + 1) * 128], rhs=HT[:, kc, 0:tn],
                                                                                       start=(kc == 0), stop=(kc == KC - 1)), reads=[wb_, hb], writes=[pb])
                    S.op("act", lambda e, pt=pt, c=c, tn=tn: e.activation(out=QT[:, c, 0:tn], in_=pt[:, 0:tn], func=AF.Identity, bias=AB[:, c:c + 1], scale=1.0),
                         reads=[pb, cb2], writes=[qb])
            if samp:
                for jj in range(2):
                    S.op("act", lambda e, jj=jj: e.copy(out=QS[:, :, jj, :].rearrange("p b (i t) -> p b i t", t=4),
                                                                in_=QT[:, jj * 4:(jj + 1) * 4, 0:64].rearrange("p i (b t) -> p b i t", t=4)), reads=[qb], writes=[qsb])
            wv, wb_ = wget(k_idx, [128, KC, 256])
            for jj in range(2):
                pt, pb = ps()
                for kc in range(KC):
                    S.op("pe", lambda e, pt=pt, wv=wv, jj=jj, kc=kc, tn=tn: e.matmul(pt[:, 0:tn], lhsT=wv[:, kc, jj * 128:(jj + 1) * 128], rhs=HT[:, kc, 0:tn],
                                                                                   start=(kc == 0), stop=(kc == KC - 1)), reads=[wb_, hb], writes=[pb])
                S.op("act", lambda e, pt=pt, jj=jj, t0=t0, tn=tn: e.activation(out=KT[:, jj, t0:t0 + tn], in_=pt[:, 0:tn], func=AF.Identity, bias=AB[:, 8 + jj:9 + jj], scale=1.0),
                     reads=[pb, cb2], writes=[ktb[ti]])
                if ti == 3:
                    S.op("dve", lambda e, pt=pt, jj=jj: e.tensor_scalar(out=KO[:, jj, :], in0=pt[:, 384:512], scalar1=AB[:, 8 + jj:9 + jj], scalar2=None, op0=ALU.add),
                         reads=[pb, cb2], writes=[kob])
                if samp:
                    S.op("dve", lambda e, pt=pt, jj=jj: e.tensor_scalar(out=KSO[:, jj, :], in0=pt[:, 0:64], scalar1=AB[:, 8 + jj:9 + jj], scalar2=None, op0=ALU.add),
                         reads=[pb, cb2], writes=[ksob])
            if ti == 3:
                S.dma("act", kTp_d[sl].rearrange("j p k -> p j k"), KO[:], reads=[kob], is_output=True)
            if samp:
                pt, pb = ps()
                for jj in range(2):
                    S.op("pe", lambda e, pt=pt, jj=jj: e.transpose(out=pt[0:64, jj * 128:(jj + 1) * 128], in_=KSO[:, jj, :], identity=ident_f[:]), reads=[ksob, cbuf], writes=[pb])
                S.op("act", lambda e, pt=pt: e.activation(out=KST[:, :], in_=pt[0:64, 0:256], func=AF.Copy), reads=[pb], writes=[kstb])
                for t_ in range(4 if CFG["so"] else 0):
                    S.dma("act", kTs_d[sl][:, 124 + t_, :], KST[t_:64:4, :], reads=[kstb], is_output=True)
            wv, wb_ = wget(v_idx, [128, KC, 256])
            nblk = 1 if samp else 4
            for bi in range(nblk):
                rows = 64 if samp else 128
                pt, pb = ps()
                for kc in range(KC):
                    S.op("pe", lambda e, pt=pt, wv=wv, kc=kc, bi=bi, rows=rows: e.matmul(pt[0:rows, 0:256], lhsT=HT[:, kc, bi * 128:bi * 128 + rows], rhs=wv[:, kc, :],
                                                                                       start=(kc == 0), stop=(kc == KC - 1)), reads=[wb_, hb], writes=[pb])
                gblk = ti * 4 + bi
                if samp:
                    S.op("dve", lambda e, pt=pt: e.tensor_tensor(out=VS[:, :], in0=pt[0:64, 0:256], in1=VB[0:64, :], op=ALU.add), reads=[pb, cb2], writes=[vtb[16]])
                    S.op("dve", lambda e, pt=pt: e.tensor_tensor(out=VO[0][0:64, :], in0=pt[0:64, 0:256], in1=VB[0:64, :], op=ALU.add), reads=[pb, cb2], writes=[vob[0]])
                    for t_ in range(4 if CFG["so"] else 0):
                        S.dma("act", vs_d[sl][:, 124 + t_, :], VO[0][t_:64:4, :], reads=[vob[0]], is_output=True)
                else:
                    S.op("dve", lambda e, pt=pt, gblk=gblk: e.tensor_tensor(out=VT[:, gblk, :], in0=pt[:, 0:256], in1=VB[:, :], op=ALU.add), reads=[pb, cb2], writes=[vtb[gblk]])
                    if gblk == 15:
                        S.op("dve", lambda e, pt=pt: e.tensor_tensor(out=VO[1][:, :], in0=pt[:, 0:256], in1=VB[:, :], op=ALU.add), reads=[pb, cb2], writes=[vob[1]])
                        S.dma("act", vp_d[sl], VO[1][:], reads=[vob[1]], is_output=True)
            oq = ti % 2
            if (samp and not CFG['as']) or (not samp and not CFG['ap']):
                S.op('dve', lambda e, oq=oq: e.memset(OT[oq][:], 0.0), writes=[ob[oq]])
            elif not samp:
                swp = SWP()
                po_state = {}
                for bi in range(4):
                    gblk = ti * 4 + bi
                    first = (gblk == 0)
                    ncol = 128 if first else 256
                    k0 = gblk * 128 if first else (gblk - 1) * 128
                    for j in range(4):
                        jj, base = j // 2, (j % 2) * 64
                        g3 = gctr % 3; g2 = gctr % 2; gctr += 1
                        sc, stt, scb_, stb_ = SC[g3], ST[g3], scb[g3], stb[g3]
                        pn, ptt, pnb_, ptb_ = PN[g2], PTs[g2], pnb[g2], ptb[g2]
                        nkb = ncol // 128
                        mk = MASKP[:, 128:256] if first else MASKP[:, :]
                        hold = {}

                        def st0a(jj=jj, base=base, bi=bi, k0=k0, ncol=ncol, hold=hold):
                            psc, pscb = ps2()
                            hold["psc"] = (psc, pscb)
                            for i in range(4):
                                cq = jj * 4 + i
                                S.op("pe", lambda e, i=i, cq=cq: e.matmul(
                                    psc[:, i * 256:i * 256 + ncol], lhsT=QT[base:base + 64, cq, bi * 128:(bi + 1) * 128], rhs=KT[base:base + 64, jj, k0:k0 + ncol],
                                    start=True, stop=True), reads=[qb] + [ktb[x] for x in {k0 // 512, (k0 + ncol - 1) // 512}], writes=[pscb[i // 2]])

                        def st0(sc=sc, stt=stt, scb_=scb_, stb_=stb_, ncol=ncol, mk=mk, j=j, hold=hold):
                            psc, pscb = hold["psc"]
                            S.op("dve", lambda e: e.tensor_tensor(
                                out=sc[:, :, 0:ncol], in0=psc.rearrange("p (a b) -> p a b", a=4)[:, :, 0:ncol], in1=mk.unsqueeze(1).broadcast_to([128, 4, ncol]), op=ALU.add),
                                reads=pscb + [cb2], writes=[scb_])
                            S.op("dve", lambda e: e.tensor_reduce(out=stt[:, 0, :], in_=sc[:, :, 0:ncol], op=ALU.max, axis=AX.X), reads=[scb_], writes=[stb_])
                            S.op("dve", lambda e: e.tensor_tensor(out=stt[:, 0, :], in0=stt[:, 0, :], in1=SKP8[:, 4 * j:4 * j + 4], op=ALU.max), reads=[stb_, cb2], writes=[stb_])
                            S.op("dve", lambda e: e.tensor_scalar(out=stt[:, 1, :], in0=stt[:, 0, :], scalar1=-0.125, scalar2=None, op0=ALU.mult), reads=[stb_], writes=[stb_])
                            S.op("dve", lambda e: e.tensor_tensor(out=stt[:, 3, :], in0=stt[:, 1, :], in1=SKP[:, 4 * j:4 * j + 4], op=ALU.add), reads=[stb_, cb2], writes=[stb_])

                        def st1(sc=sc, stt=stt, scb_=scb_, stb_=stb_, ncol=ncol):
                            for i in range(4):
                                S.op("act", lambda e, i=i: e.activation(out=sc[:, i, 0:ncol], in_=sc[:, i, 0:ncol], func=AF.Exp, bias=stt[:, 1, i:i + 1], scale=0.125,
                                                                        accum_out=stt[:, 2, i:i + 1]), reads=[scb_, stb_], writes=[scb_, stb_])
                            S.op("act", lambda e: e.activation(out=stt[:, 3, :], in_=stt[:, 3, :], func=AF.Exp), reads=[stb_], writes=[stb_])

                        def st2(sc=sc, stt=stt, scb_=scb_, stb_=stb_, pn=pn, pnb_=pnb_, ncol=ncol, nkb=nkb, hold=hold):
                            S.op("dve", lambda e: e.tensor_tensor(out=stt[:, 4, :], in0=stt[:, 2, :], in1=stt[:, 3, :], op=ALU.add), reads=[stb_], writes=[stb_])
                            S.op("dve", lambda e: e.reciprocal(out=stt[:, 5, :], in_=stt[:, 4, :]), reads=[stb_], writes=[stb_])
                            S.op("dve", lambda e: e.tensor_tensor(out=pn[:, :, 0:ncol], in0=sc[:, :, 0:ncol],
                                                                  in1=stt[:, 5, :].unsqueeze(2).broadcast_to([128, 4, ncol]), op=ALU.mult),
                                 reads=[scb_, stb_], writes=[pnb_])
                            ptp, ptpb = pst()
                            hold["ptp"] = (ptp, ptpb)
                            for i in range(4):
                                for kb in range(nkb):
                                    S.op("pe", lambda e, i=i, kb=kb: e.transpose(out=ptp[:, (i * 2 + kb) * 128:(i * 2 + kb + 1) * 128], in_=pn[:, i, kb * 128:(kb + 1) * 128],
                                                                                 identity=ident_bf[:]), reads=[pnb_, cbuf], writes=[ptpb])

                        def st3(ptt=ptt, ptb_=ptb_, hold=hold, jj=jj, base=base, j=j, nkb=nkb, gblk=gblk, first=first, bi=bi, oq=oq):
                            ptp, ptpb = hold["ptp"]
                            po, pob = ps_long()
                            S.op("act", lambda e: e.activation(out=ptt[:].rearrange("p a b -> p (a b)"), in_=ptp[:, :], func=AF.Copy), reads=[ptpb], writes=[ptb_])
                            for i in range(4):
                                cq = jj * 4 + i
                                for kb in range(nkb):
                                    vblk = gblk if (first or kb == 1) else gblk - 1
                                    S.op("pe", lambda e, i=i, kb=kb, cq=cq, vblk=vblk: e.matmul(
                                        po[base:base + 64, cq * 128:(cq + 1) * 128], lhsT=VT[:, vblk, j * 64:(j + 1) * 64], rhs=ptt[:, i * 2 + kb, :],
                                        start=(kb == 0), stop=(kb == nkb - 1), tile_position=(0, base)), reads=[ptb_, vtb[vblk]], writes=[pob[cq // 4]])
                            if j == 3:
                                S.op("act", lambda e: e.activation(out=OT[oq][:, :, bi * 128:(bi + 1) * 128], in_=po.rearrange("p (a b) -> p a b", a=8), func=AF.Copy),
                                     reads=pob, writes=[ob[oq]])

                        swp.step([st0a, st0, st1, st2, st3])
                swp.flush()
            else:
                def samp_gen(b, q2):
                    S.dma("sp", KCf[q2][:], cacheKT_d[sl][b].rearrange("j p k -> p j k"), writes=[kcfb[q2]])
                    S.dma("sp", VCf[q2][:], cacheV_d[sl][b], writes=[vcfb[q2]])
                    S.op("act", lambda e: e.copy(out=KCb[q2][:], in_=KCf[q2][:]), reads=[kcfb[q2]], writes=[kcbb[q2]])
                    S.op("act", lambda e: e.copy(out=VCb[q2][:], in_=VCf[q2][:]), reads=[vcfb[q2]], writes=[vcbb[q2]])
                    yield
                    pcs = [psb(2 * q2), psb(2 * q2 + 1)]
                    for j in range(4):
                        jj, base = j // 2, (j % 2) * 64
                        pc, pcb = pcs[j % 2]
                        S.op("pe", lambda e, pc=pc, jj=jj, base=base: e.matmul(pc[0:16, jj * 192:jj * 192 + 128], lhsT=QS[base:base + 64, b, jj, :],
                                                                            rhs=KCb[q2][base:base + 64, jj, :], start=True, stop=True), reads=[qsb, kcbb[q2]], writes=[pcb])
                        S.op("pe", lambda e, pc=pc, jj=jj, base=base: e.matmul(pc[0:16, jj * 192 + 128:jj * 192 + 192], lhsT=QS[base:base + 64, b, jj, :],
                                                                            rhs=KT[base:base + 64, jj, TP:T], start=True, stop=True), reads=[qsb, ktb[4]], writes=[pcb])
                    yield
                    ss, pns, stt = SS[q2], PNS[q2], ST[q2]
                    for par in range(2):
                        pc, pcb = pcs[par]
                        pv_ = pc[0:16, 0:384].rearrange("p (a b) -> p a b", a=2)
                        S.op("dve", lambda e, pv_=pv_, par=par: e.tensor_tensor(out=ss[:, par:4:2, 0:128], in0=pv_[:, :, 0:128],
                                                                               in1=MASKC[:, :].unsqueeze(1).broadcast_to([16, 2, 128]), op=ALU.add), reads=[pcb, cb2], writes=[ssb[q2]])
                        S.op("dve", lambda e, pv_=pv_, par=par: e.tensor_tensor(out=ss[:, par:4:2, 128:192], in0=pv_[:, :, 128:192],
                                                                               in1=MASKN[:, b, :].unsqueeze(1).broadcast_to([16, 2, 64]), op=ALU.add), reads=[pcb, cb2], writes=[ssb[q2]])
                    S.op("dve", lambda e: e.tensor_reduce(out=stt[0:16, 0, :], in_=ss[:, :, :], op=ALU.max, axis=AX.X), reads=[ssb[q2]], writes=[stb[q2]])
                    S.op("dve", lambda e: e.tensor_tensor(out=stt[0:16, 0, :], in0=stt[0:16, 0, :], in1=SKS8[:, :], op=ALU.max), reads=[stb[q2], cb2], writes=[stb[q2]])
                    S.op("dve", lambda e: e.tensor_scalar(out=stt[0:16, 1, :], in0=stt[0:16, 0, :], scalar1=-0.125, scalar2=None, op0=ALU.mult), reads=[stb[q2]], writes=[stb[q2]])
                    S.op("dve", lambda e: e.tensor_tensor(out=stt[0:16, 3, :], in0=stt[0:16, 1, :], in1=SKS[:, :], op=ALU.add), reads=[stb[q2], cb2], writes=[stb[q2]])
                    yield
                    for j in range(4):
                        S.op("act", lambda e, j=j: e.activation(out=ss[:, j, :], in_=ss[:, j, :], func=AF.Exp, bias=stt[0:16, 1, j:j + 1], scale=0.125,
                                                                accum_out=stt[0:16, 2, j:j + 1]), reads=[ssb[q2], stb[q2]], writes=[ssb[q2], stb[q2]])
                    S.op("act", lambda e: e.activation(out=stt[0:16, 3, :], in_=stt[0:16, 3, :], func=AF.Exp), reads=[stb[q2]], writes=[stb[q2]])
                    yield
                    S.op("dve", lambda e: e.tensor_tensor(out=stt[0:16, 4, :], in0=stt[0:16, 2, :], in1=stt[0:16, 3, :], op=ALU.add), reads=[stb[q2]], writes=[stb[q2]])
                    S.op("dve", lambda e: e.reciprocal(out=stt[0:16, 5, :], in_=stt[0:16, 4, :]), reads=[stb[q2]], writes=[stb[q2]])
                    S.op("dve", lambda e: e.tensor_tensor(out=pns[:, :, :], in0=ss[:, :, :], in1=stt[0:16, 5, :].unsqueeze(2).broadcast_to([16, 4, 192]), op=ALU.mult),
                         reads=[ssb[q2], stb[q2]], writes=[pnsb[q2]])
                    yield
                    ptp, ptpb = PST[:, q2 * 1024:(q2 + 1) * 1024], psbuf[6 + q2]
                    for j in range(4):
                        S.op("pe", lambda e, j=j: e.transpose(out=ptp[:, j * 16:(j + 1) * 16], in_=pns[:, j, 0:128], identity=ident_bf[0:16, 0:16]),
                             reads=[pnsb[q2], cbuf], writes=[ptpb])
                        S.op("pe", lambda e, j=j: e.transpose(out=ptp[0:64, 64 + j * 16:64 + (j + 1) * 16], in_=pns[:, j, 128:192], identity=ident_bf[0:16, 0:16]),
                             reads=[pnsb[q2], cbuf], writes=[ptpb])
                    yield
                    S.op("act", lambda e: e.activation(out=PTC[q2][:].rearrange("p a b -> p (a b)"), in_=ptp[:, 0:64], func=AF.Copy), reads=[ptpb], writes=[ptcb[q2]])
                    S.op("act", lambda e: e.activation(out=PTN[q2][:].rearrange("p a b -> p (a b)"), in_=ptp[0:64, 64:128], func=AF.Copy), reads=[ptpb], writes=[ptnb[q2]])
                    yield
                    po, pob = psb(4 + q2)
                    for j in range(4):
                        jj, base = j // 2, (j % 2) * 64
                        S.op("pe", lambda e, j=j, jj=jj, base=base: e.matmul(po[base:base + 64, jj * 16:(jj + 1) * 16], lhsT=VCb[q2][:, j * 64:(j + 1) * 64], rhs=PTC[q2][:, j, :],
                                                                           start=True, stop=False, tile_position=(0, base)), reads=[vcbb[q2], ptcb[q2]], writes=[pob])
                        S.op("pe", lambda e, j=j, jj=jj, base=base: e.matmul(po[base:base + 64, jj * 16:(jj + 1) * 16], lhsT=VS[0:64, j * 64:(j + 1) * 64], rhs=PTN[q2][0:64, j, :],
                                                                           start=False, stop=True, tile_position=(0, base)), reads=[vtb[16], ptnb[q2]], writes=[pob])
                    yield
                    S.op("act", lambda e: e.activation(out=OT[oq][:, :, 4 * b:4 * b + 4], in_=po[:, 0:32].rearrange("p (a b) -> p a b", b=4), func=AF.Copy),
                         reads=[pob], writes=[ob[oq]])

                pending = list(range(NB))
                active = {}
                for slot in range(2):
                    active[slot] = samp_gen(pending.pop(0), slot)
                while active:
                    for slot in list(active.keys()):
                        try:
                            next(active[slot])
                        except StopIteration:
                            if pending:
                                active[slot] = samp_gen(pending.pop(0), slot)
                            else:
                                del active[slot]
            for g in range(4):
                wv, wb_ = wget(o_idx[g], [128, KC, 256])
                for cc in range(2):
                    c = 2 * g + cc
                    pt, pb = ps()
                    for kc in range(KC):
                        S.op("pe", lambda e, pt=pt, wv=wv, cc=cc, kc=kc, tn=tn, oq=oq: e.matmul(pt[:, 0:tn], lhsT=wv[:, kc, cc * 128:(cc + 1) * 128], rhs=OT[oq][:, kc, 0:tn],
                                                                                              start=(kc == 0), stop=(kc == KC - 1)), reads=[wb_, ob[oq]], writes=[pb])
                    S.op("dve", lambda e, pt=pt, c=c, t0=t0, tn=tn: e.scalar_tensor_tensor(out=X[:, c, t0:t0 + tn], in0=pt[:, 0:tn], scalar=AB[:, 10 + c:11 + c], in1=X[:, c, t0:t0 + tn],
                                                                                          op0=ALU.add, op1=ALU.add), reads=[pb, cb2, xbuf[ti]], writes=[xbuf[ti]])

    HTILES = [(t0, 256) for t0 in range(0, TP, 256)] + [(TP, TS)]

    def _hmask():
        s_ = np.arange(128)[:, None]; t_ = np.arange(128)[None, :]
        same = (s_ // 16) == (t_ // 16)
        tri = (same & (s_ <= t_)).astype(np.float32)
        tri2 = (same & (s_ > t_)).astype(np.float32)
        cm = ((np.arange(128)[:, None] // 16) == np.arange(8)[None, :]).astype(np.float32)
        return np.concatenate([tri, tri2, -tri, cm], axis=1)

    def _hmask_s():
        s_ = np.arange(64)[:, None]; t_ = np.arange(64)[None, :]
        same = (s_ // 4) == (t_ // 4)
        tri = (same & (s_ <= t_)).astype(np.float32)
        tri2 = (same & (s_ > t_)).astype(np.float32)
        cm = ((np.arange(64)[:, None] // 4) == np.arange(16)[None, :]).astype(np.float32)
        return np.concatenate([tri, tri2, -tri, cm], axis=1)

    if NL > 1 and CFG["mixers"][1]:
        hmask_d = P.din("hmask", [128, 392], lambda I, c: _hmask())
        hmasks_d = P.din("hmasks", [64, 208], lambda I, c: _hmask_s())
        lbl_d = P.din("lbl", [128, 4, 1024], lambda I, c: np.broadcast_to(I["hgrn_lb_logits"][None], (128, 4, 1024)))
        hnw_d = P.din("hnw", [128, 8], lambda I, c: I["hgrn_norm_w"][0].reshape(8, 128).T)
        hgS_d = P.din("hgS", [NB, 128, 8, 128], lambda I, c: I["state_hgrn"][0, c * NB:(c + 1) * NB].transpose(0, 2, 1, 3))
        hgP_d = P.dout("hgP", [128, 8, 128])
        hgSo_d = P.dout("hgSo", [NB, 128, 8, 128])

    def hgrn(layer):
        Win = lambda I: I["hgrn_w_in"][0]
        def wt(name, c0):
            return wtile((name, c0), 2048, (lambda I, c0=c0: Win(I)[:, c0:c0 + 256].reshape(KC, 128, 256).transpose(1, 0, 2).reshape(128, -1)))
        q_idx = [wt("hq", 256 * g) for g in range(4)]
        f_idx = [wt("hf", 1024 + 256 * g) for g in range(4)]
        i_idx = [wt("hi", 2048 + 256 * g) for g in range(4)]
        g_idx = [wt("hg", 3072 + 256 * g) for g in range(4)]
        o_idx = [wtile(("ho", g), 2048, (lambda I, g=g: I["hgrn_w_o"][0][:, g * 256:(g + 1) * 256].reshape(KC, 128, 256).transpose(1, 0, 2).reshape(128, -1))) for g in range(4)]
        ar = Bump(ARENA, ARENA_SZ)
        TN = 256
        HTs = [P.sb(ar.take(KC * TN * 2), [128, KC, TN], BF16) for _ in range(2)]
        SQN2 = P.sb(ar.take(KC * TN * 2), [128, KC, TN], BF16)
        RSN2 = P.sb(ar.take(1024), [128, TN], F32)
        sqnb, rsnb = Buf(), Buf()
        hbs = [Buf(), Buf()]
        SQ = P.sb(ar.take(KC * TN * 2), [128, KC, TN], BF16)
        GS = P.sb(ar.take(KC * TN * 2), [128, KC, TN], BF16)
        OT = P.sb(ar.take(KC * 512 * 2), [128, KC, 512], BF16)
        RSn = ar.take(2048)
        lf_off = ar.take(2 * 4096)
        LF = P.sb(lf_off, [128, 2, 1024], F32)
        l1_off = ar.take(2 * 4096)
        L1 = P.sb(l1_off, [128, 2, 1024], F32)
        V = P.sb(ar.take(2 * 2048), [128, 2, 1024], BF16)
        E = P.sb(ar.take(4096), [128, 8, 128], F32)
        DEC = P.sb(ar.take(8 * 16 * 4), [128, 8, 16], F32)
        QIb = P.sb(ar.take(2048), [128, 8, 128], BF16)
        KI = P.sb(ar.take(2048), [128, 8, 128], BF16)
        KD = P.sb(ar.take(2048), [128, 1024], BF16)
        kdm_off = ar.take(16384)
        KDM = P.sb(kdm_off, [128, 8, 1024], BF16)
        AT = P.sb(ar.take(2048), [128, 8, 128], BF16)
        S32 = [P.sb(ar.take(4096), [128, 8, 128], F32) for _ in range(2)]
        SB16 = [P.sb(ar.take(2048), [128, 8, 128], BF16) for _ in range(2)]
        sbb = [Buf(), Buf()]
        SQO = P.sb(ar.take(2048), [128, 1024], BF16)
        RSTD = P.sb(ar.take(4096), [128, 1024], F32)
        FT = [P.sb(ar.take(1024), [128, 256], F32) for _ in range(2)]
        LB = P.sb(ar.take(4096), [128, 1024], F32)
        OML = P.sb(ar.take(4096), [128, 1024], F32)
        HM = P.sb(ar.take(392 * 4), [128, 392], F32)
        HMS = P.sb(ar.take(208 * 4), [64, 208], F32)
        NW = P.sb(ar.take(32), [128, 8], F32)
        TRIb = P.sb(ar.take(256), [128, 128], BF16)
        TRIsb = P.sb(ar.take(128), [64, 64], BF16)
        S0 = [P.sb(lf_off + 4096, [128, 8, 128], F32), P.sb(l1_off + 4096, [128, 8, 128], F32)]
        cb3 = Buf()
        hb, sqb, gsb, ob = Buf(), Buf(), Buf(), Buf()
        lfb, l1b, vb_ = [Buf(), Buf()], [Buf(), Buf()], [Buf(), Buf()]
        eb, decb, qib, kib, kdb, atb = Buf(), Buf(), Buf(), Buf(), Buf(), Buf()
        kdmb = [Buf() for _ in range(16)]
        sb_ = [[Buf() for _ in range(8)] for _ in range(2)]
        sqob, rstdb = Buf(), Buf()
        ftb = [Buf(), Buf()]
        s0b = [lfb[1], l1b[1]]
        S.dma("act", HM[:], hmask_d, writes=[cb3])
        S.dma("act", HMS[:], hmasks_d, writes=[cb3])
        S.dma("act", NW[:], hnw_d, writes=[cb3])
        LGt = P.sb(kdm_off, [128, 4, 1024], F32)
        S.dma("act", LGt[:], lbl_d, writes=[kdmb[0]])
        S.op("act", lambda e: e.activation(out=LGt[:], in_=LGt[:], func=AF.Exp), reads=[kdmb[0]], writes=[kdmb[0]])
        S.op("dve", lambda e: e.tensor_tensor(out=LB[:], in0=LGt[:, 0, :], in1=LGt[:, 1, :], op=ALU.add), reads=[kdmb[0]], writes=[cb3])
        S.op("dve", lambda e: e.tensor_tensor(out=LB[:], in0=LB[:], in1=LGt[:, 2, :], op=ALU.add), reads=[kdmb[0], cb3], writes=[cb3])
        S.op("dve", lambda e: e.tensor_tensor(out=LB[:], in0=LB[:], in1=LGt[:, 3, :], op=ALU.add), reads=[kdmb[0], cb3], writes=[cb3])
        S.op("dve", lambda e: e.reciprocal(out=LB[:], in_=LB[:]), reads=[cb3], writes=[cb3])
        S.op("dve", lambda e: e.tensor_tensor(out=LB[:], in0=LB[:], in1=LGt[:, 1, :], op=ALU.mult), reads=[kdmb[0], cb3], writes=[cb3] + kdmb[0:8])
        S.op("dve", lambda e: e.tensor_scalar(out=OML[:], in0=LB[:], scalar1=-1.0, scalar2=1.0, op0=ALU.mult, op1=ALU.add), reads=[cb3], writes=[cb3])
        S.op("dve", lambda e: e.tensor_copy(out=TRIb[:], in_=HM[:, 0:128]), reads=[cb3], writes=[cb3])
        S.op("dve", lambda e: e.tensor_copy(out=TRIsb[:], in_=HMS[:, 0:64]), reads=[cb3], writes=[cb3])
        for h in range(8):
            S.op("dve", lambda e, h=h: e.memset(S32[0][:, h, :], 0.0), writes=[sb_[0][h]])
        S.op("dve", lambda e: e.memset(SB16[0][:], 0.0), writes=[sbb[0]])
        def do_norm(idx):
            t0, tn = HTILES[idx]
            ti = t0 // 512
            HTd, hbd = HTs[idx % 2], hbs[idx % 2]
            S.op("act", lambda e: e.activation(out=SQN2[:, :, 0:tn], in_=X[:, :, t0:t0 + tn], func=AF.Square), reads=[xbuf[ti]], writes=[sqnb])
            pt, pb = ps()
            for c in range(KC):
                S.op("pe", lambda e, c=c: e.matmul(pt[:, 0:tn], lhsT=ones_bf[:], rhs=SQN2[:, c, 0:tn], start=(c == 0), stop=(c == KC - 1)), reads=[sqnb, cbuf], writes=[pb])
            S.op("act", lambda e: e.activation(out=RSN2[:, 0:tn], in_=pt[:, 0:tn], func=AF.Sqrt, bias=epsb[:], scale=1.0 / D), reads=[pb, cbuf], writes=[rsnb])
            S.op("dve", lambda e: e.reciprocal(out=RSN2[:, 0:tn], in_=RSN2[:, 0:tn]), reads=[rsnb], writes=[rsnb])
            for c in range(KC):
                S.op("dve", lambda e, c=c: e.scalar_tensor_tensor(out=HTd[:, c, 0:tn], in0=X[:, c, t0:t0 + tn], scalar=NORMW[:, layer, c:c + 1], in1=RSN2[:, 0:tn],
                                                                op0=ALU.mult, op1=ALU.mult), reads=[xbuf[ti], rsnb, cbuf], writes=[hbd])

        do_norm(0)
        cur = 0
        for hi_, (t0, tn) in enumerate(HTILES):
            samp = (t0 == TP)
            ti = t0 // 512
            HT, hb = HTs[hi_ % 2], hbs[hi_ % 2]
            for g in range(4):
                wv, wb_ = wget(q_idx[g], [128, KC, 256])
                for cc in range(2):
                    c = 2 * g + cc
                    pt, pb = ps()
                    for kc in range(KC):
                        S.op("pe", lambda e, pt=pt, wv=wv, cc=cc, kc=kc, tn=tn, HT=HT: e.matmul(pt[:, 0:tn], lhsT=wv[:, kc, cc * 128:(cc + 1) * 128], rhs=HT[:, kc, 0:tn],
                                                                                       start=(kc == 0), stop=(kc == KC - 1)), reads=[wb_, hb], writes=[pb])
                    S.op("act", lambda e, pt=pt, c=c, tn=tn: e.activation(out=SQ[:, c, 0:tn], in_=pt[:, 0:tn], func=AF.Silu), reads=[pb], writes=[sqb])
            for g in range(4):
                wv, wb_ = wget(g_idx[g], [128, KC, 256])
                for cc in range(2):
                    c = 2 * g + cc
                    pt, pb = ps()
                    for kc in range(KC):
                        S.op("pe", lambda e, pt=pt, wv=wv, cc=cc, kc=kc, tn=tn, HT=HT: e.matmul(pt[:, 0:tn], lhsT=wv[:, kc, cc * 128:(cc + 1) * 128], rhs=HT[:, kc, 0:tn],
                                                                                       start=(kc == 0), stop=(kc == KC - 1)), reads=[wb_, hb], writes=[pb])
                    S.op("act", lambda e, pt=pt, c=c, tn=tn: e.activation(out=GS[:, c, 0:tn], in_=pt[:, 0:tn], func=AF.Silu), reads=[pb], writes=[gsb])
                    S.op("dve", lambda e, c=c, tn=tn: e.tensor_scalar(out=GS[:, c, 0:tn], in0=GS[:, c, 0:tn], scalar1=NW[:, c:c + 1], scalar2=None, op0=ALU.mult), reads=[gsb, cb3], writes=[gsb])
            nblk = 1 if samp else 2
            rows = 64 if samp else 128
            fctr = 0
            swpf = SWP()
            for g in range(4):
                wv, wb_ = wget(f_idx[g], [128, KC, 256])
                for bi in range(nblk):
                    pt, pb = ps()
                    for kc in range(KC):
                        S.op("pe", lambda e, pt=pt, wv=wv, kc=kc, bi=bi, rows=rows, HT=HT: e.matmul(pt[0:rows, 0:256], lhsT=HT[:, kc, bi * 128:bi * 128 + rows], rhs=wv[:, kc, :],
                                                                                           start=(kc == 0), stop=(kc == KC - 1)), reads=[wb_, hb], writes=[pb])
                    fq = fctr % 2; fctr += 1
                    ft = FT[fq]
                    cs = slice(g * 256, (g + 1) * 256)
                    S.op("act", lambda e, pt=pt, ft=ft, rows=rows: e.activation(out=ft[0:rows, :], in_=pt[0:rows, 0:256], func=AF.Sigmoid), reads=[pb], writes=[ftb[fq]])
                    S.op("dve", lambda e, ft=ft, rows=rows, cs=cs: e.tensor_tensor(out=ft[0:rows, :], in0=ft[0:rows, :], in1=OML[0:rows, cs], op=ALU.mult), reads=[ftb[fq], cb3], writes=[ftb[fq]])
                    S.op("dve", lambda e, ft=ft, rows=rows, cs=cs: e.tensor_tensor(out=ft[0:rows, :], in0=ft[0:rows, :], in1=LB[0:rows, cs], op=ALU.add), reads=[ftb[fq], cb3], writes=[ftb[fq]])
                    def ftail(ft=ft, rows=rows, cs=cs, bi=bi, fq=fq):
                        S.op("act", lambda e: e.activation(out=LF[0:rows, bi, cs], in_=ft[0:rows, :], func=AF.Ln), reads=[ftb[fq]], writes=[lfb[bi]])
                        S.op("act", lambda e: e.activation(out=L1[0:rows, bi, cs], in_=ft[0:rows, :], func=AF.Ln, scale=-1.0, bias=oneb[0:rows, :]), reads=[ftb[fq], cbuf], writes=[l1b[bi]])
                    swpf.step([lambda: None, ftail])
            swpf.flush()
            for g in range(4):
                wv, wb_ = wget(i_idx[g], [128, KC, 256])
                for bi in range(nblk):
                    pt, pb = ps()
                    for kc in range(KC):
                        S.op("pe", lambda e, pt=pt, wv=wv, kc=kc, bi=bi, rows=rows, HT=HT: e.matmul(pt[0:rows, 0:256], lhsT=HT[:, kc, bi * 128:bi * 128 + rows], rhs=wv[:, kc, :],
                                                                                           start=(kc == 0), stop=(kc == KC - 1)), reads=[wb_, hb], writes=[pb])
                    S.op("act", lambda e, pt=pt, rows=rows, bi=bi, g=g: e.activation(out=V[0:rows, bi, g * 256:(g + 1) * 256], in_=pt[0:rows, 0:256], func=AF.Copy), reads=[pb], writes=[vb_[bi]])
            if hi_ + 1 < len(HTILES):
                do_norm(hi_ + 1)
            for bi in range(nblk):
                c0 = bi * 128
                if samp:
                    R, NCH = 64, 16
                    tri, tri2, ntri, cmm, trib = HMS[:, 0:64], HMS[:, 64:128], HMS[:, 128:192], HMS[:, 192:208], TRIsb
                else:
                    R, NCH = 128, 8
                    tri, tri2, ntri, cmm, trib = HM[:, 0:128], HM[:, 128:256], HM[:, 256:384], HM[:, 384:392], TRIb
                CH = R // NCH
                for hg in range(2):
                    pt, pb = ps()
                    for hh in range(4):
                        h = hg * 4 + hh
                        S.op("pe", lambda e, pt=pt, hh=hh, h=h, bi=bi, R=R, tri=tri: e.matmul(pt[:, hh * 128:hh * 128 + R], lhsT=LF[0:R, bi, h * 128:(h + 1) * 128], rhs=tri[0:R, :],
                                                                                          start=True, stop=True), reads=[lfb[bi], cb3], writes=[pb])
                    S.op("act", lambda e, pt=pt, hg=hg, R=R: e.activation(out=E[:, hg * 4:hg * 4 + 4, 0:R], in_=pt[:, :].rearrange("p (a b) -> p a b", a=4)[:, :, 0:R], func=AF.Exp),
                         reads=[pb], writes=[eb])
                S.op("dve", lambda e, R=R, NCH=NCH, CH=CH: e.tensor_copy(out=DEC[:, :, 0:NCH], in_=E[:, :, CH - 1:R:CH]), reads=[eb], writes=[decb])
                for hg in range(2):
                    pt, pb = ps()
                    for hh in range(4):
                        h = hg * 4 + hh
                        S.op("pe", lambda e, pt=pt, hh=hh, h=h, bi=bi, R=R, ntri=ntri: e.matmul(pt[:, hh * 128:hh * 128 + R], lhsT=LF[0:R, bi, h * 128:(h + 1) * 128], rhs=ntri[0:R, :],
                                                                                            start=(hh == 0), stop=False, skip_group_check=True), reads=[lfb[bi], cb3], writes=[pb])
                        S.op("pe", lambda e, pt=pt, hh=hh, h=h, bi=bi, R=R: e.matmul(pt[:, hh * 128:hh * 128 + R], lhsT=L1[0:R, bi, h * 128:(h + 1) * 128], rhs=ident_f[0:R, 0:R],
                                                                                 start=False, stop=True, skip_group_check=True), reads=[l1b[bi], cbuf], writes=[pb])
                    S.op("act", lambda e, pt=pt, hg=hg, R=R: e.activation(out=KI[:, hg * 4:hg * 4 + 4, 0:R], in_=pt[:, :].rearrange("p (a b) -> p a b", a=4)[:, :, 0:R], func=AF.Exp),
                         reads=[pb], writes=[kib])
                for hg in range(2):
                    pt, pb = ps()
                    S.op("pe", lambda e, pt=pt, hg=hg, bi=bi, R=R, tri2=tri2: e.matmul(pt[0:R, :], lhsT=tri2[0:R, :], rhs=LF[0:R, bi, hg * 512:(hg + 1) * 512], start=True, stop=False),
                         reads=[lfb[bi], cb3], writes=[pb])
                    S.op("pe", lambda e, pt=pt, hg=hg, bi=bi, R=R: e.matmul(pt[0:R, :], lhsT=ident_f[0:R, 0:R], rhs=L1[0:R, bi, hg * 512:(hg + 1) * 512], start=False, stop=True),
                         reads=[l1b[bi], cbuf], writes=[pb])
                    S.op("act", lambda e, pt=pt, hg=hg, R=R: e.activation(out=KD[0:R, hg * 512:(hg + 1) * 512], in_=pt[0:R, :], func=AF.Exp), reads=[pb], writes=[kdb])
                S.op("dve", lambda e, R=R, c0=c0: e.tensor_tensor(out=E[:, :, 0:R], in0=E[:, :, 0:R], in1=SQ[:, :, c0:c0 + R], op=ALU.mult), reads=[eb, sqb, decb], writes=[eb])
                S.op("act", lambda e, R=R: e.copy(out=QIb[:, :, 0:R], in_=E[:, :, 0:R]), reads=[eb], writes=[qib])
                for c in range(0 if samp else NCH):
                    S.op("act", lambda e, c=c, R=R, cmm=cmm: e.activation(out=KDM[0:R, c, :], in_=KD[0:R, :], func=AF.Copy, scale=cmm[0:R, c:c + 1]),
                         reads=[kdb, cb3], writes=[kdmb[c]])
                for hg in range(2):
                    pt, pb = ps()
                    for hh in range(4):
                        h = hg * 4 + hh
                        S.op("pe", lambda e, pt=pt, hh=hh, h=h, R=R: e.matmul(pt[0:R, hh * 128:hh * 128 + R], lhsT=KI[:, h, 0:R], rhs=QIb[:, h, 0:R], start=True, stop=True),
                             reads=[kib, qib], writes=[pb])
                    S.op("dve", lambda e, pt=pt, hg=hg, R=R, trib=trib: e.tensor_tensor(out=AT[0:R, hg * 4:hg * 4 + 4, 0:R], in0=pt[0:R, :].rearrange("p (a b) -> p a b", a=4)[:, :, 0:R],
                                                                                       in1=trib[0:R, 0:R].unsqueeze(1).broadcast_to([R, 4, R]), op=ALU.mult), reads=[pb, cb3], writes=[atb])
                po, pob = ps_long()
                for h in range(8):
                    S.op("pe", lambda e, po=po, h=h, bi=bi, R=R: e.matmul(po[:, h * 128:h * 128 + R], lhsT=V[0:R, bi, h * 128:(h + 1) * 128], rhs=AT[0:R, h, 0:R],
                                                                       start=(h % 4 == 0), stop=False, skip_group_check=True), reads=[vb_[bi], atb], writes=[pob[h // 4]])
                if not samp:
                    for c in range(NCH):
                        nxt = 1 - cur
                        dps = []
                        for hg in range(2):
                            pt, pb = ps()
                            dps.append((pt, pb))
                            for hh in range(4):
                                h = hg * 4 + hh
                                S.op("pe", lambda e, pt=pt, hh=hh, h=h, c=c, bi=bi: e.matmul(pt[:, hh * 128:(hh + 1) * 128], lhsT=KDM[:, c, h * 128:(h + 1) * 128], rhs=V[:, bi, h * 128:(h + 1) * 128],
                                                                                         start=True, stop=True), reads=[kdmb[c], vb_[bi]], writes=[pb])
                        for h in range(8):
                            S.op("pe", lambda e, po=po, h=h, c=c, cur=cur: e.matmul(po[:, h * 128 + c * 16:h * 128 + (c + 1) * 16], lhsT=SB16[cur][:, h, :], rhs=QIb[:, h, c * 16:(c + 1) * 16],
                                                                                start=False, stop=(c == NCH - 1), skip_group_check=True), reads=[sbb[cur], qib], writes=[pob[h // 4]])
                        for hg in range(2):
                            pt, pb = dps[hg]
                            for hh in range(4):
                                h = hg * 4 + hh
                                S.op("dve", lambda e, pt=pt, hh=hh, h=h, c=c, cur=cur, nxt=nxt: e.scalar_tensor_tensor(out=S32[nxt][:, h, :], in0=S32[cur][:, h, :], scalar=DEC[:, h, c:c + 1],
                                                                                                                in1=pt[:, hh * 128:(hh + 1) * 128], op0=ALU.mult, op1=ALU.add),
                                     reads=[sb_[cur][h], decb, pb], writes=[sb_[nxt][h]])
                        S.op("act", lambda e, nxt=nxt: e.copy(out=SB16[nxt][:], in_=S32[nxt][:]), reads=sb_[nxt], writes=[sbb[nxt]])
                        cur = nxt
                else:
                    for b in range(NB):
                        q2 = b % 2
                        S.dma("sp", S0[q2][:], hgS_d[b], writes=[s0b[q2]])
                        for h in range(8):
                            S.op("pe", lambda e, po=po, h=h, b=b, q2=q2: e.matmul(po[:, h * 128 + b * 4:h * 128 + (b + 1) * 4], lhsT=S0[q2][:, h, :], rhs=E[:, h, b * 4:(b + 1) * 4],
                                                                              start=False, stop=(b == NB - 1), skip_group_check=True), reads=[s0b[q2], eb], writes=[pob[h // 4]])
                        SNb = S32[q2]
                        S.op("act", lambda e, b=b, cmm=cmm: e.activation(out=KDM[0:64, b % 8, :], in_=KD[0:64, :], func=AF.Copy, scale=cmm[0:64, b:b + 1]),
                             reads=[kdb, cb3], writes=[kdmb[b % 8]])
                        for hg in range(2):
                            pt, pb = ps()
                            for hh in range(4):
                                h = hg * 4 + hh
                                S.op("pe", lambda e, pt=pt, hh=hh, h=h, b=b: e.matmul(pt[:, hh * 128:(hh + 1) * 128], lhsT=KDM[0:64, b % 8, h * 128:(h + 1) * 128], rhs=V[0:64, 0, h * 128:(h + 1) * 128],
                                                                                  start=True, stop=True), reads=[kdmb[b % 8], vb_[0]], writes=[pb])
                            for hh in range(4):
                                h = hg * 4 + hh
                                S.op("dve", lambda e, pt=pt, hh=hh, h=h, b=b, q2=q2, SNb=SNb: e.scalar_tensor_tensor(out=SNb[:, h, :], in0=S0[q2][:, h, :], scalar=DEC[:, h, b:b + 1],
                                                                                                             in1=pt[:, hh * 128:(hh + 1) * 128], op0=ALU.mult, op1=ALU.add),
                                     reads=[s0b[q2], decb, pb], writes=[sb_[q2][h]])
                        S.dma("sp", hgSo_d[b], SNb[:], reads=sb_[q2], is_output=True)
                S.op("act", lambda e, po=po, R=R: e.activation(out=SQO[:, :].rearrange("p (a b) -> p a b", a=8)[:, :, 0:R], in_=po.rearrange("p (a b) -> p a b", a=8)[:, :, 0:R], func=AF.Square),
                     reads=pob, writes=[sqob])
                for hg in range(2):
                    pt, pb = ps()
                    S.op("pe", lambda e, pt=pt, hg=hg: e.matmul(pt[:, :], lhsT=ones_bf[:], rhs=SQO[:, hg * 512:(hg + 1) * 512], start=True, stop=True), reads=[sqob, cbuf], writes=[pb])
                    S.op("act", lambda e, pt=pt, hg=hg: e.activation(out=RSTD[:, hg * 512:(hg + 1) * 512], in_=pt[:, :], func=AF.Ln, bias=epsb[:], scale=1.0 / 128), reads=[pb, cbuf], writes=[rstdb])
                S.op("act", lambda e: e.activation(out=RSTD[:, :], in_=RSTD[:, :], func=AF.Exp, scale=-0.5), reads=[rstdb], writes=[rstdb])
                S.op("dve", lambda e, po=po, R=R: e.tensor_tensor(out=RSTD[:, :].rearrange("p (a b) -> p a b", a=8)[:, :, 0:R], in0=po.rearrange("p (a b) -> p a b", a=8)[:, :, 0:R],
                                                                 in1=RSTD[:, :].rearrange("p (a b) -> p a b", a=8)[:, :, 0:R], op=ALU.mult), reads=pob + [rstdb], writes=[rstdb])
                S.op("dve", lambda e, R=R, c0=c0: e.tensor_tensor(out=OT[:, :, c0:c0 + R], in0=RSTD[:, :].rearrange("p (a b) -> p a b", a=8)[:, :, 0:R], in1=GS[:, :, c0:c0 + R], op=ALU.mult),
                     reads=[rstdb, gsb], writes=[ob])
            for g in range(4):
                wv, wb_ = wget(o_idx[g], [128, KC, 256])
                for cc in range(2):
                    c = 2 * g + cc
                    pt, pb = ps()
                    for kc in range(KC):
                        S.op("pe", lambda e, pt=pt, wv=wv, cc=cc, kc=kc, tn=tn: e.matmul(pt[:, 0:tn], lhsT=wv[:, kc, cc * 128:(cc + 1) * 128], rhs=OT[:, kc, 0:tn],
                                                                                       start=(kc == 0), stop=(kc == KC - 1)), reads=[wb_, ob], writes=[pb])
                    S.op("dve", lambda e, pt=pt, c=c, t0=t0, tn=tn: e.tensor_tensor(out=X[:, c, t0:t0 + tn], in0=X[:, c, t0:t0 + tn], in1=pt[:, 0:tn], op=ALU.add),
                         reads=[pb, xbuf[ti]], writes=[xbuf[ti]])
            if t0 + tn == TP:
                S.dma("act", hgP_d, S32[cur][:], reads=sb_[cur], is_output=True)

    def _ssd_masks():
        r = np.arange(128)
        triu = (r[:, None] <= r[None, :]).astype(np.float32)
        tril2 = (r[:, None] > r[None, :]).astype(np.float32)
        negm = np.where(r[:, None] <= r[None, :], 0.0, NEG).astype(np.float32)
        return np.concatenate([triu, tril2, np.tile(negm, (1, 4)), np.ones((128, 128), np.float32)], axis=1)

    def _ssd_masks_s():
        r = np.arange(64)
        same = (r[:, None] // 4) == (r[None, :] // 4)
        triu = (same & (r[:, None] <= r[None, :])).astype(np.float32)
        tril2 = (same & (r[:, None] > r[None, :])).astype(np.float32)
        negm = np.where(same & (r[:, None] <= r[None, :]), 0.0, NEG).astype(np.float32)
        seq = ((r[:, None] // 4) == np.arange(16)[None, :]).astype(np.float32)
        return np.concatenate([triu, tril2, np.tile(negm, (1, 4)), seq], axis=1)

    def _sel8():
        s_ = np.zeros((8, 8, 128), np.float32)
        for h in range(8):
            s_[h, h, :] = 1.0
        return s_

    if NL > 2 and CFG["mixers"][2]:
        smask_d = P.din("smask", [128, 896], lambda I, c: _ssd_masks())
        smasks_d = P.din("smasks", [64, 400], lambda I, c: _ssd_masks_s())
        sel8_d = P.din("sel8", [8, 8, 128], lambda I, c: _sel8())
        scw_d = P.din("scw", [128, 24, 5], lambda I, c: np.concatenate([I["ssd_conv_w"][0], I["ssd_conv_b"]], axis=0).reshape(5, 24, 128).transpose(2, 1, 0))
        sfb_d = P.din("sfb", [128, 24, NB, 3], lambda I, c: I["state_ssm_conv"][0, c * NB:(c + 1) * NB].reshape(NB, 3, 24, 128).transpose(3, 2, 0, 1))
        svec_d = P.din("svec", [128, 96], lambda I, c: np.concatenate([
            np.broadcast_to(I["ssd_dt_bias"][0][None], (128, 32)), np.broadcast_to(I["ssd_a_log"][0][None], (128, 32)),
            np.repeat(I["ssd_d"][0].reshape(16, 2), 64, axis=1).T, I["ssd_norm_w"][0].reshape(16, 128).T], axis=1))
        ssmS_d = P.din("ssmS", [NB, 128, 32, 64], lambda I, c: I["state_ssm"][0, c * NB:(c + 1) * NB].transpose(0, 3, 1, 2))
        ssmP_d = P.dout("ssmP", [128, 32, 64])
        ssmSo_d = P.dout("ssmSo", [NB, 128, 32, 64])
        sconv_d = P.dout("sconvT", [128, 24, 3 + 3 * NB])

    def ssd(layer):
        Win = lambda I: I["ssd_w_in"][0]
        def wt(name, c0):
            return wtile((name, c0), 2048, (lambda I, c0=c0: Win(I)[:, c0:c0 + 256].reshape(KC, 128, 256).transpose(1, 0, 2).reshape(128, -1)))
        z_idx = [wt("sz", 256 * g) for g in range(8)]
        x_idx = [wt("sx", 2048 + 256 * g) for g in range(12)]
        dt_idx = wtile(("sdt",), 256, (lambda I: Win(I)[:, 5120:5152].reshape(KC, 128, 32).transpose(1, 0, 2).reshape(128, -1)))
        o_idx = [wtile(("so", c), 2048, (lambda I, c=c: I["ssd_w_o"][0][:, c * 128:(c + 1) * 128].reshape(16, 128, 128).transpose(1, 0, 2).reshape(128, -1))) for c in range(8)]
        ar = Bump(ARENA, ARENA_SZ)
        TN = 256
        tile_off = ar.take(32768)
        def tile_bufs(tn_alloc):
            b_ = Bump(tile_off, 32768)
            return (P.sb(b_.take(KC * tn_alloc * 2), [128, KC, tn_alloc], BF16), P.sb(b_.take(16 * tn_alloc * 2), [128, 16, tn_alloc], BF16),
                    P.sb(b_.take(16 * tn_alloc * 2), [128, 16, tn_alloc], BF16), P.sb(b_.take(8 * tn_alloc * 2), [128, 8, tn_alloc], BF16),
                    P.sb(b_.take(16 * tn_alloc * 2), [128, 16, tn_alloc], BF16), b_)
        RSn = ar.take(1024)
        AFB = [P.sb(ar.take(259 * 4), [128, 259], F32) for _ in range(2)]
        AFS = [P.sb(ar.take(NB * 7 * 4), [128, NB, 7], F32) for _ in range(2)]
        U = [P.sb(ar.take(1024), [128, 256], F32) for _ in range(2)]
        CAR = P.sb(ar.take(24 * 3 * 4), [128, 24, 3], F32)
        FB = P.sb(ar.take(24 * NB * 3 * 4), [128, 24, NB, 3], F32)
        SCO = P.sb(ar.take(24 * 51 * 4), [128, 24, 51], F32)
        XDT = P.sb(ar.take(4096), [128, 32, 64], BF16)
        XDEC = P.sb(ar.take(4096), [128, 32, 64], BF16)
        BT = P.sb(ar.take(1024), [128, 4, 128], BF16)
        BTM = [P.sb(ar.take(1024), [64, 4, 128], BF16) for _ in range(2)]
        DTT = P.sb(ar.take(6 * 128), [128, 6, 32], F32)
        aseq_off = ar.take(2048)
        ASEQ = P.sb(aseq_off, [64, 16, 32], F32)
        els_off = ar.take(2048)
        ELS = P.sb(els_off, [128, 16, 32], F32)
        ACGH = P.sb(aseq_off, [8, 4, 128], BF16)
        ACGL = P.sb(aseq_off + 1024, [8, 4, 128], BF16)
        SELB = P.sb(els_off, [8, 8, 128], BF16)
        ACG4 = P.sb(ar.take(2048), [8, 4, 128], F32)
        EBs = [P.sb(ar.take(2048), [128, 8, 128], BF16) for _ in range(2)]
        LTs = [P.sb(ar.take(2048), [128, 8, 128], BF16) for _ in range(2)]
        MTs = [P.sb(ar.take(2048), [128, 8, 128], BF16) for _ in range(2)]
        CDECs = [P.sb(ar.take(2048), [128, 8, 128], BF16) for _ in range(2)]
        YGs = [P.sb(ar.take(2048), [128, 4, 128], F32) for _ in range(2)]
        SQYs = [P.sb(ar.take(1024), [128, 4, 128], BF16) for _ in range(2)]
        RSTDs = [P.sb(ar.take(512), [128, 128], F32) for _ in range(2)]
        EB, LT, MT, CDEC, YG, SQY, RSTD = EBs[0], LTs[0], MTs[0], CDECs[0], YGs[0], SQYs[0], RSTDs[0]
        st_off = ar.take(8192)
        ST = P.sb(st_off, [128, 32, 64], F32)
        stb_off = ar.take(4096)
        STb = P.sb(stb_off, [128, 32, 64], BF16)
        CDECF = P.sb(st_off, [128, 32, 64], F32)
        EBF = P.sb(stb_off, [128, 8, 128], F32)
        CBS4 = P.sb(ar.take(2048), [128, 4, 128], F32)
        cbsb = Buf()
        SM = P.sb(ar.take(896 * 4), [128, 896], F32)
        SMS = P.sb(ar.take(400 * 4), [64, 400], F32)
        SEL = P.sb(ar.take(4096), [8, 8, 128], F32)
        CW = P.sb(ar.take(24 * 5 * 4), [128, 24, 5], F32)
        SV = P.sb(ar.take(96 * 4), [128, 96], F32)
        NEGA = P.sb(ar.take(128), [128, 32], F32)
        NEGMB = P.sb(ar.take(1024), [128, 512], BF16)
        cb4 = Buf()
        S.dma("act", SM[:], smask_d, writes=[cb4])
        S.dma("act", SMS[:], smasks_d, writes=[cb4])
        S.dma("act", SEL[:], sel8_d, writes=[cb4])
        S.dma("act", CW[:], scw_d, writes=[cb4])
        S.dma("act", SV[:], svec_d, writes=[cb4])
        S.dma("act", FB[:], sfb_d, writes=[cb4])
        S.op("act", lambda e: e.activation(out=NEGA[:], in_=SV[:, 32:64], func=AF.Exp), reads=[cb4], writes=[cb4])
        S.op("dve", lambda e: e.tensor_scalar(out=NEGA[:], in0=NEGA[:], scalar1=-1.0, scalar2=None, op0=ALU.mult), reads=[cb4], writes=[cb4])
        S.op("dve", lambda e: e.memset(CAR[:], 0.0), writes=[cb4])
        S.op("dve", lambda e: e.tensor_copy(out=NEGMB[:], in_=SM[:, 256:768]), reads=[cb4], writes=[cb4])
        S.op("dve", lambda e: e.tensor_copy(out=SELB[:], in_=SEL[:]), reads=[cb4], writes=[cb4])
        S.op("dve", lambda e: e.memset(ST[:], 0.0), writes=[cb4])
        S.op("dve", lambda e: e.memset(STb[:], 0.0), writes=[cb4])
        DTB, DSK, SNW = SV[:, 0:32], SV[:, 64:80], SV[:, 80:96]
        hb, zsb, xtb, bctb, ob = Buf(), Buf(), Buf(), Buf(), Buf()
        afb, afsb, ub = [Buf(), Buf()], [Buf(), Buf()], [Buf(), Buf()]
        carb, scob = Buf(), Buf()
        xdtb, xdecb, btb, dttb, acgb, ebb, ltb, mtb, cdb, ygb, sqyb, rsb2, stb_, stbb, tmpb = [Buf() for _ in range(15)]
        btmb = [Buf(), Buf()]
        aseqb, elsb = Buf(), Buf()
        def finish_group(g, py, pyb, c0, R, ZS, XT, OT):
            for cl in range(4):
                c = 4 * g + cl
                S.op("dve", lambda e, cl=cl, c=c: e.scalar_tensor_tensor(out=YG[:, cl, 0:R], in0=XT[:, c, c0:c0 + R], scalar=DSK[:, c:c + 1], in1=py[:, cl * 128:cl * 128 + R],
                                                                        op0=ALU.mult, op1=ALU.add), reads=[xtb, cb4, pyb], writes=[ygb])
            S.op("dve", lambda e: e.tensor_tensor(out=YG[:, :, 0:R], in0=YG[:, :, 0:R], in1=ZS[:, 4 * g:4 * g + 4, c0:c0 + R], op=ALU.mult), reads=[ygb, zsb], writes=[ygb])
            S.op("act", lambda e: e.activation(out=SQY[:, :, 0:R], in_=YG[:, :, 0:R], func=AF.Square), reads=[ygb], writes=[sqyb])
            pt, pb = ps()
            for cl in range(4):
                S.op("pe", lambda e, cl=cl, pt=pt: e.matmul(pt[:, 0:R], lhsT=ones_bf[:], rhs=SQY[:, cl, 0:R], start=(cl == 0), stop=(cl == 3)), reads=[sqyb, cbuf], writes=[pb])
            S.op("act", lambda e, pt=pt: e.activation(out=RSTD[:, 0:R], in_=pt[:, 0:R], func=AF.Ln, bias=epsb[:], scale=1.0 / 512), reads=[pb, cbuf], writes=[rsb2])
            S.op("act", lambda e: e.activation(out=RSTD[:, 0:R], in_=RSTD[:, 0:R], func=AF.Exp, scale=-0.5), reads=[rsb2], writes=[rsb2])
            for cl in range(4):
                c = 4 * g + cl
                S.op("dve", lambda e, cl=cl, c=c: e.scalar_tensor_tensor(out=OT[:, c, c0:c0 + R], in0=YG[:, cl, 0:R], scalar=SNW[:, c:c + 1], in1=RSTD[:, 0:R],
                                                                        op0=ALU.mult, op1=ALU.mult), reads=[ygb, cb4, rsb2], writes=[ob])

        slot_bufs = [[Buf() for _ in range(7)] for _ in range(2)]

        def group_gen(slot, g, c0, BCT, ZS, XT, OT):
            R = 128
            EBx, LTx, MTx, CDx, YGx, SQx, RSx = EBs[slot], LTs[slot], MTs[slot], CDECs[slot], YGs[slot], SQYs[slot], RSTDs[slot]
            ebx, ltx, mtx, cdx, ygx, sqx, rsx = slot_bufs[slot]
            pbcs = [psb(2 * slot), psb(2 * slot + 1)]
            for hq in range(2):
                pbc, pbcb = pbcs[hq]
                for hh in range(4):
                    S.op("pe", lambda e, pbc=pbc, hh=hh, hq=hq: e.matmul(pbc[:, hh * 128:hh * 128 + R], lhsT=SELB[:, hq * 4 + hh, :], rhs=ACGH[:, g, 0:R], start=(hh == 0), stop=False,
                                                                       skip_group_check=True), reads=[acgb, cb4], writes=[pbcb])
                    S.op("pe", lambda e, pbc=pbc, hh=hh, hq=hq: e.matmul(pbc[:, hh * 128:hh * 128 + R], lhsT=SELB[:, hq * 4 + hh, :], rhs=ACGL[:, g, 0:R], start=False, stop=False,
                                                                       skip_group_check=True), reads=[acgb, cb4], writes=[pbcb])
            yield
            for hq in range(2):
                pbc, pbcb = pbcs[hq]
                S.op("act", lambda e, pbc=pbc, hq=hq: e.activation(out=EBx[:, hq * 4:hq * 4 + 4, 0:R], in_=pbc[:, :].rearrange("p (a b) -> p a b", a=4)[:, :, 0:R], func=AF.Exp),
                     reads=[pbcb], writes=[ebx])
            yield
            for hq in range(2):
                pbc, pbcb = pbcs[hq]
                S.op("pe", lambda e, pbc=pbc: e.matmul(pbc[:, :], lhsT=ident_bf[:, :], rhs=NEGMB[:, :], start=False, stop=True, skip_group_check=True),
                     reads=[cb4, cbuf, ebx], writes=[pbcb])
            yield
            for hq in range(2):
                pbc, pbcb = pbcs[hq]
                for hh in range(4):
                    h = 8 * g + hq * 4 + hh
                    S.op("act", lambda e, pbc=pbc, hh=hh, hq=hq, h=h: e.activation(out=LTx[0:R, hq * 4 + hh, 0:R], in_=pbc[0:R, hh * 128:hh * 128 + R], func=AF.Exp,
                                                                                 bias=DTT[0:R, 2, h:h + 1], scale=1.0), reads=[pbcb, dttb], writes=[ltx])
            yield
            for hq in range(2):
                S.op("dve", lambda e, hq=hq: e.tensor_tensor(out=MTx[0:R, hq * 4:hq * 4 + 4, 0:R], in0=LTx[0:R, hq * 4:hq * 4 + 4, 0:R],
                                                            in1=CBS4[0:R, g, 0:R].unsqueeze(1).broadcast_to([R, 4, R]), op=ALU.mult), reads=[ltx, cbsb], writes=[mtx])
                S.op("dve", lambda e, hq=hq: e.tensor_tensor(out=CDx[:, hq * 4:hq * 4 + 4, :], in0=EBx[:, hq * 4:hq * 4 + 4, :],
                                                            in1=BCT[:, 4 + g, c0:c0 + 128].unsqueeze(1).broadcast_to([128, 4, 128]), op=ALU.mult), reads=[ebx, bctb], writes=[cdx])
            yield
            py, pyb = psb(4 + slot)
            for hl in range(8):
                h = 8 * g + hl
                half, cl = (hl % 2) * 64, hl // 2
                S.op("pe", lambda e, h=h, hl=hl, half=half, cl=cl: e.matmul(py[half:half + 64, cl * 128:cl * 128 + R], lhsT=XDT[0:R, h, :], rhs=MTx[0:R, hl, 0:R],
                                                                          start=(hl < 2), stop=False, tile_position=(0, half), skip_group_check=True), reads=[xdtb, mtx], writes=[pyb])
                S.op("pe", lambda e, h=h, hl=hl, half=half, cl=cl: e.matmul(py[half:half + 64, cl * 128:(cl + 1) * 128], lhsT=STb[:, h, :], rhs=CDx[:, hl, :],
                                                                          start=False, stop=True, tile_position=(0, half), skip_group_check=True), reads=[stbb, cdx], writes=[pyb])
            yield
            for cl in range(4):
                c = 4 * g + cl
                S.op("dve", lambda e, cl=cl, c=c: e.scalar_tensor_tensor(out=YGx[:, cl, 0:R], in0=XT[:, c, c0:c0 + R], scalar=DSK[:, c:c + 1], in1=py[:, cl * 128:cl * 128 + R],
                                                                        op0=ALU.mult, op1=ALU.add), reads=[xtb, cb4, pyb], writes=[ygx])
            S.op("dve", lambda e: e.tensor_tensor(out=YGx[:, :, 0:R], in0=YGx[:, :, 0:R], in1=ZS[:, 4 * g:4 * g + 4, c0:c0 + R], op=ALU.mult), reads=[ygx, zsb], writes=[ygx])
            S.op("act", lambda e: e.activation(out=SQx[:, :, 0:R], in_=YGx[:, :, 0:R], func=AF.Square), reads=[ygx], writes=[sqx])
            yield
            pt, pb = pbcs[0]
            for cl in range(4):
                S.op("pe", lambda e, cl=cl: e.matmul(pt[:, 0:R], lhsT=ones_bf[:], rhs=SQx[:, cl, 0:R], start=(cl == 0), stop=(cl == 3)), reads=[sqx, cbuf], writes=[pb])
            yield
            S.op("act", lambda e: e.activation(out=RSx[:, 0:R], in_=pt[:, 0:R], func=AF.Ln, bias=epsb[:], scale=1.0 / 512), reads=[pb, cbuf], writes=[rsx])
            S.op("act", lambda e: e.activation(out=RSx[:, 0:R], in_=RSx[:, 0:R], func=AF.Exp, scale=-0.5), reads=[rsx], writes=[rsx])
            yield
            for cl in range(4):
                c = 4 * g + cl
                S.op("dve", lambda e, cl=cl, c=c: e.scalar_tensor_tensor(out=OT[:, c, c0:c0 + R], in0=YGx[:, cl, 0:R], scalar=SNW[:, c:c + 1], in1=RSx[:, 0:R],
                                                                        op0=ALU.mult, op1=ALU.mult), reads=[ygx, cb4, rsx], writes=[ob])

        def run_groups(c0, BCT, ZS, XT, OT):
            pending = [0, 1, 2, 3]
            active = {}
            for slot in range(2):
                active[slot] = group_gen(slot, pending.pop(0), c0, BCT, ZS, XT, OT)
            while active:
                for slot in list(active.keys()):
                    try:
                        next(active[slot])
                    except StopIteration:
                        if pending:
                            active[slot] = group_gen(slot, pending.pop(0), c0, BCT, ZS, XT, OT)
                        else:
                            del active[slot]

        def out_proj(t0, tn, ti, OT):
            for c in range(KC):
                wv, wb_ = wget(o_idx[c], [128, 16, 128])
                pt, pb = ps()
                for kc in range(16):
                    S.op("pe", lambda e, pt=pt, wv=wv, kc=kc: e.matmul(pt[:, 0:tn], lhsT=wv[:, kc, :], rhs=OT[:, kc, 0:tn], start=(kc == 0), stop=(kc == 15)), reads=[wb_, ob], writes=[pb])
                S.op("dve", lambda e, pt=pt, c=c: e.tensor_tensor(out=X[:, c, t0:t0 + tn], in0=X[:, c, t0:t0 + tn], in1=pt[:, 0:tn], op=ALU.add), reads=[pb, xbuf[ti]], writes=[xbuf[ti]])

        uctr = 0
        gcnt = [0]
        for hi_, (t0, tn) in enumerate(HTILES):
            samp = (t0 == TP)
            ti = t0 // 512
            if samp:
                S.barrier()
                PSMODE[0] = 1
                ylist = []
            HT, ZS, XT, BCT, OT, tb_ = tile_bufs(64 if samp else TN)
            tna = 64 if samp else TN
            SQn = P.sb(tile_off + 48 * tna * 2, [128, KC, tna], BF16)
            RS = P.sb(RSn, [128, 256], F32)
            S.op("act", lambda e, t0=t0, tn=tn, SQn=SQn: e.activation(out=SQn[:, :, 0:tn], in_=X[:, :, t0:t0 + tn], func=AF.Square), reads=[xbuf[ti]], writes=[ob])
            pt, pb = ps()
            for c in range(KC):
                S.op("pe", lambda e, c=c, tn=tn, pt=pt, SQn=SQn: e.matmul(pt[:, 0:tn], lhsT=ones_bf[:], rhs=SQn[:, c, 0:tn], start=(c == 0), stop=(c == KC - 1)), reads=[ob, cbuf], writes=[pb])
            S.op("act", lambda e, tn=tn, pt=pt: e.activation(out=RS[:, 0:tn], in_=pt[:, 0:tn], func=AF.Sqrt, bias=epsb[:], scale=1.0 / D), reads=[pb, cbuf], writes=[rsb2])
            S.op("dve", lambda e, tn=tn: e.reciprocal(out=RS[:, 0:tn], in_=RS[:, 0:tn]), reads=[rsb2], writes=[rsb2])
            for c in range(KC):
                S.op("dve", lambda e, c=c, t0=t0, tn=tn, HT=HT: e.scalar_tensor_tensor(out=HT[:, c, 0:tn], in0=X[:, c, t0:t0 + tn], scalar=NORMW[:, layer, c:c + 1], in1=RS[:, 0:tn],
                                                                                       op0=ALU.mult, op1=ALU.mult), reads=[xbuf[ti], rsb2, cbuf], writes=[hb])
            for g in range(8):
                wv, wb_ = wget(z_idx[g], [128, KC, 256])
                for cc in range(2):
                    c = 2 * g + cc
                    pt, pb = ps()
                    for kc in range(KC):
                        S.op("pe", lambda e, pt=pt, wv=wv, cc=cc, kc=kc, tn=tn, HT=HT: e.matmul(pt[:, 0:tn], lhsT=wv[:, kc, cc * 128:(cc + 1) * 128], rhs=HT[:, kc, 0:tn],
                                                                                              start=(kc == 0), stop=(kc == KC - 1)), reads=[wb_, hb], writes=[pb])
                    S.op("act", lambda e, pt=pt, c=c, tn=tn, ZS=ZS: e.activation(out=ZS[:, c, 0:tn], in_=pt[:, 0:tn], func=AF.Silu), reads=[pb], writes=[zsb])
            swpc = SWP()
            swpg = SWP()
            for g in range(12):
                wv, wb_ = wget(x_idx[g], [128, KC, 256])
                for cc in range(2):
                    ch = 2 * g + cc
                    pt, pb = ps()
                    for kc in range(KC):
                        S.op("pe", lambda e, pt=pt, wv=wv, cc=cc, kc=kc, tn=tn, HT=HT: e.matmul(pt[:, 0:tn], lhsT=wv[:, kc, cc * 128:(cc + 1) * 128], rhs=HT[:, kc, 0:tn],
                                                                                              start=(kc == 0), stop=(kc == KC - 1)), reads=[wb_, hb], writes=[pb])
                    q = ch % 2
                    u, ubf = U[uctr % 2], ub[uctr % 2]; uctr += 1
                    if not samp:
                        A_, ab_ = AFB[q], afb[q]
                        S.op("act", lambda e, A_=A_, ch=ch: e.copy(out=A_[:, 0:3], in_=CAR[:, ch, :]), reads=[carb], writes=[ab_])
                        S.op("act", lambda e, A_=A_, pt=pt, tn=tn: e.activation(out=A_[:, 3:3 + tn], in_=pt[:, 0:tn], func=AF.Copy), reads=[pb], writes=[ab_])
                        srcs = [A_[:, k:k + tn] for k in range(4)]
                        uo = u[:, 0:tn]
                        S.op("act", lambda e, A_=A_, ch=ch, tn=tn: e.copy(out=CAR[:, ch, :], in_=A_[:, tn:tn + 3]), reads=[ab_], writes=[carb])
                        if t0 + tn == TP:
                            S.op("act", lambda e, A_=A_, ch=ch, tn=tn: e.copy(out=SCO[:, ch, 0:3], in_=A_[:, tn:tn + 3]), reads=[ab_], writes=[scob])
                        dst = (XT[:, ch, 0:tn] if ch < 16 else BCT[:, ch - 16, 0:tn])
                    else:
                        A_, ab_ = AFS[q], afsb[q]
                        S.op("act", lambda e, A_=A_, ch=ch: e.copy(out=A_[:, :, 0:3], in_=FB[:, ch, :, :]), reads=[cb4], writes=[ab_])
                        S.op("act", lambda e, A_=A_, pt=pt: e.activation(out=A_[:, :, 3:7], in_=pt[:, 0:TS].rearrange("p (b t) -> p b t", t=4), func=AF.Copy), reads=[pb], writes=[ab_])
                        srcs = [A_[:, :, k:k + 4] for k in range(4)]
                        uo = u[:, 0:TS].rearrange("p (b t) -> p b t", t=4)
                        S.op("act", lambda e, A_=A_, ch=ch: e.copy(out=SCO[:, ch, 3:51].rearrange("p (b j) -> p b j", j=3), in_=A_[:, :, 4:7]), reads=[ab_], writes=[scob])
                        dst = (XT[:, ch, 0:TS] if ch < 16 else BCT[:, ch - 16, 0:TS]).rearrange("p (b t) -> p b t", t=4)
                    S.op("dve", lambda e, uo=uo, s=srcs[3], ch=ch: e.tensor_scalar(out=uo, in0=s, scalar1=CW[:, ch, 3:4], scalar2=CW[:, ch, 4:5], op0=ALU.mult, op1=ALU.add),
                         reads=[ab_, cb4], writes=[ubf])
                    for k in (2, 1, 0):
                        S.op("dve", lambda e, uo=uo, s=srcs[k], ch=ch, k=k: e.scalar_tensor_tensor(out=uo, in0=s, scalar=CW[:, ch, k:k + 1], in1=uo, op0=ALU.mult, op1=ALU.add),
                             reads=[ab_, cb4, ubf], writes=[ubf])
                    def ctail(uo=uo, dst=dst, ubf=ubf, ch=ch):
                        S.op("act", lambda e: e.activation(out=dst, in_=uo, func=AF.Silu), reads=[ubf], writes=[xtb if ch < 16 else bctb])
                    swpc.step([lambda: None, ctail])
            swpc.flush()
            wvd, wbd = wget(dt_idx, [128, KC, 32])
            nblk = 1 if samp else 2
            for bi in range(nblk):
                c0 = bi * 128
                if samp:
                    R = 64
                    triu, tril2, negm4 = SMS[:, 0:64], SMS[:, 64:128], SMS[:, 128:384]
                else:
                    R = 128
                    triu, tril2, negm4 = SM[:, 0:128], SM[:, 128:256], SM[:, 256:768]
                onesf = SM[:, 768:896]
                pt, pb = ps()
                for kc in range(KC):
                    S.op("pe", lambda e, pt=pt, kc=kc, c0=c0, R=R, HT=HT, wvd=wvd: e.matmul(pt[0:R, 0:32], lhsT=HT[:, kc, c0:c0 + R], rhs=wvd[:, kc, :], start=(kc == 0), stop=(kc == KC - 1)),
                         reads=[wbd, hb], writes=[pb])
                S.op("dve", lambda e, pt=pt, R=R: e.tensor_tensor(out=DTT[0:R, 0, :], in0=pt[0:R, 0:32], in1=DTB[0:R, :], op=ALU.add), reads=[pb, cb4], writes=[dttb])
                S.op("act", lambda e, R=R: e.activation(out=DTT[0:R, 0, :], in_=DTT[0:R, 0, :], func=AF.Exp), reads=[dttb], writes=[dttb])
                S.op("act", lambda e, R=R: e.activation(out=DTT[0:R, 0, :], in_=DTT[0:R, 0, :], func=AF.Ln, bias=oneb[0:R, :], scale=1.0), reads=[dttb, cbuf], writes=[dttb])
                S.op("dve", lambda e, R=R: e.tensor_tensor(out=DTT[0:R, 1, :], in0=DTT[0:R, 0, :], in1=NEGA[0:R, :], op=ALU.mult), reads=[dttb, cb4], writes=[dttb])
                pt, pb = ps()
                S.op("pe", lambda e, pt=pt, R=R, triu=triu: e.matmul(pt[0:R, 0:32], lhsT=triu[0:R, :], rhs=DTT[0:R, 1, :], start=True, stop=True), reads=[dttb, cb4], writes=[pb])
                S.op("pe", lambda e, pt=pt, R=R, tril2=tril2: e.matmul(pt[0:R, 32:64], lhsT=tril2[0:R, :], rhs=DTT[0:R, 1, :], start=True, stop=True), reads=[dttb, cb4], writes=[pb])
                if not samp:
                    S.op("pe", lambda e, pt=pt, onesf=onesf: e.matmul(pt[:, 64:96], lhsT=onesf[:, :], rhs=DTT[:, 1, :], start=True, stop=True), reads=[dttb, cb4], writes=[pb])
                S.op("dve", lambda e, pt=pt, R=R: e.tensor_scalar(out=DTT[0:R, 2, :], in0=pt[0:R, 0:32], scalar1=-1.0, scalar2=None, op0=ALU.mult), reads=[pb], writes=[dttb])
                S.op("act", lambda e, pt=pt, R=R: e.activation(out=DTT[0:R, 3, :], in_=pt[0:R, 32:64], func=AF.Exp), reads=[pb], writes=[dttb])
                if not samp:
                    S.op("act", lambda e, pt=pt: e.activation(out=DTT[:, 4, :], in_=pt[:, 64:96], func=AF.Exp), reads=[pb], writes=[dttb])
                else:
                    S.op("dve", lambda e: e.tensor_tensor(out=ASEQ[:, :, :], in0=DTT[0:64, 1, :].unsqueeze(1).broadcast_to([64, 16, 32]),
                                                          in1=SMS[:, 384:400].unsqueeze(2).broadcast_to([64, 16, 32]), op=ALU.mult), reads=[dttb, cb4], writes=[aseqb])
                    pt2, pb2 = ps()
                    S.op("pe", lambda e, pt2=pt2, onesf=onesf: e.matmul(pt2[:, :], lhsT=onesf[0:64, :], rhs=ASEQ[:, :, :].rearrange("p a b -> p (a b)"), start=True, stop=True),
                         reads=[aseqb, cb4], writes=[pb2])
                    S.op("act", lambda e, pt2=pt2: e.activation(out=ELS[:, :, :].rearrange("p a b -> p (a b)"), in_=pt2[:, :], func=AF.Exp), reads=[pb2], writes=[elsb])
                for half in range(2):
                    ptp, ptpb = pst()
                    for cc in range(8):
                        c = half * 8 + cc
                        S.op("pe", lambda e, ptp=ptp, cc=cc, c=c, c0=c0, R=R, XT=XT: e.transpose(out=ptp[0:R, cc * 128:(cc + 1) * 128], in_=XT[:, c, c0:c0 + R], identity=ident_bf[:]),
                             reads=[xtb, cbuf], writes=[ptpb])
                    S.op("dve", lambda e, ptp=ptp, half=half, R=R: e.tensor_tensor(out=XDT[0:R, half * 16:(half + 1) * 16, :], in0=ptp[0:R, :].rearrange("p (a b) -> p a b", b=64),
                                                                                  in1=DTT[0:R, 0, half * 16:(half + 1) * 16].unsqueeze(2).broadcast_to([R, 16, 64]), op=ALU.mult),
                         reads=[ptpb, dttb], writes=[xdtb])
                S.op("dve", lambda e, R=R: e.tensor_tensor(out=XDEC[0:R, :, :], in0=XDT[0:R, :, :], in1=DTT[0:R, 3, :].unsqueeze(2).broadcast_to([R, 32, 64]), op=ALU.mult),
                     reads=[xdtb, dttb], writes=[xdecb])
                ptp, ptpb = pst()
                for g in range(4):
                    S.op("pe", lambda e, ptp=ptp, g=g, c0=c0, R=R, BCT=BCT: e.transpose(out=ptp[0:R, g * 128:(g + 1) * 128], in_=BCT[:, g, c0:c0 + R], identity=ident_bf[:]),
                         reads=[bctb, cbuf], writes=[ptpb])
                S.op("act", lambda e, ptp=ptp, R=R: e.activation(out=BT[0:R, :, :].rearrange("p a b -> p (a b)"), in_=ptp[0:R, 0:512], func=AF.Copy), reads=[ptpb], writes=[btb])
                pt, pb = ps()
                for g in range(4):
                    S.op("pe", lambda e, pt=pt, g=g, R=R, triu=triu: e.matmul(pt[0:8, g * 128:g * 128 + R], lhsT=DTT[0:R, 1, 8 * g:8 * g + 8], rhs=triu[0:R, :], start=True, stop=True),
                         reads=[dttb, cb4], writes=[pb])
                S.op("act", lambda e, pt=pt, R=R: e.activation(out=ACG4[:, :, 0:R], in_=pt[0:8, :].rearrange("p (a b) -> p a b", a=4)[:, :, 0:R], func=AF.Copy), reads=[pb], writes=[acgb])
                if not samp:
                    S.op("dve", lambda e: e.tensor_copy(out=ACGH[:], in_=ACG4[:]), reads=[acgb], writes=[acgb])
                    S.op("dve", lambda e: e.tensor_tensor(out=ACGL[:], in0=ACG4[:], in1=ACGH[:], op=ALU.subtract), reads=[acgb], writes=[acgb])
                pcb, pcbb = ps()
                for g in range(4):
                    S.op("pe", lambda e, pcb=pcb, g=g, c0=c0, R=R, BCT=BCT: e.matmul(pcb[0:R, g * 128:g * 128 + R], lhsT=BCT[:, g, c0:c0 + R], rhs=BCT[:, 4 + g, c0:c0 + R], start=True, stop=True),
                         reads=[bctb], writes=[pcbb])
                S.op("act", lambda e, pcb=pcb, R=R: e.activation(out=CBS4[0:R, :, 0:R], in_=pcb[0:R, :].rearrange("p (a b) -> p a b", a=4)[:, :, 0:R], func=AF.Copy), reads=[pcbb], writes=[cbsb])
                if not samp:
                    run_groups(c0, BCT, ZS, XT, OT)
                for g in (range(4) if samp else []):
                    EBx, ebx = (EBF, stbb) if samp else (EB, ebb)
                    pbcs = [ps(), ps()]
                    for hq in range(2):
                        pbc, pbcb = pbcs[hq]
                        for hh in range(4):
                            S.op("pe", lambda e, pbc=pbc, hh=hh, hq=hq, R=R, g=g: e.matmul(pbc[:, hh * 128:hh * 128 + R], lhsT=SEL[:, hq * 4 + hh, :], rhs=ACG4[:, g, 0:R], start=(hh == 0), stop=False,
                                                                                        skip_group_check=True), reads=[acgb, cb4], writes=[pbcb])
                    for hq in range(2):
                        pbc, pbcb = pbcs[hq]
                        S.op("act", lambda e, pbc=pbc, hq=hq, R=R, EBx=EBx: e.activation(out=EBx[:, hq * 4:hq * 4 + 4, 0:R], in_=pbc[:, :].rearrange("p (a b) -> p a b", a=4)[:, :, 0:R], func=AF.Exp),
                             reads=[pbcb], writes=[ebx])
                    for hq in range(2):
                        pbc, pbcb = pbcs[hq]
                        if samp:
                            for hh in range(4):
                                S.op("pe", lambda e, pbc=pbc, negm4=negm4, hh=hh: e.matmul(pbc[0:64, hh * 128:hh * 128 + 64], lhsT=ident_f[0:64, 0:64], rhs=negm4[0:64, 0:64],
                                                                                        start=False, stop=True, skip_group_check=True), reads=[cb4, cbuf, stbb], writes=[pbcb])
                        else:
                            S.op("pe", lambda e, pbc=pbc: e.matmul(pbc[:, :], lhsT=ident_bf[:, :], rhs=NEGMB[:, :], start=False, stop=True, skip_group_check=True),
                                 reads=[cb4, cbuf, ebb], writes=[pbcb])
                    for hq in range(2):
                        pbc, pbcb = pbcs[hq]
                        for hh in range(4):
                            h = 8 * g + hq * 4 + hh
                            S.op("act", lambda e, pbc=pbc, hh=hh, hq=hq, h=h, R=R: e.activation(out=LT[0:R, hq * 4 + hh, 0:R], in_=pbc[0:R, hh * 128:hh * 128 + R], func=AF.Exp,
                                                                                              bias=DTT[0:R, 2, h:h + 1], scale=1.0), reads=[pbcb, dttb], writes=[ltb])
                    for hq in range(2):
                        S.op("dve", lambda e, hq=hq, R=R, g=g: e.tensor_tensor(out=MT[0:R, hq * 4:hq * 4 + 4, 0:R], in0=LT[0:R, hq * 4:hq * 4 + 4, 0:R],
                                                                              in1=CBS4[0:R, g, 0:R].unsqueeze(1).broadcast_to([R, 4, R]), op=ALU.mult), reads=[ltb, cbsb], writes=[mtb])
                        if samp:
                            S.op("dve", lambda e, hq=hq, g=g, c0=c0, BCT=BCT: e.tensor_tensor(out=CDECF[:, 8 * g + hq * 4:8 * g + hq * 4 + 4, 0:64], in0=EBF[:, hq * 4:hq * 4 + 4, 0:64],
                                                                                            in1=BCT[:, 4 + g, 0:64].unsqueeze(1).broadcast_to([128, 4, 64]), op=ALU.mult), reads=[stbb, bctb], writes=[stb_])
                        else:
                            S.op("dve", lambda e, hq=hq, g=g, c0=c0, BCT=BCT: e.tensor_tensor(out=CDEC[:, hq * 4:hq * 4 + 4, :], in0=EB[:, hq * 4:hq * 4 + 4, :],
                                                                                            in1=BCT[:, 4 + g, c0:c0 + 128].unsqueeze(1).broadcast_to([128, 4, 128]), op=ALU.mult), reads=[ebb, bctb], writes=[cdb])
                    py, pyb = psb([4, 5, 0, 1][g]) if samp else psb(4 + gcnt[0] % 2)
                    gcnt[0] += 1
                    for hl in range(8):
                        h = 8 * g + hl
                        half, cl = (hl % 2) * 64, hl // 2
                        S.op("pe", lambda e, py=py, h=h, hl=hl, half=half, cl=cl, R=R: e.matmul(py[half:half + 64, cl * 128:cl * 128 + R], lhsT=XDT[0:R, h, :], rhs=MT[0:R, hl, 0:R],
                                                                                              start=(hl < 2), stop=False, tile_position=(0, half), skip_group_check=True), reads=[xdtb, mtb], writes=[pyb])
                        if not samp:
                            S.op("pe", lambda e, py=py, h=h, hl=hl, half=half, cl=cl: e.matmul(py[half:half + 64, cl * 128:(cl + 1) * 128], lhsT=STb[:, h, :], rhs=CDEC[:, hl, :],
                                                                                             start=False, stop=True, tile_position=(0, half), skip_group_check=True), reads=[stbb, cdb], writes=[pyb])
                    if samp:
                        ylist.append((g, py, pyb))
                        continue
                    swpg.step([lambda: None, (lambda g=g, py=py, pyb=pyb, c0=c0, R=R, ZS=ZS, XT=XT, OT=OT: finish_group(g, py, pyb, c0, R, ZS, XT, OT))])
                if samp:
                    S0 = [P.sb(tb_.take(8192), [128, 32, 64], F32) for _ in range(2)]
                    SN = P.sb(tb_.take(8192), [128, 32, 64], F32)
                    s0b = [Buf(), Buf()]
                    snb = Buf()
                    for b in range(NB):
                        q2 = b % 2
                        S.dma("sp", S0[q2][:], ssmS_d[b], writes=[s0b[q2]])
                        for (g, py, pyb) in ylist:
                            for hl in range(8):
                                h = 8 * g + hl
                                half, cl = (hl % 2) * 64, hl // 2
                                S.op("pe", lambda e, py=py, h=h, half=half, cl=cl, b=b, q2=q2: e.matmul(py[half:half + 64, cl * 128 + 4 * b:cl * 128 + 4 * b + 4], lhsT=S0[q2][:, h, :], rhs=CDECF[:, h, 4 * b:4 * b + 4],
                                                                                                    start=False, stop=(b == NB - 1), tile_position=(0, half), skip_group_check=True),
                                     reads=[s0b[q2], stb_], writes=[pyb])
                        S.op("dve", lambda e, b=b, q2=q2: e.tensor_tensor(out=SN[:, :, :], in0=S0[q2][:, :, :], in1=ELS[:, b, :].unsqueeze(2).broadcast_to([128, 32, 64]), op=ALU.mult),
                             reads=[s0b[q2], elsb], writes=[snb])
                        S.op("act", lambda e, b=b, q2=q2: e.activation(out=BTM[q2][:, :, :].rearrange("p a b -> p (a b)"), in_=BT[0:64, :, :].rearrange("p a b -> p (a b)"),
                                                                        func=AF.Copy, scale=SMS[:, 384 + b:385 + b]), reads=[btb, cb4], writes=[btmb[q2]])
                        for g in range(4):
                            pt, pb = ps()
                            S.op("pe", lambda e, pt=pt, g=g, q2=q2: e.matmul(pt[:, :], lhsT=BTM[q2][:, g, :], rhs=XDEC[0:64, 8 * g:8 * g + 8, :].rearrange("p a b -> p (a b)"), start=True, stop=True),
                                 reads=[btmb[q2], xdecb], writes=[pb])
                            S.op("dve", lambda e, pt=pt, g=g: e.tensor_tensor(out=SN[:, 8 * g:8 * g + 8, :].rearrange("p a b -> p (a b)"), in0=SN[:, 8 * g:8 * g + 8, :].rearrange("p a b -> p (a b)"),
                                                                             in1=pt[:, :], op=ALU.add), reads=[snb, pb], writes=[snb])
                        S.dma("sp", ssmSo_d[b], SN[:], reads=[snb], is_output=True)
                    for (g, py, pyb) in ylist:
                        finish_group(g, py, pyb, 0, 64, ZS, XT, OT)
                    continue
                S.op("dve", lambda e: e.tensor_tensor(out=ST[:, :, :], in0=ST[:, :, :], in1=DTT[:, 4, :].unsqueeze(2).broadcast_to([128, 32, 64]), op=ALU.mult), reads=[stb_, dttb], writes=[stb_])
                for g in range(4):
                    pt, pb = ps()
                    S.op("pe", lambda e, pt=pt, g=g: e.matmul(pt[:, :], lhsT=BT[:, g, :], rhs=XDEC[:, 8 * g:8 * g + 8, :].rearrange("p a b -> p (a b)"), start=True, stop=True),
                         reads=[btb, xdecb], writes=[pb])
                    S.op("dve", lambda e, pt=pt, g=g: e.tensor_tensor(out=ST[:, 8 * g:8 * g + 8, :].rearrange("p a b -> p (a b)"), in0=ST[:, 8 * g:8 * g + 8, :].rearrange("p a b -> p (a b)"),
                                                                     in1=pt[:, :], op=ALU.add), reads=[stb_, pb], writes=[stb_])
                S.op("act", lambda e: e.copy(out=STb[:, :, :], in_=ST[:, :, :]), reads=[stb_], writes=[stbb])
            swpg.flush()
            out_proj(t0, tn, ti, OT)
            if t0 + tn == TP:
                S.dma("act", ssmP_d, ST[:], reads=[stb_], is_output=True)
            if samp:
                S.dma("act", sconv_d, SCO[:], reads=[scob], is_output=True)
                PSMODE[0] = 0
    for layer in range(NL):
        S.barrier()
        if CFG["mixers"][layer]:
            if layer % 3 == 0:
                attention(layer)
            elif layer % 3 == 1:
                hgrn(layer)
            else:
                ssd(layer)
        S.barrier()
        ffn(layer)
    S.barrier()
    ar = Bump(ARENA, ARENA_SZ)
    YF = [P.sb(ar.take(KC * 512 * 4), [128, KC, 512], F32) for i in range(2)]
    a = ar.take(KC * 512 * 2 + 2048)
    yb = [Buf(), Buf()]
    yv = yT_d.rearrange("(c p) t -> p c t", p=128)
    for ti, (t0, tn) in enumerate(TILES):
        q = ti % 2
        hbufs = {ti: yb[q]}
        rmsnorm(2 * DEPTH, [ti], YF[q], hbufs, t0, a)
        S.dma("act", yv[:, :, t0:t0 + tn], YF[q][:, :, 0:tn], reads=[yb[q]], is_output=True)

    WSTREAM[0] = nc.dram_tensor("wstream", [128, max(P.woff, 1)], F32, kind="ExternalInput").ap()
    P.wlist = wlist
    S.emit()
    return P


_PROG = None


def _prog():
    global _PROG
    if _PROG is None:
        _PROG = build()
    return _PROG


def kernel(**inputs):
    I = {k: np.asarray(v) for k, v in inputs.items()}
    P = _prog()
    wstream = np.zeros((128, max(P.woff, 1)), np.float32)
    for (o, n, builder) in P.wlist:
        wstream[:, o:o + n] = builder(I)
    in_maps = []
    for c in range(NCORES):
        m = {"wstream": wstream}
        for name, (shape, builder) in P.inputs.items():
            m[name] = np.ascontiguousarray(builder(I, c), dtype=np.float32).reshape(shape)
        in_maps.append(m)
    res = run_bass_kernel_spmd(P.nc, in_maps, core_ids=list(range(NCORES)))
    R = res.results
    out = assemble(R)
    order = ["y_prompt", "y_sample", "k_p", "v_p", "hg_p", "ssm_p", "sconv_p", "fconv_p", "k_s", "v_s", "hg_s", "ssm_s", "sconv_s", "fconv_s"]
    if all(k in out for k in order):
        return tuple(np.ascontiguousarray(out[k], dtype=np.float32) for k in order)
    return out


def assemble(R):
    yT = np.stack([R[c]["yT"] for c in range(NCORES)])
    y_prompt = np.ascontiguousarray(yT[:, :, :TP].transpose(0, 2, 1))
    y_sample = np.ascontiguousarray(yT[:, :, TP:].transpose(0, 2, 1)).reshape(NCORES * NB, 4, D)
    fc = np.stack([R[c]["fconvT"] for c in range(NCORES)])
    fc = fc.transpose(1, 0, 4, 3, 2).reshape(DEPTH, NCORES, 2 + 2 * NB, DFF)
    fconv_p = np.ascontiguousarray(fc[:, :, 0:2])
    fconv_s = np.ascontiguousarray(fc[:, :, 2:].reshape(DEPTH, NCORES * NB, 2, DFF))
    out = {"y_prompt": y_prompt, "y_sample": y_sample, "fconv_p": fconv_p, "fconv_s": fconv_s}
    if "kTp" in R[0]:
        kTp = np.stack([R[c]["kTp"] for c in range(NCORES)])
        out["k_p"] = np.ascontiguousarray(kTp.transpose(1, 0, 4, 2, 3)).reshape(2, NCORES, 128, 4, 64)
        vp = np.stack([R[c]["vp"] for c in range(NCORES)])
        out["v_p"] = np.ascontiguousarray(vp.transpose(1, 0, 2, 3)).reshape(2, NCORES, 128, 4, 64)
        ks = np.stack([R[c]["ks"] for c in range(NCORES)])
        out["k_s"] = np.ascontiguousarray(ks.transpose(1, 0, 2, 3, 4)).reshape(2, NCORES * NB, 128, 4, 64)
        vs = np.stack([R[c]["vs"] for c in range(NCORES)])
        out["v_s"] = np.ascontiguousarray(vs.transpose(1, 0, 2, 3, 4)).reshape(2, NCORES * NB, 128, 4, 64)
    if "ssmP" in R[0]:
        sp = np.stack([R[c]["ssmP"] for c in range(NCORES)])
        out["ssm_p"] = np.ascontiguousarray(sp.transpose(0, 2, 3, 1))[None]
        ss_ = np.stack([R[c]["ssmSo"] for c in range(NCORES)])
        out["ssm_s"] = np.ascontiguousarray(ss_.transpose(0, 1, 3, 4, 2)).reshape(1, NCORES * NB, 32, 64, 128)
        sc = np.stack([R[c]["sconvT"] for c in range(NCORES)])
        sc = sc.transpose(0, 3, 2, 1).reshape(NCORES, 51, 3072)
        out["sconv_p"] = np.ascontiguousarray(sc[:, 0:3])[None]
        out["sconv_s"] = np.ascontiguousarray(sc[:, 3:].reshape(NCORES * NB, 3, 3072))[None]
    if "hgP" in R[0]:
        hgP = np.stack([R[c]["hgP"] for c in range(NCORES)])
        out["hg_p"] = np.ascontiguousarray(hgP.transpose(0, 2, 1, 3))[None]
        hgS = np.stack([R[c]["hgSo"] for c in range(NCORES)])
        out["hg_s"] = np.ascontiguousarray(hgS.transpose(0, 1, 3, 2, 4)).reshape(1, NCORES * NB, 8, 128, 128)
    return out
```

```python
import contextlib
import os
import numpy as np
import concourse.bass as bass
import concourse.mybir as mybir
from concourse.bass_utils import run_bass_kernel_spmd

F32 = mybir.dt.float32
BF16 = mybir.dt.bfloat16
AF = mybir.ActivationFunctionType
ALU = mybir.AluOpType
AX = mybir.AxisListType

EPOCH = 12000
NCORES = 8
D = 1024
KC = 8
TP = 2048
TS = 64
T = TP + TS
NB = 16
DFF = 2816
NF = 22
DEPTH = 4
EPS = 1e-6
TILES = [(0, 512), (512, 512), (1024, 512), (1536, 512), (2048, 64)]
GROUPS = [[0, 1], [2, 3, 4]]

CFG = {"layers": DEPTH, "mixers": (True, True, True, True), "ap": True, "as": True, "dd": True, "so": True}


class Buf:
    __slots__ = ("name", "last_w", "readers", "excl")

    def __init__(self, name="", excl=False):
        self.name = name
        self.excl = excl
        self.last_w = None
        self.readers = []


class Op:
    __slots__ = ("eng", "pos", "fn", "waits", "is_dma", "signal", "signum", "dma_sem", "dma_val", "dma_prev")

    def __init__(self, eng, pos, fn, is_dma):
        self.eng = eng
        self.pos = pos
        self.fn = fn
        self.waits = []
        self.is_dma = is_dma
        self.signal = False
        self.signum = None
        self.dma_sem = None
        self.dma_val = None
        self.dma_prev = None


class Sched:
    ENGS = ("pe", "act", "dve", "pool", "sp")

    def __init__(self, nc):
        self.nc = nc
        self.ops = {e: [] for e in self.ENGS}
        self.waited = {e: {p: -1 for p in self.ENGS} for e in self.ENGS}
        self.dma_waited = {e: set() for e in self.ENGS}
        self.dma_pool = {"sp": 16, "act": 20, "pool": 16}
        self.dma_rr = {e: 0 for e in self.ENGS}
        self.dma_last = {}
        self.out_dmas = []
        self.out_seen = 0
        self.anchor_fn = None

    def _add_wait(self, op, dep):
        if dep is None or dep is op:
            return
        e = op.eng
        if dep.is_dma:
            if id(dep) in self.dma_waited[e]:
                return
            self.dma_waited[e].add(id(dep))
            op.waits.append(dep)
            return
        if self.waited[e][dep.eng] >= dep.pos:
            return
        self.waited[e][dep.eng] = dep.pos
        dep.signal = True
        op.waits.append(dep)

    def _issue(self, eng, fn, reads, writes, is_dma, deps=()):
        op = Op(eng, len(self.ops[eng]), fn, is_dma)
        for b in reads:
            if b.last_w is not None:
                self._add_wait(op, b.last_w)
            if b.excl:
                lastr = {}
                for r in b.readers:
                    if r.eng != eng and (r.eng not in lastr or lastr[r.eng].pos < r.pos):
                        lastr[r.eng] = r
                for r in lastr.values():
                    self._add_wait(op, r)
        for b in writes:
            w = b.last_w
            if w is not None and (is_dma or w.is_dma or w.eng != eng):
                self._add_wait(op, w)
            lastr = {}
            for r in b.readers:
                if r.is_dma:
                    self._add_wait(op, r)
                elif is_dma or r.eng != eng:
                    if r.eng not in lastr or lastr[r.eng].pos < r.pos:
                        lastr[r.eng] = r
            for r in lastr.values():
                self._add_wait(op, r)
        for d in deps:
            self._add_wait(op, d)
        for b in reads:
            b.readers.append(op)
        for b in writes:
            b.last_w = op
            b.readers = []
        self.ops[eng].append(op)
        return op

    def op(self, eng, fn, reads=(), writes=(), deps=()):
        return self._issue(eng, fn, reads, writes, False, deps)

    def dma(self, queue, out, in_, reads=(), writes=(), is_output=False, deps=(), **kw):
        def fn(e, out=out, in_=in_, kw=kw):
            return e.dma_start(out=out, in_=(in_() if callable(in_) else in_), **kw)
        op = self._issue(queue, fn, reads, writes, True, deps)
        n = self.dma_pool[queue]
        idx = self.dma_rr[queue] % n
        self.dma_rr[queue] += 1
        key = (queue, idx)
        prev = self.dma_last.get(key)
        op.dma_sem = key
        op.dma_prev = prev
        op.dma_val = (prev.dma_val if prev is not None else 0) + 16
        self.dma_last[key] = op
        if is_output:
            self.out_dmas.append(op)
        return op

    def barrier(self, engs=("pe", "act", "dve", "pool", "sp")):
        lasts = []
        for e in engs:
            for o in reversed(self.ops[e]):
                if not o.is_dma and o.fn is not None:
                    lasts.append(o)
                    break
        anchor = Op("dve", len(self.ops["dve"]), self.anchor_fn, False)
        for l in lasts:
            if l.eng != "dve":
                self._add_wait(anchor, l)
        for o in self.out_dmas[self.out_seen:]:
            self._add_wait(anchor, o)
        self.out_seen = len(self.out_dmas)
        self.ops["dve"].append(anchor)
        for e in engs:
            if e == "dve":
                continue
            op = Op(e, len(self.ops[e]), None, False)
            self._add_wait(op, anchor)
            self.ops[e].append(op)

    def emit(self):
        nc = self.nc
        with contextlib.ExitStack() as st:
            eng_sems = {}
            for e in self.ENGS:
                n = 0
                for o in self.ops[e]:
                    if o.signal and not o.is_dma:
                        n += 1
                        o.signum = n
                nep = max((n + EPOCH - 1) // EPOCH, 1)
                eng_sems[e] = [st.enter_context(nc.semaphore(f"s_{e}_{i}")) for i in range(nep)]
            dma_sems = {}
            for q, n in self.dma_pool.items():
                for i in range(n):
                    if (q, i) in self.dma_last:
                        dma_sems[(q, i)] = st.enter_context(nc.semaphore(f"d_{q}_{i}"))

            def sem_of(o):
                ep = (o.signum - 1) // EPOCH
                return eng_sems[o.eng][ep], o.signum - ep * EPOCH

            block = st.enter_context(nc.Block())

            def make(e):
                def body(eng):
                    for o in self.ops[e]:
                        for d in o.waits:
                            if d.is_dma:
                                eng.wait_ge(dma_sems[d.dma_sem], d.dma_val)
                            else:
                                s, v = sem_of(d)
                                eng.wait_ge(s, v)
                        if o.is_dma:
                            if o.dma_prev is not None:
                                eng.wait_ge(dma_sems[o.dma_sem], o.dma_prev.dma_val)
                            o.fn(eng).then_inc(dma_sems[o.dma_sem], 16)
                        elif o.fn is not None:
                            ins = o.fn(eng)
                            if o.signal:
                                ins.then_inc(sem_of(o)[0], 1)
                    for o in self.out_dmas:
                        if o.eng == e:
                            eng.wait_ge(dma_sems[o.dma_sem], self.dma_last[o.dma_sem].dma_val)
                return body

            block.tensor(make("pe"))
            block.scalar(make("act"))
            block.vector(make("dve"))
            block.gpsimd(make("pool"))
            block.sync(make("sp"))


class Bump:
    def __init__(self, start, size):
        self.start, self.size, self.cur = start, size, start

    def take(self, nbytes):
        o = self.cur
        self.cur += (nbytes + 63) // 64 * 64
        assert self.cur - self.start <= self.size, ("arena overflow", self.cur - self.start, self.size)
        return o


class SWP:
    def __init__(self):
        self.q = []

    def step(self, stages):
        self.q.append(stages)
        n = len(self.q)
        for k in range(8):
            it = n - 1 - k
            if it >= 0 and k < len(self.q[it]):
                self.q[it][k]()

    def flush(self):
        n = len(self.q)
        more = True
        d = 1
        while more:
            more = False
            for k in range(d, 8):
                it = n - 1 - (k - d)
                if it >= 0 and k < len(self.q[it]):
                    self.q[it][k]()
                    more = True
            d += 1
        self.q = []


class Prog:
    def __init__(self):
        self.nc = bass.Bass("TRN2", target_bir_lowering=False)
        self.S = Sched(self.nc)
        self.wspecs = []
        self.wcache = {}
        self.woff = 0
        self.inputs = {}
        self.outputs = {}
        self._n = 0
        nc = self.nc
        self.base = (nc.sbuf_base + 63) // 64 * 64
        self.top = nc.sbuf_top

    def uid(self, p):
        self._n += 1
        return f"{p}{self._n}"

    def sb(self, off, shape, dt):
        return self.nc.alloc_sbuf_tensor_at(self.uid("t"), list(shape), dt, offset=off)

    def din(self, name, shape, builder):
        ap = self.nc.dram_tensor(name, list(shape), F32, kind="ExternalInput").ap()
        self.inputs[name] = (tuple(shape), builder)
        return ap

    def dout(self, name, shape):
        ap = self.nc.dram_tensor(name, list(shape), F32, kind="ExternalOutput").ap()
        self.outputs[name] = tuple(shape)
        return ap


def build():
    P = Prog()
    nc, S = P.nc, P.S
    NL = CFG["layers"]

    off = P.base
    X = P.sb(off, [128, KC, T], F32); off += KC * T * 4
    off = (off + 63) // 64 * 64
    CONST = off; off += 3072
    WB_SLOT, NWB = 4096, 6
    WB0 = off; off += WB_SLOT * NWB
    ARENA = off
    ARENA_SZ = P.top - ARENA
    assert ARENA_SZ > 96000, ARENA_SZ

    xbuf = [Buf(f"x{i}") for i in range(len(TILES))]

    coff = CONST
    ones_bf = P.sb(coff, [128, 128], BF16); coff += 256
    ident_bf = P.sb(coff, [128, 128], BF16); coff += 256
    ident_f = P.sb(coff, [128, 128], F32); coff += 512
    NORMW = P.sb(coff, [128, 9, KC], F32); coff += 9 * KC * 4
    FCW = P.sb(coff, [128, DEPTH, 4, NF], F32); coff += DEPTH * 4 * NF * 4
    epsb = P.sb(coff, [128, 1], F32); coff += 32
    oneb = P.sb(coff, [128, 1], F32); coff += 32
    dummy = P.sb(coff, [128, 1], F32); coff += 32
    S.anchor_fn = lambda e: e.memset(dummy[:], 0.0)
    cbuf = Buf("const")

    xT_d = P.din("xT", [D, T], lambda I, c: np.concatenate(
        [I["x_prompt"][c].T, I["x_sample"][c * NB:(c + 1) * NB].reshape(TS, D).T], axis=1))
    normw_d = P.din("normw", [128, 9, KC], lambda I, c: np.concatenate(
        [I["norm_mix_w"], I["norm_ffn_w"], I["norm_final_w"][None]], axis=0).reshape(9, KC, 128).transpose(2, 0, 1))
    fcw_d = P.din("fcw", [128, DEPTH, 4, NF], lambda I, c: np.concatenate(
        [I["ffn_conv_w"], I["ffn_conv_b"][:, None]], axis=1).reshape(DEPTH, 4, NF, 128).transpose(3, 0, 1, 2))
    ident_d = P.din("ident", [128, 128], lambda I, c: np.eye(128, dtype=np.float32))
    fbuf_d = P.din("fbuf", [DEPTH, 128, NF, NB, 2], lambda I, c: I["state_ffn_conv"][:, c * NB:(c + 1) * NB].reshape(
        DEPTH, NB, 2, NF, 128).transpose(0, 4, 3, 1, 2))
    yT_d = P.dout("yT", [D, T])
    fconv_d = P.dout("fconvT", [DEPTH, 128, NF, 2 + 2 * NB])

    wlist = []

    def wtile(key, n, builder):
        if key in P.wcache:
            return P.wcache[key]
        idx = len(wlist)
        wlist.append((P.woff, n, builder))
        P.woff += n
        P.wcache[key] = idx
        return idx

    wbbuf = [Buf(f"wb{i}") for i in range(NWB)]
    wctr = [0]
    WSTREAM = [None]

    def wget(idx, shape):
        o, n, _ = wlist[idx]
        i = wctr[0]; wctr[0] += 1
        t = i % NWB
        bf_t = P.sb(WB0 + t * WB_SLOT, [128, n], BF16)
        S.dma("pool", bf_t[:], (lambda o=o, n=n: WSTREAM[0][:, o:o + n]), writes=[wbbuf[t]])
        view = P.sb(WB0 + t * WB_SLOT, list(shape), BF16)
        return view, wbbuf[t]

    PSA = nc.alloc_psum_tensor("psa", [128, 6 * 512], F32)
    PST = nc.alloc_psum_tensor("pst", [128, 2 * 1024], BF16)
    psbuf = [Buf(f"ps{i}", excl=True) for i in range(8)]
    psctr = [0]
    pstctr = [0]

    PSMODE = [0]

    def ps():
        if PSMODE[0] == 2:
            i = psctr[0] % 6
        elif PSMODE[0]:
            i = 2 + psctr[0] % 2
        else:
            i = psctr[0] % 4
        psctr[0] += 1
        return PSA[:, i * 512:(i + 1) * 512], psbuf[i]

    def psb(i):
        return PSA[:, i * 512:(i + 1) * 512], psbuf[i]

    def ps2():
        if psctr[0] % 2:
            psctr[0] += 1
        i = psctr[0] % 4
        psctr[0] += 2
        return PSA[:, i * 512:(i + 2) * 512], [psbuf[i], psbuf[i + 1]]

    def ps_long():
        return PSA[:, 4 * 512:6 * 512], [psbuf[4], psbuf[5]]

    def pst():
        i = pstctr[0] % 2
        pstctr[0] += 1
        return PST[:, i * 1024:(i + 1) * 1024], psbuf[6 + i]

    S.dma("act", ident_f[:], ident_d, writes=[cbuf])
    S.dma("act", NORMW[:], normw_d, writes=[cbuf])
    S.dma("act", FCW[:], fcw_d, writes=[cbuf])
    S.op("dve", lambda e: e.memset(ones_bf[:], 1.0), writes=[cbuf])
    S.op("dve", lambda e: e.memset(epsb[:], EPS), writes=[cbuf])
    S.op("dve", lambda e: e.memset(oneb[:], 1.0), writes=[cbuf])
    S.op("dve", lambda e: e.tensor_copy(out=ident_bf[:], in_=ident_f[:]), reads=[cbuf], writes=[cbuf])
    xv = xT_d.rearrange("(c p) t -> p c t", p=128)
    for ti, (t0, tn) in enumerate(TILES):
        S.dma("act", X[:, :, t0:t0 + tn], xv[:, :, t0:t0 + tn], writes=[xbuf[ti]])

    def rmsnorm(norm_idx, tiles, H, hbufs, hoff, sq_off, lnexp=False):
        SQ = [P.sb(sq_off, [128, KC, 512], BF16) for i in range(2)]
        RS = [P.sb(sq_off + KC * 512 * 2, [128, 512], F32)] * 2
        sqb = [Buf()] * 2
        rsb = [Buf()] * 2
        for j, ti in enumerate(tiles):
            t0, tn = TILES[ti]
            q = j % 2
            S.op("act", lambda e, q=q, t0=t0, tn=tn: e.activation(out=SQ[q][:, :, 0:tn], in_=X[:, :, t0:t0 + tn], func=AF.Square),
                 reads=[xbuf[ti]], writes=[sqb[q]])
            pt, pb = ps()
            for c in range(KC):
                S.op("pe", lambda e, q=q, c=c, tn=tn, pt=pt: e.matmul(pt[:, 0:tn], lhsT=ones_bf[:], rhs=SQ[q][:, c, 0:tn],
                                                                     start=(c == 0), stop=(c == KC - 1)),
                     reads=[sqb[q], cbuf], writes=[pb])
            if lnexp:
                S.op("act", lambda e, q=q, tn=tn, pt=pt: e.activation(out=RS[q][:, 0:tn], in_=pt[:, 0:tn], func=AF.Ln, bias=epsb[:], scale=1.0 / D),
                     reads=[pb, cbuf], writes=[rsb[q]])
                S.op("act", lambda e, q=q, tn=tn: e.activation(out=RS[q][:, 0:tn], in_=RS[q][:, 0:tn], func=AF.Exp, scale=-0.5), reads=[rsb[q]], writes=[rsb[q]])
            else:
                S.op("act", lambda e, q=q, tn=tn, pt=pt: e.activation(out=RS[q][:, 0:tn], in_=pt[:, 0:tn], func=AF.Sqrt,
                                                                     bias=epsb[:], scale=1.0 / D),
                     reads=[pb, cbuf], writes=[rsb[q]])
                S.op("dve", lambda e, q=q, tn=tn: e.reciprocal(out=RS[q][:, 0:tn], in_=RS[q][:, 0:tn]), reads=[rsb[q]], writes=[rsb[q]])
            for c in range(KC):
                S.op("dve", lambda e, q=q, c=c, t0=t0, tn=tn: e.scalar_tensor_tensor(
                    out=H[:, c, t0 - hoff:t0 - hoff + tn], in0=X[:, c, t0:t0 + tn], scalar=NORMW[:, norm_idx, c:c + 1],
                    in1=RS[q][:, 0:tn], op0=ALU.mult, op1=ALU.mult),
                    reads=[xbuf[ti], rsb[q], cbuf], writes=[hbufs[ti]])

    def ffn(layer):
        up_idx = [wtile(("up", layer, f), 2 * KC * 128, (lambda I, layer=layer, f=f: np.stack(
            [I["ffn_w_up"][layer][:, f * 128:(f + 1) * 128].reshape(KC, 128, 128),
             I["ffn_w_up"][layer][:, DFF + f * 128:DFF + (f + 1) * 128].reshape(KC, 128, 128)], axis=1)
            .transpose(2, 0, 1, 3).reshape(128, -1))) for f in range(NF)]
        dn_idx = [[wtile(("dn", layer, c, hh), 11 * 128, (lambda I, layer=layer, c=c, hh=hh:
                   I["ffn_w_down"][layer][hh * 1408:(hh + 1) * 1408, c * 128:(c + 1) * 128].reshape(11, 128, 128)
                   .transpose(1, 0, 2).reshape(128, -1))) for hh in range(2)] for c in range(KC)]
        ar = Bump(ARENA, ARENA_SZ)
        TG = 1088
        HGs = [P.sb(ar.take(KC * TG * 2), [128, KC, TG], BF16) for _ in range(2)]
        GG = P.sb(ar.take(NF * TG * 2), [128, NF, TG], BF16)
        AFB = [P.sb(ar.take(1026 * 4), [128, 1026], F32) for i in range(2)]
        AFS = [P.sb(ar.take(NB * 6 * 4), [128, NB, 6], F32) for i in range(2)]
        U = [P.sb(ar.take(2048), [128, 512], F32) for i in range(3)]
        CAR = P.sb(ar.take(NF * 2 * 4), [128, NF, 2], F32)
        FC = P.sb(ar.take(NF * (2 + 2 * NB) * 4), [128, NF, 2 + 2 * NB], F32)
        FB = P.sb(ar.take(NF * NB * 2 * 4), [128, NF, NB, 2], F32)
        sq_off = ar.take(KC * 512 * 2 + 2048)
        hbs = [[Buf() for _ in TILES] for _ in range(2)]
        gb = [[Buf() for _ in TILES] for _ in range(NF)]
        afb = [Buf(), Buf()]
        afsb = [Buf(), Buf()]
        ub = [Buf() for _ in range(3)]
        carb, fcb, fbb = Buf(), Buf(), Buf()
        S.dma("act", FB[:], fbuf_d[layer], writes=[fbb])
        S.op("dve", lambda e: e.memset(CAR[:], 0.0), writes=[carb])
        uctr = 0
        PSMODE[0] = 2
        swp = SWP()
        rmsnorm(DEPTH + layer, GROUPS[0], HGs[0], hbs[0], TILES[GROUPS[0][0]][0], sq_off)
        for gi, grp in enumerate(GROUPS):
            g0 = TILES[grp[0]][0]
            HG, hb = HGs[gi], hbs[gi]
            for f in range(NF):
                wv, wb_ = wget(up_idx[f], [128, KC, 2, 128])
                q = f % 2
                A_, ab_ = AFB[q], afb[q]
                As_, asb_ = AFS[q], afsb[q]
                S.op("act", lambda e, A_=A_, f=f: e.copy(out=A_[:, 0:2], in_=CAR[:, f, :]), reads=[carb], writes=[ab_])
                if gi == 1:
                    S.op("act", lambda e, As_=As_, f=f: e.copy(out=As_[:, :, 0:2], in_=FB[:, f, :, :]), reads=[fbb], writes=[asb_])
                for ti in grp:
                    t0, tn = TILES[ti]
                    l0 = t0 - g0
                    samp = (ti == 4)
                    pa, pab = ps()
                    for c in range(KC):
                        S.op("pe", lambda e, c=c, pa=pa, tn=tn, l0=l0, wv=wv, HG=HG: e.matmul(
                            pa[:, 0:tn], lhsT=wv[:, c, 0, :], rhs=HG[:, c, l0:l0 + tn], start=(c == 0), stop=(c == KC - 1)),
                            reads=[wb_, hb[ti]], writes=[pab])
                    pbt, pbb = ps()
                    for c in range(KC):
                        S.op("pe", lambda e, c=c, pbt=pbt, tn=tn, l0=l0, wv=wv, HG=HG: e.matmul(
                            pbt[:, 0:tn], lhsT=wv[:, c, 1, :], rhs=HG[:, c, l0:l0 + tn], start=(c == 0), stop=(c == KC - 1)),
                            reads=[wb_, hb[ti]], writes=[pbb])
                    u, ubf = U[uctr % 3], ub[uctr % 3]; uctr += 1
                    if not samp:
                        pav = pa[:, 0:tn]
                        adst = A_[:, 2 + l0:2 + l0 + tn]
                        srcs = [A_[:, l0 + k:l0 + k + tn] for k in range(3)]
                        uo = u[:, 0:tn]
                        pbv = pbt[:, 0:tn]
                        go = GG[:, f, l0:l0 + tn]
                        cb_ = ab_
                    else:
                        pav = pa[:, 0:TS].rearrange("p (b t) -> p b t", t=4)
                        adst = As_[:, :, 2:6]
                        srcs = [As_[:, :, k:k + 4] for k in range(3)]
                        uo = u[:, 0:TS].rearrange("p (b t) -> p b t", t=4)
                        pbv = pbt[:, 0:TS].rearrange("p (b t) -> p b t", t=4)
                        go = GG[:, f, l0:l0 + TS].rearrange("p (b t) -> p b t", t=4)
                        cb_ = asb_
                    S.op("act", lambda e, adst=adst, pav=pav: e.activation(out=adst, in_=pav, func=AF.Copy), reads=[pab], writes=[cb_])
                    S.op("act", lambda e, uo=uo, pav=pav, f=f: e.activation(out=uo, in_=pav, func=AF.Identity, scale=FCW[:, layer, 2, f:f + 1], bias=FCW[:, layer, 3, f:f + 1]),
                         reads=[pab, cbuf], writes=[ubf])
                    for k in (1, 0):
                        S.op("dve", lambda e, uo=uo, s=srcs[k], f=f, k=k: e.scalar_tensor_tensor(
                            out=uo, in0=s, scalar=FCW[:, layer, k, f:f + 1], in1=uo, op0=ALU.mult, op1=ALU.add),
                            reads=[cb_, cbuf, ubf], writes=[ubf])
                    def tail(uo=uo, pbv=pbv, go=go, ubf=ubf, pbb=pbb, gbuf=gb[f][ti]):
                        S.op("act", lambda e: e.activation(out=uo, in_=uo, func=AF.Silu), reads=[ubf], writes=[ubf])
                        S.op("dve", lambda e: e.tensor_tensor(out=go, in0=uo, in1=pbv, op=ALU.mult), reads=[ubf, pbb], writes=[gbuf])
                    swp.step([lambda: None, tail])
                if gi == 0:
                    S.op("act", lambda e, A_=A_, f=f: e.copy(out=CAR[:, f, :], in_=A_[:, 1024:1026]), reads=[ab_], writes=[carb])
                else:
                    S.op("act", lambda e, A_=A_, f=f: e.copy(out=FC[:, f, 0:2], in_=A_[:, 1024:1026]), reads=[ab_], writes=[fcb])
                    S.op("act", lambda e, As_=As_, f=f: e.copy(out=FC[:, f, 2:2 + 2 * NB].rearrange("p (b j) -> p b j", j=2),
                                                                       in_=As_[:, :, 4:6]), reads=[asb_], writes=[fcb])
            swp.flush()
            if gi == 0:
                rmsnorm(DEPTH + layer, GROUPS[1], HGs[1], hbs[1], TILES[GROUPS[1][0]][0], sq_off)
            for c in range(KC):
                w0v, w0b = wget(dn_idx[c][0], [128, 11, 128])
                w1v, w1b = wget(dn_idx[c][1], [128, 11, 128])
                for ti in grp:
                    t0, tn = TILES[ti]
                    l0 = t0 - g0
                    pt, pb = ps()
                    for f in range(NF):
                        wv_, wbb_ = (w0v, w0b) if f < 11 else (w1v, w1b)
                        S.op("pe", lambda e, f=f, pt=pt, wv_=wv_, l0=l0, tn=tn: e.matmul(
                            pt[:, 0:tn], lhsT=wv_[:, f % 11, :], rhs=GG[:, f, l0:l0 + tn], start=(f == 0), stop=(f == NF - 1)),
                            reads=[wbb_, gb[f][ti]], writes=[pb])
                    S.op("dve", lambda e, c=c, pt=pt, t0=t0, tn=tn: e.tensor_tensor(
                        out=X[:, c, t0:t0 + tn], in0=X[:, c, t0:t0 + tn], in1=pt[:, 0:tn], op=ALU.add),
                        reads=[pb, xbuf[ti]], writes=[xbuf[ti]])
        PSMODE[0] = 0
        S.dma("act", fconv_d[layer], FC[:], reads=[fcb], is_output=True)

    NEG = -30000.0
    QPERM = [8 * jj + i + 4 * hf for jj in range(2) for i in range(4) for hf in range(2)]
    qcols = np.concatenate([np.arange(h * 64, (h + 1) * 64) for h in QPERM])

    def _maskp():
        q = np.arange(128)[:, None]; s_ = np.arange(256)[None, :]
        return np.where((s_ <= q + 128) & (s_ > q), 0.0, NEG).astype(np.float32)

    def _maskc():
        t_ = (np.arange(16) % 4)[:, None]; r = np.arange(128)[None, :]
        return np.where(r > t_, 0.0, NEG).astype(np.float32)

    def _maskn():
        t_ = (np.arange(16) % 4)[:, None, None]; b_ = np.arange(NB)[None, :, None]
        col = np.arange(64)[None, None, :]
        return np.where((col // 4 == b_) & (col % 4 <= t_), 0.0, NEG).astype(np.float32)

    if any(CFG["mixers"][l] and l % 3 == 0 for l in range(NL)):
        maskp_d = P.din("maskp", [128, 256], lambda I, c: _maskp())
        maskc_d = P.din("maskc", [16, 128], lambda I, c: _maskc())
        maskn_d = P.din("maskn", [16, NB, 64], lambda I, c: _maskn())
        abias_d = P.din("abias", [2, 128, 8 + 2 + 8], lambda I, c: np.stack([np.concatenate([
            I["attn_b_qkv"][s_][qcols].reshape(8, 128).T, I["attn_b_qkv"][s_][1024:1280].reshape(2, 128).T,
            I["attn_b_o"][s_].reshape(8, 128).T], axis=1) for s_ in range(2)]))
        vbias_d = P.din("vbias", [2, 128, 256], lambda I, c: np.broadcast_to(I["attn_b_qkv"][:, None, 1280:1536], (2, 128, 256)))
        sinkp_d = P.din("sinkp", [2, 128, 16], lambda I, c: np.broadcast_to(I["attn_sinks"][:, None, :], (2, 128, 16)))
        sinks_d = P.din("sinks", [2, 16, 4], lambda I, c: np.stack([np.repeat(I["attn_sinks"][s_].reshape(4, 4).T, 4, axis=0) for s_ in range(2)]))
        cacheKT_d = P.din("cacheKT", [2, NB, 2, 128, 128], lambda I, c: I["cache_attn_k"][:, c * NB:(c + 1) * NB].reshape(2, NB, 128, 2, 128).transpose(0, 1, 3, 4, 2))
        cacheV_d = P.din("cacheV", [2, NB, 128, 256], lambda I, c: I["cache_attn_v"][:, c * NB:(c + 1) * NB].reshape(2, NB, 128, 256))
        kTp_d = P.dout("kTp", [2, 2, 128, 128])
        vp_d = P.dout("vp", [2, 128, 256])
        kTs_d = P.dout("ks", [2, NB, 128, 256])
        cacheK_d = P.din("cacheK", [2, NB, 128, 256], lambda I, c: I["cache_attn_k"][:, c * NB:(c + 1) * NB].reshape(2, NB, 128, 256))
        vs_d = P.dout("vs", [2, NB, 128, 256])

    def attention(layer):
        sl = layer // 3
        Wqkv = lambda I: I["attn_w_qkv"][sl]
        q_idx = [wtile(("aq", sl, g), 2048, (lambda I, g=g: Wqkv(I)[:, qcols[g * 256:(g + 1) * 256]].reshape(KC, 128, 256).transpose(1, 0, 2).reshape(128, -1))) for g in range(4)]
        k_idx = wtile(("ak", sl), 2048, (lambda I: Wqkv(I)[:, 1024:1280].reshape(KC, 128, 256).transpose(1, 0, 2).reshape(128, -1)))
        v_idx = wtile(("av", sl), 2048, (lambda I: Wqkv(I)[:, 1280:1536].reshape(KC, 128, 256).transpose(1, 0, 2).reshape(128, -1)))
        qrows = qcols
        o_idx = [wtile(("ao", sl, g), 2048, (lambda I, g=g: I["attn_w_o"][sl][qrows][:, g * 256:(g + 1) * 256].reshape(KC, 128, 256).transpose(1, 0, 2).reshape(128, -1))) for g in range(4)]
        ar = Bump(ARENA, ARENA_SZ)
        HT = P.sb(ar.take(KC * 512 * 2), [128, KC, 512], BF16)
        QT = P.sb(ar.take(KC * 512 * 2), [128, KC, 512], BF16)
        OT = [P.sb(ar.take(KC * 512 * 2), [128, KC, 512], BF16) for _ in range(2)]
        KT = P.sb(ar.take(2 * T * 2), [128, 2, T], BF16)
        VT = P.sb(ar.take(16 * 256 * 2), [128, 16, 256], BF16)
        VS = P.sb(ar.take(512), [64, 256], BF16)
        QS = P.sb(ar.take(1024), [128, NB, 2, 16], BF16)
        qsb = Buf()
        VO = [P.sb(ar.take(1024), [128, 256], F32)] * 2
        KO = P.sb(ar.take(2 * 128 * 4), [128, 2, 128], F32)
        KSO = P.sb(ar.take(2 * 64 * 4), [128, 2, 64], F32)
        KST = P.sb(ar.take(1024), [64, 256], F32)
        kstb = Buf()
        sq_off = ar.take(KC * 512 * 2 + 2048)
        SC = [P.sb(ar.take(4096), [128, 4, 256], F32) for _ in range(3)]
        PN = [P.sb(ar.take(2048), [128, 4, 256], BF16) for _ in range(2)]
        PTs = [P.sb(ar.take(2048), [128, 8, 128], BF16) for _ in range(2)]
        ST = [P.sb(ar.take(128), [128, 6, 4], F32) for _ in range(3)]
        MASKP = P.sb(ar.take(1024), [128, 256], F32)
        MASKC = P.sb(ar.take(512), [16, 128], F32)
        MASKN = P.sb(ar.take(NB * 64 * 4), [16, NB, 64], F32)
        AB = P.sb(ar.take(18 * 4), [128, 18], F32)
        VB = P.sb(ar.take(1024), [128, 256], F32)
        SKP = P.sb(ar.take(64), [128, 16], F32)
        SKP8 = P.sb(ar.take(64), [128, 16], F32)
        SKMX8 = P.sb(ar.take(16), [128, 4], F32)
        SKS = P.sb(ar.take(16), [16, 4], F32)
        SKS8 = P.sb(ar.take(16), [16, 4], F32)
        KCf = [P.sb(ar.take(1024), [128, 2, 128], F32) for _ in range(2)]
        VCf = [P.sb(ar.take(1024), [128, 256], F32) for _ in range(2)]
        KCb = [P.sb(ar.take(512), [128, 2, 128], BF16) for _ in range(2)]
        VCb = [P.sb(ar.take(512), [128, 256], BF16) for _ in range(2)]
        SS = [P.sb(ar.take(4 * 192 * 4), [16, 4, 192], F32) for _ in range(2)]
        PNS = [P.sb(ar.take(4 * 192 * 2), [16, 4, 192], BF16) for _ in range(2)]
        PTC = [P.sb(ar.take(128), [128, 4, 16], BF16) for _ in range(2)]
        PTN = [P.sb(ar.take(128), [64, 4, 16], BF16) for _ in range(2)]
        cb2 = Buf()
        S.dma("act", MASKP[:], maskp_d, writes=[cb2])
        S.dma("act", MASKC[:], maskc_d, writes=[cb2])
        S.dma("act", MASKN[:], maskn_d, writes=[cb2])
        S.dma("act", AB[:], abias_d[sl], writes=[cb2])
        S.dma("act", VB[:], vbias_d[sl], writes=[cb2])
        S.dma("act", SKP[:], sinkp_d[sl], writes=[cb2])
        S.dma("act", SKS[:], sinks_d[sl], writes=[cb2])
        S.op("dve", lambda e: e.tensor_scalar(out=SKP8[:], in0=SKP[:], scalar1=8.0, scalar2=None, op0=ALU.mult), reads=[cb2], writes=[cb2])
        S.op("dve", lambda e: e.tensor_scalar(out=SKS8[:], in0=SKS[:], scalar1=8.0, scalar2=None, op0=ALU.mult), reads=[cb2], writes=[cb2])
        S.op("dve", lambda e: e.tensor_reduce(out=SKMX8[:], in_=SKP8[:, :].rearrange("p (a b) -> p a b", a=4), op=ALU.max, axis=AX.X), reads=[cb2], writes=[cb2])
        if CFG["dd"]:
            S.dma("act", kTs_d[sl][:, 0:124, :], cacheK_d[sl][:, 4:128, :], is_output=True)
            S.dma("act", vs_d[sl][:, 0:124, :], cacheV_d[sl][:, 4:128, :], is_output=True)
        hb, qb = Buf(), Buf()
        ob = [Buf(), Buf()]
        ktb = [Buf() for _ in TILES]
        vtb = [Buf() for _ in range(17)]
        vob = [Buf()] * 2
        kob, ksob = Buf(), Buf()
        scb = [Buf(), Buf(), Buf()]; pnb = [Buf(), Buf()]; ptb = [Buf(), Buf()]; stb = [Buf(), Buf(), Buf()]
        kcfb = [Buf(), Buf()]; vcfb = [Buf(), Buf()]; kcbb = [Buf(), Buf()]; vcbb = [Buf(), Buf()]
        ssb = [Buf(), Buf()]; pnsb = [Buf(), Buf()]; ptcb = [Buf(), Buf()]; ptnb = [Buf(), Buf()]
        gctr = 0
        for ti, (t0, tn) in enumerate(TILES):
            samp = (ti == 4)
            rmsnorm(layer, [ti], HT, {ti: hb}, t0, sq_off, lnexp=True)
            for g in range(4):
                wv, wb_ = wget(q_idx[g], [128, KC, 256])
                for cc in range(2):
                    c = 2 * g + cc
                    pt, pb = ps()
                    for kc in range(KC):
                        S.op("pe", lambda e, pt=pt, wv=wv, cc=cc, kc=kc, tn=tn: e.matmul(pt[:, 0:tn], lhsT=wv[:, kc, cc * 128:(cc + 1) * 128], rhs=HT[:, kc, 0:tn],
                                                                                       start=(kc == 0), stop=(kc == KC - 1)), reads=[wb_, hb], writes=[pb])
                    S.op("act", lambda e, pt=pt, c=c, tn=tn: e.activation(out=QT[:, c, 0:tn], in_=pt[:, 0:tn], func=AF.Identity, bias=AB[:, c:c + 1], scale=1.0),
                         reads=[pb, cb2], writes=[qb])
            if samp:
                for jj in range(2):
                    S.op("act", lambda e, jj=jj: e.copy(out=QS[:, :, jj, :].rearrange("p b (i t) -> p b i t", t=4),
                                                                in_=QT[:, jj * 4:(jj + 1) * 4, 0:64].rearrange("p i (b t) -> p b i t", t=4)), reads=[qb], writes=[qsb])
            wv, wb_ = wget(k_idx, [128, KC, 256])
            for jj in range(2):
                pt, pb = ps()
                for kc in range(KC):
                    S.op("pe", lambda e, pt=pt, wv=wv, jj=jj, kc=kc, tn=tn: e.matmul(pt[:, 0:tn], lhsT=wv[:, kc, jj * 128:(jj + 1) * 128], rhs=HT[:, kc, 0:tn],
                                                                                   start=(kc == 0), stop=(kc == KC - 1)), reads=[wb_, hb], writes=[pb])
                S.op("act", lambda e, pt=pt, jj=jj, t0=t0, tn=tn: e.activation(out=KT[:, jj, t0:t0 + tn], in_=pt[:, 0:tn], func=AF.Identity, bias=AB[:, 8 + jj:9 + jj], scale=1.0),
                     reads=[pb, cb2], writes=[ktb[ti]])
                if ti == 3:
                    S.op("dve", lambda e, pt=pt, jj=jj: e.tensor_scalar(out=KO[:, jj, :], in0=pt[:, 384:512], scalar1=AB[:, 8 + jj:9 + jj], scalar2=None, op0=ALU.add),
                         reads=[pb, cb2], writes=[kob])
                if samp:
                    S.op("dve", lambda e, pt=pt, jj=jj: e.tensor_scalar(out=KSO[:, jj, :], in0=pt[:, 0:64], scalar1=AB[:, 8 + jj:9 + jj], scalar2=None, op0=ALU.add),
                         reads=[pb, cb2], writes=[ksob])
            if ti == 3:
                S.dma("act", kTp_d[sl].rearrange("j p k -> p j k"), KO[:], reads=[kob], is_output=True)
            if samp:
                pt, pb = ps()
                for jj in range(2):
                    S.op("pe", lambda e, pt=pt, jj=jj: e.transpose(out=pt[0:64, jj * 128:(jj + 1) * 128], in_=KSO[:, jj, :], identity=ident_f[:]), reads=[ksob, cbuf], writes=[pb])
                S.op("act", lambda e, pt=pt: e.activation(out=KST[:, :], in_=pt[0:64, 0:256], func=AF.Copy), reads=[pb], writes=[kstb])
                for t_ in range(4 if CFG["so"] else 0):
                    S.dma("act", kTs_d[sl][:, 124 + t_, :], KST[t_:64:4, :], reads=[kstb], is_output=True)
            wv, wb_ = wget(v_idx, [128, KC, 256])
            nblk = 1 if samp else 4
            for bi in range(nblk):
                rows = 64 if samp else 128
                pt, pb = ps()
                for kc in range(KC):
                    S.op("pe", lambda e, pt=pt, wv=wv, kc=kc, bi=bi, rows=rows: e.matmul(pt[0:rows, 0:256], lhsT=HT[:, kc, bi * 128:bi * 128 + rows], rhs=wv[:, kc, :],
                                                                                       start=(kc == 0), stop=(kc == KC - 1)), reads=[wb_, hb], writes=[pb])
                gblk = ti * 4 + bi
                if samp:
                    S.op("dve", lambda e, pt=pt: e.tensor_tensor(out=VS[:, :], in0=pt[0:64, 0:256], in1=VB[0:64, :], op=ALU.add), reads=[pb, cb2], writes=[vtb[16]])
                    S.op("dve", lambda e, pt=pt: e.tensor_tensor(out=VO[0][0:64, :], in0=pt[0:64, 0:256], in1=VB[0:64, :], op=ALU.add), reads=[pb, cb2], writes=[vob[0]])
                    for t_ in range(4 if CFG["so"] else 0):
                        S.dma("act", vs_d[sl][:, 124 + t_, :], VO[0][t_:64:4, :], reads=[vob[0]], is_output=True)
                else:
                    S.op("dve", lambda e, pt=pt, gblk=gblk: e.tensor_tensor(out=VT[:, gblk, :], in0=pt[:, 0:256], in1=VB[:, :], op=ALU.add), reads=[pb, cb2], writes=[vtb[gblk]])
                    if gblk == 15:
                        S.op("dve", lambda e, pt=pt: e.tensor_tensor(out=VO[1][:, :], in0=pt[:, 0:256], in1=VB[:, :], op=ALU.add), reads=[pb, cb2], writes=[vob[1]])
                        S.dma("act", vp_d[sl], VO[1][:], reads=[vob[1]], is_output=True)
            oq = ti % 2
            if (samp and not CFG['as']) or (not samp and not CFG['ap']):
                S.op('dve', lambda e, oq=oq: e.memset(OT[oq][:], 0.0), writes=[ob[oq]])
            elif not samp:
                swp = SWP()
                po_state = {}
                for bi in range(4):
                    gblk = ti * 4 + bi
                    first = (gblk == 0)
                    ncol = 128 if first else 256
                    k0 = gblk * 128 if first else (gblk - 1) * 128
                    for j in range(4):
                        jj, base = j // 2, (j % 2) * 64
                        g3 = gctr % 3; g2 = gctr % 2; gctr += 1
                        sc, stt, scb_, stb_ = SC[g3], ST[g3], scb[g3], stb[g3]
                        pn, ptt, pnb_, ptb_ = PN[g2], PTs[g2], pnb[g2], ptb[g2]
                        nkb = ncol // 128
                        mk = MASKP[:, 128:256] if first else MASKP[:, :]
                        hold = {}

                        def st0a(jj=jj, base=base, bi=bi, k0=k0, ncol=ncol, hold=hold):
                            psc, pscb = ps2()
                            hold["psc"] = (psc, pscb)
                            for i in range(4):
                                cq = jj * 4 + i
                                S.op("pe", lambda e, i=i, cq=cq: e.matmul(
                                    psc[:, i * 256:i * 256 + ncol], lhsT=QT[base:base + 64, cq, bi * 128:(bi + 1) * 128], rhs=KT[base:base + 64, jj, k0:k0 + ncol],
                                    start=True, stop=True), reads=[qb] + [ktb[x] for x in {k0 // 512, (k0 + ncol - 1) // 512}], writes=[pscb[i // 2]])

                        def st0(sc=sc, stt=stt, scb_=scb_, stb_=stb_, ncol=ncol, mk=mk, j=j, hold=hold):
                            psc, pscb = hold["psc"]
                            S.op("dve", lambda e: e.tensor_tensor(
                                out=sc[:, :, 0:ncol], in0=psc.rearrange("p (a b) -> p a b", a=4)[:, :, 0:ncol], in1=mk.unsqueeze(1).broadcast_to([128, 4, ncol]), op=ALU.add),
                                reads=pscb + [cb2], writes=[scb_])
                            S.op("dve", lambda e: e.tensor_reduce(out=stt[:, 0, :], in_=sc[:, :, 0:ncol], op=ALU.max, axis=AX.X), reads=[scb_], writes=[stb_])
                            S.op("dve", lambda e: e.tensor_tensor(out=stt[:, 0, :], in0=stt[:, 0, :], in1=SKP8[:, 4 * j:4 * j + 4], op=ALU.max), reads=[stb_, cb2], writes=[stb_])
                            S.op("dve", lambda e: e.tensor_scalar(out=stt[:, 1, :], in0=stt[:, 0, :], scalar1=-0.125, scalar2=None, op0=ALU.mult), reads=[stb_], writes=[stb_])
                            S.op("dve", lambda e: e.tensor_tensor(out=stt[:, 3, :], in0=stt[:, 1, :], in1=SKP[:, 4 * j:4 * j + 4], op=ALU.add), reads=[stb_, cb2], writes=[stb_])

                        def st1(sc=sc, stt=stt, scb_=scb_, stb_=stb_, ncol=ncol):
                            for i in range(4):
                                S.op("act", lambda e, i=i: e.activation(out=sc[:, i, 0:ncol], in_=sc[:, i, 0:ncol], func=AF.Exp, bias=stt[:, 1, i:i + 1], scale=0.125,
                                                                        accum_out=stt[:, 2, i:i + 1]), reads=[scb_, stb_], writes=[scb_, stb_])
                            S.op("act", lambda e: e.activation(out=stt[:, 3, :], in_=stt[:, 3, :], func=AF.Exp), reads=[stb_], writes=[stb_])

                        def st2(sc=sc, stt=stt, scb_=scb_, stb_=stb_, pn=pn, pnb_=pnb_, ncol=ncol, nkb=nkb, hold=hold):
                            S.op("dve", lambda e: e.tensor_tensor(out=stt[:, 4, :], in0=stt[:, 2, :], in1=stt[:, 3, :], op=ALU.add), reads=[stb_], writes=[stb_])
                            S.op("dve", lambda e: e.reciprocal(out=stt[:, 5, :], in_=stt[:, 4, :]), reads=[stb_], writes=[stb_])
                            S.op("dve", lambda e: e.tensor_tensor(out=pn[:, :, 0:ncol], in0=sc[:, :, 0:ncol],
                                                                  in1=stt[:, 5, :].unsqueeze(2).broadcast_to([128, 4, ncol]), op=ALU.mult),
                                 reads=[scb_, stb_], writes=[pnb_])
                            ptp, ptpb = pst()
                            hold["ptp"] = (ptp, ptpb)
                            for i in range(4):
                                for kb in range(nkb):
                                    S.op("pe", lambda e, i=i, kb=kb: e.transpose(out=ptp[:, (i * 2 + kb) * 128:(i * 2 + kb + 1) * 128], in_=pn[:, i, kb * 128:(kb + 1) * 128],
                                                                                 identity=ident_bf[:]), reads=[pnb_, cbuf], writes=[ptpb])

                        def st3(ptt=ptt, ptb_=ptb_, hold=hold, jj=jj, base=base, j=j, nkb=nkb, gblk=gblk, first=first, bi=bi, oq=oq):
                            ptp, ptpb = hold["ptp"]
                            po, pob = ps_long()
                            S.op("act", lambda e: e.activation(out=ptt[:].rearrange("p a b -> p (a b)"), in_=ptp[:, :], func=AF.Copy), reads=[ptpb], writes=[ptb_])
                            for i in range(4):
                                cq = jj * 4 + i
                                for kb in range(nkb):
                                    vblk = gblk if (first or kb == 1) else gblk - 1
                                    S.op("pe", lambda e, i=i, kb=kb, cq=cq, vblk=vblk: e.matmul(
                                        po[base:base + 64, cq * 128:(cq + 1) * 128], lhsT=VT[:, vblk, j * 64:(j + 1) * 64], rhs=ptt[:, i * 2 + kb, :],
                                        start=(kb == 0), stop=(kb == nkb - 1), tile_position=(0, base)), reads=[ptb_, vtb[vblk]], writes=[pob[cq // 4]])
                            if j == 3:
                                S.op("act", lambda e: e.activation(out=OT[oq][:, :, bi * 128:(bi + 1) * 128], in_=po.rearrange("p (a b) -> p a b", a=8), func=AF.Copy),
                                     reads=pob, writes=[ob[oq]])

                        swp.step([st0a, st0, st1, st2, st3])
                swp.flush()
            else:
                def samp_gen(b, q2):
                    S.dma("sp", KCf[q2][:], cacheKT_d[sl][b].rearrange("j p k -> p j k"), writes=[kcfb[q2]])
                    S.dma("sp", VCf[q2][:], cacheV_d[sl][b], writes=[vcfb[q2]])
                    S.op("act", lambda e: e.copy(out=KCb[q2][:], in_=KCf[q2][:]), reads=[kcfb[q2]], writes=[kcbb[q2]])
                    S.op("act", lambda e: e.copy(out=VCb[q2][:], in_=VCf[q2][:]), reads=[vcfb[q2]], writes=[vcbb[q2]])
                    yield
                    pcs = [psb(2 * q2), psb(2 * q2 + 1)]
                    for j in range(4):
                        jj, base = j // 2, (j % 2) * 64
                        pc, pcb = pcs[j % 2]
                        S.op("pe", lambda e, pc=pc, jj=jj, base=base: e.matmul(pc[0:16, jj * 192:jj * 192 + 128], lhsT=QS[base:base + 64, b, jj, :],
                                                                            rhs=KCb[q2][base:base + 64, jj, :], start=True, stop=True), reads=[qsb, kcbb[q2]], writes=[pcb])
                        S.op("pe", lambda e, pc=pc, jj=jj, base=base: e.matmul(pc[0:16, jj * 192 + 128:jj * 192 + 192], lhsT=QS[base:base + 64, b, jj, :],
                                                                            rhs=KT[base:base + 64, jj, TP:T], start=True, stop=True), reads=[qsb, ktb[4]], writes=[pcb])
                    yield
                    ss, pns, stt = SS[q2], PNS[q2], ST[q2]
                    for par in range(2):
                        pc, pcb = pcs[par]
                        pv_ = pc[0:16, 0:384].rearrange("p (a b) -> p a b", a=2)
                        S.op("dve", lambda e, pv_=pv_, par=par: e.tensor_tensor(out=ss[:, par:4:2, 0:128], in0=pv_[:, :, 0:128],
                                                                               in1=MASKC[:, :].unsqueeze(1).broadcast_to([16, 2, 128]), op=ALU.add), reads=[pcb, cb2], writes=[ssb[q2]])
                        S.op("dve", lambda e, pv_=pv_, par=par: e.tensor_tensor(out=ss[:, par:4:2, 128:192], in0=pv_[:, :, 128:192],
                                                                               in1=MASKN[:, b, :].unsqueeze(1).broadcast_to([16, 2, 64]), op=ALU.add), reads=[pcb, cb2], writes=[ssb[q2]])
                    S.op("dve", lambda e: e.tensor_reduce(out=stt[0:16, 0, :], in_=ss[:, :, :], op=ALU.max, axis=AX.X), reads=[ssb[q2]], writes=[stb[q2]])
                    S.op("dve", lambda e: e.tensor_tensor(out=stt[0:16, 0, :], in0=stt[0:16, 0, :], in1=SKS8[:, :], op=ALU.max), reads=[stb[q2], cb2], writes=[stb[q2]])
                    S.op("dve", lambda e: e.tensor_scalar(out=stt[0:16, 1, :], in0=stt[0:16, 0, :], scalar1=-0.125, scalar2=None, op0=ALU.mult), reads=[stb[q2]], writes=[stb[q2]])
                    S.op("dve", lambda e: e.tensor_tensor(out=stt[0:16, 3, :], in0=stt[0:16, 1, :], in1=SKS[:, :], op=ALU.add), reads=[stb[q2], cb2], writes=[stb[q2]])
                    yield
                    for j in range(4):
                        S.op("act", lambda e, j=j: e.activation(out=ss[:, j, :], in_=ss[:, j, :], func=AF.Exp, bias=stt[0:16, 1, j:j + 1], scale=0.125,
                                                                accum_out=stt[0:16, 2, j:j + 1]), reads=[ssb[q2], stb[q2]], writes=[ssb[q2], stb[q2]])
                    S.op("act", lambda e: e.activation(out=stt[0:16, 3, :], in_=stt[0:16, 3, :], func=AF.Exp), reads=[stb[q2]], writes=[stb[q2]])
                    yield
                    S.op("dve", lambda e: e.tensor_tensor(out=stt[0:16, 4, :], in0=stt[0:16, 2, :], in1=stt[0:16, 3, :], op=ALU.add), reads=[stb[q2]], writes=[stb[q2]])
                    S.op("dve", lambda e: e.reciprocal(out=stt[0:16, 5, :], in_=stt[0:16, 4, :]), reads=[stb[q2]], writes=[stb[q2]])
                    S.op("dve", lambda e: e.tensor_tensor(out=pns[:, :, :], in0=ss[:, :, :], in1=stt[0:16, 5, :].unsqueeze(2).broadcast_to([16, 4, 192]), op=ALU.mult),
                         reads=[ssb[q2], stb[q2]], writes=[pnsb[q2]])
                    yield
                    ptp, ptpb = PST[:, q2 * 1024:(q2 + 1) * 1024], psbuf[6 + q2]
                    for j in range(4):
                        S.op("pe", lambda e, j=j: e.transpose(out=ptp[:, j * 16:(j + 1) * 16], in_=pns[:, j, 0:128], identity=ident_bf[0:16, 0:16]),
                             reads=[pnsb[q2], cbuf], writes=[ptpb])
                        S.op("pe", lambda e, j=j: e.transpose(out=ptp[0:64, 64 + j * 16:64 + (j + 1) * 16], in_=pns[:, j, 128:192], identity=ident_bf[0:16, 0:16]),
                             reads=[pnsb[q2], cbuf], writes=[ptpb])
                    yield
                    S.op("act", lambda e: e.activation(out=PTC[q2][:].rearrange("p a b -> p (a b)"), in_=ptp[:, 0:64], func=AF.Copy), reads=[ptpb], writes=[ptcb[q2]])
                    S.op("act", lambda e: e.activation(out=PTN[q2][:].rearrange("p a b -> p (a b)"), in_=ptp[0:64, 64:128], func=AF.Copy), reads=[ptpb], writes=[ptnb[q2]])
                    yield
                    po, pob = psb(4 + q2)
                    for j in range(4):
                        jj, base = j // 2, (j % 2) * 64
                        S.op("pe", lambda e, j=j, jj=jj, base=base: e.matmul(po[base:base + 64, jj * 16:(jj + 1) * 16], lhsT=VCb[q2][:, j * 64:(j + 1) * 64], rhs=PTC[q2][:, j, :],
                                                                           start=True, stop=False, tile_position=(0, base)), reads=[vcbb[q2], ptcb[q2]], writes=[pob])
                        S.op("pe", lambda e, j=j, jj=jj, base=base: e.matmul(po[base:base + 64, jj * 16:(jj + 1) * 16], lhsT=VS[0:64, j * 64:(j + 1) * 64], rhs=PTN[q2][0:64, j, :],
                                                                           start=False, stop=True, tile_position=(0, base)), reads=[vtb[16], ptnb[q2]], writes=[pob])
                    yield
                    S.op("act", lambda e: e.activation(out=OT[oq][:, :, 4 * b:4 * b + 4], in_=po[:, 0:32].rearrange("p (a b) -> p a b", b=4), func=AF.Copy),
                         reads=[pob], writes=[ob[oq]])

                pending = list(range(NB))
                active = {}
                for slot in range(2):
                    active[slot] = samp_gen(pending.pop(0), slot)
                while active:
                    for slot in list(active.keys()):
                        try:
                            next(active[slot])
                        except StopIteration:
                            if pending:
                                active[slot] = samp_gen(pending.pop(0), slot)
                            else:
                                del active[slot]
            for g in range(4):
                wv, wb_ = wget(o_idx[g], [128, KC, 256])
                for cc in range(2):
                    c = 2 * g + cc
                    pt, pb = ps()
                    for kc in range(KC):
                        S.op("pe", lambda e, pt=pt, wv=wv, cc=cc, kc=kc, tn=tn, oq=oq: e.matmul(pt[:, 0:tn], lhsT=wv[:, kc, cc * 128:(cc + 1) * 128], rhs=OT[oq][:, kc, 0:tn],
                                                                                              start=(kc == 0), stop=(kc == KC - 1)), reads=[wb_, ob[oq]], writes=[pb])
                    S.op("dve", lambda e, pt=pt, c=c, t0=t0, tn=tn: e.scalar_tensor_tensor(out=X[:, c, t0:t0 + tn], in0=pt[:, 0:tn], scalar=AB[:, 10 + c:11 + c], in1=X[:, c, t0:t0 + tn],
                                                                                          op0=ALU.add, op1=ALU.add), reads=[pb, cb2, xbuf[ti]], writes=[xbuf[ti]])

    HTILES = [(t0, 256) for t0 in range(0, TP, 256)] + [(TP, TS)]

    def _hmask():
        s_ = np.arange(128)[:, None]; t_ = np.arange(128)[None, :]
        same = (s_ // 16) == (t_ // 16)
        tri = (same & (s_ <= t_)).astype(np.float32)
        tri2 = (same & (s_ > t_)).astype(np.float32)
        cm = ((np.arange(128)[:, None] // 16) == np.arange(8)[None, :]).astype(np.float32)
        return np.concatenate([tri, tri2, -tri, cm], axis=1)

    def _hmask_s():
        s_ = np.arange(64)[:, None]; t_ = np.arange(64)[None, :]
        same = (s_ // 4) == (t_ // 4)
        tri = (same & (s_ <= t_)).astype(np.float32)
        tri2 = (same & (s_ > t_)).astype(np.float32)
        cm = ((np.arange(64)[:, None] // 4) == np.arange(16)[None, :]).astype(np.float32)
        return np.concatenate([tri, tri2, -tri, cm], axis=1)

    if NL > 1 and CFG["mixers"][1]:
        hmask_d = P.din("hmask", [128, 392], lambda I, c: _hmask())
        hmasks_d = P.din("hmasks", [64, 208], lambda I, c: _hmask_s())
        lbl_d = P.din("lbl", [128, 4, 1024], lambda I, c: np.broadcast_to(I["hgrn_lb_logits"][None], (128, 4, 1024)))
        hnw_d = P.din("hnw", [128, 8], lambda I, c: I["hgrn_norm_w"][0].reshape(8, 128).T)
        hgS_d = P.din("hgS", [NB, 128, 8, 128], lambda I, c: I["state_hgrn"][0, c * NB:(c + 1) * NB].transpose(0, 2, 1, 3))
        hgP_d = P.dout("hgP", [128, 8, 128])
        hgSo_d = P.dout("hgSo", [NB, 128, 8, 128])

    def hgrn(layer):
        Win = lambda I: I["hgrn_w_in"][0]
        def wt(name, c0):
            return wtile((name, c0), 2048, (lambda I, c0=c0: Win(I)[:, c0:c0 + 256].reshape(KC, 128, 256).transpose(1, 0, 2).reshape(128, -1)))
        q_idx = [wt("hq", 256 * g) for g in range(4)]
        f_idx = [wt("hf", 1024 + 256 * g) for g in range(4)]
        i_idx = [wt("hi", 2048 + 256 * g) for g in range(4)]
        g_idx = [wt("hg", 3072 + 256 * g) for g in range(4)]
        o_idx = [wtile(("ho", g), 2048, (lambda I, g=g: I["hgrn_w_o"][0][:, g * 256:(g + 1) * 256].reshape(KC, 128, 256).transpose(1, 0, 2).reshape(128, -1))) for g in range(4)]
        ar = Bump(ARENA, ARENA_SZ)
        TN = 256
        HTs = [P.sb(ar.take(KC * TN * 2), [128, KC, TN], BF16) for _ in range(2)]
        SQN2 = P.sb(ar.take(KC * TN * 2), [128, KC, TN], BF16)
        RSN2 = P.sb(ar.take(1024), [128, TN], F32)
        sqnb, rsnb = Buf(), Buf()
        hbs = [Buf(), Buf()]
        SQ = P.sb(ar.take(KC * TN * 2), [128, KC, TN], BF16)
        GS = P.sb(ar.take(KC * TN * 2), [128, KC, TN], BF16)
        OT = P.sb(ar.take(KC * 512 * 2), [128, KC, 512], BF16)
        RSn = ar.take(2048)
        lf_off = ar.take(2 * 4096)
        LF = P.sb(lf_off, [128, 2, 1024], F32)
        l1_off = ar.take(2 * 4096)
        L1 = P.sb(l1_off, [128, 2, 1024], F32)
        V = P.sb(ar.take(2 * 2048), [128, 2, 1024], BF16)
        E = P.sb(ar.take(4096), [128, 8, 128], F32)
        DEC = P.sb(ar.take(8 * 16 * 4), [128, 8, 16], F32)
        QIb = P.sb(ar.take(2048), [128, 8, 128], BF16)
        KI = P.sb(ar.take(2048), [128, 8, 128], BF16)
        KD = P.sb(ar.take(2048), [128, 1024], BF16)
        kdm_off = ar.take(16384)
        KDM = P.sb(kdm_off, [128, 8, 1024], BF16)
        AT = P.sb(ar.take(2048), [128, 8, 128], BF16)
        S32 = [P.sb(ar.take(4096), [128, 8, 128], F32) for _ in range(2)]
        SB16 = [P.sb(ar.take(2048), [128, 8, 128], BF16) for _ in range(2)]
        sbb = [Buf(), Buf()]
        SQO = P.sb(ar.take(2048), [128, 1024], BF16)
        RSTD = P.sb(ar.take(4096), [128, 1024], F32)
        FT = [P.sb(ar.take(1024), [128, 256], F32) for _ in range(2)]
        LB = P.sb(ar.take(4096), [128, 1024], F32)
        OML = P.sb(ar.take(4096), [128, 1024], F32)
        HM = P.sb(ar.take(392 * 4), [128, 392], F32)
        HMS = P.sb(ar.take(208 * 4), [64, 208], F32)
        NW = P.sb(ar.take(32), [128, 8], F32)
        TRIb = P.sb(ar.take(256), [128, 128], BF16)
        TRIsb = P.sb(ar.take(128), [64, 64], BF16)
        S0 = [P.sb(lf_off + 4096, [128, 8, 128], F32), P.sb(l1_off + 4096, [128, 8, 128], F32)]
        cb3 = Buf()
        hb, sqb, gsb, ob = Buf(), Buf(), Buf(), Buf()
        lfb, l1b, vb_ = [Buf(), Buf()], [Buf(), Buf()], [Buf(), Buf()]
        eb, decb, qib, kib, kdb, atb = Buf(), Buf(), Buf(), Buf(), Buf(), Buf()
        kdmb = [Buf() for _ in range(16)]
        sb_ = [[Buf() for _ in range(8)] for _ in range(2)]
        sqob, rstdb = Buf(), Buf()
        ftb = [Buf(), Buf()]
        s0b = [lfb[1], l1b[1]]
        S.dma("act", HM[:], hmask_d, writes=[cb3])
        S.dma("act", HMS[:], hmasks_d, writes=[cb3])
        S.dma("act", NW[:], hnw_d, writes=[cb3])
        LGt = P.sb(kdm_off, [128, 4, 1024], F32)
        S.dma("act", LGt[:], lbl_d, writes=[kdmb[0]])
        S.op("act", lambda e: e.activation(out=LGt[:], in_=LGt[:], func=AF.Exp), reads=[kdmb[0]], writes=[kdmb[0]])
        S.op("dve", lambda e: e.tensor_tensor(out=LB[:], in0=LGt[:, 0, :], in1=LGt[:, 1, :], op=ALU.add), reads=[kdmb[0]], writes=[cb3])
        S.op("dve", lambda e: e.tensor_tensor(out=LB[:], in0=LB[:], in1=LGt[:, 2, :], op=ALU.add), reads=[kdmb[0], cb3], writes=[cb3])
        S.op("dve", lambda e: e.tensor_tensor(out=LB[:], in0=LB[:], in1=LGt[:, 3, :], op=ALU.add), reads=[kdmb[0], cb3], writes=[cb3])
        S.op("dve", lambda e: e.reciprocal(out=LB[:], in_=LB[:]), reads=[cb3], writes=[cb3])
        S.op("dve", lambda e: e.tensor_tensor(out=LB[:], in0=LB[:], in1=LGt[:, 1, :], op=ALU.mult), reads=[kdmb[0], cb3], writes=[cb3] + kdmb[0:8])
        S.op("dve", lambda e: e.tensor_scalar(out=OML[:], in0=LB[:], scalar1=-1.0, scalar2=1.0, op0=ALU.mult, op1=ALU.add), reads=[cb3], writes=[cb3])
        S.op("dve", lambda e: e.tensor_copy(out=TRIb[:], in_=HM[:, 0:128]), reads=[cb3], writes=[cb3])
        S.op("dve", lambda e: e.tensor_copy(out=TRIsb[:], in_=HMS[:, 0:64]), reads=[cb3], writes=[cb3])
        for h in range(8):
            S.op("dve", lambda e, h=h: e.memset(S32[0][:, h, :], 0.0), writes=[sb_[0][h]])
        S.op("dve", lambda e: e.memset(SB16[0][:], 0.0), writes=[sbb[0]])
        def do_norm(idx):
            t0, tn = HTILES[idx]
            ti = t0 // 512
            HTd, hbd = HTs[idx % 2], hbs[idx % 2]
            S.op("act", lambda e: e.activation(out=SQN2[:, :, 0:tn], in_=X[:, :, t0:t0 + tn], func=AF.Square), reads=[xbuf[ti]], writes=[sqnb])
            pt, pb = ps()
            for c in range(KC):
                S.op("pe", lambda e, c=c: e.matmul(pt[:, 0:tn], lhsT=ones_bf[:], rhs=SQN2[:, c, 0:tn], start=(c == 0), stop=(c == KC - 1)), reads=[sqnb, cbuf], writes=[pb])
            S.op("act", lambda e: e.activation(out=RSN2[:, 0:tn], in_=pt[:, 0:tn], func=AF.Ln, bias=epsb[:], scale=1.0 / D), reads=[pb, cbuf], writes=[rsnb])
            S.op("act", lambda e: e.activation(out=RSN2[:, 0:tn], in_=RSN2[:, 0:tn], func=AF.Exp, scale=-0.5), reads=[rsnb], writes=[rsnb])
            for c in range(KC):
                S.op("dve", lambda e, c=c: e.scalar_tensor_tensor(out=HTd[:, c, 0:tn], in0=X[:, c, t0:t0 + tn], scalar=NORMW[:, layer, c:c + 1], in1=RSN2[:, 0:tn],
                                                                op0=ALU.mult, op1=ALU.mult), reads=[xbuf[ti], rsnb, cbuf], writes=[hbd])

        do_norm(0)
        cur = 0
        for hi_, (t0, tn) in enumerate(HTILES):
            samp = (t0 == TP)
            ti = t0 // 512
            HT, hb = HTs[hi_ % 2], hbs[hi_ % 2]
            for g in range(4):
                wv, wb_ = wget(q_idx[g], [128, KC, 256])
                for cc in range(2):
                    c = 2 * g + cc
                    pt, pb = ps()
                    for kc in range(KC):
                        S.op("pe", lambda e, pt=pt, wv=wv, cc=cc, kc=kc, tn=tn, HT=HT: e.matmul(pt[:, 0:tn], lhsT=wv[:, kc, cc * 128:(cc + 1) * 128], rhs=HT[:, kc, 0:tn],
                                                                                       start=(kc == 0), stop=(kc == KC - 1)), reads=[wb_, hb], writes=[pb])
                    S.op("act", lambda e, pt=pt, c=c, tn=tn: e.activation(out=SQ[:, c, 0:tn], in_=pt[:, 0:tn], func=AF.Silu), reads=[pb], writes=[sqb])
            for g in range(4):
                wv, wb_ = wget(g_idx[g], [128, KC, 256])
                for cc in range(2):
                    c = 2 * g + cc
                    pt, pb = ps()
                    for kc in range(KC):
                        S.op("pe", lambda e, pt=pt, wv=wv, cc=cc, kc=kc, tn=tn, HT=HT: e.matmul(pt[:, 0:tn], lhsT=wv[:, kc, cc * 128:(cc + 1) * 128], rhs=HT[:, kc, 0:tn],
                                                                                       start=(kc == 0), stop=(kc == KC - 1)), reads=[wb_, hb], writes=[pb])
                    S.op("act", lambda e, pt=pt, c=c, tn=tn: e.activation(out=GS[:, c, 0:tn], in_=pt[:, 0:tn], func=AF.Silu), reads=[pb], writes=[gsb])
                    S.op("dve", lambda e, c=c, tn=tn: e.tensor_scalar(out=GS[:, c, 0:tn], in0=GS[:, c, 0:tn], scalar1=NW[:, c:c + 1], scalar2=None, op0=ALU.mult), reads=[gsb, cb3], writes=[gsb])
            nblk = 1 if samp else 2
            rows = 64 if samp else 128
            fctr = 0
            swpf = SWP()
            for g in range(4):
                wv, wb_ = wget(f_idx[g], [128, KC, 256])
                ftails = []
                for bi in range(nblk + 1):
                    if bi == nblk:
                        for ft_ in ftails:
                            ft_()
                        break
                    pt, pb = ps()
                    for kc in range(KC):
                        S.op("pe", lambda e, pt=pt, wv=wv, kc=kc, bi=bi, rows=rows, HT=HT: e.matmul(pt[0:rows, 0:256], lhsT=HT[:, kc, bi * 128:bi * 128 + rows], rhs=wv[:, kc, :],
                                                                                           start=(kc == 0), stop=(kc == KC - 1)), reads=[wb_, hb], writes=[pb])
                    fq = fctr % 2; fctr += 1
                    ft = FT[fq]
                    cs = slice(g * 256, (g + 1) * 256)
                    S.op("act", lambda e, pt=pt, ft=ft, rows=rows: e.activation(out=ft[0:rows, :], in_=pt[0:rows, 0:256], func=AF.Sigmoid), reads=[pb], writes=[ftb[fq]])
                    S.op("dve", lambda e, ft=ft, rows=rows, cs=cs: e.tensor_tensor(out=ft[0:rows, :], in0=ft[0:rows, :], in1=OML[0:rows, cs], op=ALU.mult), reads=[ftb[fq], cb3], writes=[ftb[fq]])
                    S.op("dve", lambda e, ft=ft, rows=rows, cs=cs: e.tensor_tensor(out=ft[0:rows, :], in0=ft[0:rows, :], in1=LB[0:rows, cs], op=ALU.add), reads=[ftb[fq], cb3], writes=[ftb[fq]])
                    def ftail(ft=ft, rows=rows, cs=cs, bi=bi, fq=fq):
                        S.op("act", lambda e: e.activation(out=LF[0:rows, bi, cs], in_=ft[0:rows, :], func=AF.Ln), reads=[ftb[fq]], writes=[lfb[bi]])
                        S.op("act", lambda e: e.activation(out=L1[0:rows, bi, cs], in_=ft[0:rows, :], func=AF.Ln, scale=-1.0, bias=oneb[0:rows, :]), reads=[ftb[fq], cbuf], writes=[l1b[bi]])
                    ftails.append(ftail)
            swpf.flush()
            for g in range(4):
                wv, wb_ = wget(i_idx[g], [128, KC, 256])
                for bi in range(nblk):
                    pt, pb = ps()
                    for kc in range(KC):
                        S.op("pe", lambda e, pt=pt, wv=wv, kc=kc, bi=bi, rows=rows, HT=HT: e.matmul(pt[0:rows, 0:256], lhsT=HT[:, kc, bi * 128:bi * 128 + rows], rhs=wv[:, kc, :],
                                                                                           start=(kc == 0), stop=(kc == KC - 1)), reads=[wb_, hb], writes=[pb])
                    S.op("act", lambda e, pt=pt, rows=rows, bi=bi, g=g: e.activation(out=V[0:rows, bi, g * 256:(g + 1) * 256], in_=pt[0:rows, 0:256], func=AF.Copy), reads=[pb], writes=[vb_[bi]])
            if hi_ + 1 < len(HTILES):
                do_norm(hi_ + 1)
            for bi in range(nblk):
                c0 = bi * 128
                if samp:
                    R, NCH = 64, 16
                    tri, tri2, ntri, cmm, trib = HMS[:, 0:64], HMS[:, 64:128], HMS[:, 128:192], HMS[:, 192:208], TRIsb
                else:
                    R, NCH = 128, 8
                    tri, tri2, ntri, cmm, trib = HM[:, 0:128], HM[:, 128:256], HM[:, 256:384], HM[:, 384:392], TRIb
                CH = R // NCH
                for hg in range(2):
                    pt, pb = ps()
                    for hh in range(4):
                        h = hg * 4 + hh
                        S.op("pe", lambda e, pt=pt, hh=hh, h=h, bi=bi, R=R, tri=tri: e.matmul(pt[:, hh * 128:hh * 128 + R], lhsT=LF[0:R, bi, h * 128:(h + 1) * 128], rhs=tri[0:R, :],
                                                                                          start=True, stop=True), reads=[lfb[bi], cb3], writes=[pb])
                    S.op("act", lambda e, pt=pt, hg=hg, R=R: e.activation(out=E[:, hg * 4:hg * 4 + 4, 0:R], in_=pt[:, :].rearrange("p (a b) -> p a b", a=4)[:, :, 0:R], func=AF.Exp),
                         reads=[pb], writes=[eb])
                S.op("dve", lambda e, R=R, NCH=NCH, CH=CH: e.tensor_copy(out=DEC[:, :, 0:NCH], in_=E[:, :, CH - 1:R:CH]), reads=[eb], writes=[decb])
                for hg in range(2):
                    pt, pb = ps()
                    for hh in range(4):
                        h = hg * 4 + hh
                        S.op("pe", lambda e, pt=pt, hh=hh, h=h, bi=bi, R=R, ntri=ntri: e.matmul(pt[:, hh * 128:hh * 128 + R], lhsT=LF[0:R, bi, h * 128:(h + 1) * 128], rhs=ntri[0:R, :],
                                                                                            start=(hh == 0), stop=False, skip_group_check=True), reads=[lfb[bi], cb3], writes=[pb])
                        S.op("pe", lambda e, pt=pt, hh=hh, h=h, bi=bi, R=R: e.matmul(pt[:, hh * 128:hh * 128 + R], lhsT=L1[0:R, bi, h * 128:(h + 1) * 128], rhs=ident_f[0:R, 0:R],
                                                                                 start=False, stop=True, skip_group_check=True), reads=[l1b[bi], cbuf], writes=[pb])
                    S.op("act", lambda e, pt=pt, hg=hg, R=R: e.activation(out=KI[:, hg * 4:hg * 4 + 4, 0:R], in_=pt[:, :].rearrange("p (a b) -> p a b", a=4)[:, :, 0:R], func=AF.Exp),
                         reads=[pb], writes=[kib])
                for hg in range(2):
                    pt, pb = ps()
                    S.op("pe", lambda e, pt=pt, hg=hg, bi=bi, R=R, tri2=tri2: e.matmul(pt[0:R, :], lhsT=tri2[0:R, :], rhs=LF[0:R, bi, hg * 512:(hg + 1) * 512], start=True, stop=False),
                         reads=[lfb[bi], cb3], writes=[pb])
                    S.op("pe", lambda e, pt=pt, hg=hg, bi=bi, R=R: e.matmul(pt[0:R, :], lhsT=ident_f[0:R, 0:R], rhs=L1[0:R, bi, hg * 512:(hg + 1) * 512], start=False, stop=True),
                         reads=[l1b[bi], cbuf], writes=[pb])
                    S.op("act", lambda e, pt=pt, hg=hg, R=R: e.activation(out=KD[0:R, hg * 512:(hg + 1) * 512], in_=pt[0:R, :], func=AF.Exp), reads=[pb], writes=[kdb])
                S.op("dve", lambda e, R=R, c0=c0: e.tensor_tensor(out=E[:, :, 0:R], in0=E[:, :, 0:R], in1=SQ[:, :, c0:c0 + R], op=ALU.mult), reads=[eb, sqb, decb], writes=[eb])
                S.op("act", lambda e, R=R: e.copy(out=QIb[:, :, 0:R], in_=E[:, :, 0:R]), reads=[eb], writes=[qib])
                for c in range(0 if samp else NCH):
                    S.op("act", lambda e, c=c, R=R, cmm=cmm: e.activation(out=KDM[0:R, c, :], in_=KD[0:R, :], func=AF.Copy, scale=cmm[0:R, c:c + 1]),
                         reads=[kdb, cb3], writes=[kdmb[c]])
                for hg in range(2):
                    pt, pb = ps()
                    for hh in range(4):
                        h = hg * 4 + hh
                        S.op("pe", lambda e, pt=pt, hh=hh, h=h, R=R: e.matmul(pt[0:R, hh * 128:hh * 128 + R], lhsT=KI[:, h, 0:R], rhs=QIb[:, h, 0:R], start=True, stop=True),
                             reads=[kib, qib], writes=[pb])
                    S.op("dve", lambda e, pt=pt, hg=hg, R=R, trib=trib: e.tensor_tensor(out=AT[0:R, hg * 4:hg * 4 + 4, 0:R], in0=pt[0:R, :].rearrange("p (a b) -> p a b", a=4)[:, :, 0:R],
                                                                                       in1=trib[0:R, 0:R].unsqueeze(1).broadcast_to([R, 4, R]), op=ALU.mult), reads=[pb, cb3], writes=[atb])
                po, pob = ps_long()
                for h in range(8):
                    S.op("pe", lambda e, po=po, h=h, bi=bi, R=R: e.matmul(po[:, h * 128:h * 128 + R], lhsT=V[0:R, bi, h * 128:(h + 1) * 128], rhs=AT[0:R, h, 0:R],
                                                                       start=(h % 4 == 0), stop=False, skip_group_check=True), reads=[vb_[bi], atb], writes=[pob[h // 4]])
                if not samp:
                    for c in range(NCH):
                        nxt = 1 - cur
                        dps = []
                        for hg in range(2):
                            pt, pb = ps()
                            dps.append((pt, pb))
                            for hh in range(4):
                                h = hg * 4 + hh
                                S.op("pe", lambda e, pt=pt, hh=hh, h=h, c=c, bi=bi: e.matmul(pt[:, hh * 128:(hh + 1) * 128], lhsT=KDM[:, c, h * 128:(h + 1) * 128], rhs=V[:, bi, h * 128:(h + 1) * 128],
                                                                                         start=True, stop=True), reads=[kdmb[c], vb_[bi]], writes=[pb])
                        for h in range(8):
                            S.op("pe", lambda e, po=po, h=h, c=c, cur=cur: e.matmul(po[:, h * 128 + c * 16:h * 128 + (c + 1) * 16], lhsT=SB16[cur][:, h, :], rhs=QIb[:, h, c * 16:(c + 1) * 16],
                                                                                start=False, stop=(c == NCH - 1), skip_group_check=True), reads=[sbb[cur], qib], writes=[pob[h // 4]])
                        for hg in range(2):
                            pt, pb = dps[hg]
                            for hh in range(4):
                                h = hg * 4 + hh
                                S.op("dve", lambda e, pt=pt, hh=hh, h=h, c=c, cur=cur, nxt=nxt: e.scalar_tensor_tensor(out=S32[nxt][:, h, :], in0=S32[cur][:, h, :], scalar=DEC[:, h, c:c + 1],
                                                                                                                in1=pt[:, hh * 128:(hh + 1) * 128], op0=ALU.mult, op1=ALU.add),
                                     reads=[sb_[cur][h], decb, pb], writes=[sb_[nxt][h]])
                        S.op("act", lambda e, nxt=nxt: e.copy(out=SB16[nxt][:], in_=S32[nxt][:]), reads=sb_[nxt], writes=[sbb[nxt]])
                        cur = nxt
                else:
                    for b in range(NB):
                        q2 = b % 2
                        S.dma("sp", S0[q2][:], hgS_d[b], writes=[s0b[q2]])
                        for h in range(8):
                            S.op("pe", lambda e, po=po, h=h, b=b, q2=q2: e.matmul(po[:, h * 128 + b * 4:h * 128 + (b + 1) * 4], lhsT=S0[q2][:, h, :], rhs=E[:, h, b * 4:(b + 1) * 4],
                                                                              start=False, stop=(b == NB - 1), skip_group_check=True), reads=[s0b[q2], eb], writes=[pob[h // 4]])
                        SNb = S32[q2]
                        S.op("act", lambda e, b=b, cmm=cmm: e.activation(out=KDM[0:64, b % 8, :], in_=KD[0:64, :], func=AF.Copy, scale=cmm[0:64, b:b + 1]),
                             reads=[kdb, cb3], writes=[kdmb[b % 8]])
                        for hg in range(2):
                            pt, pb = ps()
                            for hh in range(4):
                                h = hg * 4 + hh
                                S.op("pe", lambda e, pt=pt, hh=hh, h=h, b=b: e.matmul(pt[:, hh * 128:(hh + 1) * 128], lhsT=KDM[0:64, b % 8, h * 128:(h + 1) * 128], rhs=V[0:64, 0, h * 128:(h + 1) * 128],
                                                                                  start=True, stop=True), reads=[kdmb[b % 8], vb_[0]], writes=[pb])
                            for hh in range(4):
                                h = hg * 4 + hh
                                S.op("dve", lambda e, pt=pt, hh=hh, h=h, b=b, q2=q2, SNb=SNb: e.scalar_tensor_tensor(out=SNb[:, h, :], in0=S0[q2][:, h, :], scalar=DEC[:, h, b:b + 1],
                                                                                                             in1=pt[:, hh * 128:(hh + 1) * 128], op0=ALU.mult, op1=ALU.add),
                                     reads=[s0b[q2], decb, pb], writes=[sb_[q2][h]])
                        S.dma("sp", hgSo_d[b], SNb[:], reads=sb_[q2], is_output=True)
                S.op("act", lambda e, po=po, R=R: e.activation(out=SQO[:, :].rearrange("p (a b) -> p a b", a=8)[:, :, 0:R], in_=po.rearrange("p (a b) -> p a b", a=8)[:, :, 0:R], func=AF.Square),
                     reads=pob, writes=[sqob])
                for hg in range(2):
                    pt, pb = ps()
                    S.op("pe", lambda e, pt=pt, hg=hg: e.matmul(pt[:, :], lhsT=ones_bf[:], rhs=SQO[:, hg * 512:(hg + 1) * 512], start=True, stop=True), reads=[sqob, cbuf], writes=[pb])
                    S.op("act", lambda e, pt=pt, hg=hg: e.activation(out=RSTD[:, hg * 512:(hg + 1) * 512], in_=pt[:, :], func=AF.Ln, bias=epsb[:], scale=1.0 / 128), reads=[pb, cbuf], writes=[rstdb])
                S.op("act", lambda e: e.activation(out=RSTD[:, :], in_=RSTD[:, :], func=AF.Exp, scale=-0.5), reads=[rstdb], writes=[rstdb])
                S.op("dve", lambda e, po=po, R=R: e.tensor_tensor(out=RSTD[:, :].rearrange("p (a b) -> p a b", a=8)[:, :, 0:R], in0=po.rearrange("p (a b) -> p a b", a=8)[:, :, 0:R],
                                                                 in1=RSTD[:, :].rearrange("p (a b) -> p a b", a=8)[:, :, 0:R], op=ALU.mult), reads=pob + [rstdb], writes=[rstdb])
                S.op("dve", lambda e, R=R, c0=c0: e.tensor_tensor(out=OT[:, :, c0:c0 + R], in0=RSTD[:, :].rearrange("p (a b) -> p a b", a=8)[:, :, 0:R], in1=GS[:, :, c0:c0 + R], op=ALU.mult),
                     reads=[rstdb, gsb], writes=[ob])
            for g in range(4):
                wv, wb_ = wget(o_idx[g], [128, KC, 256])
                for cc in range(2):
                    c = 2 * g + cc
                    pt, pb = ps()
                    for kc in range(KC):
                        S.op("pe", lambda e, pt=pt, wv=wv, cc=cc, kc=kc, tn=tn: e.matmul(pt[:, 0:tn], lhsT=wv[:, kc, cc * 128:(cc + 1) * 128], rhs=OT[:, kc, 0:tn],
                                                                                       start=(kc == 0), stop=(kc == KC - 1)), reads=[wb_, ob], writes=[pb])
                    S.op("dve", lambda e, pt=pt, c=c, t0=t0, tn=tn: e.tensor_tensor(out=X[:, c, t0:t0 + tn], in0=X[:, c, t0:t0 + tn], in1=pt[:, 0:tn], op=ALU.add),
                         reads=[pb, xbuf[ti]], writes=[xbuf[ti]])
            if t0 + tn == TP:
                S.dma("act", hgP_d, S32[cur][:], reads=sb_[cur], is_output=True)

    def _ssd_masks():
        r = np.arange(128)
        triu = (r[:, None] <= r[None, :]).astype(np.float32)
        tril2 = (r[:, None] > r[None, :]).astype(np.float32)
        negm = np.where(r[:, None] <= r[None, :], 0.0, NEG).astype(np.float32)
        return np.concatenate([triu, tril2, np.tile(negm, (1, 4)), np.ones((128, 128), np.float32)], axis=1)

    def _ssd_masks_s():
        r = np.arange(64)
        same = (r[:, None] // 4) == (r[None, :] // 4)
        triu = (same & (r[:, None] <= r[None, :])).astype(np.float32)
        tril2 = (same & (r[:, None] > r[None, :])).astype(np.float32)
        negm = np.where(same & (r[:, None] <= r[None, :]), 0.0, NEG).astype(np.float32)
        seq = ((r[:, None] // 4) == np.arange(16)[None, :]).astype(np.float32)
        return np.concatenate([triu, tril2, np.tile(negm, (1, 4)), seq], axis=1)

    def _sel8():
        s_ = np.zeros((8, 8, 128), np.float32)
        for h in range(8):
            s_[h, h, :] = 1.0
        return s_

    if NL > 2 and CFG["mixers"][2]:
        smask_d = P.din("smask", [128, 896], lambda I, c: _ssd_masks())
        smasks_d = P.din("smasks", [64, 400], lambda I, c: _ssd_masks_s())
        sel8_d = P.din("sel8", [8, 8, 128], lambda I, c: _sel8())
        scw_d = P.din("scw", [128, 24, 5], lambda I, c: np.concatenate([I["ssd_conv_w"][0], I["ssd_conv_b"]], axis=0).reshape(5, 24, 128).transpose(2, 1, 0))
        sfb_d = P.din("sfb", [128, 24, NB, 3], lambda I, c: I["state_ssm_conv"][0, c * NB:(c + 1) * NB].reshape(NB, 3, 24, 128).transpose(3, 2, 0, 1))
        svec_d = P.din("svec", [128, 96], lambda I, c: np.concatenate([
            np.broadcast_to(I["ssd_dt_bias"][0][None], (128, 32)), np.broadcast_to(I["ssd_a_log"][0][None], (128, 32)),
            np.repeat(I["ssd_d"][0].reshape(16, 2), 64, axis=1).T, I["ssd_norm_w"][0].reshape(16, 128).T], axis=1))
        ssmS_d = P.din("ssmS", [NB, 128, 32, 64], lambda I, c: I["state_ssm"][0, c * NB:(c + 1) * NB].transpose(0, 3, 1, 2))
        ssmP_d = P.dout("ssmP", [128, 32, 64])
        ssmSo_d = P.dout("ssmSo", [NB, 128, 32, 64])
        sconv_d = P.dout("sconvT", [128, 24, 3 + 3 * NB])

    def ssd(layer):
        Win = lambda I: I["ssd_w_in"][0]
        def wt(name, c0):
            return wtile((name, c0), 2048, (lambda I, c0=c0: Win(I)[:, c0:c0 + 256].reshape(KC, 128, 256).transpose(1, 0, 2).reshape(128, -1)))
        z_idx = [wt("sz", 256 * g) for g in range(8)]
        x_idx = [wt("sx", 2048 + 256 * g) for g in range(12)]
        dt_idx = wtile(("sdt",), 256, (lambda I: Win(I)[:, 5120:5152].reshape(KC, 128, 32).transpose(1, 0, 2).reshape(128, -1)))
        o_idx = [wtile(("so", c), 2048, (lambda I, c=c: I["ssd_w_o"][0][:, c * 128:(c + 1) * 128].reshape(16, 128, 128).transpose(1, 0, 2).reshape(128, -1))) for c in range(8)]
        ar = Bump(ARENA, ARENA_SZ)
        TN = 256
        tile_off = ar.take(32768)
        def tile_bufs(tn_alloc):
            b_ = Bump(tile_off, 32768)
            return (P.sb(b_.take(KC * tn_alloc * 2), [128, KC, tn_alloc], BF16), P.sb(b_.take(16 * tn_alloc * 2), [128, 16, tn_alloc], BF16),
                    P.sb(b_.take(16 * tn_alloc * 2), [128, 16, tn_alloc], BF16), P.sb(b_.take(8 * tn_alloc * 2), [128, 8, tn_alloc], BF16),
                    P.sb(b_.take(16 * tn_alloc * 2), [128, 16, tn_alloc], BF16), b_)
        RSn = ar.take(1024)
        AFB = [P.sb(ar.take(259 * 4), [128, 259], F32) for _ in range(2)]
        AFS = [P.sb(ar.take(NB * 7 * 4), [128, NB, 7], F32) for _ in range(2)]
        U = [P.sb(ar.take(1024), [128, 256], F32) for _ in range(2)]
        CAR = P.sb(ar.take(24 * 3 * 4), [128, 24, 3], F32)
        FB = P.sb(ar.take(24 * NB * 3 * 4), [128, 24, NB, 3], F32)
        SCO = P.sb(ar.take(24 * 51 * 4), [128, 24, 51], F32)
        XDT = P.sb(ar.take(4096), [128, 32, 64], BF16)
        XDEC = P.sb(ar.take(4096), [128, 32, 64], BF16)
        BT = P.sb(ar.take(1024), [128, 4, 128], BF16)
        BTM = [P.sb(ar.take(1024), [64, 4, 128], BF16) for _ in range(2)]
        DTT = P.sb(ar.take(6 * 128), [128, 6, 32], F32)
        aseq_off = ar.take(2048)
        ASEQ = P.sb(aseq_off, [64, 16, 32], F32)
        els_off = ar.take(2048)
        ELS = P.sb(els_off, [128, 16, 32], F32)
        ACGH = P.sb(aseq_off, [8, 4, 128], BF16)
        ACGL = P.sb(aseq_off + 1024, [8, 4, 128], BF16)
        SELB = P.sb(els_off, [8, 8, 128], BF16)
        ACG4 = P.sb(ar.take(2048), [8, 4, 128], F32)
        EBs = [P.sb(ar.take(2048), [128, 8, 128], BF16) for _ in range(2)]
        LTs = [P.sb(ar.take(2048), [128, 8, 128], BF16) for _ in range(2)]
        MTs = [P.sb(ar.take(2048), [128, 8, 128], BF16) for _ in range(2)]
        CDECs = [P.sb(ar.take(2048), [128, 8, 128], BF16) for _ in range(2)]
        YGs = [P.sb(ar.take(2048), [128, 4, 128], F32) for _ in range(2)]
        SQYs = [P.sb(ar.take(1024), [128, 4, 128], BF16) for _ in range(2)]
        RSTDs = [P.sb(ar.take(512), [128, 128], F32) for _ in range(2)]
        EB, LT, MT, CDEC, YG, SQY, RSTD = EBs[0], LTs[0], MTs[0], CDECs[0], YGs[0], SQYs[0], RSTDs[0]
        st_off = ar.take(8192)
        ST = P.sb(st_off, [128, 32, 64], F32)
        stb_off = ar.take(4096)
        STb = P.sb(stb_off, [128, 32, 64], BF16)
        CDECF = P.sb(st_off, [128, 32, 64], F32)
        EBF = P.sb(stb_off, [128, 8, 128], F32)
        CBS4 = P.sb(ar.take(2048), [128, 4, 128], F32)
        cbsb = Buf()
        SM = P.sb(ar.take(896 * 4), [128, 896], F32)
        SMS = P.sb(ar.take(400 * 4), [64, 400], F32)
        SEL = P.sb(ar.take(4096), [8, 8, 128], F32)
        CW = P.sb(ar.take(24 * 5 * 4), [128, 24, 5], F32)
        SV = P.sb(ar.take(96 * 4), [128, 96], F32)
        NEGA = P.sb(ar.take(128), [128, 32], F32)
        NEGMB = P.sb(ar.take(1024), [128, 512], BF16)
        cb4 = Buf()
        S.dma("act", SM[:], smask_d, writes=[cb4])
        S.dma("act", SMS[:], smasks_d, writes=[cb4])
        S.dma("act", SEL[:], sel8_d, writes=[cb4])
        S.dma("act", CW[:], scw_d, writes=[cb4])
        S.dma("act", SV[:], svec_d, writes=[cb4])
        S.dma("act", FB[:], sfb_d, writes=[cb4])
        S.op("act", lambda e: e.activation(out=NEGA[:], in_=SV[:, 32:64], func=AF.Exp), reads=[cb4], writes=[cb4])
        S.op("dve", lambda e: e.tensor_scalar(out=NEGA[:], in0=NEGA[:], scalar1=-1.0, scalar2=None, op0=ALU.mult), reads=[cb4], writes=[cb4])
        S.op("dve", lambda e: e.memset(CAR[:], 0.0), writes=[cb4])
        S.op("dve", lambda e: e.tensor_copy(out=NEGMB[:], in_=SM[:, 256:768]), reads=[cb4], writes=[cb4])
        S.op("dve", lambda e: e.tensor_copy(out=SELB[:], in_=SEL[:]), reads=[cb4], writes=[cb4])
        S.op("dve", lambda e: e.memset(ST[:], 0.0), writes=[cb4])
        S.op("dve", lambda e: e.memset(STb[:], 0.0), writes=[cb4])
        DTB, DSK, SNW = SV[:, 0:32], SV[:, 64:80], SV[:, 80:96]
        hb, zsb, xtb, bctb, ob = Buf(), Buf(), Buf(), Buf(), Buf()
        afb, afsb, ub = [Buf(), Buf()], [Buf(), Buf()], [Buf(), Buf()]
        carb, scob = Buf(), Buf()
        xdtb, xdecb, btb, dttb, acgb, ebb, ltb, mtb, cdb, ygb, sqyb, rsb2, stb_, stbb, tmpb = [Buf() for _ in range(15)]
        btmb = [Buf(), Buf()]
        aseqb, elsb = Buf(), Buf()
        def finish_group(g, py, pyb, c0, R, ZS, XT, OT):
            for cl in range(4):
                c = 4 * g + cl
                S.op("dve", lambda e, cl=cl, c=c: e.scalar_tensor_tensor(out=YG[:, cl, 0:R], in0=XT[:, c, c0:c0 + R], scalar=DSK[:, c:c + 1], in1=py[:, cl * 128:cl * 128 + R],
                                                                        op0=ALU.mult, op1=ALU.add), reads=[xtb, cb4, pyb], writes=[ygb])
            S.op("dve", lambda e: e.tensor_tensor(out=YG[:, :, 0:R], in0=YG[:, :, 0:R], in1=ZS[:, 4 * g:4 * g + 4, c0:c0 + R], op=ALU.mult), reads=[ygb, zsb], writes=[ygb])
            S.op("act", lambda e: e.activation(out=SQY[:, :, 0:R], in_=YG[:, :, 0:R], func=AF.Square), reads=[ygb], writes=[sqyb])
            pt, pb = ps()
            for cl in range(4):
                S.op("pe", lambda e, cl=cl, pt=pt: e.matmul(pt[:, 0:R], lhsT=ones_bf[:], rhs=SQY[:, cl, 0:R], start=(cl == 0), stop=(cl == 3)), reads=[sqyb, cbuf], writes=[pb])
            S.op("act", lambda e, pt=pt: e.activation(out=RSTD[:, 0:R], in_=pt[:, 0:R], func=AF.Ln, bias=epsb[:], scale=1.0 / 512), reads=[pb, cbuf], writes=[rsb2])
            S.op("act", lambda e: e.activation(out=RSTD[:, 0:R], in_=RSTD[:, 0:R], func=AF.Exp, scale=-0.5), reads=[rsb2], writes=[rsb2])
            for cl in range(4):
                c = 4 * g + cl
                S.op("dve", lambda e, cl=cl, c=c: e.scalar_tensor_tensor(out=OT[:, c, c0:c0 + R], in0=YG[:, cl, 0:R], scalar=SNW[:, c:c + 1], in1=RSTD[:, 0:R],
                                                                        op0=ALU.mult, op1=ALU.mult), reads=[ygb, cb4, rsb2], writes=[ob])

        slot_bufs = [[Buf() for _ in range(7)] for _ in range(2)]

        def group_gen(slot, g, c0, BCT, ZS, XT, OT):
            R = 128
            EBx, LTx, MTx, CDx, YGx, SQx, RSx = EBs[slot], LTs[slot], MTs[slot], CDECs[slot], YGs[slot], SQYs[slot], RSTDs[slot]
            ebx, ltx, mtx, cdx, ygx, sqx, rsx = slot_bufs[slot]
            pbcs = [psb(2 * slot), psb(2 * slot + 1)]
            for hq in range(2):
                pbc, pbcb = pbcs[hq]
                for hh in range(4):
                    S.op("pe", lambda e, pbc=pbc, hh=hh, hq=hq: e.matmul(pbc[:, hh * 128:hh * 128 + R], lhsT=SELB[:, hq * 4 + hh, :], rhs=ACGH[:, g, 0:R], start=(hh == 0), stop=False,
                                                                       skip_group_check=True), reads=[acgb, cb4], writes=[pbcb])
                    S.op("pe", lambda e, pbc=pbc, hh=hh, hq=hq: e.matmul(pbc[:, hh * 128:hh * 128 + R], lhsT=SELB[:, hq * 4 + hh, :], rhs=ACGL[:, g, 0:R], start=False, stop=False,
                                                                       skip_group_check=True), reads=[acgb, cb4], writes=[pbcb])
            yield
            for hq in range(2):
                pbc, pbcb = pbcs[hq]
                S.op("act", lambda e, pbc=pbc, hq=hq: e.activation(out=EBx[:, hq * 4:hq * 4 + 4, 0:R], in_=pbc[:, :].rearrange("p (a b) -> p a b", a=4)[:, :, 0:R], func=AF.Exp),
                     reads=[pbcb], writes=[ebx])
            yield
            for hq in range(2):
                pbc, pbcb = pbcs[hq]
                S.op("pe", lambda e, pbc=pbc: e.matmul(pbc[:, :], lhsT=ident_bf[:, :], rhs=NEGMB[:, :], start=False, stop=True, skip_group_check=True),
                     reads=[cb4, cbuf, ebx], writes=[pbcb])
            yield
            for hq in range(2):
                pbc, pbcb = pbcs[hq]
                for hh in range(4):
                    h = 8 * g + hq * 4 + hh
                    S.op("act", lambda e, pbc=pbc, hh=hh, hq=hq, h=h: e.activation(out=LTx[0:R, hq * 4 + hh, 0:R], in_=pbc[0:R, hh * 128:hh * 128 + R], func=AF.Exp,
                                                                                 bias=DTT[0:R, 2, h:h + 1], scale=1.0), reads=[pbcb, dttb], writes=[ltx])
            yield
            for hq in range(2):
                S.op("dve", lambda e, hq=hq: e.tensor_tensor(out=MTx[0:R, hq * 4:hq * 4 + 4, 0:R], in0=LTx[0:R, hq * 4:hq * 4 + 4, 0:R],
                                                            in1=CBS4[0:R, g, 0:R].unsqueeze(1).broadcast_to([R, 4, R]), op=ALU.mult), reads=[ltx, cbsb], writes=[mtx])
                S.op("dve", lambda e, hq=hq: e.tensor_tensor(out=CDx[:, hq * 4:hq * 4 + 4, :], in0=EBx[:, hq * 4:hq * 4 + 4, :],
                                                            in1=BCT[:, 4 + g, c0:c0 + 128].unsqueeze(1).broadcast_to([128, 4, 128]), op=ALU.mult), reads=[ebx, bctb], writes=[cdx])
            yield
            py, pyb = psb(4 + slot)
            for hl in range(8):
                h = 8 * g + hl
                half, cl = (hl % 2) * 64, hl // 2
                S.op("pe", lambda e, h=h, hl=hl, half=half, cl=cl: e.matmul(py[half:half + 64, cl * 128:cl * 128 + R], lhsT=XDT[0:R, h, :], rhs=MTx[0:R, hl, 0:R],
                                                                          start=(hl < 2), stop=False, tile_position=(0, half), skip_group_check=True), reads=[xdtb, mtx], writes=[pyb])
                S.op("pe", lambda e, h=h, hl=hl, half=half, cl=cl: e.matmul(py[half:half + 64, cl * 128:(cl + 1) * 128], lhsT=STb[:, h, :], rhs=CDx[:, hl, :],
                                                                          start=False, stop=True, tile_position=(0, half), skip_group_check=True), reads=[stbb, cdx], writes=[pyb])
            yield
            for cl in range(4):
                c = 4 * g + cl
                S.op("dve", lambda e, cl=cl, c=c: e.scalar_tensor_tensor(out=YGx[:, cl, 0:R], in0=XT[:, c, c0:c0 + R], scalar=DSK[:, c:c + 1], in1=py[:, cl * 128:cl * 128 + R],
                                                                        op0=ALU.mult, op1=ALU.add), reads=[xtb, cb4, pyb], writes=[ygx])
            S.op("dve", lambda e: e.tensor_tensor(out=YGx[:, :, 0:R], in0=YGx[:, :, 0:R], in1=ZS[:, 4 * g:4 * g + 4, c0:c0 + R], op=ALU.mult), reads=[ygx, zsb], writes=[ygx])
            S.op("act", lambda e: e.activation(out=SQx[:, :, 0:R], in_=YGx[:, :, 0:R], func=AF.Square), reads=[ygx], writes=[sqx])
            yield
            pt, pb = pbcs[0]
            for cl in range(4):
                S.op("pe", lambda e, cl=cl: e.matmul(pt[:, 0:R], lhsT=ones_bf[:], rhs=SQx[:, cl, 0:R], start=(cl == 0), stop=(cl == 3)), reads=[sqx, cbuf], writes=[pb])
            yield
            S.op("act", lambda e: e.activation(out=RSx[:, 0:R], in_=pt[:, 0:R], func=AF.Ln, bias=epsb[:], scale=1.0 / 512), reads=[pb, cbuf], writes=[rsx])
            S.op("act", lambda e: e.activation(out=RSx[:, 0:R], in_=RSx[:, 0:R], func=AF.Exp, scale=-0.5), reads=[rsx], writes=[rsx])
            yield
            for cl in range(4):
                c = 4 * g + cl
                S.op("dve", lambda e, cl=cl, c=c: e.scalar_tensor_tensor(out=OT[:, c, c0:c0 + R], in0=YGx[:, cl, 0:R], scalar=SNW[:, c:c + 1], in1=RSx[:, 0:R],
                                                                        op0=ALU.mult, op1=ALU.mult), reads=[ygx, cb4, rsx], writes=[ob])

        def run_groups(c0, BCT, ZS, XT, OT):
            pending = [0, 1, 2, 3]
            active = {}
            for slot in range(2):
                active[slot] = group_gen(slot, pending.pop(0), c0, BCT, ZS, XT, OT)
            while active:
                for slot in list(active.keys()):
                    try:
                        next(active[slot])
                    except StopIteration:
                        if pending:
                            active[slot] = group_gen(slot, pending.pop(0), c0, BCT, ZS, XT, OT)
                        else:
                            del active[slot]

        def out_proj(t0, tn, ti, OT):
            for c in range(KC):
                wv, wb_ = wget(o_idx[c], [128, 16, 128])
                pt, pb = ps()
                for kc in range(16):
                    S.op("pe", lambda e, pt=pt, wv=wv, kc=kc: e.matmul(pt[:, 0:tn], lhsT=wv[:, kc, :], rhs=OT[:, kc, 0:tn], start=(kc == 0), stop=(kc == 15)), reads=[wb_, ob], writes=[pb])
                S.op("dve", lambda e, pt=pt, c=c: e.tensor_tensor(out=X[:, c, t0:t0 + tn], in0=X[:, c, t0:t0 + tn], in1=pt[:, 0:tn], op=ALU.add), reads=[pb, xbuf[ti]], writes=[xbuf[ti]])

        uctr = 0
        gcnt = [0]
        for hi_, (t0, tn) in enumerate(HTILES):
            samp = (t0 == TP)
            ti = t0 // 512
            if samp:
                S.barrier()
                PSMODE[0] = 1
                ylist = []
            HT, ZS, XT, BCT, OT, tb_ = tile_bufs(64 if samp else TN)
            tna = 64 if samp else TN
            SQn = P.sb(tile_off + 48 * tna * 2, [128, KC, tna], BF16)
            RS = P.sb(RSn, [128, 256], F32)
            S.op("act", lambda e, t0=t0, tn=tn, SQn=SQn: e.activation(out=SQn[:, :, 0:tn], in_=X[:, :, t0:t0 + tn], func=AF.Square), reads=[xbuf[ti]], writes=[ob])
            pt, pb = ps()
            for c in range(KC):
                S.op("pe", lambda e, c=c, tn=tn, pt=pt, SQn=SQn: e.matmul(pt[:, 0:tn], lhsT=ones_bf[:], rhs=SQn[:, c, 0:tn], start=(c == 0), stop=(c == KC - 1)), reads=[ob, cbuf], writes=[pb])
            S.op("act", lambda e, tn=tn, pt=pt: e.activation(out=RS[:, 0:tn], in_=pt[:, 0:tn], func=AF.Ln, bias=epsb[:], scale=1.0 / D), reads=[pb, cbuf], writes=[rsb2])
            S.op("act", lambda e, tn=tn: e.activation(out=RS[:, 0:tn], in_=RS[:, 0:tn], func=AF.Exp, scale=-0.5), reads=[rsb2], writes=[rsb2])
            for c in range(KC):
                S.op("dve", lambda e, c=c, t0=t0, tn=tn, HT=HT: e.scalar_tensor_tensor(out=HT[:, c, 0:tn], in0=X[:, c, t0:t0 + tn], scalar=NORMW[:, layer, c:c + 1], in1=RS[:, 0:tn],
                                                                                       op0=ALU.mult, op1=ALU.mult), reads=[xbuf[ti], rsb2, cbuf], writes=[hb])
            for g in range(8):
                wv, wb_ = wget(z_idx[g], [128, KC, 256])
                for cc in range(2):
                    c = 2 * g + cc
                    pt, pb = ps()
                    for kc in range(KC):
                        S.op("pe", lambda e, pt=pt, wv=wv, cc=cc, kc=kc, tn=tn, HT=HT: e.matmul(pt[:, 0:tn], lhsT=wv[:, kc, cc * 128:(cc + 1) * 128], rhs=HT[:, kc, 0:tn],
                                                                                              start=(kc == 0), stop=(kc == KC - 1)), reads=[wb_, hb], writes=[pb])
                    S.op("act", lambda e, pt=pt, c=c, tn=tn, ZS=ZS: e.activation(out=ZS[:, c, 0:tn], in_=pt[:, 0:tn], func=AF.Silu), reads=[pb], writes=[zsb])
            swpc = SWP()
            swpg = SWP()
            for g in range(12):
                wv, wb_ = wget(x_idx[g], [128, KC, 256])
                for cc in range(2):
                    ch = 2 * g + cc
                    pt, pb = ps()
                    for kc in range(KC):
                        S.op("pe", lambda e, pt=pt, wv=wv, cc=cc, kc=kc, tn=tn, HT=HT: e.matmul(pt[:, 0:tn], lhsT=wv[:, kc, cc * 128:(cc + 1) * 128], rhs=HT[:, kc, 0:tn],
                                                                                              start=(kc == 0), stop=(kc == KC - 1)), reads=[wb_, hb], writes=[pb])
                    q = ch % 2
                    u, ubf = U[uctr % 2], ub[uctr % 2]; uctr += 1
                    if not samp:
                        A_, ab_ = AFB[q], afb[q]
                        S.op("act", lambda e, A_=A_, ch=ch: e.copy(out=A_[:, 0:3], in_=CAR[:, ch, :]), reads=[carb], writes=[ab_])
                        S.op("act", lambda e, A_=A_, pt=pt, tn=tn: e.activation(out=A_[:, 3:3 + tn], in_=pt[:, 0:tn], func=AF.Copy), reads=[pb], writes=[ab_])
                        srcs = [A_[:, k:k + tn] for k in range(4)]
                        uo = u[:, 0:tn]
                        S.op("act", lambda e, A_=A_, ch=ch, tn=tn: e.copy(out=CAR[:, ch, :], in_=A_[:, tn:tn + 3]), reads=[ab_], writes=[carb])
                        if t0 + tn == TP:
                            S.op("act", lambda e, A_=A_, ch=ch, tn=tn: e.copy(out=SCO[:, ch, 0:3], in_=A_[:, tn:tn + 3]), reads=[ab_], writes=[scob])
                        dst = (XT[:, ch, 0:tn] if ch < 16 else BCT[:, ch - 16, 0:tn])
                    else:
                        A_, ab_ = AFS[q], afsb[q]
                        S.op("act", lambda e, A_=A_, ch=ch: e.copy(out=A_[:, :, 0:3], in_=FB[:, ch, :, :]), reads=[cb4], writes=[ab_])
                        S.op("act", lambda e, A_=A_, pt=pt: e.activation(out=A_[:, :, 3:7], in_=pt[:, 0:TS].rearrange("p (b t) -> p b t", t=4), func=AF.Copy), reads=[pb], writes=[ab_])
                        srcs = [A_[:, :, k:k + 4] for k in range(4)]
                        uo = u[:, 0:TS].rearrange("p (b t) -> p b t", t=4)
                        S.op("act", lambda e, A_=A_, ch=ch: e.copy(out=SCO[:, ch, 3:51].rearrange("p (b j) -> p b j", j=3), in_=A_[:, :, 4:7]), reads=[ab_], writes=[scob])
                        dst = (XT[:, ch, 0:TS] if ch < 16 else BCT[:, ch - 16, 0:TS]).rearrange("p (b t) -> p b t", t=4)
                    S.op("dve", lambda e, uo=uo, s=srcs[3], ch=ch: e.tensor_scalar(out=uo, in0=s, scalar1=CW[:, ch, 3:4], scalar2=CW[:, ch, 4:5], op0=ALU.mult, op1=ALU.add),
                         reads=[ab_, cb4], writes=[ubf])
                    for k in (2, 1, 0):
                        S.op("dve", lambda e, uo=uo, s=srcs[k], ch=ch, k=k: e.scalar_tensor_tensor(out=uo, in0=s, scalar=CW[:, ch, k:k + 1], in1=uo, op0=ALU.mult, op1=ALU.add),
                             reads=[ab_, cb4, ubf], writes=[ubf])
                    def ctail(uo=uo, dst=dst, ubf=ubf, ch=ch):
                        S.op("act", lambda e: e.activation(out=dst, in_=uo, func=AF.Silu), reads=[ubf], writes=[xtb if ch < 16 else bctb])
                    swpc.step([lambda: None, ctail])
            swpc.flush()
            wvd, wbd = wget(dt_idx, [128, KC, 32])
            nblk = 1 if samp else 2
            for bi in range(nblk):
                c0 = bi * 128
                if samp:
                    R = 64
                    triu, tril2, negm4 = SMS[:, 0:64], SMS[:, 64:128], SMS[:, 128:384]
                else:
                    R = 128
                    triu, tril2, negm4 = SM[:, 0:128], SM[:, 128:256], SM[:, 256:768]
                onesf = SM[:, 768:896]
                pt, pb = ps()
                for kc in range(KC):
                    S.op("pe", lambda e, pt=pt, kc=kc, c0=c0, R=R, HT=HT, wvd=wvd: e.matmul(pt[0:R, 0:32], lhsT=HT[:, kc, c0:c0 + R], rhs=wvd[:, kc, :], start=(kc == 0), stop=(kc == KC - 1)),
                         reads=[wbd, hb], writes=[pb])
                S.op("dve", lambda e, pt=pt, R=R: e.tensor_tensor(out=DTT[0:R, 0, :], in0=pt[0:R, 0:32], in1=DTB[0:R, :], op=ALU.add), reads=[pb, cb4], writes=[dttb])
                S.op("act", lambda e, R=R: e.activation(out=DTT[0:R, 0, :], in_=DTT[0:R, 0, :], func=AF.Exp), reads=[dttb], writes=[dttb])
                S.op("act", lambda e, R=R: e.activation(out=DTT[0:R, 0, :], in_=DTT[0:R, 0, :], func=AF.Ln, bias=oneb[0:R, :], scale=1.0), reads=[dttb, cbuf], writes=[dttb])
                S.op("dve", lambda e, R=R: e.tensor_tensor(out=DTT[0:R, 1, :], in0=DTT[0:R, 0, :], in1=NEGA[0:R, :], op=ALU.mult), reads=[dttb, cb4], writes=[dttb])
                pt, pb = ps()
                S.op("pe", lambda e, pt=pt, R=R, triu=triu: e.matmul(pt[0:R, 0:32], lhsT=triu[0:R, :], rhs=DTT[0:R, 1, :], start=True, stop=True), reads=[dttb, cb4], writes=[pb])
                S.op("pe", lambda e, pt=pt, R=R, tril2=tril2: e.matmul(pt[0:R, 32:64], lhsT=tril2[0:R, :], rhs=DTT[0:R, 1, :], start=True, stop=True), reads=[dttb, cb4], writes=[pb])
                if not samp:
                    S.op("pe", lambda e, pt=pt, onesf=onesf: e.matmul(pt[:, 64:96], lhsT=onesf[:, :], rhs=DTT[:, 1, :], start=True, stop=True), reads=[dttb, cb4], writes=[pb])
                S.op("dve", lambda e, pt=pt, R=R: e.tensor_scalar(out=DTT[0:R, 2, :], in0=pt[0:R, 0:32], scalar1=-1.0, scalar2=None, op0=ALU.mult), reads=[pb], writes=[dttb])
                S.op("act", lambda e, pt=pt, R=R: e.activation(out=DTT[0:R, 3, :], in_=pt[0:R, 32:64], func=AF.Exp), reads=[pb], writes=[dttb])
                if not samp:
                    S.op("act", lambda e, pt=pt: e.activation(out=DTT[:, 4, :], in_=pt[:, 64:96], func=AF.Exp), reads=[pb], writes=[dttb])
                else:
                    S.op("dve", lambda e: e.tensor_tensor(out=ASEQ[:, :, :], in0=DTT[0:64, 1, :].unsqueeze(1).broadcast_to([64, 16, 32]),
                                                          in1=SMS[:, 384:400].unsqueeze(2).broadcast_to([64, 16, 32]), op=ALU.mult), reads=[dttb, cb4], writes=[aseqb])
                    pt2, pb2 = ps()
                    S.op("pe", lambda e, pt2=pt2, onesf=onesf: e.matmul(pt2[:, :], lhsT=onesf[0:64, :], rhs=ASEQ[:, :, :].rearrange("p a b -> p (a b)"), start=True, stop=True),
                         reads=[aseqb, cb4], writes=[pb2])
                    S.op("act", lambda e, pt2=pt2: e.activation(out=ELS[:, :, :].rearrange("p a b -> p (a b)"), in_=pt2[:, :], func=AF.Exp), reads=[pb2], writes=[elsb])
                for half in range(2):
                    ptp, ptpb = pst()
                    for cc in range(8):
                        c = half * 8 + cc
                        S.op("pe", lambda e, ptp=ptp, cc=cc, c=c, c0=c0, R=R, XT=XT: e.transpose(out=ptp[0:R, cc * 128:(cc + 1) * 128], in_=XT[:, c, c0:c0 + R], identity=ident_bf[:]),
                             reads=[xtb, cbuf], writes=[ptpb])
                    S.op("dve", lambda e, ptp=ptp, half=half, R=R: e.tensor_tensor(out=XDT[0:R, half * 16:(half + 1) * 16, :], in0=ptp[0:R, :].rearrange("p (a b) -> p a b", b=64),
                                                                                  in1=DTT[0:R, 0, half * 16:(half + 1) * 16].unsqueeze(2).broadcast_to([R, 16, 64]), op=ALU.mult),
                         reads=[ptpb, dttb], writes=[xdtb])
                S.op("dve", lambda e, R=R: e.tensor_tensor(out=XDEC[0:R, :, :], in0=XDT[0:R, :, :], in1=DTT[0:R, 3, :].unsqueeze(2).broadcast_to([R, 32, 64]), op=ALU.mult),
                     reads=[xdtb, dttb], writes=[xdecb])
                ptp, ptpb = pst()
                for g in range(4):
                    S.op("pe", lambda e, ptp=ptp, g=g, c0=c0, R=R, BCT=BCT: e.transpose(out=ptp[0:R, g * 128:(g + 1) * 128], in_=BCT[:, g, c0:c0 + R], identity=ident_bf[:]),
                         reads=[bctb, cbuf], writes=[ptpb])
                S.op("act", lambda e, ptp=ptp, R=R: e.activation(out=BT[0:R, :, :].rearrange("p a b -> p (a b)"), in_=ptp[0:R, 0:512], func=AF.Copy), reads=[ptpb], writes=[btb])
                pt, pb = ps()
                for g in range(4):
                    S.op("pe", lambda e, pt=pt, g=g, R=R, triu=triu: e.matmul(pt[0:8, g * 128:g * 128 + R], lhsT=DTT[0:R, 1, 8 * g:8 * g + 8], rhs=triu[0:R, :], start=True, stop=True),
                         reads=[dttb, cb4], writes=[pb])
                S.op("act", lambda e, pt=pt, R=R: e.activation(out=ACG4[:, :, 0:R], in_=pt[0:8, :].rearrange("p (a b) -> p a b", a=4)[:, :, 0:R], func=AF.Copy), reads=[pb], writes=[acgb])
                if not samp:
                    S.op("dve", lambda e: e.tensor_copy(out=ACGH[:], in_=ACG4[:]), reads=[acgb], writes=[acgb])
                    S.op("dve", lambda e: e.tensor_tensor(out=ACGL[:], in0=ACG4[:], in1=ACGH[:], op=ALU.subtract), reads=[acgb], writes=[acgb])
                pcb, pcbb = ps()
                for g in range(4):
                    S.op("pe", lambda e, pcb=pcb, g=g, c0=c0, R=R, BCT=BCT: e.matmul(pcb[0:R, g * 128:g * 128 + R], lhsT=BCT[:, g, c0:c0 + R], rhs=BCT[:, 4 + g, c0:c0 + R], start=True, stop=True),
                         reads=[bctb], writes=[pcbb])
                S.op("act", lambda e, pcb=pcb, R=R: e.activation(out=CBS4[0:R, :, 0:R], in_=pcb[0:R, :].rearrange("p (a b) -> p a b", a=4)[:, :, 0:R], func=AF.Copy), reads=[pcbb], writes=[cbsb])
                if not samp:
                    run_groups(c0, BCT, ZS, XT, OT)
                for g in (range(4) if samp else []):
                    EBx, ebx = (EBF, stbb) if samp else (EB, ebb)
                    pbcs = [ps(), ps()]
                    for hq in range(2):
                        pbc, pbcb = pbcs[hq]
                        for hh in range(4):
                            S.op("pe", lambda e, pbc=pbc, hh=hh, hq=hq, R=R, g=g: e.matmul(pbc[:, hh * 128:hh * 128 + R], lhsT=SEL[:, hq * 4 + hh, :], rhs=ACG4[:, g, 0:R], start=(hh == 0), stop=False,
                                                                                        skip_group_check=True), reads=[acgb, cb4], writes=[pbcb])
                    for hq in range(2):
                        pbc, pbcb = pbcs[hq]
                        S.op("act", lambda e, pbc=pbc, hq=hq, R=R, EBx=EBx: e.activation(out=EBx[:, hq * 4:hq * 4 + 4, 0:R], in_=pbc[:, :].rearrange("p (a b) -> p a b", a=4)[:, :, 0:R], func=AF.Exp),
                             reads=[pbcb], writes=[ebx])
                    for hq in range(2):
                        pbc, pbcb = pbcs[hq]
                        if samp:
                            for hh in range(4):
                                S.op("pe", lambda e, pbc=pbc, negm4=negm4, hh=hh: e.matmul(pbc[0:64, hh * 128:hh * 128 + 64], lhsT=ident_f[0:64, 0:64], rhs=negm4[0:64, 0:64],
                                                                                        start=False, stop=True, skip_group_check=True), reads=[cb4, cbuf, stbb], writes=[pbcb])
                        else:
                            S.op("pe", lambda e, pbc=pbc: e.matmul(pbc[:, :], lhsT=ident_bf[:, :], rhs=NEGMB[:, :], start=False, stop=True, skip_group_check=True),
                                 reads=[cb4, cbuf, ebb], writes=[pbcb])
                    for hq in range(2):
                        pbc, pbcb = pbcs[hq]
                        for hh in range(4):
                            h = 8 * g + hq * 4 + hh
                            S.op("act", lambda e, pbc=pbc, hh=hh, hq=hq, h=h, R=R: e.activation(out=LT[0:R, hq * 4 + hh, 0:R], in_=pbc[0:R, hh * 128:hh * 128 + R], func=AF.Exp,
                                                                                              bias=DTT[0:R, 2, h:h + 1], scale=1.0), reads=[pbcb, dttb], writes=[ltb])
                    for hq in range(2):
                        S.op("dve", lambda e, hq=hq, R=R, g=g: e.tensor_tensor(out=MT[0:R, hq * 4:hq * 4 + 4, 0:R], in0=LT[0:R, hq * 4:hq * 4 + 4, 0:R],
                                                                              in1=CBS4[0:R, g, 0:R].unsqueeze(1).broadcast_to([R, 4, R]), op=ALU.mult), reads=[ltb, cbsb], writes=[mtb])
                        if samp:
                            S.op("dve", lambda e, hq=hq, g=g, c0=c0, BCT=BCT: e.tensor_tensor(out=CDECF[:, 8 * g + hq * 4:8 * g + hq * 4 + 4, 0:64], in0=EBF[:, hq * 4:hq * 4 + 4, 0:64],
                                                                                            in1=BCT[:, 4 + g, 0:64].unsqueeze(1).broadcast_to([128, 4, 64]), op=ALU.mult), reads=[stbb, bctb], writes=[stb_])
                        else:
                            S.op("dve", lambda e, hq=hq, g=g, c0=c0, BCT=BCT: e.tensor_tensor(out=CDEC[:, hq * 4:hq * 4 + 4, :], in0=EB[:, hq * 4:hq * 4 + 4, :],
                                                                                            in1=BCT[:, 4 + g, c0:c0 + 128].unsqueeze(1).broadcast_to([128, 4, 128]), op=ALU.mult), reads=[ebb, bctb], writes=[cdb])
                    py, pyb = psb([4, 5, 0, 1][g]) if samp else psb(4 + gcnt[0] % 2)
                    gcnt[0] += 1
                    for hl in range(8):
                        h = 8 * g + hl
                        half, cl = (hl % 2) * 64, hl // 2
                        S.op("pe", lambda e, py=py, h=h, hl=hl, half=half, cl=cl, R=R: e.matmul(py[half:half + 64, cl * 128:cl * 128 + R], lhsT=XDT[0:R, h, :], rhs=MT[0:R, hl, 0:R],
                                                                                              start=(hl < 2), stop=False, tile_position=(0, half), skip_group_check=True), reads=[xdtb, mtb], writes=[pyb])
                        if not samp:
                            S.op("pe", lambda e, py=py, h=h, hl=hl, half=half, cl=cl: e.matmul(py[half:half + 64, cl * 128:(cl + 1) * 128], lhsT=STb[:, h, :], rhs=CDEC[:, hl, :],
                                                                                             start=False, stop=True, tile_position=(0, half), skip_group_check=True), reads=[stbb, cdb], writes=[pyb])
                    if samp:
                        ylist.append((g, py, pyb))
                        continue
                    swpg.step([lambda: None, (lambda g=g, py=py, pyb=pyb, c0=c0, R=R, ZS=ZS, XT=XT, OT=OT: finish_group(g, py, pyb, c0, R, ZS, XT, OT))])
                if samp:
                    S0 = [P.sb(tb_.take(8192), [128, 32, 64], F32) for _ in range(2)]
                    SN = P.sb(tb_.take(8192), [128, 32, 64], F32)
                    s0b = [Buf(), Buf()]
                    snb = Buf()
                    for b in range(NB):
                        q2 = b % 2
                        S.dma("sp", S0[q2][:], ssmS_d[b], writes=[s0b[q2]])
                        for (g, py, pyb) in ylist:
                            for hl in range(8):
                                h = 8 * g + hl
                                half, cl = (hl % 2) * 64, hl // 2
                                S.op("pe", lambda e, py=py, h=h, half=half, cl=cl, b=b, q2=q2: e.matmul(py[half:half + 64, cl * 128 + 4 * b:cl * 128 + 4 * b + 4], lhsT=S0[q2][:, h, :], rhs=CDECF[:, h, 4 * b:4 * b + 4],
                                                                                                    start=False, stop=(b == NB - 1), tile_position=(0, half), skip_group_check=True),
                                     reads=[s0b[q2], stb_], writes=[pyb])
                        S.op("dve", lambda e, b=b, q2=q2: e.tensor_tensor(out=SN[:, :, :], in0=S0[q2][:, :, :], in1=ELS[:, b, :].unsqueeze(2).broadcast_to([128, 32, 64]), op=ALU.mult),
                             reads=[s0b[q2], elsb], writes=[snb])
                        S.op("act", lambda e, b=b, q2=q2: e.activation(out=BTM[q2][:, :, :].rearrange("p a b -> p (a b)"), in_=BT[0:64, :, :].rearrange("p a b -> p (a b)"),
                                                                        func=AF.Copy, scale=SMS[:, 384 + b:385 + b]), reads=[btb, cb4], writes=[btmb[q2]])
                        for g in range(4):
                            pt, pb = ps()
                            S.op("pe", lambda e, pt=pt, g=g, q2=q2: e.matmul(pt[:, :], lhsT=BTM[q2][:, g, :], rhs=XDEC[0:64, 8 * g:8 * g + 8, :].rearrange("p a b -> p (a b)"), start=True, stop=True),
                                 reads=[btmb[q2], xdecb], writes=[pb])
                            S.op("dve", lambda e, pt=pt, g=g: e.tensor_tensor(out=SN[:, 8 * g:8 * g + 8, :].rearrange("p a b -> p (a b)"), in0=SN[:, 8 * g:8 * g + 8, :].rearrange("p a b -> p (a b)"),
                                                                             in1=pt[:, :], op=ALU.add), reads=[snb, pb], writes=[snb])
                        S.dma("sp", ssmSo_d[b], SN[:], reads=[snb], is_output=True)
                    for (g, py, pyb) in ylist:
                        finish_group(g, py, pyb, 0, 64, ZS, XT, OT)
                    continue
                S.op("dve", lambda e: e.tensor_tensor(out=ST[:, :, :], in0=ST[:, :, :], in1=DTT[:, 4, :].unsqueeze(2).broadcast_to([128, 32, 64]), op=ALU.mult), reads=[stb_, dttb], writes=[stb_])
                for g in range(4):
                    pt, pb = ps()
                    S.op("pe", lambda e, pt=pt, g=g: e.matmul(pt[:, :], lhsT=BT[:, g, :], rhs=XDEC[:, 8 * g:8 * g + 8, :].rearrange("p a b -> p (a b)"), start=True, stop=True),
                         reads=[btb, xdecb], writes=[pb])
                    S.op("dve", lambda e, pt=pt, g=g: e.tensor_tensor(out=ST[:, 8 * g:8 * g + 8, :].rearrange("p a b -> p (a b)"), in0=ST[:, 8 * g:8 * g + 8, :].rearrange("p a b -> p (a b)"),
                                                                     in1=pt[:, :], op=ALU.add), reads=[stb_, pb], writes=[stb_])
                S.op("act", lambda e: e.copy(out=STb[:, :, :], in_=ST[:, :, :]), reads=[stb_], writes=[stbb])
            swpg.flush()
            out_proj(t0, tn, ti, OT)
            if t0 + tn == TP:
                S.dma("act", ssmP_d, ST[:], reads=[stb_], is_output=True)
            if samp:
                S.dma("act", sconv_d, SCO[:], reads=[scob], is_output=True)
                PSMODE[0] = 0
    for layer in range(NL):
        S.barrier()
        if CFG["mixers"][layer]:
            if layer % 3 == 0:
                attention(layer)
            elif layer % 3 == 1:
                hgrn(layer)
            else:
                ssd(layer)
        S.barrier()
        ffn(layer)
    S.barrier()
    ar = Bump(ARENA, ARENA_SZ)
    YF = [P.sb(ar.take(KC * 512 * 4), [128, KC, 512], F32) for i in range(2)]
    a = ar.take(KC * 512 * 2 + 2048)
    yb = [Buf(), Buf()]
    yv = yT_d.rearrange("(c p) t -> p c t", p=128)
    for ti, (t0, tn) in enumerate(TILES):
        q = ti % 2
        hbufs = {ti: yb[q]}
        rmsnorm(2 * DEPTH, [ti], YF[q], hbufs, t0, a)
        S.dma("act", yv[:, :, t0:t0 + tn], YF[q][:, :, 0:tn], reads=[yb[q]], is_output=True)

    WSTREAM[0] = nc.dram_tensor("wstream", [128, max(P.woff, 1)], F32, kind="ExternalInput").ap()
    P.wlist = wlist
    S.emit()
    return P


_PROG = None


def _prog():
    global _PROG
    if _PROG is None:
        _PROG = build()
    return _PROG


def kernel(**inputs):
    I = {k: np.asarray(v) for k, v in inputs.items()}
    P = _prog()
    wstream = np.zeros((128, max(P.woff, 1)), np.float32)
    for (o, n, builder) in P.wlist:
        wstream[:, o:o + n] = builder(I)
    in_maps = []
    for c in range(NCORES):
        m = {"wstream": wstream}
        for name, (shape, builder) in P.inputs.items():
            m[name] = np.ascontiguousarray(builder(I, c), dtype=np.float32).reshape(shape)
        in_maps.append(m)
    res = run_bass_kernel_spmd(P.nc, in_maps, core_ids=list(range(NCORES)))
    R = res.results
    out = assemble(R)
    order = ["y_prompt", "y_sample", "k_p", "v_p", "hg_p", "ssm_p", "sconv_p", "fconv_p", "k_s", "v_s", "hg_s", "ssm_s", "sconv_s", "fconv_s"]
    if all(k in out for k in order):
        return tuple(np.ascontiguousarray(out[k], dtype=np.float32) for k in order)
    return out


def assemble(R):
    yT = np.stack([R[c]["yT"] for c in range(NCORES)])
    y_prompt = np.ascontiguousarray(yT[:, :, :TP].transpose(0, 2, 1))
    y_sample = np.ascontiguousarray(yT[:, :, TP:].transpose(0, 2, 1)).reshape(NCORES * NB, 4, D)
    fc = np.stack([R[c]["fconvT"] for c in range(NCORES)])
    fc = fc.transpose(1, 0, 4, 3, 2).reshape(DEPTH, NCORES, 2 + 2 * NB, DFF)
    fconv_p = np.ascontiguousarray(fc[:, :, 0:2])
    fconv_s = np.ascontiguousarray(fc[:, :, 2:].reshape(DEPTH, NCORES * NB, 2, DFF))
    out = {"y_prompt": y_prompt, "y_sample": y_sample, "fconv_p": fconv_p, "fconv_s": fconv_s}
    if "kTp" in R[0]:
        kTp = np.stack([R[c]["kTp"] for c in range(NCORES)])
        out["k_p"] = np.ascontiguousarray(kTp.transpose(1, 0, 4, 2, 3)).reshape(2, NCORES, 128, 4, 64)
        vp = np.stack([R[c]["vp"] for c in range(NCORES)])
        out["v_p"] = np.ascontiguousarray(vp.transpose(1, 0, 2, 3)).reshape(2, NCORES, 128, 4, 64)
        ks = np.stack([R[c]["ks"] for c in range(NCORES)])
        out["k_s"] = np.ascontiguousarray(ks.transpose(1, 0, 2, 3, 4)).reshape(2, NCORES * NB, 128, 4, 64)
        vs = np.stack([R[c]["vs"] for c in range(NCORES)])
        out["v_s"] = np.ascontiguousarray(vs.transpose(1, 0, 2, 3, 4)).reshape(2, NCORES * NB, 128, 4, 64)
    if "ssmP" in R[0]:
        sp = np.stack([R[c]["ssmP"] for c in range(NCORES)])
        out["ssm_p"] = np.ascontiguousarray(sp.transpose(0, 2, 3, 1))[None]
        ss_ = np.stack([R[c]["ssmSo"] for c in range(NCORES)])
        out["ssm_s"] = np.ascontiguousarray(ss_.transpose(0, 1, 3, 4, 2)).reshape(1, NCORES * NB, 32, 64, 128)
        sc = np.stack([R[c]["sconvT"] for c in range(NCORES)])
        sc = sc.transpose(0, 3, 2, 1).reshape(NCORES, 51, 3072)
        out["sconv_p"] = np.ascontiguousarray(sc[:, 0:3])[None]
        out["sconv_s"] = np.ascontiguousarray(sc[:, 3:].reshape(NCORES * NB, 3, 3072))[None]
    if "hgP" in R[0]:
        hgP = np.stack([R[c]["hgP"] for c in range(NCORES)])
        out["hg_p"] = np.ascontiguousarray(hgP.transpose(0, 2, 1, 3))[None]
        hgS = np.stack([R[c]["hgSo"] for c in range(NCORES)])
        out["hg_s"] = np.ascontiguousarray(hgS.transpose(0, 1, 3, 2, 4)).reshape(1, NCORES * NB, 8, 128, 128)
    return out
```

```python
import contextlib
import os
import numpy as np
import concourse.bass as bass
import concourse.mybir as mybir
from concourse.bass_utils import run_bass_kernel_spmd

F32 = mybir.dt.float32
BF16 = mybir.dt.bfloat16
AF = mybir.ActivationFunctionType
ALU = mybir.AluOpType
AX = mybir.AxisListType

EPOCH = 12000
NCORES = 8
D = 1024
KC = 8
TP = 2048
TS = 64
T = TP + TS
NB = 16
DFF = 2816
NF = 22
DEPTH = 4
EPS = 1e-6
TILES = [(0, 512), (512, 512), (1024, 512), (1536, 512), (2048, 64)]
GROUPS = [[0, 1], [2, 3, 4]]

CFG = {"layers": DEPTH, "mixers": (True, True, True, True), "ap": True, "as": True, "dd": True, "so": True}


class Buf:
    __slots__ = ("name", "last_w", "readers", "excl")

    def __init__(self, name="", excl=False):
        self.name = name
        self.excl = excl
        self.last_w = None
        self.readers = []


class Op:
    __slots__ = ("eng", "pos", "fn", "waits", "is_dma", "signal", "signum", "dma_sem", "dma_val", "dma_prev")

    def __init__(self, eng, pos, fn, is_dma):
        self.eng = eng
        self.pos = pos
        self.fn = fn
        self.waits = []
        self.is_dma = is_dma
        self.signal = False
        self.signum = None
        self.dma_sem = None
        self.dma_val = None
        self.dma_prev = None


class Sched:
    ENGS = ("pe", "act", "dve", "pool", "sp")

    def __init__(self, nc):
        self.nc = nc
        self.ops = {e: [] for e in self.ENGS}
        self.waited = {e: {p: -1 for p in self.ENGS} for e in self.ENGS}
        self.dma_waited = {e: set() for e in self.ENGS}
        self.dma_pool = {"sp": 16, "act": 20, "pool": 16}
        self.dma_rr = {e: 0 for e in self.ENGS}
        self.dma_last = {}
        self.out_dmas = []
        self.out_seen = 0
        self.anchor_fn = None

    def _add_wait(self, op, dep):
        if dep is None or dep is op:
            return
        e = op.eng
        if dep.is_dma:
            if id(dep) in self.dma_waited[e]:
                return
            self.dma_waited[e].add(id(dep))
            op.waits.append(dep)
            return
        if self.waited[e][dep.eng] >= dep.pos:
            return
        self.waited[e][dep.eng] = dep.pos
        dep.signal = True
        op.waits.append(dep)

    def _issue(self, eng, fn, reads, writes, is_dma, deps=()):
        op = Op(eng, len(self.ops[eng]), fn, is_dma)
        for b in reads:
            if b.last_w is not None:
                self._add_wait(op, b.last_w)
            if b.excl:
                lastr = {}
                for r in b.readers:
                    if r.eng != eng and (r.eng not in lastr or lastr[r.eng].pos < r.pos):
                        lastr[r.eng] = r
                for r in lastr.values():
                    self._add_wait(op, r)
        for b in writes:
            w = b.last_w
            if w is not None and (is_dma or w.is_dma or w.eng != eng):
                self._add_wait(op, w)
            lastr = {}
            for r in b.readers:
                if r.is_dma:
                    self._add_wait(op, r)
                elif is_dma or r.eng != eng:
                    if r.eng not in lastr or lastr[r.eng].pos < r.pos:
                        lastr[r.eng] = r
            for r in lastr.values():
                self._add_wait(op, r)
        for d in deps:
            self._add_wait(op, d)
        for b in reads:
            b.readers.append(op)
        for b in writes:
            b.last_w = op
            b.readers = []
        self.ops[eng].append(op)
        return op

    def op(self, eng, fn, reads=(), writes=(), deps=()):
        return self._issue(eng, fn, reads, writes, False, deps)

    def dma(self, queue, out, in_, reads=(), writes=(), is_output=False, deps=(), **kw):
        def fn(e, out=out, in_=in_, kw=kw):
            return e.dma_start(out=out, in_=(in_() if callable(in_) else in_), **kw)
        op = self._issue(queue, fn, reads, writes, True, deps)
        n = self.dma_pool[queue]
        idx = self.dma_rr[queue] % n
        self.dma_rr[queue] += 1
        key = (queue, idx)
        prev = self.dma_last.get(key)
        op.dma_sem = key
        op.dma_prev = prev
        op.dma_val = (prev.dma_val if prev is not None else 0) + 16
        self.dma_last[key] = op
        if is_output:
            self.out_dmas.append(op)
        return op

    def barrier(self, engs=("pe", "act", "dve", "pool", "sp")):
        lasts = []
        for e in engs:
            for o in reversed(self.ops[e]):
                if not o.is_dma and o.fn is not None:
                    lasts.append(o)
                    break
        anchor = Op("dve", len(self.ops["dve"]), self.anchor_fn, False)
        for l in lasts:
            if l.eng != "dve":
                self._add_wait(anchor, l)
        for o in self.out_dmas[self.out_seen:]:
            self._add_wait(anchor, o)
        self.out_seen = len(self.out_dmas)
        self.ops["dve"].append(anchor)
        for e in engs:
            if e == "dve":
                continue
            op = Op(e, len(self.ops[e]), None, False)
            self._add_wait(op, anchor)
            self.ops[e].append(op)

    def emit(self):
        nc = self.nc
        with contextlib.ExitStack() as st:
            eng_sems = {}
            for e in self.ENGS:
                n = 0
                for o in self.ops[e]:
                    if o.signal and not o.is_dma:
                        n += 1
                        o.signum = n
                nep = max((n + EPOCH - 1) // EPOCH, 1)
                eng_sems[e] = [st.enter_context(nc.semaphore(f"s_{e}_{i}")) for i in range(nep)]
            dma_sems = {}
            for q, n in self.dma_pool.items():
                for i in range(n):
                    if (q, i) in self.dma_last:
                        dma_sems[(q, i)] = st.enter_context(nc.semaphore(f"d_{q}_{i}"))

            def sem_of(o):
                ep = (o.signum - 1) // EPOCH
                return eng_sems[o.eng][ep], o.signum - ep * EPOCH

            block = st.enter_context(nc.Block())

            def make(e):
                def body(eng):
                    for o in self.ops[e]:
                        for d in o.waits:
                            if d.is_dma:
                                eng.wait_ge(dma_sems[d.dma_sem], d.dma_val)
                            else:
                                s, v = sem_of(d)
                                eng.wait_ge(s, v)
                        if o.is_dma:
                            if o.dma_prev is not None:
                                eng.wait_ge(dma_sems[o.dma_sem], o.dma_prev.dma_val)
                            o.fn(eng).then_inc(dma_sems[o.dma_sem], 16)
                        elif o.fn is not None:
                            ins = o.fn(eng)
                            if o.signal:
                                ins.then_inc(sem_of(o)[0], 1)
                    for o in self.out_dmas:
                        if o.eng == e:
                            eng.wait_ge(dma_sems[o.dma_sem], self.dma_last[o.dma_sem].dma_val)
                return body

            block.tensor(make("pe"))
            block.scalar(make("act"))
            block.vector(make("dve"))
            block.gpsimd(make("pool"))
            block.sync(make("sp"))


class Bump:
    def __init__(self, start, size):
        self.start, self.size, self.cur = start, size, start

    def take(self, nbytes):
        o = self.cur
        self.cur += (nbytes + 63) // 64 * 64
        assert self.cur - self.start <= self.size, ("arena overflow", self.cur - self.start, self.size)
        return o


class SWP:
    def __init__(self):
        self.q = []

    def step(self, stages):
        self.q.append(stages)
        n = len(self.q)
        for k in range(8):
            it = n - 1 - k
            if it >= 0 and k < len(self.q[it]):
                self.q[it][k]()

    def flush(self):
        n = len(self.q)
        more = True
        d = 1
        while more:
            more = False
            for k in range(d, 8):
                it = n - 1 - (k - d)
                if it >= 0 and k < len(self.q[it]):
                    self.q[it][k]()
                    more = True
            d += 1
        self.q = []


class Prog:
    def __init__(self):
        self.nc = bass.Bass("TRN2", target_bir_lowering=False)
        self.S = Sched(self.nc)
        self.wspecs = []
        self.wcache = {}
        self.woff = 0
        self.inputs = {}
        self.outputs = {}
        self._n = 0
        nc = self.nc
        self.base = (nc.sbuf_base + 63) // 64 * 64
        self.top = nc.sbuf_top

    def uid(self, p):
        self._n += 1
        return f"{p}{self._n}"

    def sb(self, off, shape, dt):
        return self.nc.alloc_sbuf_tensor_at(self.uid("t"), list(shape), dt, offset=off)

    def din(self, name, shape, builder):
        ap = self.nc.dram_tensor(name, list(shape), F32, kind="ExternalInput").ap()
        self.inputs[name] = (tuple(shape), builder)
        return ap

    def dout(self, name, shape):
        ap = self.nc.dram_tensor(name, list(shape), F32, kind="ExternalOutput").ap()
        self.outputs[name] = tuple(shape)
        return ap


def build():
    P = Prog()
    nc, S = P.nc, P.S
    NL = CFG["layers"]

    off = P.base
    X = P.sb(off, [128, KC, T], F32); off += KC * T * 4
    off = (off + 63) // 64 * 64
    CONST = off; off += 3072
    WB_SLOT, NWB = 4096, 6
    WB0 = off; off += WB_SLOT * NWB
    ARENA = off
    ARENA_SZ = P.top - ARENA
    assert ARENA_SZ > 96000, ARENA_SZ

    xbuf = [Buf(f"x{i}") for i in range(len(TILES))]

    coff = CONST
    ones_bf = P.sb(coff, [128, 128], BF16); coff += 256
    ident_bf = P.sb(coff, [128, 128], BF16); coff += 256
    ident_f = P.sb(coff, [128, 128], F32); coff += 512
    NORMW = P.sb(coff, [128, 9, KC], F32); coff += 9 * KC * 4
    FCW = P.sb(coff, [128, DEPTH, 4, NF], F32); coff += DEPTH * 4 * NF * 4
    epsb = P.sb(coff, [128, 1], F32); coff += 32
    oneb = P.sb(coff, [128, 1], F32); coff += 32
    dummy = P.sb(coff, [128, 1], F32); coff += 32
    S.anchor_fn = lambda e: e.memset(dummy[:], 0.0)
    cbuf = Buf("const")

    xT_d = P.din("xT", [D, T], lambda I, c: np.concatenate(
        [I["x_prompt"][c].T, I["x_sample"][c * NB:(c + 1) * NB].reshape(TS, D).T], axis=1))
    normw_d = P.din("normw", [128, 9, KC], lambda I, c: np.concatenate(
        [I["norm_mix_w"], I["norm_ffn_w"], I["norm_final_w"][None]], axis=0).reshape(9, KC, 128).transpose(2, 0, 1))
    fcw_d = P.din("fcw", [128, DEPTH, 4, NF], lambda I, c: np.concatenate(
        [I["ffn_conv_w"], I["ffn_conv_b"][:, None]], axis=1).reshape(DEPTH, 4, NF, 128).transpose(3, 0, 1, 2))
    ident_d = P.din("ident", [128, 128], lambda I, c: np.eye(128, dtype=np.float32))
    fbuf_d = P.din("fbuf", [DEPTH, 128, NF, NB, 2], lambda I, c: I["state_ffn_conv"][:, c * NB:(c + 1) * NB].reshape(
        DEPTH, NB, 2, NF, 128).transpose(0, 4, 3, 1, 2))
    yT_d = P.dout("yT", [D, T])
    fconv_d = P.dout("fconvT", [DEPTH, 128, NF, 2 + 2 * NB])

    wlist = []

    def wtile(key, n, builder):
        if key in P.wcache:
            return P.wcache[key]
        idx = len(wlist)
        wlist.append((P.woff, n, builder))
        P.woff += n
        P.wcache[key] = idx
        return idx

    wbbuf = [Buf(f"wb{i}") for i in range(NWB)]
    wctr = [0]
    WSTREAM = [None]

    def wget(idx, shape):
        o, n, _ = wlist[idx]
        i = wctr[0]; wctr[0] += 1
        t = i % NWB
        bf_t = P.sb(WB0 + t * WB_SLOT, [128, n], BF16)
        S.dma("pool", bf_t[:], (lambda o=o, n=n: WSTREAM[0][:, o:o + n]), writes=[wbbuf[t]])
        view = P.sb(WB0 + t * WB_SLOT, list(shape), BF16)
        return view, wbbuf[t]

    PSA = nc.alloc_psum_tensor("psa", [128, 6 * 512], F32)
    PST = nc.alloc_psum_tensor("pst", [128, 2 * 1024], BF16)
    psbuf = [Buf(f"ps{i}", excl=True) for i in range(8)]
    psctr = [0]
    pstctr = [0]

    PSMODE = [0]

    def ps():
        if PSMODE[0] == 2:
            i = psctr[0] % 6
        elif PSMODE[0]:
            i = 2 + psctr[0] % 2
        else:
            i = psctr[0] % 4
        psctr[0] += 1
        return PSA[:, i * 512:(i + 1) * 512], psbuf[i]

    def psb(i):
        return PSA[:, i * 512:(i + 1) * 512], psbuf[i]

    def ps2():
        if psctr[0] % 2:
            psctr[0] += 1
        i = psctr[0] % 4
        psctr[0] += 2
        return PSA[:, i * 512:(i + 2) * 512], [psbuf[i], psbuf[i + 1]]

    def ps_long():
        return PSA[:, 4 * 512:6 * 512], [psbuf[4], psbuf[5]]

    def pst():
        i = pstctr[0] % 2
        pstctr[0] += 1
        return PST[:, i * 1024:(i + 1) * 1024], psbuf[6 + i]

    S.dma("act", ident_f[:], ident_d, writes=[cbuf])
    S.dma("act", NORMW[:], normw_d, writes=[cbuf])
    S.dma("act", FCW[:], fcw_d, writes=[cbuf])
    S.op("dve", lambda e: e.memset(ones_bf[:], 1.0), writes=[cbuf])
    S.op("dve", lambda e: e.memset(epsb[:], EPS), writes=[cbuf])
    S.op("dve", lambda e: e.memset(oneb[:], 1.0), writes=[cbuf])
    S.op("dve", lambda e: e.tensor_copy(out=ident_bf[:], in_=ident_f[:]), reads=[cbuf], writes=[cbuf])
    xv = xT_d.rearrange("(c p) t -> p c t", p=128)
    for ti, (t0, tn) in enumerate(TILES):
        S.dma("act", X[:, :, t0:t0 + tn], xv[:, :, t0:t0 + tn], writes=[xbuf[ti]])

    def rmsnorm(norm_idx, tiles, H, hbufs, hoff, sq_off, lnexp=False):
        SQ = [P.sb(sq_off, [128, KC, 512], BF16) for i in range(2)]
        RS = [P.sb(sq_off + KC * 512 * 2, [128, 512], F32)] * 2
        sqb = [Buf()] * 2
        rsb = [Buf()] * 2
        for j, ti in enumerate(tiles):
            t0, tn = TILES[ti]
            q = j % 2
            S.op("act", lambda e, q=q, t0=t0, tn=tn: e.activation(out=SQ[q][:, :, 0:tn], in_=X[:, :, t0:t0 + tn], func=AF.Square),
                 reads=[xbuf[ti]], writes=[sqb[q]])
            pt, pb = ps()
            for c in range(KC):
                S.op("pe", lambda e, q=q, c=c, tn=tn, pt=pt: e.matmul(pt[:, 0:tn], lhsT=ones_bf[:], rhs=SQ[q][:, c, 0:tn],
                                                                     start=(c == 0), stop=(c == KC - 1)),
                     reads=[sqb[q], cbuf], writes=[pb])
            if lnexp:
                S.op("act", lambda e, q=q, tn=tn, pt=pt: e.activation(out=RS[q][:, 0:tn], in_=pt[:, 0:tn], func=AF.Ln, bias=epsb[:], scale=1.0 / D),
                     reads=[pb, cbuf], writes=[rsb[q]])
                S.op("act", lambda e, q=q, tn=tn: e.activation(out=RS[q][:, 0:tn], in_=RS[q][:, 0:tn], func=AF.Exp, scale=-0.5), reads=[rsb[q]], writes=[rsb[q]])
            else:
                S.op("act", lambda e, q=q, tn=tn, pt=pt: e.activation(out=RS[q][:, 0:tn], in_=pt[:, 0:tn], func=AF.Sqrt,
                                                                     bias=epsb[:], scale=1.0 / D),
                     reads=[pb, cbuf], writes=[rsb[q]])
                S.op("dve", lambda e, q=q, tn=tn: e.reciprocal(out=RS[q][:, 0:tn], in_=RS[q][:, 0:tn]), reads=[rsb[q]], writes=[rsb[q]])
            for c in range(KC):
                S.op("dve", lambda e, q=q, c=c, t0=t0, tn=tn: e.scalar_tensor_tensor(
                    out=H[:, c, t0 - hoff:t0 - hoff + tn], in0=X[:, c, t0:t0 + tn], scalar=NORMW[:, norm_idx, c:c + 1],
                    in1=RS[q][:, 0:tn], op0=ALU.mult, op1=ALU.mult),
                    reads=[xbuf[ti], rsb[q], cbuf], writes=[hbufs[ti]])

    def ffn(layer):
        up_idx = [wtile(("up", layer, f), 2 * KC * 128, (lambda I, layer=layer, f=f: np.stack(
            [I["ffn_w_up"][layer][:, f * 128:(f + 1) * 128].reshape(KC, 128, 128),
             I["ffn_w_up"][layer][:, DFF + f * 128:DFF + (f + 1) * 128].reshape(KC, 128, 128)], axis=1)
            .transpose(2, 0, 1, 3).reshape(128, -1))) for f in range(NF)]
        dn_idx = [[wtile(("dn", layer, c, hh), 11 * 128, (lambda I, layer=layer, c=c, hh=hh:
                   I["ffn_w_down"][layer][hh * 1408:(hh + 1) * 1408, c * 128:(c + 1) * 128].reshape(11, 128, 128)
                   .transpose(1, 0, 2).reshape(128, -1))) for hh in range(2)] for c in range(KC)]
        ar = Bump(ARENA, ARENA_SZ)
        TG = 1088
        HGs = [P.sb(ar.take(KC * TG * 2), [128, KC, TG], BF16) for _ in range(2)]
        GG = P.sb(ar.take(NF * TG * 2), [128, NF, TG], BF16)
        AFB = [P.sb(ar.take(1026 * 4), [128, 1026], F32) for i in range(2)]
        AFS = [P.sb(ar.take(NB * 6 * 4), [128, NB, 6], F32) for i in range(2)]
        U = [P.sb(ar.take(2048), [128, 512], F32) for i in range(3)]
        CAR = P.sb(ar.take(NF * 2 * 4), [128, NF, 2], F32)
        FC = P.sb(ar.take(NF * (2 + 2 * NB) * 4), [128, NF, 2 + 2 * NB], F32)
        FB = P.sb(ar.take(NF * NB * 2 * 4), [128, NF, NB, 2], F32)
        sq_off = ar.take(KC * 512 * 2 + 2048)
        hbs = [[Buf() for _ in TILES] for _ in range(2)]
        gb = [[Buf() for _ in TILES] for _ in range(NF)]
        afb = [Buf(), Buf()]
        afsb = [Buf(), Buf()]
        ub = [Buf() for _ in range(3)]
        carb, fcb, fbb = Buf(), Buf(), Buf()
        S.dma("act", FB[:], fbuf_d[layer], writes=[fbb])
        S.op("dve", lambda e: e.memset(CAR[:], 0.0), writes=[carb])
        uctr = 0
        PSMODE[0] = 2
        swp = SWP()
        rmsnorm(DEPTH + layer, GROUPS[0], HGs[0], hbs[0], TILES[GROUPS[0][0]][0], sq_off)
        for gi, grp in enumerate(GROUPS):
            g0 = TILES[grp[0]][0]
            HG, hb = HGs[gi], hbs[gi]
            for f in range(NF):
                wv, wb_ = wget(up_idx[f], [128, KC, 2, 128])
                q = f % 2
                A_, ab_ = AFB[q], afb[q]
                As_, asb_ = AFS[q], afsb[q]
                S.op("act", lambda e, A_=A_, f=f: e.copy(out=A_[:, 0:2], in_=CAR[:, f, :]), reads=[carb], writes=[ab_])
                if gi == 1:
                    S.op("act", lambda e, As_=As_, f=f: e.copy(out=As_[:, :, 0:2], in_=FB[:, f, :, :]), reads=[fbb], writes=[asb_])
                for ti in grp:
                    t0, tn = TILES[ti]
                    l0 = t0 - g0
                    samp = (ti == 4)
                    pa, pab = ps()
                    for c in range(KC):
                        S.op("pe", lambda e, c=c, pa=pa, tn=tn, l0=l0, wv=wv, HG=HG: e.matmul(
                            pa[:, 0:tn], lhsT=wv[:, c, 0, :], rhs=HG[:, c, l0:l0 + tn], start=(c == 0), stop=(c == KC - 1)),
                            reads=[wb_, hb[ti]], writes=[pab])
                    pbt, pbb = ps()
                    for c in range(KC):
                        S.op("pe", lambda e, c=c, pbt=pbt, tn=tn, l0=l0, wv=wv, HG=HG: e.matmul(
                            pbt[:, 0:tn], lhsT=wv[:, c, 1, :], rhs=HG[:, c, l0:l0 + tn], start=(c == 0), stop=(c == KC - 1)),
                            reads=[wb_, hb[ti]], writes=[pbb])
                    u, ubf = U[uctr % 3], ub[uctr % 3]; uctr += 1
                    if not samp:
                        pav = pa[:, 0:tn]
                        adst = A_[:, 2 + l0:2 + l0 + tn]
                        srcs = [A_[:, l0 + k:l0 + k + tn] for k in range(3)]
                        uo = u[:, 0:tn]
                        pbv = pbt[:, 0:tn]
                        go = GG[:, f, l0:l0 + tn]
                        cb_ = ab_
                    else:
                        pav = pa[:, 0:TS].rearrange("p (b t) -> p b t", t=4)
                        adst = As_[:, :, 2:6]
                        srcs = [As_[:, :, k:k + 4] for k in range(3)]
                        uo = u[:, 0:TS].rearrange("p (b t) -> p b t", t=4)
                        pbv = pbt[:, 0:TS].rearrange("p (b t) -> p b t", t=4)
                        go = GG[:, f, l0:l0 + TS].rearrange("p (b t) -> p b t", t=4)
                        cb_ = asb_
                    S.op("act", lambda e, adst=adst, pav=pav: e.activation(out=adst, in_=pav, func=AF.Copy), reads=[pab], writes=[cb_])
                    S.op("act", lambda e, uo=uo, pav=pav, f=f: e.activation(out=uo, in_=pav, func=AF.Identity, scale=FCW[:, layer, 2, f:f + 1], bias=FCW[:, layer, 3, f:f + 1]),
                         reads=[pab, cbuf], writes=[ubf])
                    for k in (1, 0):
                        S.op("dve", lambda e, uo=uo, s=srcs[k], f=f, k=k: e.scalar_tensor_tensor(
                            out=uo, in0=s, scalar=FCW[:, layer, k, f:f + 1], in1=uo, op0=ALU.mult, op1=ALU.add),
                            reads=[cb_, cbuf, ubf], writes=[ubf])
                    def tail(uo=uo, pbv=pbv, go=go, ubf=ubf, pbb=pbb, gbuf=gb[f][ti]):
                        S.op("act", lambda e: e.activation(out=uo, in_=uo, func=AF.Silu), reads=[ubf], writes=[ubf])
                        S.op("dve", lambda e: e.tensor_tensor(out=go, in0=uo, in1=pbv, op=ALU.mult), reads=[ubf, pbb], writes=[gbuf])
                    swp.step([lambda: None, tail])
                if gi == 0:
                    S.op("act", lambda e, A_=A_, f=f: e.copy(out=CAR[:, f, :], in_=A_[:, 1024:1026]), reads=[ab_], writes=[carb])
                else:
                    S.op("act", lambda e, A_=A_, f=f: e.copy(out=FC[:, f, 0:2], in_=A_[:, 1024:1026]), reads=[ab_], writes=[fcb])
                    S.op("act", lambda e, As_=As_, f=f: e.copy(out=FC[:, f, 2:2 + 2 * NB].rearrange("p (b j) -> p b j", j=2),
                                                                       in_=As_[:, :, 4:6]), reads=[asb_], writes=[fcb])
            swp.flush()
            if gi == 0:
                rmsnorm(DEPTH + layer, GROUPS[1], HGs[1], hbs[1], TILES[GROUPS[1][0]][0], sq_off)
            for c in range(KC):
                w0v, w0b = wget(dn_idx[c][0], [128, 11, 128])
                w1v, w1b = wget(dn_idx[c][1], [128, 11, 128])
                for ti in grp:
                    t0, tn = TILES[ti]
                    l0 = t0 - g0
                    pt, pb = ps()
                    for f in range(NF):
                        wv_, wbb_ = (w0v, w0b) if f < 11 else (w1v, w1b)
                        S.op("pe", lambda e, f=f, pt=pt, wv_=wv_, l0=l0, tn=tn: e.matmul(
                            pt[:, 0:tn], lhsT=wv_[:, f % 11, :], rhs=GG[:, f, l0:l0 + tn], start=(f == 0), stop=(f == NF - 1)),
                            reads=[wbb_, gb[f][ti]], writes=[pb])
                    S.op("dve", lambda e, c=c, pt=pt, t0=t0, tn=tn: e.tensor_tensor(
                        out=X[:, c, t0:t0 + tn], in0=X[:, c, t0:t0 + tn], in1=pt[:, 0:tn], op=ALU.add),
                        reads=[pb, xbuf[ti]], writes=[xbuf[ti]])
        PSMODE[0] = 0
        S.dma("act", fconv_d[layer], FC[:], reads=[fcb], is_output=True)

    NEG = -30000.0
    QPERM = [8 * jj + i + 4 * hf for jj in range(2) for i in range(4) for hf in range(2)]
    qcols = np.concatenate([np.arange(h * 64, (h + 1) * 64) for h in QPERM])

    def _maskp():
        q = np.arange(128)[:, None]; s_ = np.arange(256)[None, :]
        return np.where((s_ <= q + 128) & (s_ > q), 0.0, NEG).astype(np.float32)

    def _maskc():
        t_ = (np.arange(16) % 4)[:, None]; r = np.arange(128)[None, :]
        return np.where(r > t_, 0.0, NEG).astype(np.float32)

    def _maskn():
        t_ = (np.arange(16) % 4)[:, None, None]; b_ = np.arange(NB)[None, :, None]
        col = np.arange(64)[None, None, :]
        return np.where((col // 4 == b_) & (col % 4 <= t_), 0.0, NEG).astype(np.float32)

    if any(CFG["mixers"][l] and l % 3 == 0 for l in range(NL)):
        maskp_d = P.din("maskp", [128, 256], lambda I, c: _maskp())
        maskc_d = P.din("maskc", [16, 128], lambda I, c: _maskc())
        maskn_d = P.din("maskn", [16, NB, 64], lambda I, c: _maskn())
        abias_d = P.din("abias", [2, 128, 8 + 2 + 8], lambda I, c: np.stack([np.concatenate([
            I["attn_b_qkv"][s_][qcols].reshape(8, 128).T, I["attn_b_qkv"][s_][1024:1280].reshape(2, 128).T,
            I["attn_b_o"][s_].reshape(8, 128).T], axis=1) for s_ in range(2)]))
        vbias_d = P.din("vbias", [2, 128, 256], lambda I, c: np.broadcast_to(I["attn_b_qkv"][:, None, 1280:1536], (2, 128, 256)))
        sinkp_d = P.din("sinkp", [2, 128, 16], lambda I, c: np.broadcast_to(I["attn_sinks"][:, None, :], (2, 128, 16)))
        sinks_d = P.din("sinks", [2, 16, 4], lambda I, c: np.stack([np.repeat(I["attn_sinks"][s_].reshape(4, 4).T, 4, axis=0) for s_ in range(2)]))
        cacheKT_d = P.din("cacheKT", [2, NB, 2, 128, 128], lambda I, c: I["cache_attn_k"][:, c * NB:(c + 1) * NB].reshape(2, NB, 128, 2, 128).transpose(0, 1, 3, 4, 2))
        cacheV_d = P.din("cacheV", [2, NB, 128, 256], lambda I, c: I["cache_attn_v"][:, c * NB:(c + 1) * NB].reshape(2, NB, 128, 256))
        kTp_d = P.dout("kTp", [2, 2, 128, 128])
        vp_d = P.dout("vp", [2, 128, 256])
        kTs_d = P.dout("ks", [2, NB, 128, 256])
        cacheK_d = P.din("cacheK", [2, NB, 128, 256], lambda I, c: I["cache_attn_k"][:, c * NB:(c + 1) * NB].reshape(2, NB, 128, 256))
        vs_d = P.dout("vs", [2, NB, 128, 256])

    def attention(layer):
        sl = layer // 3
        Wqkv = lambda I: I["attn_w_qkv"][sl]
        q_idx = [wtile(("aq", sl, g), 2048, (lambda I, g=g: Wqkv(I)[:, qcols[g * 256:(g + 1) * 256]].reshape(KC, 128, 256).transpose(1, 0, 2).reshape(128, -1))) for g in range(4)]
        k_idx = wtile(("ak", sl), 2048, (lambda I: Wqkv(I)[:, 1024:1280].reshape(KC, 128, 256).transpose(1, 0, 2).reshape(128, -1)))
        v_idx = wtile(("av", sl), 2048, (lambda I: Wqkv(I)[:, 1280:1536].reshape(KC, 128, 256).transpose(1, 0, 2).reshape(128, -1)))
        qrows = qcols
        o_idx = [wtile(("ao", sl, g), 2048, (lambda I, g=g: I["attn_w_o"][sl][qrows][:, g * 256:(g + 1) * 256].reshape(KC, 128, 256).transpose(1, 0, 2).reshape(128, -1))) for g in range(4)]
        ar = Bump(ARENA, ARENA_SZ)
        HT = P.sb(ar.take(KC * 512 * 2), [128, KC, 512], BF16)
        QT = P.sb(ar.take(KC * 512 * 2), [128, KC, 512], BF16)
        OT = [P.sb(ar.take(KC * 512 * 2), [128, KC, 512], BF16) for _ in range(2)]
        KT = P.sb(ar.take(2 * T * 2), [128, 2, T], BF16)
        VT = P.sb(ar.take(16 * 256 * 2), [128, 16, 256], BF16)
        VS = P.sb(ar.take(512), [64, 256], BF16)
        QS = P.sb(ar.take(1024), [128, NB, 2, 16], BF16)
        qsb = Buf()
        VO = [P.sb(ar.take(1024), [128, 256], F32)] * 2
        KO = P.sb(ar.take(2 * 128 * 4), [128, 2, 128], F32)
        KSO = P.sb(ar.take(2 * 64 * 4), [128, 2, 64], F32)
        KST = P.sb(ar.take(1024), [64, 256], F32)
        kstb = Buf()
        sq_off = ar.take(KC * 512 * 2 + 2048)
        SC = [P.sb(ar.take(4096), [128, 4, 256], F32) for _ in range(3)]
        PN = [P.sb(ar.take(2048), [128, 4, 256], BF16) for _ in range(2)]
        PTs = [P.sb(ar.take(2048), [128, 8, 128], BF16) for _ in range(2)]
        ST = [P.sb(ar.take(128), [128, 6, 4], F32) for _ in range(3)]
        MASKP = P.sb(ar.take(1024), [128, 256], F32)
        MASKC = P.sb(ar.take(512), [16, 128], F32)
        MASKN = P.sb(ar.take(NB * 64 * 4), [16, NB, 64], F32)
        AB = P.sb(ar.take(18 * 4), [128, 18], F32)
        VB = P.sb(ar.take(1024), [128, 256], F32)
        SKP = P.sb(ar.take(64), [128, 16], F32)
        SKP8 = P.sb(ar.take(64), [128, 16], F32)
        SKMX8 = P.sb(ar.take(16), [128, 4], F32)
        SKS = P.sb(ar.take(16), [16, 4], F32)
        SKS8 = P.sb(ar.take(16), [16, 4], F32)
        KCf = [P.sb(ar.take(1024), [128, 2, 128], F32) for _ in range(2)]
        VCf = [P.sb(ar.take(1024), [128, 256], F32) for _ in range(2)]
        KCb = [P.sb(ar.take(512), [128, 2, 128], BF16) for _ in range(2)]
        VCb = [P.sb(ar.take(512), [128, 256], BF16) for _ in range(2)]
        SS = [P.sb(ar.take(4 * 192 * 4), [16, 4, 192], F32) for _ in range(2)]
        PNS = [P.sb(ar.take(4 * 192 * 2), [16, 4, 192], BF16) for _ in range(2)]
        PTC = [P.sb(ar.take(128), [128, 4, 16], BF16) for _ in range(2)]
        PTN = [P.sb(ar.take(128), [64, 4, 16], BF16) for _ in range(2)]
        cb2 = Buf()
        S.dma("act", MASKP[:], maskp_d, writes=[cb2])
        S.dma("act", MASKC[:], maskc_d, writes=[cb2])
        S.dma("act", MASKN[:], maskn_d, writes=[cb2])
        S.dma("act", AB[:], abias_d[sl], writes=[cb2])
        S.dma("act", VB[:], vbias_d[sl], writes=[cb2])
        S.dma("act", SKP[:], sinkp_d[sl], writes=[cb2])
        S.dma("act", SKS[:], sinks_d[sl], writes=[cb2])
        S.op("dve", lambda e: e.tensor_scalar(out=SKP8[:], in0=SKP[:], scalar1=8.0, scalar2=None, op0=ALU.mult), reads=[cb2], writes=[cb2])
        S.op("dve", lambda e: e.tensor_scalar(out=SKS8[:], in0=SKS[:], scalar1=8.0, scalar2=None, op0=ALU.mult), reads=[cb2], writes=[cb2])
        S.op("dve", lambda e: e.tensor_reduce(out=SKMX8[:], in_=SKP8[:, :].rearrange("p (a b) -> p a b", a=4), op=ALU.max, axis=AX.X), reads=[cb2], writes=[cb2])
        if CFG["dd"]:
            S.dma("act", kTs_d[sl][:, 0:124, :], cacheK_d[sl][:, 4:128, :], is_output=True)
            S.dma("act", vs_d[sl][:, 0:124, :], cacheV_d[sl][:, 4:128, :], is_output=True)
        hb, qb = Buf(), Buf()
        ob = [Buf(), Buf()]
        ktb = [Buf() for _ in TILES]
        vtb = [Buf() for _ in range(17)]
        vob = [Buf()] * 2
        kob, ksob = Buf(), Buf()
        scb = [Buf(), Buf(), Buf()]; pnb = [Buf(), Buf()]; ptb = [Buf(), Buf()]; stb = [Buf(), Buf(), Buf()]
        kcfb = [Buf(), Buf()]; vcfb = [Buf(), Buf()]; kcbb = [Buf(), Buf()]; vcbb = [Buf(), Buf()]
        ssb = [Buf(), Buf()]; pnsb = [Buf(), Buf()]; ptcb = [Buf(), Buf()]; ptnb = [Buf(), Buf()]
        gctr = 0
        for ti, (t0, tn) in enumerate(TILES):
            samp = (ti == 4)
            rmsnorm(layer, [ti], HT, {ti: hb}, t0, sq_off, lnexp=True)
            for g in range(4):
                wv, wb_ = wget(q_idx[g], [128, KC, 256])
                for cc in range(2):
                    c = 2 * g + cc
                    pt, pb = ps()
                    for kc in range(KC):
                        S.op("pe", lambda e, pt=pt, wv=wv, cc=cc, kc=kc, tn=tn: e.matmul(pt[:, 0:tn], lhsT=wv[:, kc, cc * 128:(cc + 1) * 128], rhs=HT[:, kc, 0:tn],
                                                                                       start=(kc == 0), stop=(kc == KC - 1)), reads=[wb_, hb], writes=[pb])
                    S.op("act", lambda e, pt=pt, c=c, tn=tn: e.activation(out=QT[:, c, 0:tn], in_=pt[:, 0:tn], func=AF.Identity, bias=AB[:, c:c + 1], scale=1.0),
                         reads=[pb, cb2], writes=[qb])
            if samp:
                for jj in range(2):
                    S.op("act", lambda e, jj=jj: e.copy(out=QS[:, :, jj, :].rearrange("p b (i t) -> p b i t", t=4),
                                                                in_=QT[:, jj * 4:(jj + 1) * 4, 0:64].rearrange("p i (b t) -> p b i t", t=4)), reads=[qb], writes=[qsb])
            wv, wb_ = wget(k_idx, [128, KC, 256])
            for jj in range(2):
                pt, pb = ps()
                for kc in range(KC):
                    S.op("pe", lambda e, pt=pt, wv=wv, jj=jj, kc=kc, tn=tn: e.matmul(pt[:, 0:tn], lhsT=wv[:, kc, jj * 128:(jj + 1) * 128], rhs=HT[:, kc, 0:tn],
                                                                                   start=(kc == 0), stop=(kc == KC - 1)), reads=[wb_, hb], writes=[pb])
                S.op("act", lambda e, pt=pt, jj=jj, t0=t0, tn=tn: e.activation(out=KT[:, jj, t0:t0 + tn], in_=pt[:, 0:tn], func=AF.Identity, bias=AB[:, 8 + jj:9 + jj], scale=1.0),
                     reads=[pb, cb2], writes=[ktb[ti]])
                if ti == 3:
                    S.op("dve", lambda e, pt=pt, jj=jj: e.tensor_scalar(out=KO[:, jj, :], in0=pt[:, 384:512], scalar1=AB[:, 8 + jj:9 + jj], scalar2=None, op0=ALU.add),
                         reads=[pb, cb2], writes=[kob])
                if samp:
                    S.op("dve", lambda e, pt=pt, jj=jj: e.tensor_scalar(out=KSO[:, jj, :], in0=pt[:, 0:64], scalar1=AB[:, 8 + jj:9 + jj], scalar2=None, op0=ALU.add),
                         reads=[pb, cb2], writes=[ksob])
            if ti == 3:
                S.dma("act", kTp_d[sl].rearrange("j p k -> p j k"), KO[:], reads=[kob], is_output=True)
            if samp:
                pt, pb = ps()
                for jj in range(2):
                    S.op("pe", lambda e, pt=pt, jj=jj: e.transpose(out=pt[0:64, jj * 128:(jj + 1) * 128], in_=KSO[:, jj, :], identity=ident_f[:]), reads=[ksob, cbuf], writes=[pb])
                S.op("act", lambda e, pt=pt: e.activation(out=KST[:, :], in_=pt[0:64, 0:256], func=AF.Copy), reads=[pb], writes=[kstb])
                for t_ in range(4 if CFG["so"] else 0):
                    S.dma("act", kTs_d[sl][:, 124 + t_, :], KST[t_:64:4, :], reads=[kstb], is_output=True)
            wv, wb_ = wget(v_idx, [128, KC, 256])
            nblk = 1 if samp else 4
            for bi in range(nblk):
                rows = 64 if samp else 128
                pt, pb = ps()
                for kc in range(KC):
                    S.op("pe", lambda e, pt=pt, wv=wv, kc=kc, bi=bi, rows=rows: e.matmul(pt[0:rows, 0:256], lhsT=HT[:, kc, bi * 128:bi * 128 + rows], rhs=wv[:, kc, :],
                                                                                       start=(kc == 0), stop=(kc == KC - 1)), reads=[wb_, hb], writes=[pb])
                gblk = ti * 4 + bi
                if samp:
                    S.op("dve", lambda e, pt=pt: e.tensor_tensor(out=VS[:, :], in0=pt[0:64, 0:256], in1=VB[0:64, :], op=ALU.add), reads=[pb, cb2], writes=[vtb[16]])
                    S.op("dve", lambda e, pt=pt: e.tensor_tensor(out=VO[0][0:64, :], in0=pt[0:64, 0:256], in1=VB[0:64, :], op=ALU.add), reads=[pb, cb2], writes=[vob[0]])
                    for t_ in range(4 if CFG["so"] else 0):
                        S.dma("act", vs_d[sl][:, 124 + t_, :], VO[0][t_:64:4, :], reads=[vob[0]], is_output=True)
                else:
                    S.op("dve", lambda e, pt=pt, gblk=gblk: e.tensor_tensor(out=VT[:, gblk, :], in0=pt[:, 0:256], in1=VB[:, :], op=ALU.add), reads=[pb, cb2], writes=[vtb[gblk]])
                    if gblk == 15:
                        S.op("dve", lambda e, pt=pt: e.tensor_tensor(out=VO[1][:, :], in0=pt[:, 0:256], in1=VB[:, :], op=ALU.add), reads=[pb, cb2], writes=[vob[1]])
                        S.dma("act", vp_d[sl], VO[1][:], reads=[vob[1]], is_output=True)
            oq = ti % 2
            if (samp and not CFG['as']) or (not samp and not CFG['ap']):
                S.op('dve', lambda e, oq=oq: e.memset(OT[oq][:], 0.0), writes=[ob[oq]])
            elif not samp:
                swp = SWP()
                po_state = {}
                for bi in range(4):
                    gblk = ti * 4 + bi
                    first = (gblk == 0)
                    ncol = 128 if first else 256
                    k0 = gblk * 128 if first else (gblk - 1) * 128
                    for j in range(4):
                        jj, base = j // 2, (j % 2) * 64
                        g3 = gctr % 3; g2 = gctr % 2; gctr += 1
                        sc, stt, scb_, stb_ = SC[g3], ST[g3], scb[g3], stb[g3]
                        pn, ptt, pnb_, ptb_ = PN[g2], PTs[g2], pnb[g2], ptb[g2]
                        nkb = ncol // 128
                        mk = MASKP[:, 128:256] if first else MASKP[:, :]
                        hold = {}

                        def st0a(jj=jj, base=base, bi=bi, k0=k0, ncol=ncol, hold=hold):
                            psc, pscb = ps2()
                            hold["psc"] = (psc, pscb)
                            for i in range(4):
                                cq = jj * 4 + i
                                S.op("pe", lambda e, i=i, cq=cq: e.matmul(
                                    psc[:, i * 256:i * 256 + ncol], lhsT=QT[base:base + 64, cq, bi * 128:(bi + 1) * 128], rhs=KT[base:base + 64, jj, k0:k0 + ncol],
                                    start=True, stop=True), reads=[qb] + [ktb[x] for x in {k0 // 512, (k0 + ncol - 1) // 512}], writes=[pscb[i // 2]])

                        def st0(sc=sc, stt=stt, scb_=scb_, stb_=stb_, ncol=ncol, mk=mk, j=j, hold=hold):
                            psc, pscb = hold["psc"]
                            S.op("dve", lambda e: e.tensor_tensor(
                                out=sc[:, :, 0:ncol], in0=psc.rearrange("p (a b) -> p a b", a=4)[:, :, 0:ncol], in1=mk.unsqueeze(1).broadcast_to([128, 4, ncol]), op=ALU.add),
                                reads=pscb + [cb2], writes=[scb_])
                            S.op("dve", lambda e: e.tensor_reduce(out=stt[:, 0, :], in_=sc[:, :, 0:ncol], op=ALU.max, axis=AX.X), reads=[scb_], writes=[stb_])
                            S.op("dve", lambda e: e.tensor_tensor(out=stt[:, 0, :], in0=stt[:, 0, :], in1=SKP8[:, 4 * j:4 * j + 4], op=ALU.max), reads=[stb_, cb2], writes=[stb_])
                            S.op("dve", lambda e: e.tensor_scalar(out=stt[:, 1, :], in0=stt[:, 0, :], scalar1=-0.125, scalar2=None, op0=ALU.mult), reads=[stb_], writes=[stb_])
                            S.op("dve", lambda e: e.tensor_tensor(out=stt[:, 3, :], in0=stt[:, 1, :], in1=SKP[:, 4 * j:4 * j + 4], op=ALU.add), reads=[stb_, cb2], writes=[stb_])

                        def st1(sc=sc, stt=stt, scb_=scb_, stb_=stb_, ncol=ncol):
                            for i in range(4):
                                S.op("act", lambda e, i=i: e.activation(out=sc[:, i, 0:ncol], in_=sc[:, i, 0:ncol], func=AF.Exp, bias=stt[:, 1, i:i + 1], scale=0.125,
                                                                        accum_out=stt[:, 2, i:i + 1]), reads=[scb_, stb_], writes=[scb_, stb_])
                            S.op("act", lambda e: e.activation(out=stt[:, 3, :], in_=stt[:, 3, :], func=AF.Exp), reads=[stb_], writes=[stb_])

                        def st2(sc=sc, stt=stt, scb_=scb_, stb_=stb_, pn=pn, pnb_=pnb_, ncol=ncol, nkb=nkb, hold=hold):
                            S.op("dve", lambda e: e.tensor_tensor(out=stt[:, 4, :], in0=stt[:, 2, :], in1=stt[:, 3, :], op=ALU.add), reads=[stb_], writes=[stb_])
                            S.op("dve", lambda e: e.reciprocal(out=stt[:, 5, :], in_=stt[:, 4, :]), reads=[stb_], writes=[stb_])
                            S.op("dve", lambda e: e.tensor_tensor(out=pn[:, :, 0:ncol], in0=sc[:, :, 0:ncol],
                                                                  in1=stt[:, 5, :].unsqueeze(2).broadcast_to([128, 4, ncol]), op=ALU.mult),
                                 reads=[scb_, stb_], writes=[pnb_])
                            ptp, ptpb = pst()
                            hold["ptp"] = (ptp, ptpb)
                            for i in range(4):
                                for kb in range(nkb):
                                    S.op("pe", lambda e, i=i, kb=kb: e.transpose(out=ptp[:, (i * 2 + kb) * 128:(i * 2 + kb + 1) * 128], in_=pn[:, i, kb * 128:(kb + 1) * 128],
                                                                                 identity=ident_bf[:]), reads=[pnb_, cbuf], writes=[ptpb])

                        def st3(ptt=ptt, ptb_=ptb_, hold=hold, jj=jj, base=base, j=j, nkb=nkb, gblk=gblk, first=first, bi=bi, oq=oq):
                            ptp, ptpb = hold["ptp"]
                            po, pob = ps_long()
                            S.op("act", lambda e: e.activation(out=ptt[:].rearrange("p a b -> p (a b)"), in_=ptp[:, :], func=AF.Copy), reads=[ptpb], writes=[ptb_])
                            for i in range(4):
                                cq = jj * 4 + i
                                for kb in range(nkb):
                                    vblk = gblk if (first or kb == 1) else gblk - 1
                                    S.op("pe", lambda e, i=i, kb=kb, cq=cq, vblk=vblk: e.matmul(
                                        po[base:base + 64, cq * 128:(cq + 1) * 128], lhsT=VT[:, vblk, j * 64:(j + 1) * 64], rhs=ptt[:, i * 2 + kb, :],
                                        start=(kb == 0), stop=(kb == nkb - 1), tile_position=(0, base)), reads=[ptb_, vtb[vblk]], writes=[pob[cq // 4]])
                            if j == 3:
                                S.op("act", lambda e: e.activation(out=OT[oq][:, :, bi * 128:(bi + 1) * 128], in_=po.rearrange("p (a b) -> p a b", a=8), func=AF.Copy),
                                     reads=pob, writes=[ob[oq]])

                        swp.step([st0a, st0, st1, st2, st3])
                swp.flush()
            else:
                def samp_gen(b, q2):
                    S.dma("sp", KCf[q2][:], cacheKT_d[sl][b].rearrange("j p k -> p j k"), writes=[kcfb[q2]])
                    S.dma("sp", VCf[q2][:], cacheV_d[sl][b], writes=[vcfb[q2]])
                    S.op("act", lambda e: e.copy(out=KCb[q2][:], in_=KCf[q2][:]), reads=[kcfb[q2]], writes=[kcbb[q2]])
                    S.op("act", lambda e: e.copy(out=VCb[q2][:], in_=VCf[q2][:]), reads=[vcfb[q2]], writes=[vcbb[q2]])
                    yield
                    pcs = [psb(2 * q2), psb(2 * q2 + 1)]
                    for j in range(4):
                        jj, base = j // 2, (j % 2) * 64
                        pc, pcb = pcs[j % 2]
                        S.op("pe", lambda e, pc=pc, jj=jj, base=base: e.matmul(pc[0:16, jj * 192:jj * 192 + 128], lhsT=QS[base:base + 64, b, jj, :],
                                                                            rhs=KCb[q2][base:base + 64, jj, :], start=True, stop=True), reads=[qsb, kcbb[q2]], writes=[pcb])
                        S.op("pe", lambda e, pc=pc, jj=jj, base=base: e.matmul(pc[0:16, jj * 192 + 128:jj * 192 + 192], lhsT=QS[base:base + 64, b, jj, :],
                                                                            rhs=KT[base:base + 64, jj, TP:T], start=True, stop=True), reads=[qsb, ktb[4]], writes=[pcb])
                    yield
                    ss, pns, stt = SS[q2], PNS[q2], ST[q2]
                    for par in range(2):
                        pc, pcb = pcs[par]
                        pv_ = pc[0:16, 0:384].rearrange("p (a b) -> p a b", a=2)
                        S.op("dve", lambda e, pv_=pv_, par=par: e.tensor_tensor(out=ss[:, par:4:2, 0:128], in0=pv_[:, :, 0:128],
                                                                               in1=MASKC[:, :].unsqueeze(1).broadcast_to([16, 2, 128]), op=ALU.add), reads=[pcb, cb2], writes=[ssb[q2]])
                        S.op("dve", lambda e, pv_=pv_, par=par: e.tensor_tensor(out=ss[:, par:4:2, 128:192], in0=pv_[:, :, 128:192],
                                                                               in1=MASKN[:, b, :].unsqueeze(1).broadcast_to([16, 2, 64]), op=ALU.add), reads=[pcb, cb2], writes=[ssb[q2]])
                    S.op("dve", lambda e: e.tensor_reduce(out=stt[0:16, 0, :], in_=ss[:, :, :], op=ALU.max, axis=AX.X), reads=[ssb[q2]], writes=[stb[q2]])
                    S.op("dve", lambda e: e.tensor_tensor(out=stt[0:16, 0, :], in0=stt[0:16, 0, :], in1=SKS8[:, :], op=ALU.max), reads=[stb[q2], cb2], writes=[stb[q2]])
                    S.op("dve", lambda e: e.tensor_scalar(out=stt[0:16, 1, :], in0=stt[0:16, 0, :], scalar1=-0.125, scalar2=None, op0=ALU.mult), reads=[stb[q2]], writes=[stb[q2]])
                    S.op("dve", lambda e: e.tensor_tensor(out=stt[0:16, 3, :], in0=stt[0:16, 1, :], in1=SKS[:, :], op=ALU.add), reads=[stb[q2], cb2], writes=[stb[q2]])
                    yield
                    for j in range(4):
                        S.op("act", lambda e, j=j: e.activation(out=ss[:, j, :], in_=ss[:, j, :], func=AF.Exp, bias=stt[0:16, 1, j:j + 1], scale=0.125,
                                                                accum_out=stt[0:16, 2, j:j + 1]), reads=[ssb[q2], stb[q2]], writes=[ssb[q2], stb[q2]])
                    S.op("act", lambda e: e.activation(out=stt[0:16, 3, :], in_=stt[0:16, 3, :], func=AF.Exp), reads=[stb[q2]], writes=[stb[q2]])
                    yield
                    S.op("dve", lambda e: e.tensor_tensor(out=stt[0:16, 4, :], in0=stt[0:16, 2, :], in1=stt[0:16, 3, :], op=ALU.add), reads=[stb[q2]], writes=[stb[q2]])
                    S.op("dve", lambda e: e.reciprocal(out=stt[0:16, 5, :], in_=stt[0:16, 4, :]), reads=[stb[q2]], writes=[stb[q2]])
                    S.op("dve", lambda e: e.tensor_tensor(out=pns[:, :, :], in0=ss[:, :, :], in1=stt[0:16, 5, :].unsqueeze(2).broadcast_to([16, 4, 192]), op=ALU.mult),
                         reads=[ssb[q2], stb[q2]], writes=[pnsb[q2]])
                    yield
                    ptp, ptpb = PST[:, q2 * 1024:(q2 + 1) * 1024], psbuf[6 + q2]
                    for j in range(4):
                        S.op("pe", lambda e, j=j: e.transpose(out=ptp[:, j * 16:(j + 1) * 16], in_=pns[:, j, 0:128], identity=ident_bf[0:16, 0:16]),
                             reads=[pnsb[q2], cbuf], writes=[ptpb])
                        S.op("pe", lambda e, j=j: e.transpose(out=ptp[0:64, 64 + j * 16:64 + (j + 1) * 16], in_=pns[:, j, 128:192], identity=ident_bf[0:16, 0:16]),
                             reads=[pnsb[q2], cbuf], writes=[ptpb])
                    yield
                    S.op("act", lambda e: e.activation(out=PTC[q2][:].rearrange("p a b -> p (a b)"), in_=ptp[:, 0:64], func=AF.Copy), reads=[ptpb], writes=[ptcb[q2]])
                    S.op("act", lambda e: e.activation(out=PTN[q2][:].rearrange("p a b -> p (a b)"), in_=ptp[0:64, 64:128], func=AF.Copy), reads=[ptpb], writes=[ptnb[q2]])
                    yield
                    po, pob = psb(4 + q2)
                    for j in range(4):
                        jj, base = j // 2, (j % 2) * 64
                        S.op("pe", lambda e, j=j, jj=jj, base=base: e.matmul(po[base:base + 64, jj * 16:(jj + 1) * 16], lhsT=VCb[q2][:, j * 64:(j + 1) * 64], rhs=PTC[q2][:, j, :],
                                                                           start=True, stop=False, tile_position=(0, base)), reads=[vcbb[q2], ptcb[q2]], writes=[pob])
                        S.op("pe", lambda e, j=j, jj=jj, base=base: e.matmul(po[base:base + 64, jj * 16:(jj + 1) * 16], lhsT=VS[0:64, j * 64:(j + 1) * 64], rhs=PTN[q2][0:64, j, :],
                                                                           start=False, stop=True, tile_position=(0, base)), reads=[vtb[16], ptnb[q2]], writes=[pob])
                    yield
                    S.op("act", lambda e: e.activation(out=OT[oq][:, :, 4 * b:4 * b + 4], in_=po[:, 0:32].rearrange("p (a b) -> p a b", b=4), func=AF.Copy),
                         reads=[pob], writes=[ob[oq]])

                pending = list(range(NB))
                active = {}
                for slot in range(2):
                    active[slot] = samp_gen(pending.pop(0), slot)
                while active:
                    for slot in list(active.keys()):
                        try:
                            next(active[slot])
                        except StopIteration:
                            if pending:
                                active[slot] = samp_gen(pending.pop(0), slot)
                            else:
                                del active[slot]
            for g in range(4):
                wv, wb_ = wget(o_idx[g], [128, KC, 256])
                for cc in range(2):
                    c = 2 * g + cc
                    pt, pb = ps()
                    for kc in range(KC):
                        S.op("pe", lambda e, pt=pt, wv=wv, cc=cc, kc=kc, tn=tn, oq=oq: e.matmul(pt[:, 0:tn], lhsT=wv[:, kc, cc * 128:(cc + 1) * 128], rhs=OT[oq][:, kc, 0:tn],
                                                                                              start=(kc == 0), stop=(kc == KC - 1)), reads=[wb_, ob[oq]], writes=[pb])
                    S.op("dve", lambda e, pt=pt, c=c, t0=t0, tn=tn: e.scalar_tensor_tensor(out=X[:, c, t0:t0 + tn], in0=pt[:, 0:tn], scalar=AB[:, 10 + c:11 + c], in1=X[:, c, t0:t0 + tn],
                                                                                          op0=ALU.add, op1=ALU.add), reads=[pb, cb2, xbuf[ti]], writes=[xbuf[ti]])

    HTILES = [(t0, 256) for t0 in range(0, TP, 256)] + [(TP, TS)]

    def _hmask():
        s_ = np.arange(128)[:, None]; t_ = np.arange(128)[None, :]
        same = (s_ // 16) == (t_ // 16)
        tri = (same & (s_ <= t_)).astype(np.float32)
        tri2 = (same & (s_ > t_)).astype(np.float32)
        cm = ((np.arange(128)[:, None] // 16) == np.arange(8)[None, :]).astype(np.float32)
        return np.concatenate([tri, tri2, -tri, cm], axis=1)

    def _hmask_s():
        s_ = np.arange(64)[:, None]; t_ = np.arange(64)[None, :]
        same = (s_ // 4) == (t_ // 4)
        tri = (same & (s_ <= t_)).astype(np.float32)
        tri2 = (same & (s_ > t_)).astype(np.float32)
        cm = ((np.arange(64)[:, None] // 4) == np.arange(16)[None, :]).astype(np.float32)
        return np.concatenate([tri, tri2, -tri, cm], axis=1)

    if NL > 1 and CFG["mixers"][1]:
        hmask_d = P.din("hmask", [128, 392], lambda I, c: _hmask())
        hmasks_d = P.din("hmasks", [64, 208], lambda I, c: _hmask_s())
        lbl_d = P.din("lbl", [128, 4, 1024], lambda I, c: np.broadcast_to(I["hgrn_lb_logits"][None], (128, 4, 1024)))
        hnw_d = P.din("hnw", [128, 8], lambda I, c: I["hgrn_norm_w"][0].reshape(8, 128).T)
        hgS_d = P.din("hgS", [NB, 128, 8, 128], lambda I, c: I["state_hgrn"][0, c * NB:(c + 1) * NB].transpose(0, 2, 1, 3))
        hgP_d = P.dout("hgP", [128, 8, 128])
        hgSo_d = P.dout("hgSo", [NB, 128, 8, 128])

    def hgrn(layer):
        Win = lambda I: I["hgrn_w_in"][0]
        def wt(name, c0):
            return wtile((name, c0), 2048, (lambda I, c0=c0: Win(I)[:, c0:c0 + 256].reshape(KC, 128, 256).transpose(1, 0, 2).reshape(128, -1)))
        q_idx = [wt("hq", 256 * g) for g in range(4)]
        f_idx = [wt("hf", 1024 + 256 * g) for g in range(4)]
        i_idx = [wt("hi", 2048 + 256 * g) for g in range(4)]
        g_idx = [wt("hg", 3072 + 256 * g) for g in range(4)]
        o_idx = [wtile(("ho", g), 2048, (lambda I, g=g: I["hgrn_w_o"][0][:, g * 256:(g + 1) * 256].reshape(KC, 128, 256).transpose(1, 0, 2).reshape(128, -1))) for g in range(4)]
        ar = Bump(ARENA, ARENA_SZ)
        TN = 256
        HTs = [P.sb(ar.take(KC * TN * 2), [128, KC, TN], BF16) for _ in range(2)]
        SQN2 = P.sb(ar.take(KC * TN * 2), [128, KC, TN], BF16)
        RSN2 = P.sb(ar.take(1024), [128, TN], F32)
        sqnb, rsnb = Buf(), Buf()
        hbs = [Buf(), Buf()]
        SQ = P.sb(ar.take(KC * TN * 2), [128, KC, TN], BF16)
        GS = P.sb(ar.take(KC * TN * 2), [128, KC, TN], BF16)
        OT = P.sb(ar.take(KC * 512 * 2), [128, KC, 512], BF16)
        RSn = ar.take(2048)
        lf_off = ar.take(2 * 4096)
        LF = P.sb(lf_off, [128, 2, 1024], F32)
        l1_off = ar.take(2 * 4096)
        L1 = P.sb(l1_off, [128, 2, 1024], F32)
        V = P.sb(ar.take(2 * 2048), [128, 2, 1024], BF16)
        E = P.sb(ar.take(4096), [128, 8, 128], F32)
        DEC = P.sb(ar.take(8 * 16 * 4), [128, 8, 16], F32)
        QIb = P.sb(ar.take(2048), [128, 8, 128], BF16)
        KI = P.sb(ar.take(2048), [128, 8, 128], BF16)
        KD = P.sb(ar.take(2048), [128, 1024], BF16)
        kdm_off = ar.take(16384)
        KDM = P.sb(kdm_off, [128, 8, 1024], BF16)
        AT = P.sb(ar.take(2048), [128, 8, 128], BF16)
        S32 = [P.sb(ar.take(4096), [128, 8, 128], F32) for _ in range(2)]
        SB16 = [P.sb(ar.take(2048), [128, 8, 128], BF16) for _ in range(2)]
        sbb = [Buf(), Buf()]
        SQO = P.sb(ar.take(2048), [128, 1024], BF16)
        RSTD = P.sb(ar.take(4096), [128, 1024], F32)
        FT = [P.sb(ar.take(1024), [128, 256], F32) for _ in range(2)]
        LB = P.sb(ar.take(4096), [128, 1024], F32)
        OML = P.sb(ar.take(4096), [128, 1024], F32)
        HM = P.sb(ar.take(392 * 4), [128, 392], F32)
        HMS = P.sb(ar.take(208 * 4), [64, 208], F32)
        NW = P.sb(ar.take(32), [128, 8], F32)
        TRIb = P.sb(ar.take(256), [128, 128], BF16)
        TRIsb = P.sb(ar.take(128), [64, 64], BF16)
        S0 = [P.sb(lf_off + 4096, [128, 8, 128], F32), P.sb(l1_off + 4096, [128, 8, 128], F32)]
        cb3 = Buf()
        hb, sqb, gsb, ob = Buf(), Buf(), Buf(), Buf()
        lfb, l1b, vb_ = [Buf(), Buf()], [Buf(), Buf()], [Buf(), Buf()]
        eb, decb, qib, kib, kdb, atb = Buf(), Buf(), Buf(), Buf(), Buf(), Buf()
        kdmb = [Buf() for _ in range(16)]
        sb_ = [[Buf() for _ in range(8)] for _ in range(2)]
        sqob, rstdb = Buf(), Buf()
        ftb = [Buf(), Buf()]
        s0b = [lfb[1], l1b[1]]
        S.dma("act", HM[:], hmask_d, writes=[cb3])
        S.dma("act", HMS[:], hmasks_d, writes=[cb3])
        S.dma("act", NW[:], hnw_d, writes=[cb3])
        LGt = P.sb(kdm_off, [128, 4, 1024], F32)
        S.dma("act", LGt[:], lbl_d, writes=[kdmb[0]])
        S.op("act", lambda e: e.activation(out=LGt[:], in_=LGt[:], func=AF.Exp), reads=[kdmb[0]], writes=[kdmb[0]])
        S.op("dve", lambda e: e.tensor_tensor(out=LB[:], in0=LGt[:, 0, :], in1=LGt[:, 1, :], op=ALU.add), reads=[kdmb[0]], writes=[cb3])
        S.op("dve", lambda e: e.tensor_tensor(out=LB[:], in0=LB[:], in1=LGt[:, 2, :], op=ALU.add), reads=[kdmb[0], cb3], writes=[cb3])
        S.op("dve", lambda e: e.tensor_tensor(out=LB[:], in0=LB[:], in1=LGt[:, 3, :], op=ALU.add), reads=[kdmb[0], cb3], writes=[cb3])
        S.op("dve", lambda e: e.reciprocal(out=LB[:], in_=LB[:]), reads=[cb3], writes=[cb3])
        S.op("dve", lambda e: e.tensor_tensor(out=LB[:], in0=LB[:], in1=LGt[:, 1, :], op=ALU.mult), reads=[kdmb[0], cb3], writes=[cb3] + kdmb[0:8])
        S.op("dve", lambda e: e.tensor_scalar(out=OML[:], in0=LB[:], scalar1=-1.0, scalar2=1.0, op0=ALU.mult, op1=ALU.add), reads=[cb3], writes=[cb3])
        S.op("dve", lambda e: e.tensor_copy(out=TRIb[:], in_=HM[:, 0:128]), reads=[cb3], writes=[cb3])
        S.op("dve", lambda e: e.tensor_copy(out=TRIsb[:], in_=HMS[:, 0:64]), reads=[cb3], writes=[cb3])
        for h in range(8):
            S.op("dve", lambda e, h=h: e.memset(S32[0][:, h, :], 0.0), writes=[sb_[0][h]])
        S.op("dve", lambda e: e.memset(SB16[0][:], 0.0), writes=[sbb[0]])
        def do_norm(idx):
            t0, tn = HTILES[idx]
            ti = t0 // 512
            HTd, hbd = HTs[idx % 2], hbs[idx % 2]
            S.op("act", lambda e: e.activation(out=SQN2[:, :, 0:tn], in_=X[:, :, t0:t0 + tn], func=AF.Square), reads=[xbuf[ti]], writes=[sqnb])
            pt, pb = ps()
            for c in range(KC):
                S.op("pe", lambda e, c=c: e.matmul(pt[:, 0:tn], lhsT=ones_bf[:], rhs=SQN2[:, c, 0:tn], start=(c == 0), stop=(c == KC - 1)), reads=[sqnb, cbuf], writes=[pb])
            S.op("act", lambda e: e.activation(out=RSN2[:, 0:tn], in_=pt[:, 0:tn], func=AF.Ln, bias=epsb[:], scale=1.0 / D), reads=[pb, cbuf], writes=[rsnb])
            S.op("act", lambda e: e.activation(out=RSN2[:, 0:tn], in_=RSN2[:, 0:tn], func=AF.Exp, scale=-0.5), reads=[rsnb], writes=[rsnb])
            for c in range(KC):
                S.op("dve", lambda e, c=c: e.scalar_tensor_tensor(out=HTd[:, c, 0:tn], in0=X[:, c, t0:t0 + tn], scalar=NORMW[:, layer, c:c + 1], in1=RSN2[:, 0:tn],
                                                                op0=ALU.mult, op1=ALU.mult), reads=[xbuf[ti], rsnb, cbuf], writes=[hbd])

        do_norm(0)
        cur = 0
        for hi_, (t0, tn) in enumerate(HTILES):
            samp = (t0 == TP)
            ti = t0 // 512
            HT, hb = HTs[hi_ % 2], hbs[hi_ % 2]
            for g in range(4):
                wv, wb_ = wget(q_idx[g], [128, KC, 256])
                for cc in range(2):
                    c = 2 * g + cc
                    pt, pb = ps()
                    for kc in range(KC):
                        S.op("pe", lambda e, pt=pt, wv=wv, cc=cc, kc=kc, tn=tn, HT=HT: e.matmul(pt[:, 0:tn], lhsT=wv[:, kc, cc * 128:(cc + 1) * 128], rhs=HT[:, kc, 0:tn],
                                                                                       start=(kc == 0), stop=(kc == KC - 1)), reads=[wb_, hb], writes=[pb])
                    S.op("act", lambda e, pt=pt, c=c, tn=tn: e.activation(out=SQ[:, c, 0:tn], in_=pt[:, 0:tn], func=AF.Silu), reads=[pb], writes=[sqb])
            for g in range(4):
                wv, wb_ = wget(g_idx[g], [128, KC, 256])
                for cc in range(2):
                    c = 2 * g + cc
                    pt, pb = ps()
                    for kc in range(KC):
                        S.op("pe", lambda e, pt=pt, wv=wv, cc=cc, kc=kc, tn=tn, HT=HT: e.matmul(pt[:, 0:tn], lhsT=wv[:, kc, cc * 128:(cc + 1) * 128], rhs=HT[:, kc, 0:tn],
                                                                                       start=(kc == 0), stop=(kc == KC - 1)), reads=[wb_, hb], writes=[pb])
                    S.op("act", lambda e, pt=pt, c=c, tn=tn: e.activation(out=GS[:, c, 0:tn], in_=pt[:, 0:tn], func=AF.Silu), reads=[pb], writes=[gsb])
                    S.op("dve", lambda e, c=c, tn=tn: e.tensor_scalar(out=GS[:, c, 0:tn], in0=GS[:, c, 0:tn], scalar1=NW[:, c:c + 1], scalar2=None, op0=ALU.mult), reads=[gsb, cb3], writes=[gsb])
            nblk = 1 if samp else 2
            rows = 64 if samp else 128
            fctr = 0
            swpf = SWP()
            for g in range(4):
                wv, wb_ = wget(f_idx[g], [128, KC, 256])
                ftails = []
                for bi in range(nblk + 1):
                    if bi == nblk:
                        for ft_ in ftails:
                            ft_()
                        break
                    pt, pb = ps()
                    for kc in range(KC):
                        S.op("pe", lambda e, pt=pt, wv=wv, kc=kc, bi=bi, rows=rows, HT=HT: e.matmul(pt[0:rows, 0:256], lhsT=HT[:, kc, bi * 128:bi * 128 + rows], rhs=wv[:, kc, :],
                                                                                           start=(kc == 0), stop=(kc == KC - 1)), reads=[wb_, hb], writes=[pb])
                    fq = fctr % 2; fctr += 1
                    ft = FT[fq]
                    cs = slice(g * 256, (g + 1) * 256)
                    S.op("act", lambda e, pt=pt, ft=ft, rows=rows: e.activation(out=ft[0:rows, :], in_=pt[0:rows, 0:256], func=AF.Exp, scale=-1.0), reads=[pb], writes=[ftb[fq]])
                    S.op("act", lambda e, ft=ft, rows=rows: e.activation(out=ft[0:rows, :], in_=ft[0:rows, :], func=AF.Ln, bias=oneb[0:rows, :], scale=1.0), reads=[ftb[fq], cbuf], writes=[ftb[fq]])
                    S.op("act", lambda e, ft=ft, rows=rows: e.activation(out=ft[0:rows, :], in_=ft[0:rows, :], func=AF.Exp, scale=-1.0), reads=[ftb[fq]], writes=[ftb[fq]])
                    S.op("dve", lambda e, ft=ft, rows=rows, cs=cs: e.tensor_tensor(out=ft[0:rows, :], in0=ft[0:rows, :], in1=OML[0:rows, cs], op=ALU.mult), reads=[ftb[fq], cb3], writes=[ftb[fq]])
                    S.op("dve", lambda e, ft=ft, rows=rows, cs=cs: e.tensor_tensor(out=ft[0:rows, :], in0=ft[0:rows, :], in1=LB[0:rows, cs], op=ALU.add), reads=[ftb[fq], cb3], writes=[ftb[fq]])
                    def ftail(ft=ft, rows=rows, cs=cs, bi=bi, fq=fq):
                        S.op("act", lambda e: e.activation(out=LF[0:rows, bi, cs], in_=ft[0:rows, :], func=AF.Ln), reads=[ftb[fq]], writes=[lfb[bi]])
                        S.op("act", lambda e: e.activation(out=L1[0:rows, bi, cs], in_=ft[0:rows, :], func=AF.Ln, scale=-1.0, bias=oneb[0:rows, :]), reads=[ftb[fq], cbuf], writes=[l1b[bi]])
                    ftails.append(ftail)
            swpf.flush()
            for g in range(4):
                wv, wb_ = wget(i_idx[g], [128, KC, 256])
                for bi in range(nblk):
                    pt, pb = ps()
                    for kc in range(KC):
                        S.op("pe", lambda e, pt=pt, wv=wv, kc=kc, bi=bi, rows=rows, HT=HT: e.matmul(pt[0:rows, 0:256], lhsT=HT[:, kc, bi * 128:bi * 128 + rows], rhs=wv[:, kc, :],
                                                                                           start=(kc == 0), stop=(kc == KC - 1)), reads=[wb_, hb], writes=[pb])
                    S.op("act", lambda e, pt=pt, rows=rows, bi=bi, g=g: e.activation(out=V[0:rows, bi, g * 256:(g + 1) * 256], in_=pt[0:rows, 0:256], func=AF.Copy), reads=[pb], writes=[vb_[bi]])
            if hi_ + 1 < len(HTILES):
                do_norm(hi_ + 1)
            for bi in range(nblk):
                c0 = bi * 128
                if samp:
                    R, NCH = 64, 16
                    tri, tri2, ntri, cmm, trib = HMS[:, 0:64], HMS[:, 64:128], HMS[:, 128:192], HMS[:, 192:208], TRIsb
                else:
                    R, NCH = 128, 8
                    tri, tri2, ntri, cmm, trib = HM[:, 0:128], HM[:, 128:256], HM[:, 256:384], HM[:, 384:392], TRIb
                CH = R // NCH
                for hg in range(2):
                    pt, pb = ps()
                    for hh in range(4):
                        h = hg * 4 + hh
                        S.op("pe", lambda e, pt=pt, hh=hh, h=h, bi=bi, R=R, tri=tri: e.matmul(pt[:, hh * 128:hh * 128 + R], lhsT=LF[0:R, bi, h * 128:(h + 1) * 128], rhs=tri[0:R, :],
                                                                                          start=True, stop=True), reads=[lfb[bi], cb3], writes=[pb])
                    S.op("act", lambda e, pt=pt, hg=hg, R=R: e.activation(out=E[:, hg * 4:hg * 4 + 4, 0:R], in_=pt[:, :].rearrange("p (a b) -> p a b", a=4)[:, :, 0:R], func=AF.Exp),
                         reads=[pb], writes=[eb])
                S.op("dve", lambda e, R=R, NCH=NCH, CH=CH: e.tensor_copy(out=DEC[:, :, 0:NCH], in_=E[:, :, CH - 1:R:CH]), reads=[eb], writes=[decb])
                for hg in range(2):
                    pt, pb = ps()
                    for hh in range(4):
                        h = hg * 4 + hh
                        S.op("pe", lambda e, pt=pt, hh=hh, h=h, bi=bi, R=R, ntri=ntri: e.matmul(pt[:, hh * 128:hh * 128 + R], lhsT=LF[0:R, bi, h * 128:(h + 1) * 128], rhs=ntri[0:R, :],
                                                                                            start=(hh == 0), stop=False, skip_group_check=True), reads=[lfb[bi], cb3], writes=[pb])
                        S.op("pe", lambda e, pt=pt, hh=hh, h=h, bi=bi, R=R: e.matmul(pt[:, hh * 128:hh * 128 + R], lhsT=L1[0:R, bi, h * 128:(h + 1) * 128], rhs=ident_f[0:R, 0:R],
                                                                                 start=False, stop=True, skip_group_check=True), reads=[l1b[bi], cbuf], writes=[pb])
                    S.op("act", lambda e, pt=pt, hg=hg, R=R: e.activation(out=KI[:, hg * 4:hg * 4 + 4, 0:R], in_=pt[:, :].rearrange("p (a b) -> p a b", a=4)[:, :, 0:R], func=AF.Exp),
                         reads=[pb], writes=[kib])
                for hg in range(2):
                    pt, pb = ps()
                    S.op("pe", lambda e, pt=pt, hg=hg, bi=bi, R=R, tri2=tri2: e.matmul(pt[0:R, :], lhsT=tri2[0:R, :], rhs=LF[0:R, bi, hg * 512:(hg + 1) * 512], start=True, stop=False),
                         reads=[lfb[bi], cb3], writes=[pb])
                    S.op("pe", lambda e, pt=pt, hg=hg, bi=bi, R=R: e.matmul(pt[0:R, :], lhsT=ident_f[0:R, 0:R], rhs=L1[0:R, bi, hg * 512:(hg + 1) * 512], start=False, stop=True),
                         reads=[l1b[bi], cbuf], writes=[pb])
                    S.op("act", lambda e, pt=pt, hg=hg, R=R: e.activation(out=KD[0:R, hg * 512:(hg + 1) * 512], in_=pt[0:R, :], func=AF.Exp), reads=[pb], writes=[kdb])
                S.op("dve", lambda e, R=R, c0=c0: e.tensor_tensor(out=E[:, :, 0:R], in0=E[:, :, 0:R], in1=SQ[:, :, c0:c0 + R], op=ALU.mult), reads=[eb, sqb, decb], writes=[eb])
                S.op("act", lambda e, R=R: e.copy(out=QIb[:, :, 0:R], in_=E[:, :, 0:R]), reads=[eb], writes=[qib])
                for c in range(0 if samp else NCH):
                    S.op("act", lambda e, c=c, R=R, cmm=cmm: e.activation(out=KDM[0:R, c, :], in_=KD[0:R, :], func=AF.Copy, scale=cmm[0:R, c:c + 1]),
                         reads=[kdb, cb3], writes=[kdmb[c]])
                for hg in range(2):
                    pt, pb = ps()
                    for hh in range(4):
                        h = hg * 4 + hh
                        S.op("pe", lambda e, pt=pt, hh=hh, h=h, R=R: e.matmul(pt[0:R, hh * 128:hh * 128 + R], lhsT=KI[:, h, 0:R], rhs=QIb[:, h, 0:R], start=True, stop=True),
                             reads=[kib, qib], writes=[pb])
                    S.op("dve", lambda e, pt=pt, hg=hg, R=R, trib=trib: e.tensor_tensor(out=AT[0:R, hg * 4:hg * 4 + 4, 0:R], in0=pt[0:R, :].rearrange("p (a b) -> p a b", a=4)[:, :, 0:R],
                                                                                       in1=trib[0:R, 0:R].unsqueeze(1).broadcast_to([R, 4, R]), op=ALU.mult), reads=[pb, cb3], writes=[atb])
                po, pob = ps_long()
                for h in range(8):
                    S.op("pe", lambda e, po=po, h=h, bi=bi, R=R: e.matmul(po[:, h * 128:h * 128 + R], lhsT=V[0:R, bi, h * 128:(h + 1) * 128], rhs=AT[0:R, h, 0:R],
                                                                       start=(h % 4 == 0), stop=False, skip_group_check=True), reads=[vb_[bi], atb], writes=[pob[h // 4]])
                if not samp:
                    for c in range(NCH):
                        nxt = 1 - cur
                        dps = []
                        for hg in range(2):
                            pt, pb = ps()
                            dps.append((pt, pb))
                            for hh in range(4):
                                h = hg * 4 + hh
                                S.op("pe", lambda e, pt=pt, hh=hh, h=h, c=c, bi=bi: e.matmul(pt[:, hh * 128:(hh + 1) * 128], lhsT=KDM[:, c, h * 128:(h + 1) * 128], rhs=V[:, bi, h * 128:(h + 1) * 128],
                                                                                         start=True, stop=True), reads=[kdmb[c], vb_[bi]], writes=[pb])
                        for h in range(8):
                            S.op("pe", lambda e, po=po, h=h, c=c, cur=cur: e.matmul(po[:, h * 128 + c * 16:h * 128 + (c + 1) * 16], lhsT=SB16[cur][:, h, :], rhs=QIb[:, h, c * 16:(c + 1) * 16],
                                                                                start=False, stop=(c == NCH - 1), skip_group_check=True), reads=[sbb[cur], qib], writes=[pob[h // 4]])
                        for hg in range(2):
                            pt, pb = dps[hg]
                            for hh in range(4):
                                h = hg * 4 + hh
                                S.op("dve", lambda e, pt=pt, hh=hh, h=h, c=c, cur=cur, nxt=nxt: e.scalar_tensor_tensor(out=S32[nxt][:, h, :], in0=S32[cur][:, h, :], scalar=DEC[:, h, c:c + 1],
                                                                                                                in1=pt[:, hh * 128:(hh + 1) * 128], op0=ALU.mult, op1=ALU.add),
                                     reads=[sb_[cur][h], decb, pb], writes=[sb_[nxt][h]])
                        S.op("act", lambda e, nxt=nxt: e.copy(out=SB16[nxt][:], in_=S32[nxt][:]), reads=sb_[nxt], writes=[sbb[nxt]])
                        cur = nxt
                else:
                    for b in range(NB):
                        q2 = b % 2
                        S.dma("sp", S0[q2][:], hgS_d[b], writes=[s0b[q2]])
                        for h in range(8):
                            S.op("pe", lambda e, po=po, h=h, b=b, q2=q2: e.matmul(po[:, h * 128 + b * 4:h * 128 + (b + 1) * 4], lhsT=S0[q2][:, h, :], rhs=E[:, h, b * 4:(b + 1) * 4],
                                                                              start=False, stop=(b == NB - 1), skip_group_check=True), reads=[s0b[q2], eb], writes=[pob[h // 4]])
                        SNb = S32[q2]
                        S.op("act", lambda e, b=b, cmm=cmm: e.activation(out=KDM[0:64, b % 8, :], in_=KD[0:64, :], func=AF.Copy, scale=cmm[0:64, b:b + 1]),
                             reads=[kdb, cb3], writes=[kdmb[b % 8]])
                        for hg in range(2):
                            pt, pb = ps()
                            for hh in range(4):
                                h = hg * 4 + hh
                                S.op("pe", lambda e, pt=pt, hh=hh, h=h, b=b: e.matmul(pt[:, hh * 128:(hh + 1) * 128], lhsT=KDM[0:64, b % 8, h * 128:(h + 1) * 128], rhs=V[0:64, 0, h * 128:(h + 1) * 128],
                                                                                  start=True, stop=True), reads=[kdmb[b % 8], vb_[0]], writes=[pb])
                            for hh in range(4):
                                h = hg * 4 + hh
                                S.op("dve", lambda e, pt=pt, hh=hh, h=h, b=b, q2=q2, SNb=SNb: e.scalar_tensor_tensor(out=SNb[:, h, :], in0=S0[q2][:, h, :], scalar=DEC[:, h, b:b + 1],
                                                                                                             in1=pt[:, hh * 128:(hh + 1) * 128], op0=ALU.mult, op1=ALU.add),
                                     reads=[s0b[q2], decb, pb], writes=[sb_[q2][h]])
                        S.dma("sp", hgSo_d[b], SNb[:], reads=sb_[q2], is_output=True)
                S.op("act", lambda e, po=po, R=R: e.activation(out=SQO[:, :].rearrange("p (a b) -> p a b", a=8)[:, :, 0:R], in_=po.rearrange("p (a b) -> p a b", a=8)[:, :, 0:R], func=AF.Square),
                     reads=pob, writes=[sqob])
                for hg in range(2):
                    pt, pb = ps()
                    S.op("pe", lambda e, pt=pt, hg=hg: e.matmul(pt[:, :], lhsT=ones_bf[:], rhs=SQO[:, hg * 512:(hg + 1) * 512], start=True, stop=True), reads=[sqob, cbuf], writes=[pb])
                    S.op("act", lambda e, pt=pt, hg=hg: e.activation(out=RSTD[:, hg * 512:(hg + 1) * 512], in_=pt[:, :], func=AF.Ln, bias=epsb[:], scale=1.0 / 128), reads=[pb, cbuf], writes=[rstdb])
                S.op("act", lambda e: e.activation(out=RSTD[:, :], in_=RSTD[:, :], func=AF.Exp, scale=-0.5), reads=[rstdb], writes=[rstdb])
                S.op("dve", lambda e, po=po, R=R: e.tensor_tensor(out=RSTD[:, :].rearrange("p (a b) -> p a b", a=8)[:, :, 0:R], in0=po.rearrange("p (a b) -> p a b", a=8)[:, :, 0:R],
                                                                 in1=RSTD[:, :].rearrange("p (a b) -> p a b", a=8)[:, :, 0:R], op=ALU.mult), reads=pob + [rstdb], writes=[rstdb])
                S.op("dve", lambda e, R=R, c0=c0: e.tensor_tensor(out=OT[:, :, c0:c0 + R], in0=RSTD[:, :].rearrange("p (a b) -> p a b", a=8)[:, :, 0:R], in1=GS[:, :, c0:c0 + R], op=ALU.mult),
                     reads=[rstdb, gsb], writes=[ob])
            for g in range(4):
                wv, wb_ = wget(o_idx[g], [128, KC, 256])
                for cc in range(2):
                    c = 2 * g + cc
                    pt, pb = ps()
                    for kc in range(KC):
                        S.op("pe", lambda e, pt=pt, wv=wv, cc=cc, kc=kc, tn=tn: e.matmul(pt[:, 0:tn], lhsT=wv[:, kc, cc * 128:(cc + 1) * 128], rhs=OT[:, kc, 0:tn],
                                                                                       start=(kc == 0), stop=(kc == KC - 1)), reads=[wb_, ob], writes=[pb])
                    S.op("dve", lambda e, pt=pt, c=c, t0=t0, tn=tn: e.tensor_tensor(out=X[:, c, t0:t0 + tn], in0=X[:, c, t0:t0 + tn], in1=pt[:, 0:tn], op=ALU.add),
                         reads=[pb, xbuf[ti]], writes=[xbuf[ti]])
            if t0 + tn == TP:
                S.dma("act", hgP_d, S32[cur][:], reads=sb_[cur], is_output=True)

    def _ssd_masks():
        r = np.arange(128)
        triu = (r[:, None] <= r[None, :]).astype(np.float32)
        tril2 = (r[:, None] > r[None, :]).astype(np.float32)
        negm = np.where(r[:, None] <= r[None, :], 0.0, NEG).astype(np.float32)
        return np.concatenate([triu, tril2, np.tile(negm, (1, 4)), np.ones((128, 128), np.float32)], axis=1)

    def _ssd_masks_s():
        r = np.arange(64)
        same = (r[:, None] // 4) == (r[None, :] // 4)
        triu = (same & (r[:, None] <= r[None, :])).astype(np.float32)
        tril2 = (same & (r[:, None] > r[None, :])).astype(np.float32)
        negm = np.where(same & (r[:, None] <= r[None, :]), 0.0, NEG).astype(np.float32)
        seq = ((r[:, None] // 4) == np.arange(16)[None, :]).astype(np.float32)
        return np.concatenate([triu, tril2, np.tile(negm, (1, 4)), seq], axis=1)

    def _sel8():
        s_ = np.zeros((8, 8, 128), np.float32)
        for h in range(8):
            s_[h, h, :] = 1.0
        return s_

    if NL > 2 and CFG["mixers"][2]:
        smask_d = P.din("smask", [128, 896], lambda I, c: _ssd_masks())
        smasks_d = P.din("smasks", [64, 400], lambda I, c: _ssd_masks_s())
        sel8_d = P.din("sel8", [8, 8, 128], lambda I, c: _sel8())
        scw_d = P.din("scw", [128, 24, 5], lambda I, c: np.concatenate([I["ssd_conv_w"][0], I["ssd_conv_b"]], axis=0).reshape(5, 24, 128).transpose(2, 1, 0))
        sfb_d = P.din("sfb", [128, 24, NB, 3], lambda I, c: I["state_ssm_conv"][0, c * NB:(c + 1) * NB].reshape(NB, 3, 24, 128).transpose(3, 2, 0, 1))
        svec_d = P.din("svec", [128, 96], lambda I, c: np.concatenate([
            np.broadcast_to(I["ssd_dt_bias"][0][None], (128, 32)), np.broadcast_to(I["ssd_a_log"][0][None], (128, 32)),
            np.repeat(I["ssd_d"][0].reshape(16, 2), 64, axis=1).T, I["ssd_norm_w"][0].reshape(16, 128).T], axis=1))
        ssmS_d = P.din("ssmS", [NB, 128, 32, 64], lambda I, c: I["state_ssm"][0, c * NB:(c + 1) * NB].transpose(0, 3, 1, 2))
        ssmP_d = P.dout("ssmP", [128, 32, 64])
        ssmSo_d = P.dout("ssmSo", [NB, 128, 32, 64])
        sconv_d = P.dout("sconvT", [128, 24, 3 + 3 * NB])

    def ssd(layer):
        Win = lambda I: I["ssd_w_in"][0]
        def wt(name, c0):
            return wtile((name, c0), 2048, (lambda I, c0=c0: Win(I)[:, c0:c0 + 256].reshape(KC, 128, 256).transpose(1, 0, 2).reshape(128, -1)))
        z_idx = [wt("sz", 256 * g) for g in range(8)]
        x_idx = [wt("sx", 2048 + 256 * g) for g in range(12)]
        dt_idx = wtile(("sdt",), 256, (lambda I: Win(I)[:, 5120:5152].reshape(KC, 128, 32).transpose(1, 0, 2).reshape(128, -1)))
        o_idx = [wtile(("so", c), 2048, (lambda I, c=c: I["ssd_w_o"][0][:, c * 128:(c + 1) * 128].reshape(16, 128, 128).transpose(1, 0, 2).reshape(128, -1))) for c in range(8)]
        ar = Bump(ARENA, ARENA_SZ)
        TN = 256
        tile_off = ar.take(32768)
        def tile_bufs(tn_alloc):
            b_ = Bump(tile_off, 32768)
            return (P.sb(b_.take(KC * tn_alloc * 2), [128, KC, tn_alloc], BF16), P.sb(b_.take(16 * tn_alloc * 2), [128, 16, tn_alloc], BF16),
                    P.sb(b_.take(16 * tn_alloc * 2), [128, 16, tn_alloc], BF16), P.sb(b_.take(8 * tn_alloc * 2), [128, 8, tn_alloc], BF16),
                    P.sb(b_.take(16 * tn_alloc * 2), [128, 16, tn_alloc], BF16), b_)
        RSn = ar.take(1024)
        AFB = [P.sb(ar.take(259 * 4), [128, 259], F32) for _ in range(2)]
        AFS = [P.sb(ar.take(NB * 7 * 4), [128, NB, 7], F32) for _ in range(2)]
        U = [P.sb(ar.take(1024), [128, 256], F32) for _ in range(2)]
        CAR = P.sb(ar.take(24 * 3 * 4), [128, 24, 3], F32)
        FB = P.sb(ar.take(24 * NB * 3 * 4), [128, 24, NB, 3], F32)
        SCO = P.sb(ar.take(24 * 51 * 4), [128, 24, 51], F32)
        XDT = P.sb(ar.take(4096), [128, 32, 64], BF16)
        XDEC = P.sb(ar.take(4096), [128, 32, 64], BF16)
        BT = P.sb(ar.take(1024), [128, 4, 128], BF16)
        BTM = [P.sb(ar.take(1024), [64, 4, 128], BF16) for _ in range(2)]
        DTT = P.sb(ar.take(6 * 128), [128, 6, 32], F32)
        aseq_off = ar.take(2048)
        ASEQ = P.sb(aseq_off, [64, 16, 32], F32)
        els_off = ar.take(2048)
        ELS = P.sb(els_off, [128, 16, 32], F32)
        ACGH = P.sb(aseq_off, [8, 4, 128], BF16)
        ACGL = P.sb(aseq_off + 1024, [8, 4, 128], BF16)
        SELB = P.sb(els_off, [8, 8, 128], BF16)
        ACG4 = P.sb(ar.take(2048), [8, 4, 128], F32)
        EBs = [P.sb(ar.take(2048), [128, 8, 128], BF16) for _ in range(2)]
        LTs = [P.sb(ar.take(2048), [128, 8, 128], BF16) for _ in range(2)]
        MTs = [P.sb(ar.take(2048), [128, 8, 128], BF16) for _ in range(2)]
        CDECs = [P.sb(ar.take(2048), [128, 8, 128], BF16) for _ in range(2)]
        YGs = [P.sb(ar.take(2048), [128, 4, 128], F32) for _ in range(2)]
        SQYs = [P.sb(ar.take(1024), [128, 4, 128], BF16) for _ in range(2)]
        RSTDs = [P.sb(ar.take(512), [128, 128], F32) for _ in range(2)]
        EB, LT, MT, CDEC, YG, SQY, RSTD = EBs[0], LTs[0], MTs[0], CDECs[0], YGs[0], SQYs[0], RSTDs[0]
        st_off = ar.take(8192)
        ST = P.sb(st_off, [128, 32, 64], F32)
        stb_off = ar.take(4096)
        STb = P.sb(stb_off, [128, 32, 64], BF16)
        CDECF = P.sb(st_off, [128, 32, 64], F32)
        EBF = P.sb(stb_off, [128, 8, 128], F32)
        CBS4 = P.sb(ar.take(2048), [128, 4, 128], F32)
        cbsb = Buf()
        SM = P.sb(ar.take(896 * 4), [128, 896], F32)
        SMS = P.sb(ar.take(400 * 4), [64, 400], F32)
        SEL = P.sb(ar.take(4096), [8, 8, 128], F32)
        CW = P.sb(ar.take(24 * 5 * 4), [128, 24, 5], F32)
        SV = P.sb(ar.take(96 * 4), [128, 96], F32)
        NEGA = P.sb(ar.take(128), [128, 32], F32)
        NEGMB = P.sb(ar.take(1024), [128, 512], BF16)
        cb4 = Buf()
        S.dma("act", SM[:], smask_d, writes=[cb4])
        S.dma("act", SMS[:], smasks_d, writes=[cb4])
        S.dma("act", SEL[:], sel8_d, writes=[cb4])
        S.dma("act", CW[:], scw_d, writes=[cb4])
        S.dma("act", SV[:], svec_d, writes=[cb4])
        S.dma("act", FB[:], sfb_d, writes=[cb4])
        S.op("act", lambda e: e.activation(out=NEGA[:], in_=SV[:, 32:64], func=AF.Exp), reads=[cb4], writes=[cb4])
        S.op("dve", lambda e: e.tensor_scalar(out=NEGA[:], in0=NEGA[:], scalar1=-1.0, scalar2=None, op0=ALU.mult), reads=[cb4], writes=[cb4])
        S.op("dve", lambda e: e.memset(CAR[:], 0.0), writes=[cb4])
        S.op("dve", lambda e: e.tensor_copy(out=NEGMB[:], in_=SM[:, 256:768]), reads=[cb4], writes=[cb4])
        S.op("dve", lambda e: e.tensor_copy(out=SELB[:], in_=SEL[:]), reads=[cb4], writes=[cb4])
        S.op("dve", lambda e: e.memset(ST[:], 0.0), writes=[cb4])
        S.op("dve", lambda e: e.memset(STb[:], 0.0), writes=[cb4])
        DTB, DSK, SNW = SV[:, 0:32], SV[:, 64:80], SV[:, 80:96]
        hb, zsb, xtb, bctb, ob = Buf(), Buf(), Buf(), Buf(), Buf()
        afb, afsb, ub = [Buf(), Buf()], [Buf(), Buf()], [Buf(), Buf()]
        carb, scob = Buf(), Buf()
        xdtb, xdecb, btb, dttb, acgb, ebb, ltb, mtb, cdb, ygb, sqyb, rsb2, stb_, stbb, tmpb = [Buf() for _ in range(15)]
        btmb = [Buf(), Buf()]
        aseqb, elsb = Buf(), Buf()
        def finish_group(g, py, pyb, c0, R, ZS, XT, OT):
            for cl in range(4):
                c = 4 * g + cl
                S.op("dve", lambda e, cl=cl, c=c: e.scalar_tensor_tensor(out=YG[:, cl, 0:R], in0=XT[:, c, c0:c0 + R], scalar=DSK[:, c:c + 1], in1=py[:, cl * 128:cl * 128 + R],
                                                                        op0=ALU.mult, op1=ALU.add), reads=[xtb, cb4, pyb], writes=[ygb])
            S.op("dve", lambda e: e.tensor_tensor(out=YG[:, :, 0:R], in0=YG[:, :, 0:R], in1=ZS[:, 4 * g:4 * g + 4, c0:c0 + R], op=ALU.mult), reads=[ygb, zsb], writes=[ygb])
            S.op("act", lambda e: e.activation(out=SQY[:, :, 0:R], in_=YG[:, :, 0:R], func=AF.Square), reads=[ygb], writes=[sqyb])
            pt, pb = ps()
            for cl in range(4):
                S.op("pe", lambda e, cl=cl, pt=pt: e.matmul(pt[:, 0:R], lhsT=ones_bf[:], rhs=SQY[:, cl, 0:R], start=(cl == 0), stop=(cl == 3)), reads=[sqyb, cbuf], writes=[pb])
            S.op("act", lambda e, pt=pt: e.activation(out=RSTD[:, 0:R], in_=pt[:, 0:R], func=AF.Ln, bias=epsb[:], scale=1.0 / 512), reads=[pb, cbuf], writes=[rsb2])
            S.op("act", lambda e: e.activation(out=RSTD[:, 0:R], in_=RSTD[:, 0:R], func=AF.Exp, scale=-0.5), reads=[rsb2], writes=[rsb2])
            for cl in range(4):
                c = 4 * g + cl
                S.op("dve", lambda e, cl=cl, c=c: e.scalar_tensor_tensor(out=OT[:, c, c0:c0 + R], in0=YG[:, cl, 0:R], scalar=SNW[:, c:c + 1], in1=RSTD[:, 0:R],
                                                                        op0=ALU.mult, op1=ALU.mult), reads=[ygb, cb4, rsb2], writes=[ob])

        slot_bufs = [[Buf() for _ in range(7)] for _ in range(2)]

        def group_gen(slot, g, c0, BCT, ZS, XT, OT):
            R = 128
            EBx, LTx, MTx, CDx, YGx, SQx, RSx = EBs[slot], LTs[slot], MTs[slot], CDECs[slot], YGs[slot], SQYs[slot], RSTDs[slot]
            ebx, ltx, mtx, cdx, ygx, sqx, rsx = slot_bufs[slot]
            pbcs = [psb(2 * slot), psb(2 * slot + 1)]
            for hq in range(2):
                pbc, pbcb = pbcs[hq]
                for hh in range(4):
                    S.op("pe", lambda e, pbc=pbc, hh=hh, hq=hq: e.matmul(pbc[:, hh * 128:hh * 128 + R], lhsT=SELB[:, hq * 4 + hh, :], rhs=ACGH[:, g, 0:R], start=(hh == 0), stop=False,
                                                                       skip_group_check=True), reads=[acgb, cb4], writes=[pbcb])
                    S.op("pe", lambda e, pbc=pbc, hh=hh, hq=hq: e.matmul(pbc[:, hh * 128:hh * 128 + R], lhsT=SELB[:, hq * 4 + hh, :], rhs=ACGL[:, g, 0:R], start=False, stop=False,
                                                                       skip_group_check=True), reads=[acgb, cb4], writes=[pbcb])
            yield
            for hq in range(2):
                pbc, pbcb = pbcs[hq]
                S.op("act", lambda e, pbc=pbc, hq=hq: e.activation(out=EBx[:, hq * 4:hq * 4 + 4, 0:R], in_=pbc[:, :].rearrange("p (a b) -> p a b", a=4)[:, :, 0:R], func=AF.Exp),
                     reads=[pbcb], writes=[ebx])
            yield
            for hq in range(2):
                pbc, pbcb = pbcs[hq]
                S.op("pe", lambda e, pbc=pbc: e.matmul(pbc[:, :], lhsT=ident_bf[:, :], rhs=NEGMB[:, :], start=False, stop=True, skip_group_check=True),
                     reads=[cb4, cbuf, ebx], writes=[pbcb])
            yield
            for hq in range(2):
                pbc, pbcb = pbcs[hq]
                for hh in range(4):
                    h = 8 * g + hq * 4 + hh
                    S.op("act", lambda e, pbc=pbc, hh=hh, hq=hq, h=h: e.activation(out=LTx[0:R, hq * 4 + hh, 0:R], in_=pbc[0:R, hh * 128:hh * 128 + R], func=AF.Exp,
                                                                                 bias=DTT[0:R, 2, h:h + 1], scale=1.0), reads=[pbcb, dttb], writes=[ltx])
            yield
            for hq in range(2):
                S.op("dve", lambda e, hq=hq: e.tensor_tensor(out=MTx[0:R, hq * 4:hq * 4 + 4, 0:R], in0=LTx[0:R, hq * 4:hq * 4 + 4, 0:R],
                                                            in1=CBS4[0:R, g, 0:R].unsqueeze(1).broadcast_to([R, 4, R]), op=ALU.mult), reads=[ltx, cbsb], writes=[mtx])
                S.op("dve", lambda e, hq=hq: e.tensor_tensor(out=CDx[:, hq * 4:hq * 4 + 4, :], in0=EBx[:, hq * 4:hq * 4 + 4, :],
                                                            in1=BCT[:, 4 + g, c0:c0 + 128].unsqueeze(1).broadcast_to([128, 4, 128]), op=ALU.mult), reads=[ebx, bctb], writes=[cdx])
            yield
            py, pyb = psb(4 + slot)
            for hl in range(8):
                h = 8 * g + hl
                half, cl = (hl % 2) * 64, hl // 2
                S.op("pe", lambda e, h=h, hl=hl, half=half, cl=cl: e.matmul(py[half:half + 64, cl * 128:cl * 128 + R], lhsT=XDT[0:R, h, :], rhs=MTx[0:R, hl, 0:R],
                                                                          start=(hl < 2), stop=False, tile_position=(0, half), skip_group_check=True), reads=[xdtb, mtx], writes=[pyb])
                S.op("pe", lambda e, h=h, hl=hl, half=half, cl=cl: e.matmul(py[half:half + 64, cl * 128:(cl + 1) * 128], lhsT=STb[:, h, :], rhs=CDx[:, hl, :],
                                                                          start=False, stop=True, tile_position=(0, half), skip_group_check=True), reads=[stbb, cdx], writes=[pyb])
            yield
            for cl in range(4):
                c = 4 * g + cl
                S.op("dve", lambda e, cl=cl, c=c: e.scalar_tensor_tensor(out=YGx[:, cl, 0:R], in0=XT[:, c, c0:c0 + R], scalar=DSK[:, c:c + 1], in1=py[:, cl * 128:cl * 128 + R],
                                                                        op0=ALU.mult, op1=ALU.add), reads=[xtb, cb4, pyb], writes=[ygx])
            S.op("dve", lambda e: e.tensor_tensor(out=YGx[:, :, 0:R], in0=YGx[:, :, 0:R], in1=ZS[:, 4 * g:4 * g + 4, c0:c0 + R], op=ALU.mult), reads=[ygx, zsb], writes=[ygx])
            S.op("act", lambda e: e.activation(out=SQx[:, :, 0:R], in_=YGx[:, :, 0:R], func=AF.Square), reads=[ygx], writes=[sqx])
            yield
            pt, pb = pbcs[0]
            for cl in range(4):
                S.op("pe", lambda e, cl=cl: e.matmul(pt[:, 0:R], lhsT=ones_bf[:], rhs=SQx[:, cl, 0:R], start=(cl == 0), stop=(cl == 3)), reads=[sqx, cbuf], writes=[pb])
            yield
            S.op("act", lambda e: e.activation(out=RSx[:, 0:R], in_=pt[:, 0:R], func=AF.Ln, bias=epsb[:], scale=1.0 / 512), reads=[pb, cbuf], writes=[rsx])
            S.op("act", lambda e: e.activation(out=RSx[:, 0:R], in_=RSx[:, 0:R], func=AF.Exp, scale=-0.5), reads=[rsx], writes=[rsx])
            yield
            for cl in range(4):
                c = 4 * g + cl
                S.op("dve", lambda e, cl=cl, c=c: e.scalar_tensor_tensor(out=OT[:, c, c0:c0 + R], in0=YGx[:, cl, 0:R], scalar=SNW[:, c:c + 1], in1=RSx[:, 0:R],
                                                                        op0=ALU.mult, op1=ALU.mult), reads=[ygx, cb4, rsx], writes=[ob])

        def run_groups(c0, BCT, ZS, XT, OT):
            pending = [0, 1, 2, 3]
            active = {}
            for slot in range(2):
                active[slot] = group_gen(slot, pending.pop(0), c0, BCT, ZS, XT, OT)
            while active:
                for slot in list(active.keys()):
                    try:
                        next(active[slot])
                    except StopIteration:
                        if pending:
                            active[slot] = group_gen(slot, pending.pop(0), c0, BCT, ZS, XT, OT)
                        else:
                            del active[slot]

        def out_proj(t0, tn, ti, OT):
            for c in range(KC):
                wv, wb_ = wget(o_idx[c], [128, 16, 128])
                pt, pb = ps()
                for kc in range(16):
                    S.op("pe", lambda e, pt=pt, wv=wv, kc=kc: e.matmul(pt[:, 0:tn], lhsT=wv[:, kc, :], rhs=OT[:, kc, 0:tn], start=(kc == 0), stop=(kc == 15)), reads=[wb_, ob], writes=[pb])
                S.op("dve", lambda e, pt=pt, c=c: e.tensor_tensor(out=X[:, c, t0:t0 + tn], in0=X[:, c, t0:t0 + tn], in1=pt[:, 0:tn], op=ALU.add), reads=[pb, xbuf[ti]], writes=[xbuf[ti]])

        uctr = 0
        gcnt = [0]
        for hi_, (t0, tn) in enumerate(HTILES):
            samp = (t0 == TP)
            ti = t0 // 512
            if samp:
                S.barrier()
                PSMODE[0] = 1
                ylist = []
            HT, ZS, XT, BCT, OT, tb_ = tile_bufs(64 if samp else TN)
            tna = 64 if samp else TN
            SQn = P.sb(tile_off + 48 * tna * 2, [128, KC, tna], BF16)
            RS = P.sb(RSn, [128, 256], F32)
            S.op("act", lambda e, t0=t0, tn=tn, SQn=SQn: e.activation(out=SQn[:, :, 0:tn], in_=X[:, :, t0:t0 + tn], func=AF.Square), reads=[xbuf[ti]], writes=[ob])
            pt, pb = ps()
            for c in range(KC):
                S.op("pe", lambda e, c=c, tn=tn, pt=pt, SQn=SQn: e.matmul(pt[:, 0:tn], lhsT=ones_bf[:], rhs=SQn[:, c, 0:tn], start=(c == 0), stop=(c == KC - 1)), reads=[ob, cbuf], writes=[pb])
            S.op("act", lambda e, tn=tn, pt=pt: e.activation(out=RS[:, 0:tn], in_=pt[:, 0:tn], func=AF.Ln, bias=epsb[:], scale=1.0 / D), reads=[pb, cbuf], writes=[rsb2])
            S.op("act", lambda e, tn=tn: e.activation(out=RS[:, 0:tn], in_=RS[:, 0:tn], func=AF.Exp, scale=-0.5), reads=[rsb2], writes=[rsb2])
            for c in range(KC):
                S.op("dve", lambda e, c=c, t0=t0, tn=tn, HT=HT: e.scalar_tensor_tensor(out=HT[:, c, 0:tn], in0=X[:, c, t0:t0 + tn], scalar=NORMW[:, layer, c:c + 1], in1=RS[:, 0:tn],
                                                                                       op0=ALU.mult, op1=ALU.mult), reads=[xbuf[ti], rsb2, cbuf], writes=[hb])
            for g in range(8):
                wv, wb_ = wget(z_idx[g], [128, KC, 256])
                for cc in range(2):
                    c = 2 * g + cc
                    pt, pb = ps()
                    for kc in range(KC):
                        S.op("pe", lambda e, pt=pt, wv=wv, cc=cc, kc=kc, tn=tn, HT=HT: e.matmul(pt[:, 0:tn], lhsT=wv[:, kc, cc * 128:(cc + 1) * 128], rhs=HT[:, kc, 0:tn],
                                                                                              start=(kc == 0), stop=(kc == KC - 1)), reads=[wb_, hb], writes=[pb])
                    S.op("act", lambda e, pt=pt, c=c, tn=tn, ZS=ZS: e.activation(out=ZS[:, c, 0:tn], in_=pt[:, 0:tn], func=AF.Silu), reads=[pb], writes=[zsb])
            swpc = SWP()
            swpg = SWP()
            for g in range(12):
                wv, wb_ = wget(x_idx[g], [128, KC, 256])
                for cc in range(2):
                    ch = 2 * g + cc
                    pt, pb = ps()
                    for kc in range(KC):
                        S.op("pe", lambda e, pt=pt, wv=wv, cc=cc, kc=kc, tn=tn, HT=HT: e.matmul(pt[:, 0:tn], lhsT=wv[:, kc, cc * 128:(cc + 1) * 128], rhs=HT[:, kc, 0:tn],
                                                                                              start=(kc == 0), stop=(kc == KC - 1)), reads=[wb_, hb], writes=[pb])
                    q = ch % 2
                    u, ubf = U[uctr % 2], ub[uctr % 2]; uctr += 1
                    if not samp:
                        A_, ab_ = AFB[q], afb[q]
                        S.op("act", lambda e, A_=A_, ch=ch: e.copy(out=A_[:, 0:3], in_=CAR[:, ch, :]), reads=[carb], writes=[ab_])
                        S.op("act", lambda e, A_=A_, pt=pt, tn=tn: e.activation(out=A_[:, 3:3 + tn], in_=pt[:, 0:tn], func=AF.Copy), reads=[pb], writes=[ab_])
                        srcs = [A_[:, k:k + tn] for k in range(4)]
                        uo = u[:, 0:tn]
                        S.op("act", lambda e, A_=A_, ch=ch, tn=tn: e.copy(out=CAR[:, ch, :], in_=A_[:, tn:tn + 3]), reads=[ab_], writes=[carb])
                        if t0 + tn == TP:
                            S.op("act", lambda e, A_=A_, ch=ch, tn=tn: e.copy(out=SCO[:, ch, 0:3], in_=A_[:, tn:tn + 3]), reads=[ab_], writes=[scob])
                        dst = (XT[:, ch, 0:tn] if ch < 16 else BCT[:, ch - 16, 0:tn])
                    else:
                        A_, ab_ = AFS[q], afsb[q]
                        S.op("act", lambda e, A_=A_, ch=ch: e.copy(out=A_[:, :, 0:3], in_=FB[:, ch, :, :]), reads=[cb4], writes=[ab_])
                        S.op("act", lambda e, A_=A_, pt=pt: e.activation(out=A_[:, :, 3:7], in_=pt[:, 0:TS].rearrange("p (b t) -> p b t", t=4), func=AF.Copy), reads=[pb], writes=[ab_])
                        srcs = [A_[:, :, k:k + 4] for k in range(4)]
                        uo = u[:, 0:TS].rearrange("p (b t) -> p b t", t=4)
                        S.op("act", lambda e, A_=A_, ch=ch: e.copy(out=SCO[:, ch, 3:51].rearrange("p (b j) -> p b j", j=3), in_=A_[:, :, 4:7]), reads=[ab_], writes=[scob])
                        dst = (XT[:, ch, 0:TS] if ch < 16 else BCT[:, ch - 16, 0:TS]).rearrange("p (b t) -> p b t", t=4)
                    S.op("dve", lambda e, uo=uo, s=srcs[3], ch=ch: e.tensor_scalar(out=uo, in0=s, scalar1=CW[:, ch, 3:4], scalar2=CW[:, ch, 4:5], op0=ALU.mult, op1=ALU.add),
                         reads=[ab_, cb4], writes=[ubf])
                    for k in (2, 1, 0):
                        S.op("dve", lambda e, uo=uo, s=srcs[k], ch=ch, k=k: e.scalar_tensor_tensor(out=uo, in0=s, scalar=CW[:, ch, k:k + 1], in1=uo, op0=ALU.mult, op1=ALU.add),
                             reads=[ab_, cb4, ubf], writes=[ubf])
                    def ctail(uo=uo, dst=dst, ubf=ubf, ch=ch):
                        S.op("act", lambda e: e.activation(out=dst, in_=uo, func=AF.Silu), reads=[ubf], writes=[xtb if ch < 16 else bctb])
                    swpc.step([lambda: None, ctail])
            swpc.flush()
            wvd, wbd = wget(dt_idx, [128, KC, 32])
            nblk = 1 if samp else 2
            for bi in range(nblk):
                c0 = bi * 128
                if samp:
                    R = 64
                    triu, tril2, negm4 = SMS[:, 0:64], SMS[:, 64:128], SMS[:, 128:384]
                else:
                    R = 128
                    triu, tril2, negm4 = SM[:, 0:128], SM[:, 128:256], SM[:, 256:768]
                onesf = SM[:, 768:896]
                pt, pb = ps()
                for kc in range(KC):
                    S.op("pe", lambda e, pt=pt, kc=kc, c0=c0, R=R, HT=HT, wvd=wvd: e.matmul(pt[0:R, 0:32], lhsT=HT[:, kc, c0:c0 + R], rhs=wvd[:, kc, :], start=(kc == 0), stop=(kc == KC - 1)),
                         reads=[wbd, hb], writes=[pb])
                S.op("dve", lambda e, pt=pt, R=R: e.tensor_tensor(out=DTT[0:R, 0, :], in0=pt[0:R, 0:32], in1=DTB[0:R, :], op=ALU.add), reads=[pb, cb4], writes=[dttb])
                S.op("act", lambda e, R=R: e.activation(out=DTT[0:R, 0, :], in_=DTT[0:R, 0, :], func=AF.Exp), reads=[dttb], writes=[dttb])
                S.op("act", lambda e, R=R: e.activation(out=DTT[0:R, 0, :], in_=DTT[0:R, 0, :], func=AF.Ln, bias=oneb[0:R, :], scale=1.0), reads=[dttb, cbuf], writes=[dttb])
                S.op("dve", lambda e, R=R: e.tensor_tensor(out=DTT[0:R, 1, :], in0=DTT[0:R, 0, :], in1=NEGA[0:R, :], op=ALU.mult), reads=[dttb, cb4], writes=[dttb])
                pt, pb = ps()
                S.op("pe", lambda e, pt=pt, R=R, triu=triu: e.matmul(pt[0:R, 0:32], lhsT=triu[0:R, :], rhs=DTT[0:R, 1, :], start=True, stop=True), reads=[dttb, cb4], writes=[pb])
                S.op("pe", lambda e, pt=pt, R=R, tril2=tril2: e.matmul(pt[0:R, 32:64], lhsT=tril2[0:R, :], rhs=DTT[0:R, 1, :], start=True, stop=True), reads=[dttb, cb4], writes=[pb])
                if not samp:
                    S.op("pe", lambda e, pt=pt, onesf=onesf: e.matmul(pt[:, 64:96], lhsT=onesf[:, :], rhs=DTT[:, 1, :], start=True, stop=True), reads=[dttb, cb4], writes=[pb])
                S.op("dve", lambda e, pt=pt, R=R: e.tensor_scalar(out=DTT[0:R, 2, :], in0=pt[0:R, 0:32], scalar1=-1.0, scalar2=None, op0=ALU.mult), reads=[pb], writes=[dttb])
                S.op("act", lambda e, pt=pt, R=R: e.activation(out=DTT[0:R, 3, :], in_=pt[0:R, 32:64], func=AF.Exp), reads=[pb], writes=[dttb])
                if not samp:
                    S.op("act", lambda e, pt=pt: e.activation(out=DTT[:, 4, :], in_=pt[:, 64:96], func=AF.Exp), reads=[pb], writes=[dttb])
                else:
                    S.op("dve", lambda e: e.tensor_tensor(out=ASEQ[:, :, :], in0=DTT[0:64, 1, :].unsqueeze(1).broadcast_to([64, 16, 32]),
                                                          in1=SMS[:, 384:400].unsqueeze(2).broadcast_to([64, 16, 32]), op=ALU.mult), reads=[dttb, cb4], writes=[aseqb])
                    pt2, pb2 = ps()
                    S.op("pe", lambda e, pt2=pt2, onesf=onesf: e.matmul(pt2[:, :], lhsT=onesf[0:64, :], rhs=ASEQ[:, :, :].rearrange("p a b -> p (a b)"), start=True, stop=True),
                         reads=[aseqb, cb4], writes=[pb2])
                    S.op("act", lambda e, pt2=pt2: e.activation(out=ELS[:, :, :].rearrange("p a b -> p (a b)"), in_=pt2[:, :], func=AF.Exp), reads=[pb2], writes=[elsb])
                for half in range(2):
                    ptp, ptpb = pst()
                    for cc in range(8):
                        c = half * 8 + cc
                        S.op("pe", lambda e, ptp=ptp, cc=cc, c=c, c0=c0, R=R, XT=XT: e.transpose(out=ptp[0:R, cc * 128:(cc + 1) * 128], in_=XT[:, c, c0:c0 + R], identity=ident_bf[:]),
                             reads=[xtb, cbuf], writes=[ptpb])
                    S.op("dve", lambda e, ptp=ptp, half=half, R=R: e.tensor_tensor(out=XDT[0:R, half * 16:(half + 1) * 16, :], in0=ptp[0:R, :].rearrange("p (a b) -> p a b", b=64),
                                                                                  in1=DTT[0:R, 0, half * 16:(half + 1) * 16].unsqueeze(2).broadcast_to([R, 16, 64]), op=ALU.mult),
                         reads=[ptpb, dttb], writes=[xdtb])
                S.op("dve", lambda e, R=R: e.tensor_tensor(out=XDEC[0:R, :, :], in0=XDT[0:R, :, :], in1=DTT[0:R, 3, :].unsqueeze(2).broadcast_to([R, 32, 64]), op=ALU.mult),
                     reads=[xdtb, dttb], writes=[xdecb])
                ptp, ptpb = pst()
                for g in range(4):
                    S.op("pe", lambda e, ptp=ptp, g=g, c0=c0, R=R, BCT=BCT: e.transpose(out=ptp[0:R, g * 128:(g + 1) * 128], in_=BCT[:, g, c0:c0 + R], identity=ident_bf[:]),
                         reads=[bctb, cbuf], writes=[ptpb])
                S.op("act", lambda e, ptp=ptp, R=R: e.activation(out=BT[0:R, :, :].rearrange("p a b -> p (a b)"), in_=ptp[0:R, 0:512], func=AF.Copy), reads=[ptpb], writes=[btb])
                pt, pb = ps()
                for g in range(4):
                    S.op("pe", lambda e, pt=pt, g=g, R=R, triu=triu: e.matmul(pt[0:8, g * 128:g * 128 + R], lhsT=DTT[0:R, 1, 8 * g:8 * g + 8], rhs=triu[0:R, :], start=True, stop=True),
                         reads=[dttb, cb4], writes=[pb])
                S.op("act", lambda e, pt=pt, R=R: e.activation(out=ACG4[:, :, 0:R], in_=pt[0:8, :].rearrange("p (a b) -> p a b", a=4)[:, :, 0:R], func=AF.Copy), reads=[pb], writes=[acgb])
                if not samp:
                    S.op("dve", lambda e: e.tensor_copy(out=ACGH[:], in_=ACG4[:]), reads=[acgb], writes=[acgb])
                    S.op("dve", lambda e: e.tensor_tensor(out=ACGL[:], in0=ACG4[:], in1=ACGH[:], op=ALU.subtract), reads=[acgb], writes=[acgb])
                pcb, pcbb = ps()
                for g in range(4):
                    S.op("pe", lambda e, pcb=pcb, g=g, c0=c0, R=R, BCT=BCT: e.matmul(pcb[0:R, g * 128:g * 128 + R], lhsT=BCT[:, g, c0:c0 + R], rhs=BCT[:, 4 + g, c0:c0 + R], start=True, stop=True),
                         reads=[bctb], writes=[pcbb])
                S.op("act", lambda e, pcb=pcb, R=R: e.activation(out=CBS4[0:R, :, 0:R], in_=pcb[0:R, :].rearrange("p (a b) -> p a b", a=4)[:, :, 0:R], func=AF.Copy), reads=[pcbb], writes=[cbsb])
                if not samp:
                    run_groups(c0, BCT, ZS, XT, OT)
                for g in (range(4) if samp else []):
                    EBx, ebx = (EBF, stbb) if samp else (EB, ebb)
                    pbcs = [ps(), ps()]
                    for hq in range(2):
                        pbc, pbcb = pbcs[hq]
                        for hh in range(4):
                            S.op("pe", lambda e, pbc=pbc, hh=hh, hq=hq, R=R, g=g: e.matmul(pbc[:, hh * 128:hh * 128 + R], lhsT=SEL[:, hq * 4 + hh, :], rhs=ACG4[:, g, 0:R], start=(hh == 0), stop=False,
                                                                                        skip_group_check=True), reads=[acgb, cb4], writes=[pbcb])
                    for hq in range(2):
                        pbc, pbcb = pbcs[hq]
                        S.op("act", lambda e, pbc=pbc, hq=hq, R=R, EBx=EBx: e.activation(out=EBx[:, hq * 4:hq * 4 + 4, 0:R], in_=pbc[:, :].rearrange("p (a b) -> p a b", a=4)[:, :, 0:R], func=AF.Exp),
                             reads=[pbcb], writes=[ebx])
                    for hq in range(2):
                        pbc, pbcb = pbcs[hq]
                        if samp:
                            for hh in range(4):
                                S.op("pe", lambda e, pbc=pbc, negm4=negm4, hh=hh: e.matmul(pbc[0:64, hh * 128:hh * 128 + 64], lhsT=ident_f[0:64, 0:64], rhs=negm4[0:64, 0:64],
                                                                                        start=False, stop=True, skip_group_check=True), reads=[cb4, cbuf, stbb], writes=[pbcb])
                        else:
                            S.op("pe", lambda e, pbc=pbc: e.matmul(pbc[:, :], lhsT=ident_bf[:, :], rhs=NEGMB[:, :], start=False, stop=True, skip_group_check=True),
                                 reads=[cb4, cbuf, ebb], writes=[pbcb])
                    for hq in range(2):
                        pbc, pbcb = pbcs[hq]
                        for hh in range(4):
                            h = 8 * g + hq * 4 + hh
                            S.op("act", lambda e, pbc=pbc, hh=hh, hq=hq, h=h, R=R: e.activation(out=LT[0:R, hq * 4 + hh, 0:R], in_=pbc[0:R, hh * 128:hh * 128 + R], func=AF.Exp,
                                                                                              bias=DTT[0:R, 2, h:h + 1], scale=1.0), reads=[pbcb, dttb], writes=[ltb])
                    for hq in range(2):
                        S.op("dve", lambda e, hq=hq, R=R, g=g: e.tensor_tensor(out=MT[0:R, hq * 4:hq * 4 + 4, 0:R], in0=LT[0:R, hq * 4:hq * 4 + 4, 0:R],
                                                                              in1=CBS4[0:R, g, 0:R].unsqueeze(1).broadcast_to([R, 4, R]), op=ALU.mult), reads=[ltb, cbsb], writes=[mtb])
                        if samp:
                            S.op("dve", lambda e, hq=hq, g=g, c0=c0, BCT=BCT: e.tensor_tensor(out=CDECF[:, 8 * g + hq * 4:8 * g + hq * 4 + 4, 0:64], in0=EBF[:, hq * 4:hq * 4 + 4, 0:64],
                                                                                            in1=BCT[:, 4 + g, 0:64].unsqueeze(1).broadcast_to([128, 4, 64]), op=ALU.mult), reads=[stbb, bctb], writes=[stb_])
                        else:
                            S.op("dve", lambda e, hq=hq, g=g, c0=c0, BCT=BCT: e.tensor_tensor(out=CDEC[:, hq * 4:hq * 4 + 4, :], in0=EB[:, hq * 4:hq * 4 + 4, :],
                                                                                            in1=BCT[:, 4 + g, c0:c0 + 128].unsqueeze(1).broadcast_to([128, 4, 128]), op=ALU.mult), reads=[ebb, bctb], writes=[cdb])
                    py, pyb = psb([4, 5, 0, 1][g]) if samp else psb(4 + gcnt[0] % 2)
                    gcnt[0] += 1
                    for hl in range(8):
                        h = 8 * g + hl
                        half, cl = (hl % 2) * 64, hl // 2
                        S.op("pe", lambda e, py=py, h=h, hl=hl, half=half, cl=cl, R=R: e.matmul(py[half:half + 64, cl * 128:cl * 128 + R], lhsT=XDT[0:R, h, :], rhs=MT[0:R, hl, 0:R],
                                                                                              start=(hl < 2), stop=False, tile_position=(0, half), skip_group_check=True), reads=[xdtb, mtb], writes=[pyb])
                        if not samp:
                            S.op("pe", lambda e, py=py, h=h, hl=hl, half=half, cl=cl: e.matmul(py[half:half + 64, cl * 128:(cl + 1) * 128], lhsT=STb[:, h, :], rhs=CDEC[:, hl, :],
                                                                                             start=False, stop=True, tile_position=(0, half), skip_group_check=True), reads=[stbb, cdb], writes=[pyb])
                    if samp:
                        ylist.append((g, py, pyb))
                        continue
                    swpg.step([lambda: None, (lambda g=g, py=py, pyb=pyb, c0=c0, R=R, ZS=ZS, XT=XT, OT=OT: finish_group(g, py, pyb, c0, R, ZS, XT, OT))])
                if samp:
                    S0 = [P.sb(tb_.take(8192), [128, 32, 64], F32) for _ in range(2)]
                    SN = P.sb(tb_.take(8192), [128, 32, 64], F32)
                    s0b = [Buf(), Buf()]
                    snb = Buf()
                    for b in range(NB):
                        q2 = b % 2
                        S.dma("sp", S0[q2][:], ssmS_d[b], writes=[s0b[q2]])
                        for (g, py, pyb) in ylist:
                            for hl in range(8):
                                h = 8 * g + hl
                                half, cl = (hl % 2) * 64, hl // 2
                                S.op("pe", lambda e, py=py, h=h, half=half, cl=cl, b=b, q2=q2: e.matmul(py[half:half + 64, cl * 128 + 4 * b:cl * 128 + 4 * b + 4], lhsT=S0[q2][:, h, :], rhs=CDECF[:, h, 4 * b:4 * b + 4],
                                                                                                    start=False, stop=(b == NB - 1), tile_position=(0, half), skip_group_check=True),
                                     reads=[s0b[q2], stb_], writes=[pyb])
                        S.op("dve", lambda e, b=b, q2=q2: e.tensor_tensor(out=SN[:, :, :], in0=S0[q2][:, :, :], in1=ELS[:, b, :].unsqueeze(2).broadcast_to([128, 32, 64]), op=ALU.mult),
                             reads=[s0b[q2], elsb], writes=[snb])
                        S.op("act", lambda e, b=b, q2=q2: e.activation(out=BTM[q2][:, :, :].rearrange("p a b -> p (a b)"), in_=BT[0:64, :, :].rearrange("p a b -> p (a b)"),
                                                                        func=AF.Copy, scale=SMS[:, 384 + b:385 + b]), reads=[btb, cb4], writes=[btmb[q2]])
                        for g in range(4):
                            pt, pb = ps()
                            S.op("pe", lambda e, pt=pt, g=g, q2=q2: e.matmul(pt[:, :], lhsT=BTM[q2][:, g, :], rhs=XDEC[0:64, 8 * g:8 * g + 8, :].rearrange("p a b -> p (a b)"), start=True, stop=True),
                                 reads=[btmb[q2], xdecb], writes=[pb])
                            S.op("dve", lambda e, pt=pt, g=g: e.tensor_tensor(out=SN[:, 8 * g:8 * g + 8, :].rearrange("p a b -> p (a b)"), in0=SN[:, 8 * g:8 * g + 8, :].rearrange("p a b -> p (a b)"),
                                                                             in1=pt[:, :], op=ALU.add), reads=[snb, pb], writes=[snb])
                        S.dma("sp", ssmSo_d[b], SN[:], reads=[snb], is_output=True)
                    for (g, py, pyb) in ylist:
                        finish_group(g, py, pyb, 0, 64, ZS, XT, OT)
                    continue
                S.op("dve", lambda e: e.tensor_tensor(out=ST[:, :, :], in0=ST[:, :, :], in1=DTT[:, 4, :].unsqueeze(2).broadcast_to([128, 32, 64]), op=ALU.mult), reads=[stb_, dttb], writes=[stb_])
                for g in range(4):
                    pt, pb = ps()
                    S.op("pe", lambda e, pt=pt, g=g: e.matmul(pt[:, :], lhsT=BT[:, g, :], rhs=XDEC[:, 8 * g:8 * g + 8, :].rearrange("p a b -> p (a b)"), start=True, stop=True),
                         reads=[btb, xdecb], writes=[pb])
                    S.op("dve", lambda e, pt=pt, g=g: e.tensor_tensor(out=ST[:, 8 * g:8 * g + 8, :].rearrange("p a b -> p (a b)"), in0=ST[:, 8 * g:8 * g + 8, :].rearrange("p a b -> p (a b)"),
                                                                     in1=pt[:, :], op=ALU.add), reads=[stb_, pb], writes=[stb_])
                S.op("act", lambda e: e.copy(out=STb[:, :, :], in_=ST[:, :, :]), reads=[stb_], writes=[stbb])
            swpg.flush()
            out_proj(t0, tn, ti, OT)
            if t0 + tn == TP:
                S.dma("act", ssmP_d, ST[:], reads=[stb_], is_output=True)
            if samp:
                S.dma("act", sconv_d, SCO[:], reads=[scob], is_output=True)
                PSMODE[0] = 0
    for layer in range(NL):
        S.barrier()
        if CFG["mixers"][layer]:
            if layer % 3 == 0:
                attention(layer)
            elif layer % 3 == 1:
                hgrn(layer)
            else:
                ssd(layer)
        S.barrier()
        ffn(layer)
    S.barrier()
    ar = Bump(ARENA, ARENA_SZ)
    YF = [P.sb(ar.take(KC * 512 * 4), [128, KC, 512], F32) for i in range(2)]
    a = ar.take(KC * 512 * 2 + 2048)
    yb = [Buf(), Buf()]
    yv = yT_d.rearrange("(c p) t -> p c t", p=128)
    for ti, (t0, tn) in enumerate(TILES):
        q = ti % 2
        hbufs = {ti: yb[q]}
        rmsnorm(2 * DEPTH, [ti], YF[q], hbufs, t0, a)
        S.dma("act", yv[:, :, t0:t0 + tn], YF[q][:, :, 0:tn], reads=[yb[q]], is_output=True)

    WSTREAM[0] = nc.dram_tensor("wstream", [128, max(P.woff, 1)], F32, kind="ExternalInput").ap()
    P.wlist = wlist
    S.emit()
    return P


_PROG = None


def _prog():
    global _PROG
    if _PROG is None:
        _PROG = build()
    return _PROG


def kernel(**inputs):
    I = {k: np.asarray(v) for k, v in inputs.items()}
    P = _prog()
    wstream = np.zeros((128, max(P.woff, 1)), np.float32)
    for (o, n, builder) in P.wlist:
        wstream[:, o:o + n] = builder(I)
    in_maps = []
    for c in range(NCORES):
        m = {"wstream": wstream}
        for name, (shape, builder) in P.inputs.items():
            m[name] = np.ascontiguousarray(builder(I, c), dtype=np.float32).reshape(shape)
        in_maps.append(m)
    res = run_bass_kernel_spmd(P.nc, in_maps, core_ids=list(range(NCORES)))
    R = res.results
    out = assemble(R)
    order = ["y_prompt", "y_sample", "k_p", "v_p", "hg_p", "ssm_p", "sconv_p", "fconv_p", "k_s", "v_s", "hg_s", "ssm_s", "sconv_s", "fconv_s"]
    if all(k in out for k in order):
        return tuple(np.ascontiguousarray(out[k], dtype=np.float32) for k in order)
    return out


def assemble(R):
    yT = np.stack([R[c]["yT"] for c in range(NCORES)])
    y_prompt = np.ascontiguousarray(yT[:, :, :TP].transpose(0, 2, 1))
    y_sample = np.ascontiguousarray(yT[:, :, TP:].transpose(0, 2, 1)).reshape(NCORES * NB, 4, D)
    fc = np.stack([R[c]["fconvT"] for c in range(NCORES)])
    fc = fc.transpose(1, 0, 4, 3, 2).reshape(DEPTH, NCORES, 2 + 2 * NB, DFF)
    fconv_p = np.ascontiguousarray(fc[:, :, 0:2])
    fconv_s = np.ascontiguousarray(fc[:, :, 2:].reshape(DEPTH, NCORES * NB, 2, DFF))
    out = {"y_prompt": y_prompt, "y_sample": y_sample, "fconv_p": fconv_p, "fconv_s": fconv_s}
    if "kTp" in R[0]:
        kTp = np.stack([R[c]["kTp"] for c in range(NCORES)])
        out["k_p"] = np.ascontiguousarray(kTp.transpose(1, 0, 4, 2, 3)).reshape(2, NCORES, 128, 4, 64)
        vp = np.stack([R[c]["vp"] for c in range(NCORES)])
        out["v_p"] = np.ascontiguousarray(vp.transpose(1, 0, 2, 3)).reshape(2, NCORES, 128, 4, 64)
        ks = np.stack([R[c]["ks"] for c in range(NCORES)])
        out["k_s"] = np.ascontiguousarray(ks.transpose(1, 0, 2, 3, 4)).reshape(2, NCORES * NB, 128, 4, 64)
        vs = np.stack([R[c]["vs"] for c in range(NCORES)])
        out["v_s"] = np.ascontiguousarray(vs.transpose(1, 0, 2, 3, 4)).reshape(2, NCORES * NB, 128, 4, 64)
    if "ssmP" in R[0]:
        sp = np.stack([R[c]["ssmP"] for c in range(NCORES)])
        out["ssm_p"] = np.ascontiguousarray(sp.transpose(0, 2, 3, 1))[None]
        ss_ = np.stack([R[c]["ssmSo"] for c in range(NCORES)])
        out["ssm_s"] = np.ascontiguousarray(ss_.transpose(0, 1, 3, 4, 2)).reshape(1, NCORES * NB, 32, 64, 128)
        sc = np.stack([R[c]["sconvT"] for c in range(NCORES)])
        sc = sc.transpose(0, 3, 2, 1).reshape(NCORES, 51, 3072)
        out["sconv_p"] = np.ascontiguousarray(sc[:, 0:3])[None]
        out["sconv_s"] = np.ascontiguousarray(sc[:, 3:].reshape(NCORES * NB, 3, 3072))[None]
    if "hgP" in R[0]:
        hgP = np.stack([R[c]["hgP"] for c in range(NCORES)])
        out["hg_p"] = np.ascontiguousarray(hgP.transpose(0, 2, 1, 3))[None]
        hgS = np.stack([R[c]["hgSo"] for c in range(NCORES)])
        out["hg_s"] = np.ascontiguousarray(hgS.transpose(0, 1, 3, 2, 4)).reshape(1, NCORES * NB, 8, 128, 128)
    return out
```
